# Optimizing a Trainium2 kernel written in Bass

```python
import math
import jax, jax.numpy as jnp
from jax import lax
import numpy as np

D_MODEL = 2048
BATCH = 2
SEQ = 4096
DEPTH = 4

CHUNK = 64
EPS = 1e-6
SSD_HEADDIM = 64
SSD_INNER = D_MODEL
SSD_HEADS = SSD_INNER // SSD_HEADDIM
SSD_GROUPS = 8
SSD_STATE = 128
SSD_CONV = 4
SSD_CONV_DIM = SSD_INNER + 2 * SSD_GROUPS * SSD_STATE
SC_WIDTH = D_MODEL // 2
SC_CONV = 3
ATT_HEADS = 8
ATT_HEADDIM = 128
ATT_WIDTH = ATT_HEADS * ATT_HEADDIM
ATT_SCALE = ATT_HEADDIM ** -0.5
IDX_HEADS = 16
IDX_DIM = 64
TOPK_MAX = 256
Q_BLOCK = 128
N_BRANCH = 3
FFN_HIDDEN = -(-8 * D_MODEL // (3 * 256)) * 256
IN_SIZES = (
    SSD_INNER,
    SSD_CONV_DIM,
    SSD_HEADS,
    SC_WIDTH, SC_WIDTH, SC_WIDTH,
    ATT_WIDTH, ATT_WIDTH, ATT_WIDTH,
    IDX_HEADS * IDX_DIM,
    IDX_DIM,
    IDX_HEADS,
    N_BRANCH * D_MODEL,
)
N_IN = sum(IN_SIZES)

kernel_name = "hybrid_ssd_shortconv_dsa_trunk"


def _split_points(sizes):
    pts, acc = [], 0
    for s in sizes[:-1]:
        acc += s
        pts.append(acc)
    return pts


def rmsnorm(x, g):
    xf = x.astype(jnp.float32)
    y = xf * lax.rsqrt(jnp.mean(xf * xf, axis=-1, keepdims=True) + EPS)
    return (y * g.astype(jnp.float32)).astype(x.dtype)


def causal_depthwise_conv(x, w):
    K, C = w.shape
    return lax.conv_general_dilated(
        x, w[:, None, :].astype(x.dtype), window_strides=(1,), padding=[(K - 1, 0)],
        dimension_numbers=("NWC", "WIO", "NWC"), feature_group_count=C)


def ssd_scan(x, dt, A, Bm, Cm):
    Bsz, L, H, P = x.shape
    G, N = Bm.shape[2], Bm.shape[3]
    R = H // G
    nc = L // CHUNK
    f32 = jnp.float32
    x = x.reshape(Bsz, nc, CHUNK, G, R, P).astype(f32)
    dt = dt.reshape(Bsz, nc, CHUNK, G, R)
    Bm = Bm.reshape(Bsz, nc, CHUNK, G, N).astype(f32)
    Cm = Cm.reshape(Bsz, nc, CHUNK, G, N).astype(f32)
    a_cs = jnp.cumsum(dt * A.reshape(G, R), axis=2)
    xdt = x * dt[..., None]
    seg = a_cs[:, :, :, None] - a_cs[:, :, None, :]
    causal = jnp.tril(jnp.ones((CHUNK, CHUNK), dtype=bool))
    decay = jnp.exp(jnp.where(causal[:, :, None, None], seg, -jnp.inf))
    cb = jnp.einsum("bctgn,bcsgn->bctsg", Cm, Bm)
    y_diag = jnp.einsum("bctsgr,bcsgrp->bctgrp", cb[..., None] * decay, xdt)
    decay_to_end = jnp.exp(a_cs[:, :, -1:] - a_cs)
    states = jnp.einsum("bcsgn,bcsgrp->bcgrpn", Bm, xdt * decay_to_end[..., None])
    chunk_decay = jnp.exp(a_cs[:, :, -1])

    def step(h, inp):
        s_c, d_c = inp
        return h * d_c[..., None, None] + s_c, h

    h0 = jnp.zeros((Bsz, G, R, P, N), f32)
    _, h_prev = lax.scan(step, h0, (jnp.moveaxis(states, 1, 0), jnp.moveaxis(chunk_decay, 1, 0)))
    h_prev = jnp.moveaxis(h_prev, 0, 1)
    y_off = jnp.einsum("bctgn,bcgrpn->bctgrp", Cm, h_prev) * jnp.exp(a_cs)[..., None]
    return (y_diag + y_off).reshape(Bsz, L, H, P)


def ssd_mixer(z, xbc, dt_raw, conv_w, conv_b, dt_bias, a_log, d_skip, norm_g):
    Bsz, L, _ = z.shape
    xbc = jax.nn.silu(causal_depthwise_conv(xbc, conv_w) + conv_b)
    xs, Bm, Cm = jnp.split(xbc, [SSD_INNER, SSD_INNER + SSD_GROUPS * SSD_STATE], axis=-1)
    xs = xs.reshape(Bsz, L, SSD_HEADS, SSD_HEADDIM)
    Bm = Bm.reshape(Bsz, L, SSD_GROUPS, SSD_STATE)
    Cm = Cm.reshape(Bsz, L, SSD_GROUPS, SSD_STATE)
    dt = jax.nn.softplus(dt_raw.astype(jnp.float32) + dt_bias.astype(jnp.float32))
    A = -jnp.exp(a_log.astype(jnp.float32))
    y = ssd_scan(xs, dt, A, Bm, Cm)
    y = y + d_skip.astype(jnp.float32)[:, None] * xs.astype(jnp.float32)
    y = y.reshape(Bsz, L, SSD_INNER)
    y = rmsnorm(y * jax.nn.silu(z.astype(jnp.float32)), norm_g)
    return y.astype(z.dtype)


def short_conv_mixer(b, cg, h, conv_w):
    return b * causal_depthwise_conv(cg * h, conv_w)


def dsa_attention(q, k, v, iq, ik, iw):
    Bsz, S, H, Dh = q.shape
    topk = min(TOPK_MAX, S // 4)
    nblk = S // Q_BLOCK
    key_chunk = jnp.arange(S) // CHUNK
    ik32 = ik.astype(jnp.float32)

    def block(i):
        start = i * Q_BLOCK
        qb = lax.dynamic_slice_in_dim(q, start, Q_BLOCK, axis=1).astype(jnp.float32)
        iqb = lax.dynamic_slice_in_dim(iq, start, Q_BLOCK, axis=1).astype(jnp.float32)
        iwb = lax.dynamic_slice_in_dim(iw, start, Q_BLOCK, axis=1).astype(jnp.float32)
        logits = jnp.einsum("bthd,bsd->bths", iqb, ik32)
        scores = jnp.einsum("bth,bths->bts", iwb, jax.nn.relu(logits))
        q_chunk = (start + jnp.arange(Q_BLOCK)) // CHUNK
        admissible = key_chunk[None, :] <= q_chunk[:, None]
        scores = jnp.where(admissible[None], scores, -jnp.inf)
        top_scores, idx = lax.top_k(scores, topk)
        valid = jnp.isfinite(top_scores)
        kb = jax.vmap(lambda kk, ii: kk[ii])(k, idx).astype(jnp.float32)
        vb = jax.vmap(lambda vv, ii: vv[ii])(v, idx).astype(jnp.float32)
        att = jnp.einsum("bthd,btkhd->bthk", qb, kb) * ATT_SCALE
        att = jnp.where(valid[:, :, None, :], att, -jnp.inf)
        p = jax.nn.softmax(att, axis=-1)
        return jnp.einsum("bthk,btkhd->bthd", p, vb).astype(q.dtype)

    outs = lax.map(block, jnp.arange(nblk))
    return jnp.moveaxis(outs, 0, 1).reshape(Bsz, S, H * Dh)


def hybrid_mixer(h, w_in, b_gate, ssd_conv_w, ssd_conv_b, ssd_dt_bias, ssd_a_log, ssd_d,
                 ssd_norm_g, sc_conv_w, w_br_ssd, w_br_sc, w_br_att, w_out):
    Bsz, S, D = h.shape
    proj = h @ w_in
    (z, xbc, dt_raw, sc_b, sc_c, sc_h, q, k, v, iq, ik, iw, gate_pre) = jnp.split(
        proj, _split_points(IN_SIZES), axis=-1)
    y_ssd = ssd_mixer(z, xbc, dt_raw, ssd_conv_w, ssd_conv_b, ssd_dt_bias, ssd_a_log,
                      ssd_d, ssd_norm_g)
    y_sc = short_conv_mixer(sc_b, sc_c, sc_h, sc_conv_w)
    y_att = dsa_attention(
        q.reshape(Bsz, S, ATT_HEADS, ATT_HEADDIM),
        k.reshape(Bsz, S, ATT_HEADS, ATT_HEADDIM),
        v.reshape(Bsz, S, ATT_HEADS, ATT_HEADDIM),
        iq.reshape(Bsz, S, IDX_HEADS, IDX_DIM), ik, iw)
    gates = jax.nn.sigmoid(gate_pre.reshape(Bsz, S, N_BRANCH, D) + b_gate)
    merged = (gates[:, :, 0] * (y_ssd @ w_br_ssd)
              + gates[:, :, 1] * (y_sc @ w_br_sc)
              + gates[:, :, 2] * (y_att @ w_br_att))
    return merged @ w_out


def swiglu(h, w_gate, w_up, w_down):
    return (jax.nn.silu(h @ w_gate) * (h @ w_up)) @ w_down


def setup_inputs(seed: int = 0) -> dict:
    key = jax.random.key(seed)
    ks = jax.random.split(key, 24)
    f32 = jnp.float32
    L, D = DEPTH, D_MODEL

    def nrm(k, shape, fan_in, scale=1.0):
        return jax.random.normal(k, shape, f32) * (scale * fan_in ** -0.5)

    def small(k, shape, s):
        return s * jax.random.normal(k, shape, f32)

    dt0 = jnp.exp(jax.random.uniform(ks[10], (L, SSD_HEADS), f32,
                                     minval=math.log(1e-3), maxval=math.log(1e-1)))
    return {
        "x": jax.random.normal(ks[0], (BATCH, SEQ, D), f32),
        "c": jax.random.normal(ks[1], (BATCH, D), f32),
        "w_ada": nrm(ks[2], (L, D, 6 * D), D, 0.5),
        "b_ada": small(ks[3], (L, 6 * D), 0.02),
        "g_mix": 1.0 + small(ks[4], (L, D), 0.02),
        "w_in": nrm(ks[5], (L, D, N_IN), D),
        "b_gate": small(ks[6], (L, N_BRANCH, D), 0.1),
        "ssd_conv_w": nrm(ks[7], (L, SSD_CONV, SSD_CONV_DIM), SSD_CONV),
        "ssd_conv_b": small(ks[8], (L, SSD_CONV_DIM), 0.02),
        "ssd_dt_bias": dt0 + jnp.log(-jnp.expm1(-dt0)),
        "ssd_a_log": jnp.log(jax.random.uniform(ks[11], (L, SSD_HEADS), f32, minval=1.0, maxval=16.0)),
        "ssd_d": 1.0 + small(ks[12], (L, SSD_HEADS), 0.1),
        "ssd_norm_g": 1.0 + small(ks[13], (L, SSD_INNER), 0.02),
        "sc_conv_w": nrm(ks[14], (L, SC_CONV, SC_WIDTH), SC_CONV),
        "w_br_ssd": nrm(ks[15], (L, SSD_INNER, D), SSD_INNER),
        "w_br_sc": nrm(ks[16], (L, SC_WIDTH, D), SC_WIDTH),
        "w_br_att": nrm(ks[17], (L, ATT_WIDTH, D), ATT_WIDTH),
        "w_out": nrm(ks[18], (L, D, D), D),
        "g_ffn": 1.0 + small(ks[19], (L, D), 0.02),
        "w_ffn_gate": nrm(ks[20], (L, D, FFN_HIDDEN), D),
        "w_ffn_up": nrm(ks[21], (L, D, FFN_HIDDEN), D),
        "w_ffn_down": nrm(ks[22], (L, FFN_HIDDEN, D), FFN_HIDDEN),
        "g_final": 1.0 + small(ks[23], (D,), 0.02),
    }


def reference(x, c, w_ada, b_ada, g_mix, w_in, b_gate, ssd_conv_w, ssd_conv_b, ssd_dt_bias,
              ssd_a_log, ssd_d, ssd_norm_g, sc_conv_w, w_br_ssd, w_br_sc, w_br_att, w_out,
              g_ffn, w_ffn_gate, w_ffn_up, w_ffn_down, g_final):
    cs = jax.nn.silu(c)
    for l in range(DEPTH):
        mod = cs @ w_ada[l] + b_ada[l]
        sh_m, sc_m, gt_m, sh_f, sc_f, gt_f = jnp.split(mod[:, None, :], 6, axis=-1)
        h = rmsnorm(x, g_mix[l]) * (1.0 + sc_m) + sh_m
        x = x + gt_m * hybrid_mixer(h, w_in[l], b_gate[l], ssd_conv_w[l], ssd_conv_b[l],
                                    ssd_dt_bias[l], ssd_a_log[l], ssd_d[l], ssd_norm_g[l],
                                    sc_conv_w[l], w_br_ssd[l], w_br_sc[l], w_br_att[l], w_out[l])
        h = rmsnorm(x, g_ffn[l]) * (1.0 + sc_f) + sh_f
        x = x + gt_f * swiglu(h, w_ffn_gate[l], w_ffn_up[l], w_ffn_down[l])
    return rmsnorm(x, g_final)
```

```python
import numpy as np
from contextlib import ExitStack
import concourse.bass as bass
import concourse.mybir as mybir
from concourse.bass_utils import run_bass_kernel_spmd

F32 = mybir.dt.float32
BF16 = mybir.dt.bfloat16
ALU = mybir.AluOpType
AF = mybir.ActivationFunctionType

D = 2048
KC = 16
N_IN = 19568
FFN = 5632
HC = FFN // 128
EPS = 1e-6
NEG = -1.0e30
C_Z, C_XBC, C_DT, C_SC, C_Q, C_K, C_V, C_IQ, C_IK, C_IW, C_G = (
    0, 2048, 6144, 6176, 9248, 10272, 11296, 12320, 13344, 13408, 13424)
T_XS, T_B, T_SZ, T_V, T_DT, T_A, T_IW, T_END = 0, 2048, 3072, 5120, 6144, 6176, 6208, 6224


class Res:
    __slots__ = ("name", "w", "rs", "joined")

    def __init__(self, name=""):
        self.name = name
        self.w = []
        self.rs = {}
        self.joined = False


class Sched:
    ENG = ("pe", "act", "dve", "pool", "sp")

    def __init__(self, nc, stack, n_dma_sems=12):
        self.nc = nc
        self.eng = {"pe": nc.tensor, "act": nc.scalar, "dve": nc.vector,
                    "pool": nc.gpsimd, "sp": nc.sync}
        self.ops = {k: [] for k in self.ENG}
        self.cnt = {k: 0 for k in self.ENG}
        self.seen = {k: {} for k in self.ENG}
        self.sem = {k: stack.enter_context(nc.semaphore("s_" + k)) for k in self.ENG}
        self.live = set()
        self.dpool = {}
        self.drr = {}
        for q in ("sp", "pool"):
            self.dpool[q] = [[stack.enter_context(nc.semaphore(f"d_{q}{i}")), 0, (q, i)]
                             for i in range(n_dma_sems)]
            self.drr[q] = 0

    def _need(self, E, tok):
        key, sem, val = tok
        if key == E and E == "pe":
            return
        if self.seen[E].get(key, 0) >= val:
            return
        self.seen[E][key] = val
        eng = self.eng[E]
        self.ops[E].append(lambda: eng.wait_ge(sem, val))

    def _deps(self, E, reads, writes, join=False):
        for r in reads:
            for t in r.w:
                self._need(E, t)
        for w in writes:
            if not (join and w.joined):
                for t in w.w:
                    self._need(E, t)
            for t in w.rs.values():
                self._need(E, t)

    def _commit(self, tok, reads, writes, join=False):
        for r in reads:
            r.rs[tok[0]] = tok
            self.live.add(r)
        for w in writes:
            if join and w.joined:
                w.w.append(tok)
            else:
                w.w = [tok]
            w.joined = join
            w.rs = {}
            self.live.add(w)

    def op(self, E, fn, reads=(), writes=(), sig=True):
        reads = [r for r in reads if r is not None]
        writes = [w for w in writes if w is not None]
        self._deps(E, reads, writes)
        sem = self.sem[E]
        eng = self.eng[E]
        if sig:
            self.cnt[E] += 1
            tok = (E, sem, self.cnt[E])
            self.ops[E].append(lambda: fn(eng).then_inc(sem, 1))
        else:
            assert E == "pe"
            tok = (E, sem, self.cnt[E] + 1)
            self.ops[E].append(lambda: fn(eng))
        self._commit(tok, reads, writes)
        return tok

    def dma(self, q, out, in_, reads=(), writes=(), join=False):
        reads = [r for r in reads if r is not None]
        writes = [w for w in writes if w is not None]
        pool = self.dpool[q]
        ent = pool[self.drr[q]]
        self.drr[q] = (self.drr[q] + 1) % len(pool)
        if ent[1] > 0:
            self._need(q, (ent[2], ent[0], ent[1]))
        self._deps(q, reads, writes, join)
        ent[1] += 16
        sem = ent[0]
        tok = (ent[2], sem, ent[1])
        eng = self.eng[q]
        self.ops[q].append(lambda: eng.dma_start(out=out, in_=in_).then_inc(sem, 16))
        self._commit(tok, reads, writes, join)
        return tok

    def barrier(self):
        toks = []
        for P in self.ENG:
            if self.cnt[P] > 0:
                toks.append((P, self.sem[P], self.cnt[P]))
        for q in self.dpool:
            for ent in self.dpool[q]:
                if ent[1] > 0:
                    toks.append((ent[2], ent[0], ent[1]))
        for E in self.ENG:
            for t in toks:
                self._need(E, t)
        for r in self.live:
            r.w = []
            r.rs = {}
            r.joined = False
        self.live = set()

    def emit(self):
        ops = self.ops
        with self.nc.Block() as block:
            @block.sync
            def _(e):
                for f in ops["sp"]:
                    f()

            @block.scalar
            def _(e):
                for f in ops["act"]:
                    f()

            @block.vector
            def _(e):
                for f in ops["dve"]:
                    f()

            @block.gpsimd
            def _(e):
                for f in ops["pool"]:
                    f()

            @block.tensor
            def _(e):
                for f in ops["pe"]:
                    f()


class Stage:
    def __init__(self, k):
        self.k = k
        self.st = ExitStack()
        self.n = 0
        self.cache = {}

    def __enter__(self):
        return self

    def sb(self, shape, dt, name=None):
        self.n += 1
        self.k.uid += 1
        t = self.st.enter_context(self.k.nc.sbuf_tensor(f"{name or 't'}_{self.k.uid}", list(shape), dt))
        return t, Res(name or "t")

    def __exit__(self, *a):
        self.k.S.barrier()
        self.st.close()
        return False


class Builder:
    def __init__(self, S_len, depth, dbg=()):
        self.S_len = S_len
        self.depth = depth
        self.dbg = dbg
        self.uid = 0
        self.nc = bass.Bass("TRN2", target_bir_lowering=False)
        self.top = ExitStack()
        self.S = Sched(self.nc, self.top)

    def din(self, name, shape, dt=F32):
        return self.nc.dram_tensor(name, list(shape), dt, kind="ExternalInput").ap()

    def dscr(self, name, shape, dt=F32):
        kind = "ExternalOutput" if name in self.dbg else "Internal"
        return self.nc.dram_tensor(name, list(shape), dt, kind=kind).ap(), None

    def build(self):
        nc, S, L, SL = self.nc, self.S, self.depth, self.S_len
        I = {}
        I["xT"] = self.din("xT", [D, SL])
        I["c_pk"] = self.din("c_pk", [128, KC])
        I["w_ada"] = self.din("w_ada", [L, D, 6 * D])
        I["b_ada_pk"] = self.din("b_ada_pk", [L, 128, 96])
        I["g_mix_pk"] = self.din("g_mix_pk", [L, 128, KC])
        I["g_ffn_pk"] = self.din("g_ffn_pk", [L, 128, KC])
        I["g_final_pk"] = self.din("g_final_pk", [128, KC])
        I["w_in"] = self.din("w_in", [L, D, N_IN])
        I["b_gate_pk"] = self.din("b_gate_pk", [L, 128, 48])
        I["conv_w_pk"] = self.din("conv_w_pk", [L, 128, 32, 4])
        I["conv_b_pk"] = self.din("conv_b_pk", [L, 128, 32])
        I["dtb"] = self.din("dtb", [L, 32, 1])
        I["alog"] = self.din("alog", [L, 32, 1])
        I["dsk32"] = self.din("dsk32", [L, 32])
        I["ng_row"] = self.din("ng_row", [L, D])
        I["scw_pk"] = self.din("scw_pk", [L, 128, 8, 3])
        I["w_br_ssd"] = self.din("w_br_ssd", [L, D, D])
        I["w_br_sc"] = self.din("w_br_sc", [L, 1024, D])
        I["w_br_att"] = self.din("w_br_att", [L, 1024, D])
        I["w_out"] = self.din("w_out", [L, D, D])
        I["w_g"] = self.din("w_g", [L, D, FFN])
        I["w_u"] = self.din("w_u", [L, D, FFN])
        I["w_d"] = self.din("w_d", [L, FFN, D])
        self.I = I
        self.outT = self.nc.dram_tensor("outT", [D, SL], F32, kind="ExternalOutput").ap()
        self.xres, self.r_xres = self.dscr("xres", [D, SL])
        self.TT, self.r_TT = self.dscr("TT", [T_END, SL])
        self.TOK, self.r_TOK = self.dscr("TOK", [SL, T_END])
        self.xbcT, self.r_xbcT = self.dscr("xbcT", [4096, SL])
        self.scT, self.r_scT = self.dscr("scT", [3072, SL])
        self.qT, self.r_qT = self.dscr("qT", [1024, SL], BF16)
        self.kT, self.r_kT = self.dscr("kT", [1024, SL], BF16)
        self.iqT, self.r_iqT = self.dscr("iqT", [1024, SL], BF16)
        self.ikT, self.r_ikT = self.dscr("ikT", [64, SL], BF16)
        self.gT, self.r_gT = self.dscr("gT", [3 * D, SL])
        self.BCT, self.r_BCT = self.dscr("BCT", [2048, SL], BF16)
        self.yscT, self.r_yscT = self.dscr("yscT", [1024, SL], BF16)
        self.TOK2, self.r_TOK2 = self.dscr("TOK2", [SL, 3072])
        self.TT2, self.r_TT2 = self.dscr("TT2", [3072, SL])

        top = self.top
        self.pb = []
        for i in range(8):
            t = top.enter_context(nc.psum_tensor(f"pb{i}", [128, 512], F32))
            self.pb.append((t, Res(f"pb{i}")))
        def psb(name, shape, dt=F32):
            return top.enter_context(nc.sbuf_tensor(name, list(shape), dt)), Res(name)
        self.ident, self.r_ident = psb("ident", [128, 128])
        self.ones, self.r_ones = psb("ones", [128, 128])
        self.triu, self.r_triu = psb("triu", [64, 64])
        self.ntriu, self.r_ntriu = psb("ntriu", [64, 64])
        self.r3, self.r_r3 = psb("r3", [64, 32, 64])
        self.eps_t, self.r_eps = psb("eps_t", [128, 1])
        self.nbig, self.r_nbig = psb("nbig", [128, 1])
        self.p2tab, self.r_p2tab = psb("p2tab", [128, 32])
        self.mod, self.r_mod = psb("mod", [128, L, 96])
        self.gm, self.r_gm = psb("gm", [128, L, 2, KC])
        self.csb, self.r_csb = psb("csb", [128, KC], BF16)
        ident, ones, triu, ntriu, r3 = self.ident, self.ones, self.triu, self.ntriu, self.r3
        S.op("pool", lambda e: e.memset(ident[:], 0.0), writes=[self.r_ident])
        S.op("pool", lambda e: e.affine_select(out=ident[:], in_=ident[:], pattern=[[-1, 128]],
                                               compare_op=ALU.not_equal, fill=1.0, base=0, channel_multiplier=1),
             reads=[self.r_ident], writes=[self.r_ident])
        S.op("pool", lambda e: e.memset(ones[:], 1.0), writes=[self.r_ones])
        S.op("pool", lambda e: e.memset(triu[:], 1.0), writes=[self.r_triu])
        S.op("pool", lambda e: e.affine_select(out=triu[:], in_=triu[:], pattern=[[1, 64]],
                                               compare_op=ALU.is_ge, fill=0.0, base=0, channel_multiplier=-1),
             reads=[self.r_triu], writes=[self.r_triu])
        S.op("pool", lambda e: e.tensor_scalar(out=ntriu[:], in0=triu[:], scalar1=-1.0, scalar2=None, op0=ALU.mult),
             reads=[self.r_triu], writes=[self.r_ntriu])
        S.op("pool", lambda e: e.memset(r3[:], 0.0), writes=[self.r_r3])
        S.op("pool", lambda e: e.affine_select(out=r3[:], in_=r3[:], pattern=[[0, 32], [1, 64]],
                                               compare_op=ALU.is_ge, fill=-30000.0, base=0, channel_multiplier=-1),
             reads=[self.r_r3], writes=[self.r_r3])
        S.op("dve", lambda e: e.memset(self.eps_t[:], EPS), writes=[self.r_eps])
        S.op("dve", lambda e: e.memset(self.nbig[:], NEG / 2), writes=[self.r_nbig])
        for j in range(32):
            S.op("dve", lambda e, j=j: e.memset(self.p2tab[:, j:j + 1], 2.0 ** -j), writes=[self.r_p2tab])

        self.stage_mod()
        for l in range(L):
            self.stage_proj(l, I["xT"] if l == 0 else self.xres)
            if "stop_proj" in self.dbg:
                break
            self.stage_conv(l)
            self.transpose_pass(self.TT, self.r_TT, self.TOK, self.r_TOK, T_END, SL)
            if "stop_conv" in self.dbg:
                break
            self.stage_ssd(l)
            if "stop_ssd" in self.dbg:
                break
            self.stage_attn(l)
            self.transpose_pass(self.TOK2, self.r_TOK2, self.TT2, self.r_TT2, SL, 3072)
            if "stop_attn" in self.dbg:
                break
            self.stage_merge(l, I["xT"] if l == 0 else self.xres)
            if "stop_merge" in self.dbg:
                break
            self.stage_ffn(l)
        else:
            self.stage_final()
        S.barrier()
        S.emit()
        return nc

    def load_wgroup(self, stg_t, stg_r, W, col0, ncols, nk):
        src = W[:, col0:col0 + ncols].rearrange("(kc p) n -> p kc n", p=128)
        half = nk // 2 if nk >= 8 else nk
        for k0 in range(0, nk, half):
            self.S.dma("pool", stg_t[:, k0:k0 + half, 0:ncols], src[:, k0:k0 + half, :], writes=[stg_r], join=True)

    def norm_adaln(self, stg, xt, r_xt, N, gm_ap, sh_ap, hT, r_hT):
        S = self.S
        if ("norm", N) not in stg.cache:
            stg.cache[("norm", N)] = ([stg.sb([128, 512], F32, "sq") for _ in range(2)], stg.sb([128, N], F32, "rstd"),
                                      [stg.sb([128, N], F32, "ntmp") for _ in range(2)])
        sqs, (rstd, r_rstd), tmps = stg.cache[("norm", N)]
        ones, eps_t = self.ones, self.eps_t
        for hf in range(N // 512):
            pbt, pbr = self.pb[hf % 2]
            for dc in range(KC):
                s_t, s_r = sqs[dc % 2]
                S.op("act", lambda e, s_t=s_t, dc=dc, hf=hf: e.activation(
                    out=s_t[:], in_=xt[:, dc, hf * 512:(hf + 1) * 512], func=AF.Square),
                    reads=[r_xt], writes=[s_r])
                S.op("pe", lambda e, s_t=s_t, dc=dc, pbt=pbt: e.matmul(
                    pbt[:], lhsT=ones[:], rhs=s_t[:], start=(dc == 0), stop=(dc == KC - 1)),
                    reads=[s_r, self.r_ones], writes=[pbr])
            S.op("act", lambda e, pbt=pbt, hf=hf: e.activation(
                out=rstd[:, hf * 512:(hf + 1) * 512], in_=pbt[:], func=AF.Sqrt, scale=1.0 / D, bias=eps_t[:, 0:1]),
                reads=[pbr, self.r_eps], writes=[r_rstd])
        S.op("dve", lambda e: e.reciprocal(out=rstd[:], in_=rstd[:]), reads=[r_rstd], writes=[r_rstd])
        for dc in range(KC):
            tm, r_tm = tmps[dc % 2]
            S.op("dve", lambda e, dc=dc, tm=tm: e.tensor_tensor(out=tm[:], in0=xt[:, dc, :], in1=rstd[:], op=ALU.mult),
                 reads=[r_xt, r_rstd], writes=[r_tm])
            if sh_ap is not None:
                S.op("act", lambda e, dc=dc, tm=tm: e.activation(out=hT[:, dc, :], in_=tm[:], func=AF.Identity,
                                                                 scale=gm_ap[:, dc:dc + 1], bias=sh_ap[:, dc:dc + 1]),
                     reads=[r_tm, self.r_mod, self.r_gm], writes=[r_hT])
            else:
                S.op("act", lambda e, dc=dc, tm=tm: e.activation(out=hT[:, dc, :], in_=tm[:], func=AF.Copy,
                                                                 scale=gm_ap[:, dc:dc + 1]),
                     reads=[r_tm, self.r_mod, self.r_gm], writes=[r_hT])

    def stage_mod(self):
        S, I, L = self.S, self.I, self.depth
        with Stage(self) as stg:
            cf, r_cf = stg.sb([128, KC], F32, "cf")
            S.dma("sp", cf[:], I["c_pk"], writes=[r_cf])
            S.op("act", lambda e: e.activation(out=self.csb[:], in_=cf[:], func=AF.Silu), reads=[r_cf], writes=[self.r_csb])
            wb = [stg.sb([128, KC, 512], BF16, "wada") for _ in range(2)]
            bpk, r_bpk = stg.sb([128, L, 96], F32, "bpk")
            gmx, r_gmx = stg.sb([128, L, 2, KC], F32, "gmx")
            for l in range(L):
                S.dma("sp", bpk[:, l, :], I["b_ada_pk"][l], writes=[r_bpk], join=True)
                S.dma("sp", gmx[:, l, 0, :], I["g_mix_pk"][l], writes=[r_gmx], join=True)
                S.dma("sp", gmx[:, l, 1, :], I["g_ffn_pk"][l], writes=[r_gmx], join=True)
            gi = 0
            for l in range(L):
                pbt, pbr = self.pb[l % 2]
                for g in range(24):
                    wt, wr = wb[gi % 2]
                    gi += 1
                    self.load_wgroup(wt, wr, I["w_ada"][l], g * 512, 512, KC)
                    for j in range(4):
                        col = g * 4 + j
                        for kc in range(KC):
                            S.op("pe", lambda e, wt=wt, j=j, kc=kc, col=col, pbt=pbt: e.matmul(
                                pbt[:, col:col + 1], lhsT=wt[:, kc, j * 128:(j + 1) * 128], rhs=self.csb[:, kc:kc + 1],
                                start=(kc == 0), stop=(kc == KC - 1)),
                                reads=[wr, self.r_csb], writes=[pbr], sig=(kc == KC - 1))
                S.op("dve", lambda e, l=l, pbt=pbt: e.tensor_tensor(out=self.mod[:, l, :], in0=pbt[:, 0:96], in1=bpk[:, l, :],
                                                                    op=ALU.add),
                     reads=[pbr, r_bpk], writes=[self.r_mod])
                for v, sci in ((0, 1), (1, 4)):
                    S.op("dve", lambda e, l=l, v=v, sci=sci: e.scalar_tensor_tensor(
                        out=self.gm[:, l, v, :], in0=self.mod[:, l, sci * 16:(sci + 1) * 16], scalar=1.0,
                        in1=gmx[:, l, v, :], op0=ALU.add, op1=ALU.mult),
                        reads=[self.r_mod, r_gmx], writes=[self.r_gm])

    def stage_proj(self, l, xsrc):
        S, I, SL = self.S, self.I, self.S_len
        TB = min(1024, SL)
        W = I["w_in"][l]
        segs = [
            (C_Z, 2048, "silu", self.TT, self.r_TT, T_SZ),
            (C_XBC, 4096, "copy", self.xbcT, self.r_xbcT, 0),
            (C_DT, 32, "dt", self.TT, self.r_TT, T_DT),
            (C_SC, 3072, "copy", self.scT, self.r_scT, 0),
            (C_Q, 1024, "copy16", self.qT, self.r_qT, 0),
            (C_K, 1024, "copy16", self.kT, self.r_kT, 0),
            (C_V, 1024, "copy", self.TT, self.r_TT, T_V),
            (C_IQ, 1024, "copy16", self.iqT, self.r_iqT, 0),
            (C_IK, 64, "copy16", self.ikT, self.r_ikT, 0),
            (C_IW, 16, "copy", self.TT, self.r_TT, T_IW),
            (C_G, 6144, "gate", self.gT, self.r_gT, 0),
        ]
        with Stage(self) as stg:
            xt, r_xt = stg.sb([128, KC, TB], F32, "xt")
            hT, r_hT = stg.sb([128, KC, TB], BF16, "hT")
            wb = [stg.sb([128, KC, 512], BF16, "win") for _ in range(2)]
            ob32 = [stg.sb([128, TB], F32, "ob32") for _ in range(3)]
            ob16 = [stg.sb([128, TB], BF16, "ob16") for _ in range(2)]
            bg, r_bg = stg.sb([128, 48], F32, "bg")
            dtb, r_dtb = stg.sb([32, 1], F32, "dtb")
            nA, r_nA = stg.sb([32, 1], F32, "nA")
            av, r_av = stg.sb([32, TB], F32, "av")
            S.dma("sp", bg[:], I["b_gate_pk"][l], writes=[r_bg])
            S.dma("sp", dtb[:], I["dtb"][l], writes=[r_dtb])
            S.dma("sp", nA[:], I["alog"][l], writes=[r_nA])
            S.op("act", lambda e: e.activation(out=nA[:], in_=nA[:], func=AF.Exp), reads=[r_nA], writes=[r_nA])
            S.op("dve", lambda e: e.tensor_scalar(out=nA[:], in0=nA[:], scalar1=-1.0, scalar2=None, op0=ALU.mult),
                 reads=[r_nA], writes=[r_nA])
            gi = 0
            oi = 0
            ev = 0
            for tb in range(SL // TB):
                t0 = tb * TB
                xv = xsrc[:, t0:t0 + TB].rearrange("(dc p) t -> p dc t", p=128)
                for q4 in range(4):
                    S.dma("sp", xt[:, q4 * 4:(q4 + 1) * 4, :], xv[:, q4 * 4:(q4 + 1) * 4, :],
                          reads=[self.r_xres], writes=[r_xt], join=True)
                self.norm_adaln(stg, xt, r_xt, TB, self.gm[:, l, 0, :], self.mod[:, l, 0:16], hT, r_hT)
                for (c0, ncols, kind, dst, dst_r, drow) in segs:
                    for g0 in range(0, ncols, 512):
                        gn = min(512, ncols - g0)
                        wt, wr = wb[gi % 2]
                        gi += 1
                        self.load_wgroup(wt, wr, W, c0 + g0, gn, KC)
                        for j0 in range(0, gn, 128):
                            m = min(128, gn - j0)
                            use16 = kind == "copy16"
                            if use16:
                                ot, orr = ob16[oi % 2]
                            else:
                                ot, orr = ob32[oi % 3]
                            oi += 1
                            for hf in range(TB // 512):
                                pbt, pbr = self.pb[2 + (ev % 4)]
                                for kc in range(KC):
                                    S.op("pe", lambda e, wt=wt, kc=kc, j0=j0, m=m, hf=hf, pbt=pbt: e.matmul(
                                        pbt[0:m, :], lhsT=wt[:, kc, j0:j0 + m], rhs=hT[:, kc, hf * 512:(hf + 1) * 512],
                                        start=(kc == 0), stop=(kc == KC - 1)),
                                        reads=[wr, r_hT], writes=[pbr], sig=(kc == KC - 1))
                                osl = ot[0:m, hf * 512:(hf + 1) * 512]
                                if kind == "silu":
                                    S.op("act", lambda e, osl=osl, pbt=pbt, m=m: e.activation(out=osl, in_=pbt[0:m, :], func=AF.Silu),
                                         reads=[pbr], writes=[orr])
                                elif kind == "gate":
                                    gc = (g0 + j0) // 128
                                    S.op("act", lambda e, osl=osl, pbt=pbt, gc=gc: e.activation(
                                        out=osl, in_=pbt[:, :], func=AF.Sigmoid, bias=bg[:, gc:gc + 1]),
                                        reads=[pbr, r_bg], writes=[orr])
                                elif kind == "dt":
                                    S.op("act", lambda e, osl=osl, pbt=pbt: e.activation(
                                        out=osl, in_=pbt[0:32, :], func=AF.Exp, bias=dtb[:, 0:1]),
                                        reads=[pbr, r_dtb], writes=[orr])
                                    S.op("act", lambda e, osl=osl: e.activation(out=osl, in_=osl, func=AF.Ln, bias=1.0),
                                         reads=[orr], writes=[orr])
                                    S.op("dve", lambda e, osl=osl, hf=hf: e.tensor_scalar(
                                        out=av[:, hf * 512:(hf + 1) * 512], in0=osl, scalar1=nA[:, 0:1], scalar2=None, op0=ALU.mult),
                                        reads=[orr, r_nA], writes=[r_av])
                                else:
                                    if ev % 2 == 0:
                                        S.op("act", lambda e, osl=osl, pbt=pbt, m=m: e.copy(osl, pbt[0:m, :]),
                                             reads=[pbr], writes=[orr])
                                    else:
                                        S.op("dve", lambda e, osl=osl, pbt=pbt, m=m: e.tensor_copy(osl, pbt[0:m, :]),
                                             reads=[pbr], writes=[orr])
                                ev += 1
                            r0 = drow + g0 + j0
                            S.dma("sp", dst[r0:r0 + m, t0:t0 + TB], ot[0:m, :], reads=[orr], writes=[dst_r])
                            if kind == "dt":
                                S.dma("sp", self.TT[T_A:T_A + 32, t0:t0 + TB], av[:, :], reads=[r_av], writes=[self.r_TT])

    def stage_conv(self, l):
        S, I, SL = self.S, self.I, self.S_len
        TB = min(1024, SL)
        with Stage(self) as stg:
            cw, r_cw = stg.sb([128, 32, 4], F32, "cw")
            cb, r_cb = stg.sb([128, 32], F32, "cb")
            sw, r_sw = stg.sb([128, 8, 3], F32, "sw")
            S.dma("sp", cw[:], I["conv_w_pk"][l], writes=[r_cw])
            S.dma("sp", cb[:], I["conv_b_pk"][l], writes=[r_cb])
            S.dma("sp", sw[:], I["scw_pk"][l], writes=[r_sw])
            xin = [stg.sb([128, TB + 3], F32, "xin") for _ in range(2)]
            acc = [stg.sb([128, TB], F32, "acc") for _ in range(2)]
            o32 = [stg.sb([128, TB], F32, "o32") for _ in range(2)]
            o16 = [stg.sb([128, TB], BF16, "o16") for _ in range(2)]
            it = 0
            for rc in range(32):
                for tb in range(SL // TB):
                    t0 = tb * TB
                    xi, xr = xin[it % 2]
                    ac, ar = acc[it % 2]
                    o3, o3r = o32[it % 2]
                    o6, o6r = o16[it % 2]
                    it += 1
                    rows = slice(rc * 128, (rc + 1) * 128)
                    if tb == 0:
                        S.op("dve", lambda e, xi=xi: e.memset(xi[:, 0:3], 0.0), writes=[xr])
                        S.dma("sp", xi[:, 3:3 + TB], self.xbcT[rows, 0:TB], reads=[self.r_xbcT], writes=[xr])
                    else:
                        S.dma("sp", xi[:, :], self.xbcT[rows, t0 - 3:t0 + TB], reads=[self.r_xbcT], writes=[xr])
                    S.op("act", lambda e, xi=xi, ac=ac, rc=rc: e.activation(
                        out=ac[:], in_=xi[:, 3:3 + TB], func=AF.Identity, scale=cw[:, rc, 3:4], bias=cb[:, rc:rc + 1]),
                        reads=[xr, r_cw, r_cb], writes=[ar])
                    for k in (2, 1, 0):
                        S.op("dve", lambda e, xi=xi, ac=ac, rc=rc, k=k: e.scalar_tensor_tensor(
                            out=ac[:], in0=xi[:, k:k + TB], scalar=cw[:, rc, k:k + 1], in1=ac[:], op0=ALU.mult, op1=ALU.add),
                            reads=[xr, r_cw, ar], writes=[ar])
                    if rc < 24:
                        S.op("act", lambda e, ac=ac, o3=o3: e.activation(out=o3[:], in_=ac[:], func=AF.Silu),
                             reads=[ar], writes=[o3r])
                        S.dma("pool", self.TT[rc * 128:(rc + 1) * 128, t0:t0 + TB], o3[:], reads=[o3r], writes=[self.r_TT])
                        if rc >= 16:
                            S.op("dve", lambda e, o3=o3, o6=o6: e.tensor_copy(o6[:], o3[:]), reads=[o3r], writes=[o6r])
                    else:
                        S.op("act", lambda e, ac=ac, o6=o6: e.activation(out=o6[:], in_=ac[:], func=AF.Silu),
                             reads=[ar], writes=[o6r])
                    if rc >= 16:
                        S.dma("pool", self.BCT[(rc - 16) * 128:(rc - 15) * 128, t0:t0 + TB], o6[:], reads=[o6r], writes=[self.r_BCT])
            cin = [stg.sb([128, TB + 2], F32, "cin") for _ in range(2)]
            hin = [stg.sb([128, TB + 2], F32, "hin") for _ in range(2)]
            bin_ = [stg.sb([128, TB], F32, "bin") for _ in range(2)]
            for rc in range(8):
                for tb in range(SL // TB):
                    t0 = tb * TB
                    ci, cr = cin[it % 2]
                    hi, hr = hin[it % 2]
                    bi, br = bin_[it % 2]
                    ac, ar = acc[it % 2]
                    o6, o6r = o16[it % 2]
                    it += 1
                    if tb == 0:
                        S.op("dve", lambda e, ci=ci: e.memset(ci[:, 0:2], 0.0), writes=[cr])
                        S.op("dve", lambda e, hi=hi: e.memset(hi[:, 0:2], 0.0), writes=[hr])
                        S.dma("sp", ci[:, 2:2 + TB], self.scT[1024 + rc * 128:1024 + (rc + 1) * 128, 0:TB],
                              reads=[self.r_scT], writes=[cr])
                        S.dma("sp", hi[:, 2:2 + TB], self.scT[2048 + rc * 128:2048 + (rc + 1) * 128, 0:TB],
                              reads=[self.r_scT], writes=[hr])
                    else:
                        S.dma("sp", ci[:, :], self.scT[1024 + rc * 128:1024 + (rc + 1) * 128, t0 - 2:t0 + TB],
                              reads=[self.r_scT], writes=[cr])
                        S.dma("sp", hi[:, :], self.scT[2048 + rc * 128:2048 + (rc + 1) * 128, t0 - 2:t0 + TB],
                              reads=[self.r_scT], writes=[hr])
                    S.dma("sp", bi[:, :], self.scT[rc * 128:(rc + 1) * 128, t0:t0 + TB], reads=[self.r_scT], writes=[br])
                    S.op("dve", lambda e, ci=ci, hi=hi: e.tensor_tensor(out=ci[:], in0=ci[:], in1=hi[:], op=ALU.mult),
                         reads=[cr, hr], writes=[cr])
                    S.op("act", lambda e, ci=ci, ac=ac, rc=rc: e.activation(
                        out=ac[:], in_=ci[:, 2:2 + TB], func=AF.Copy, scale=sw[:, rc, 2:3]),
                        reads=[cr, r_sw], writes=[ar])
                    for k in (1, 0):
                        S.op("dve", lambda e, ci=ci, ac=ac, rc=rc, k=k: e.scalar_tensor_tensor(
                            out=ac[:], in0=ci[:, k:k + TB], scalar=sw[:, rc, k:k + 1], in1=ac[:], op0=ALU.mult, op1=ALU.add),
                            reads=[cr, r_sw, ar], writes=[ar])
                    S.op("dve", lambda e, ac=ac, bi=bi, o6=o6: e.tensor_tensor(out=o6[:], in0=ac[:], in1=bi[:], op=ALU.mult),
                         reads=[ar, br], writes=[o6r])
                    S.dma("pool", self.yscT[rc * 128:(rc + 1) * 128, t0:t0 + TB], o6[:], reads=[o6r], writes=[self.r_yscT])

    def transpose_pass(self, src, r_src, dst, r_dst, R, C):
        S = self.S
        with Stage(self) as stg:
            it_ = [stg.sb([128, 8, 512], F32, "tin") for _ in range(2)]
            ot_ = [stg.sb([128, 4, 1024], F32, "tout") for _ in range(2)]
            it = 0
            ev = 0
            for r0 in range(0, R, 1024):
                rn = min(1024, R - r0)
                nch = (rn + 127) // 128
                for c0 in range(0, C, 512):
                    ti, tir = it_[it % 2]
                    to, tor = ot_[it % 2]
                    it += 1
                    nfull = rn // 128
                    if nfull:
                        S.dma("sp", ti[:, 0:nfull, :],
                              src[r0:r0 + nfull * 128, c0:c0 + 512].rearrange("(k p) c -> p k c", p=128),
                              reads=[r_src], writes=[tir], join=True)
                    if rn % 128:
                        mm = rn % 128
                        S.dma("sp", ti[0:mm, nfull, :], src[r0 + nfull * 128:r0 + rn, c0:c0 + 512],
                              reads=[r_src], writes=[tir], join=True)
                    for j in range(4):
                        for k4 in range(0, nch, 4):
                            pbt, pbr = self.pb[ev % 4]
                            kn = min(4, nch - k4)
                            wtot = 0
                            for k in range(k4, k4 + kn):
                                m = min(128, rn - k * 128)
                                S.op("pe", lambda e, ti=ti, k=k, j=j, m=m, pbt=pbt, k4=k4: e.transpose(
                                    pbt[:, (k - k4) * 128:(k - k4) * 128 + m], ti[0:m, k, j * 128:(j + 1) * 128], self.ident[0:m, 0:m]),
                                    reads=[tir, self.r_ident], writes=[pbr])
                                wtot = (k - k4) * 128 + m
                            dsl = to[:, j, k4 * 128:k4 * 128 + wtot]
                            if ev % 2 == 0:
                                S.op("act", lambda e, dsl=dsl, pbt=pbt, wtot=wtot: e.copy(dsl, pbt[:, 0:wtot]), reads=[pbr], writes=[tor])
                            else:
                                S.op("dve", lambda e, dsl=dsl, pbt=pbt, wtot=wtot: e.tensor_copy(dsl, pbt[:, 0:wtot]), reads=[pbr], writes=[tor])
                            ev += 1
                    S.dma("pool", dst[c0:c0 + 512, r0:r0 + rn].rearrange("(j p) r -> p j r", p=128), to[:, :, 0:rn],
                          reads=[tor], writes=[r_dst])

    def stage_ssd(self, l):
        S, I, SL = self.S, self.I, self.S_len
        pb = self.pb
        A_, B_, C_, Z_, CB_ = (pb[0], pb[1]), (pb[2], pb[3]), (pb[4], pb[5]), pb[6], pb[7]
        with Stage(self) as stg:
            dbc, r_dbc = stg.sb([64, 32], F32, "dbc")
            ngb, r_ngb = stg.sb([64, D], F32, "ngb")
            S.dma("sp", dbc[:], I["dsk32"][l:l + 1, :].to_broadcast([64, 32]), writes=[r_dbc])
            S.dma("sp", ngb[:], I["ng_row"][l:l + 1, :].to_broadcast([64, D]), writes=[r_ngb])
            h32, r_h32 = stg.sb([128, D], F32, "h32")
            h16, r_h16 = stg.sb([128, D], BF16, "h16")
            S.op("dve", lambda e: e.memset(h32[:], 0.0), writes=[r_h32])
            S.op("dve", lambda e: e.memset(h16[:], 0.0), writes=[r_h16])
            tokx_ = [stg.sb([64, 4096], F32, "tokx") for _ in range(3)]
            dta_ = [stg.sb([64, 64], F32, "dta") for _ in range(3)]
            bcb_ = [stg.sb([128, 16, 256], BF16, "bcb") for _ in range(2)]
            acs, r_acs = stg.sb([64, 32], F32, "acs")
            ecs_ = [stg.sb([64, 32], F32, "ecs") for _ in range(2)]
            dte, r_dte = stg.sb([64, 32], F32, "dte")
            cd_ = [stg.sb([128, 32], F32, "cd") for _ in range(2)]
            R1, r_R1 = stg.sb([64, 32, 64], F32, "R1")
            R2, r_R2 = stg.sb([64, 32, 64], F32, "R2")
            xdt_ = [stg.sb([64, D], BF16, "xdt") for _ in range(2)]
            xdtd_ = [stg.sb([64, D], BF16, "xdtd") for _ in range(2)]
            bt16_ = [stg.sb([64, 1024], BF16, "bt16") for _ in range(3)]
            cbs, r_cbs = stg.sb([64, 512], BF16, "cbs")
            LT, r_LT = stg.sb([64, D], BF16, "LT")
            MT_ = [stg.sb([64, D], BF16, "MT") for _ in range(2)]
            yv, r_yv = stg.sb([64, 1024], F32, "yv")
            t2, r_t2 = stg.sb([64, 1024], F32, "t2")
            gb, r_gb = stg.sb([64, D], F32, "gb")
            gn_ = [stg.sb([64, D], F32, "gn") for _ in range(1)]
            hs, r_hs = stg.sb([128, 1024], F32, "hs")
            ss, r_ss = stg.sb([64, 2], F32, "ss")
            triu, ntriu, ones, r3, ident = self.triu, self.ntriu, self.ones, self.r3, self.ident
            nchunk = SL // 64

            def loads(c):
                t0 = c * 64
                tokx, r_tokx = tokx_[c % 3]
                dta, r_dta = dta_[c % 3]
                bt16, r_bt16 = bt16_[c % 3]
                if c % 4 == 0:
                    bcb, r_bcb = bcb_[(c // 4) % 2]
                    S.dma("sp", bcb[:], self.BCT[:, t0:t0 + 256].rearrange("(g n) t -> n g t", n=128), writes=[r_bcb])
                S.dma("sp", tokx[:, 0:D], self.TOK[t0:t0 + 64, 0:D], writes=[r_tokx], join=True)
                S.dma("sp", tokx[:, D:2 * D], self.TOK[t0:t0 + 64, T_SZ:T_SZ + D], writes=[r_tokx], join=True)
                S.dma("sp", dta[:], self.TOK[t0:t0 + 64, T_DT:T_DT + 64], writes=[r_dta])
                S.dma("pool", bt16[:], self.TOK[t0:t0 + 64, T_B:T_B + 1024], writes=[r_bt16])

            def phase1(c):
                tokx, r_tokx = tokx_[c % 3]
                dta, r_dta = dta_[c % 3]
                bcb, r_bcb = bcb_[(c // 4) % 2]
                ecs, r_ecs = ecs_[c % 2]
                cd, r_cd = cd_[c % 2]
                xdt, r_xdt = xdt_[c % 2]
                xdtd, r_xdtd = xdtd_[c % 2]
                MT, r_MT = MT_[c % 2]
                tq = (c % 4) * 64
                dt_ap = dta[:, 0:32]
                a_ap = dta[:, 32:64]
                zt, zr = Z_
                S.op("pe", lambda e: e.matmul(zt[0:64, 0:32], lhsT=triu[:, :], rhs=a_ap, start=True, stop=True),
                     reads=[r_dta, self.r_triu], writes=[zr])
                S.op("pe", lambda e: e.matmul(zt[:, 32:64], lhsT=ones[0:64, :], rhs=a_ap, start=True, stop=True),
                     reads=[r_dta, self.r_ones], writes=[zr])
                S.op("act", lambda e: e.copy(acs[:], zt[0:64, 0:32]), reads=[zr], writes=[r_acs])
                S.op("act", lambda e: e.activation(out=ecs[:], in_=zt[0:64, 0:32], func=AF.Exp), reads=[zr], writes=[r_ecs])
                S.op("act", lambda e: e.activation(out=cd[:], in_=zt[:, 32:64], func=AF.Exp), reads=[zr], writes=[r_cd])
                S.op("dve", lambda e: e.tensor_tensor(out=dte[:], in0=zt[0:64, 32:64], in1=acs[:], op=ALU.subtract),
                     reads=[zr, r_acs], writes=[r_dte])
                S.op("act", lambda e: e.activation(out=dte[:], in_=dte[:], func=AF.Exp), reads=[r_dte], writes=[r_dte])
                S.op("dve", lambda e: e.tensor_tensor(
                    out=R1[:], in0=a_ap.unsqueeze(2).to_broadcast([64, 32, 64]),
                    in1=triu[:, :].unsqueeze(1).to_broadcast([64, 32, 64]), op=ALU.mult),
                    reads=[r_dta, self.r_triu], writes=[r_R1])
                S.op("pool", lambda e: e.tensor_copy(R2[:], a_ap.unsqueeze(2).to_broadcast([64, 32, 64])),
                     reads=[r_dta], writes=[r_R2])
                S.op("dve", lambda e: e.tensor_tensor(
                    out=xdt[:].rearrange("s (h p) -> s h p", h=32), in0=tokx[:, 0:D].rearrange("s (h p) -> s h p", h=32),
                    in1=dt_ap.unsqueeze(2).to_broadcast([64, 32, 64]), op=ALU.mult),
                    reads=[r_tokx, r_dta], writes=[r_xdt])
                S.op("dve", lambda e: e.tensor_tensor(
                    out=xdtd[:].rearrange("s (h p) -> s h p", h=32), in0=xdt[:].rearrange("s (h p) -> s h p", h=32),
                    in1=dte[:, :].unsqueeze(2).to_broadcast([64, 32, 64]), op=ALU.mult),
                    reads=[r_xdt, r_dte], writes=[r_xdtd])
                cbt, cbr = CB_
                for g in range(8):
                    S.op("pe", lambda e, g=g: e.matmul(
                        cbt[0:64, g * 64:(g + 1) * 64], lhsT=bcb[:, g, tq:tq + 64], rhs=bcb[:, 8 + g, tq:tq + 64],
                        start=True, stop=True), reads=[r_bcb], writes=[cbr], sig=(g == 7))
                S.op("act", lambda e: e.copy(cbs[:], cbt[0:64, :]), reads=[cbr], writes=[r_cbs])
                for q in range(4):
                    at, ar = A_[q % 2]
                    hsl = slice(q * 8, q * 8 + 8)
                    S.op("pe", lambda e, at=at, hsl=hsl: e.matmul(at[0:64, :], lhsT=ones[0:64, 0:64], rhs=R1[:, hsl, :],
                                                                  start=True, stop=False),
                         reads=[r_R1, self.r_ones], writes=[ar], sig=False)
                    S.op("pe", lambda e, at=at, hsl=hsl: e.matmul(at[0:64, :], lhsT=ntriu[:, :], rhs=R2[:, hsl, :],
                                                                  start=False, stop=False),
                         reads=[r_R2, self.r_ntriu], writes=[ar], sig=False)
                    S.op("pe", lambda e, at=at, hsl=hsl: e.matmul(at[0:64, :], lhsT=ident[0:64, 0:64], rhs=r3[:, hsl, :],
                                                                  start=False, stop=True),
                         reads=[self.r_r3, self.r_ident], writes=[ar])
                    S.op("act", lambda e, at=at, q=q: e.activation(out=LT[:, q * 512:(q + 1) * 512], in_=at[0:64, :], func=AF.Exp),
                         reads=[ar], writes=[r_LT])
                S.op("dve", lambda e: e.tensor_tensor(
                    out=MT[:].rearrange("s (g r t) -> s g r t", g=8, r=4),
                    in0=LT[:].rearrange("s (g r t) -> s g r t", g=8, r=4),
                    in1=cbs[:, :].rearrange("s (g t) -> s g t", g=8).unsqueeze(2).to_broadcast([64, 8, 4, 64]),
                    op=ALU.mult), reads=[r_LT, r_cbs], writes=[r_MT])

            def phase2(c):
                t0 = c * 64
                tokx, r_tokx = tokx_[c % 3]
                bcb, r_bcb = bcb_[(c // 4) % 2]
                ecs, r_ecs = ecs_[c % 2]
                cd, r_cd = cd_[c % 2]
                xdt, r_xdt = xdt_[c % 2]
                xdtd, r_xdtd = xdtd_[c % 2]
                bt16, r_bt16 = bt16_[c % 3]
                MT, r_MT = MT_[c % 2]
                gnt, r_gnt = gn_[0]
                tq = (c % 4) * 64
                for hh in range(2):
                    hs0 = hh * 16
                    for h in range(16):
                        bt_, br_ = B_[h // 8]
                        S.op("pe", lambda e, h=h, bt_=bt_, hs0=hs0: e.matmul(
                            bt_[0:64, (h % 8) * 64:(h % 8 + 1) * 64], lhsT=MT[:, (hs0 + h) * 64:(hs0 + h + 1) * 64],
                            rhs=xdt[:, (hs0 + h) * 64:(hs0 + h + 1) * 64], start=True, stop=True),
                            reads=[r_MT, r_xdt], writes=[br_], sig=(h % 8 == 7))
                    for g in range(4):
                        ct_, cr_ = C_[g // 2]
                        gg = hh * 4 + g
                        S.op("pe", lambda e, g=g, gg=gg, ct_=ct_: e.matmul(
                            ct_[0:64, (g % 2) * 256:(g % 2 + 1) * 256], lhsT=bcb[:, 8 + gg, tq:tq + 64],
                            rhs=h16[:, gg * 256:(gg + 1) * 256], start=True, stop=True),
                            reads=[r_bcb, r_h16], writes=[cr_], sig=(g % 2 == 1))
                    for b in range(2):
                        ct_, cr_ = C_[b]
                        bt_, br_ = B_[b]
                        S.op("dve", lambda e, ct_=ct_, b=b, hs0=hs0: e.tensor_tensor(
                            out=yv[:, b * 512:(b + 1) * 512].rearrange("s (h p) -> s h p", h=8),
                            in0=ct_[0:64, :].rearrange("s (h p) -> s h p", h=8),
                            in1=ecs[:, hs0 + b * 8:hs0 + b * 8 + 8].unsqueeze(2).to_broadcast([64, 8, 64]), op=ALU.mult),
                            reads=[cr_, r_ecs], writes=[r_yv])
                        S.op("dve", lambda e, bt_=bt_, b=b: e.tensor_tensor(
                            out=yv[:, b * 512:(b + 1) * 512], in0=yv[:, b * 512:(b + 1) * 512], in1=bt_[0:64, :], op=ALU.add),
                            reads=[br_, r_yv], writes=[r_yv])
                    for g in range(4):
                        ct_, cr_ = C_[g // 2]
                        gg = hh * 4 + g
                        S.op("pe", lambda e, g=g, gg=gg, ct_=ct_: e.matmul(
                            ct_[:, (g % 2) * 256:(g % 2 + 1) * 256], lhsT=bt16[:, gg * 128:(gg + 1) * 128],
                            rhs=xdtd[:, gg * 256:(gg + 1) * 256], start=True, stop=True),
                            reads=[r_bt16, r_xdtd], writes=[cr_], sig=(g % 2 == 1))
                    S.op("pool", lambda e, hh=hh: e.tensor_tensor(
                        out=t2[:].rearrange("s (h p) -> s h p", h=16),
                        in0=tokx[:, hh * 1024:(hh + 1) * 1024].rearrange("s (h p) -> s h p", h=16),
                        in1=dbc[:, hh * 16:(hh + 1) * 16].unsqueeze(2).to_broadcast([64, 16, 64]), op=ALU.mult),
                        reads=[r_tokx, r_dbc], writes=[r_t2])
                    S.op("pool", lambda e: e.tensor_tensor(out=t2[:], in0=t2[:], in1=yv[:], op=ALU.add),
                         reads=[r_t2, r_yv], writes=[r_t2])
                    S.op("dve", lambda e, hh=hh: e.tensor_tensor(
                        out=gb[:, hh * 1024:(hh + 1) * 1024], in0=t2[:], in1=tokx[:, D + hh * 1024:D + (hh + 1) * 1024], op=ALU.mult),
                        reads=[r_t2, r_tokx], writes=[r_gb])
                    S.op("dve", lambda e, hh=hh, hs0=hs0: e.tensor_tensor(
                        out=hs[:].rearrange("n (h p) -> n h p", h=16),
                        in0=h32[:, hh * 1024:(hh + 1) * 1024].rearrange("n (h p) -> n h p", h=16),
                        in1=cd[:, hs0:hs0 + 16].unsqueeze(2).to_broadcast([128, 16, 64]), op=ALU.mult),
                        reads=[r_h32, r_cd], writes=[r_hs])
                    for b in range(2):
                        ct_, cr_ = C_[b]
                        S.op("dve", lambda e, ct_=ct_, b=b, hh=hh: e.tensor_tensor(
                            out=h32[:, hh * 1024 + b * 512:hh * 1024 + (b + 1) * 512], in0=hs[:, b * 512:(b + 1) * 512],
                            in1=ct_[:, :], op=ALU.add), reads=[r_hs, cr_], writes=[r_h32])
                    S.op("act", lambda e, hh=hh: e.copy(h16[:, hh * 1024:(hh + 1) * 1024], h32[:, hh * 1024:(hh + 1) * 1024]),
                         reads=[r_h32], writes=[r_h16])
                S.op("act", lambda e: e.activation(out=gnt[:], in_=gb[:], func=AF.Square, accum_out=ss[:, 0:1]),
                     reads=[r_gb], writes=[r_gnt, r_ss])
                S.op("act", lambda e: e.activation(out=ss[:, 1:2], in_=ss[:, 0:1], func=AF.Sqrt, scale=1.0 / D, bias=self.eps_t[0:64, 0:1]),
                     reads=[r_ss, self.r_eps], writes=[r_ss])
                S.op("dve", lambda e: e.reciprocal(out=ss[:, 1:2], in_=ss[:, 1:2]), reads=[r_ss], writes=[r_ss])
                S.op("dve", lambda e: e.scalar_tensor_tensor(out=gnt[:], in0=gb[:], scalar=ss[:, 1:2], in1=ngb[:],
                                                             op0=ALU.mult, op1=ALU.mult),
                     reads=[r_gb, r_ss, r_ngb], writes=[r_gnt])
                S.dma("sp", self.TOK2[t0:t0 + 64, 0:D], gnt[:], reads=[r_gnt])

            loads(0)
            if nchunk > 1:
                loads(1)
            phase1(0)
            for c in range(nchunk):
                if c + 2 < nchunk:
                    loads(c + 2)
                if c + 1 < nchunk:
                    phase1(c + 1)
                phase2(c)

    def stage_attn(self, l):
        S, I, SL = self.S, self.I, self.S_len
        pb = self.pb
        NT = SL // 128
        NQB = SL // 512
        NBIS = 22
        scale = 128.0 ** -0.5
        with Stage(self) as stg:
            ik2, r_ik2 = stg.sb([128, SL], BF16, "ik2")
            S.dma("sp", ik2[0:64, :], self.ikT[:, :], writes=[r_ik2], join=True)
            S.dma("sp", ik2[64:128, :], self.ikT[:, :], writes=[r_ik2], join=True)
            acc, r_acc = stg.sb([128, SL], F32, "acc")
            wk, r_wk = stg.sb([128, SL], F32, "wk")
            bs, r_bs = stg.sb([128, 8], F32, "bs")
            wtab, r_wtab = stg.sb([128, 32], F32, "wtab")
            mT_ = [stg.sb([128, NT, 512], BF16, "maskT") for _ in range(2)]
            iq_ = [stg.sb([128, 8, 128], BF16, "iqt") for _ in range(2)]
            wt_ = [stg.sb([128, 16], F32, "wtok") for _ in range(2)]
            rb_ = [stg.sb([128, 512], F32, "rbuf") for _ in range(4)]
            tp_ = [stg.sb([128, 512], F32, "tp") for _ in range(2)]
            kh_ = [stg.sb([128, SL], BF16, "kh") for _ in range(2)]
            qh_ = [stg.sb([128, 512], BF16, "qh") for _ in range(2)]
            va_ = [stg.sb([128, NT, 129], BF16, "va") for _ in range(2)]
            pt_ = [stg.sb([128, 512], BF16, "pt") for _ in range(3)]
            ya_ = [stg.sb([128, 4, 1024], BF16, "ya") for _ in range(1)]
            rd, r_rd = stg.sb([128, 4], F32, "rd")
            for v in range(2):
                S.op("pool", lambda e, v=v: e.memset(va_[v][0][:, :, 128:129], 1.0), writes=[va_[v][1]])
            st = {"qi": 0, "ri": 0, "pi": 0, "hi": 0, "bi": 0}

            def index_tile(qb, u):
                maskT, r_maskT = mT_[qb % 2]
                qt = 4 * qb + u
                t0 = qt * 128
                Kq = 128 * (qt + 1)
                iqt, r_iqt = iq_[st["qi"] % 2]
                wtk, r_wtk = wt_[st["qi"] % 2]
                st["qi"] += 1
                S.dma("sp", iqt[:], self.iqT[:, t0:t0 + 128].rearrange("(c p) t -> p c t", p=128), writes=[r_iqt])
                S.dma("sp", wtk[:], self.TOK[t0:t0 + 128, T_IW:T_IW + 16], writes=[r_wtk])
                for s0 in range(0, Kq, 512):
                    ncol = min(512, Kq - s0)
                    tp, r_tp = tp_[st["bi"] % 2]
                    st["bi"] += 1
                    for h in range(16):
                        pbt, pbr = pb[2 + h % 2]
                        hp = (h % 2) * 64
                        S.op("pe", lambda e, iqt=iqt, h=h, hp=hp, s0=s0, ncol=ncol, pbt=pbt: e.matmul(
                            pbt[:, 0:ncol], lhsT=iqt[hp:hp + 64, h // 2, :], rhs=ik2[hp:hp + 64, s0:s0 + ncol],
                            start=True, stop=True), reads=[r_iqt, r_ik2], writes=[pbr])
                        rbt, rbr = rb_[st["ri"] % 4]
                        st["ri"] += 1
                        S.op("act", lambda e, rbt=rbt, pbt=pbt, ncol=ncol: e.activation(
                            out=rbt[:, 0:ncol], in_=pbt[:, 0:ncol], func=AF.Relu), reads=[pbr], writes=[rbr])
                        if h % 4 != 3:
                            if h == 0:
                                S.op("dve", lambda e, rbt=rbt, wtk=wtk, s0=s0, ncol=ncol: e.tensor_scalar(
                                    out=acc[:, s0:s0 + ncol], in0=rbt[:, 0:ncol], scalar1=wtk[:, 0:1], scalar2=None, op0=ALU.mult),
                                    reads=[rbr, r_wtk], writes=[r_acc])
                            else:
                                S.op("dve", lambda e, rbt=rbt, wtk=wtk, s0=s0, ncol=ncol, h=h: e.scalar_tensor_tensor(
                                    out=acc[:, s0:s0 + ncol], in0=rbt[:, 0:ncol], scalar=wtk[:, h:h + 1], in1=acc[:, s0:s0 + ncol],
                                    op0=ALU.mult, op1=ALU.add), reads=[rbr, r_wtk, r_acc], writes=[r_acc])
                        else:
                            if h == 3:
                                S.op("pool", lambda e, rbt=rbt, wtk=wtk, ncol=ncol, tp=tp, h=h: e.tensor_scalar(
                                    out=tp[:, 0:ncol], in0=rbt[:, 0:ncol], scalar1=wtk[:, h:h + 1], scalar2=None, op0=ALU.mult),
                                    reads=[rbr, r_wtk], writes=[r_tp])
                            else:
                                S.op("pool", lambda e, rbt=rbt, wtk=wtk, ncol=ncol, h=h: e.tensor_scalar(
                                    out=rbt[:, 0:ncol], in0=rbt[:, 0:ncol], scalar1=wtk[:, h:h + 1], scalar2=None, op0=ALU.mult),
                                    reads=[rbr, r_wtk], writes=[rbr])
                                S.op("pool", lambda e, rbt=rbt, ncol=ncol, tp=tp: e.tensor_tensor(
                                    out=tp[:, 0:ncol], in0=tp[:, 0:ncol], in1=rbt[:, 0:ncol], op=ALU.add),
                                    reads=[rbr, r_tp], writes=[r_tp])
                    S.op("dve", lambda e, tp=tp, s0=s0, ncol=ncol: e.tensor_tensor(
                        out=acc[:, s0:s0 + ncol], in0=acc[:, s0:s0 + ncol], in1=tp[:, 0:ncol], op=ALU.add),
                        reads=[r_acc, r_tp], writes=[r_acc])
                if Kq > 256:
                    S.op("dve", lambda e, Kq=Kq: e.tensor_reduce(out=bs[:, 0:1], in_=acc[:, 0:Kq], axis=mybir.AxisListType.X, op=ALU.min),
                         reads=[r_acc], writes=[r_bs])
                    S.op("dve", lambda e, Kq=Kq: e.tensor_reduce(out=bs[:, 5:6], in_=acc[:, 0:Kq], axis=mybir.AxisListType.X, op=ALU.max),
                         reads=[r_acc], writes=[r_bs])
                    S.op("dve", lambda e: e.tensor_tensor(out=bs[:, 1:2], in0=bs[:, 5:6], in1=bs[:, 0:1], op=ALU.subtract),
                         reads=[r_bs], writes=[r_bs])
                    S.op("dve", lambda e: e.tensor_scalar(out=bs[:, 1:2], in0=bs[:, 1:2], scalar1=1.0001, scalar2=1e-6,
                                                          op0=ALU.mult, op1=ALU.add), reads=[r_bs], writes=[r_bs])
                S.op("dve", lambda e, Kq=Kq: e.memset(acc[0:64, Kq - 64:Kq], NEG), reads=[r_acc], writes=[r_acc])
                if Kq > 256:
                    S.op("dve", lambda e: e.tensor_scalar(out=wtab[:, 0:NBIS + 2], in0=self.p2tab[:, 0:NBIS + 2], scalar1=bs[:, 1:2], scalar2=None,
                                                          op0=ALU.mult), reads=[r_bs, self.r_p2tab], writes=[r_wtab])
                    S.op("dve", lambda e: e.tensor_tensor(out=bs[:, 2:3], in0=bs[:, 0:1], in1=wtab[:, 1:2], op=ALU.add),
                         reads=[r_bs, r_wtab], writes=[r_bs])
                    for k in range(NBIS):
                        S.op("dve", lambda e, Kq=Kq: e.tensor_scalar(out=wk[:, 0:Kq], in0=acc[:, 0:Kq], scalar1=bs[:, 2:3], scalar2=0.0,
                                                                     op0=ALU.is_ge, op1=ALU.add, accum_out=bs[:, 3:4]),
                             reads=[r_acc, r_bs], writes=[r_wk, r_bs])
                        S.op("dve", lambda e: e.tensor_scalar(out=bs[:, 4:5], in0=bs[:, 3:4], scalar1=256.0, scalar2=0.5,
                                                              op0=ALU.is_ge, op1=ALU.subtract), reads=[r_bs], writes=[r_bs])
                        S.op("dve", lambda e, k=k: e.scalar_tensor_tensor(out=bs[:, 2:3], in0=bs[:, 4:5], scalar=wtab[:, k + 1:k + 2],
                                                                          in1=bs[:, 2:3], op0=ALU.mult, op1=ALU.add),
                             reads=[r_bs, r_wtab], writes=[r_bs])
                    S.op("dve", lambda e: e.tensor_tensor(out=bs[:, 0:1], in0=bs[:, 2:3], in1=wtab[:, NBIS + 1:NBIS + 2], op=ALU.subtract),
                         reads=[r_bs, r_wtab], writes=[r_bs])
                    thr_ap, thr_r = bs, r_bs
                else:
                    thr_ap, thr_r = self.nbig, self.r_nbig
                S.op("dve", lambda e, Kq=Kq, thr_ap=thr_ap: e.tensor_scalar(
                    out=wk[:, 0:Kq], in0=acc[:, 0:Kq], scalar1=thr_ap[:, 0:1], scalar2=None, op0=ALU.is_ge),
                    reads=[r_acc, thr_r], writes=[r_wk])

            def mask_transposes(qb, u):
                maskT, r_maskT = mT_[qb % 2]
                qt = 4 * qb + u
                for j4 in range(0, qt + 1, 4):
                    jn = min(4, qt + 1 - j4)
                    pbt, pbr = pb[2]
                    for j in range(j4, j4 + jn):
                        S.op("pe", lambda e, j=j, j4=j4, pbt=pbt: e.transpose(
                            pbt[:, (j - j4) * 128:(j - j4 + 1) * 128], wk[:, j * 128:(j + 1) * 128], self.ident[:, :]),
                            reads=[r_wk, self.r_ident], writes=[pbr])
                    S.op("act", lambda e, j4=j4, jn=jn, u=u, pbt=pbt, maskT=maskT: e.copy(
                        maskT[:, j4:j4 + jn, u * 128:(u + 1) * 128], pbt[:, 0:jn * 128].rearrange("p (j t) -> p j t", j=jn)),
                        reads=[pbr], writes=[r_maskT])

            def head_loads(qb, h):
                nk = 4 * (qb + 1)
                K = nk * 128
                kh, r_kh = kh_[h % 2]
                qh, r_qh = qh_[h % 2]
                va, r_va = va_[h % 2]
                S.dma("sp", kh[:, 0:K], self.kT[h * 128:(h + 1) * 128, 0:K], writes=[r_kh])
                S.dma("sp", qh[:, :], self.qT[h * 128:(h + 1) * 128, qb * 512:(qb + 1) * 512], writes=[r_qh])
                for j8 in range(0, nk, 8):
                    jn8 = min(8, nk - j8)
                    S.dma("pool", va[:, j8:j8 + jn8, 0:128],
                          self.TOK[j8 * 128:(j8 + jn8) * 128, T_V + h * 128:T_V + (h + 1) * 128].rearrange("(j s) d -> s j d", s=128),
                          writes=[r_va], join=True)

            def attn_head(qb, h):
                maskT, r_maskT = mT_[qb % 2]
                ya, r_ya = ya_[0]
                nk = 4 * (qb + 1)
                kh, r_kh = kh_[h % 2]
                qh, r_qh = qh_[h % 2]
                va, r_va = va_[h % 2]
                for j in range(nk):
                    pbt, pbr = pb[j % 2]
                    S.op("pe", lambda e, kh=kh, qh=qh, j=j, pbt=pbt: e.matmul(
                        pbt[:, :], lhsT=kh[:, j * 128:(j + 1) * 128], rhs=qh[:, :], start=True, stop=True),
                        reads=[r_kh, r_qh], writes=[pbr])
                    pt, r_pt = pt_[st["pi"] % 3]
                    st["pi"] += 1
                    S.op("act", lambda e, pt=pt, pbt=pbt: e.activation(out=pt[:], in_=pbt[:, :], func=AF.Exp, scale=scale),
                         reads=[pbr], writes=[r_pt])
                    S.op("pool", lambda e, pt=pt, j=j, maskT=maskT: e.tensor_tensor(out=pt[:], in0=pt[:], in1=maskT[:, j, :], op=ALU.mult),
                         reads=[r_pt, r_maskT], writes=[r_pt])
                    for u in range(4):
                        jl = 4 * qb + u
                        if j > jl:
                            continue
                        ot, orr = pb[4 + u]
                        S.op("pe", lambda e, pt=pt, va=va, j=j, u=u, ot=ot, jl=jl: e.matmul(
                            ot[:, 0:129], lhsT=pt[:, u * 128:(u + 1) * 128], rhs=va[:, j, :], start=(j == 0), stop=(j == jl)),
                            reads=[r_pt, r_va], writes=[orr])
                for u in range(4):
                    ot, orr = pb[4 + u]
                    S.op("act", lambda e, ot=ot, u=u: e.copy(rd[:, u:u + 1], ot[:, 128:129]), reads=[orr], writes=[r_rd])
                    S.op("dve", lambda e, u=u: e.reciprocal(out=rd[:, u:u + 1], in_=rd[:, u:u + 1]), reads=[r_rd], writes=[r_rd])
                    S.op("act", lambda e, ot=ot, u=u, h=h: e.activation(
                        out=ya[:, u, h * 128:(h + 1) * 128], in_=ot[:, 0:128], func=AF.Copy, scale=rd[:, u:u + 1]),
                        reads=[orr, r_rd], writes=[r_ya])
                if h == 7:
                    S.dma("pool", self.TOK2[qb * 512:(qb + 1) * 512, D:D + 1024].rearrange("(u p) c -> p u c", p=128), ya[:],
                          reads=[r_ya])

            for qb in range(NQB + 1):
                if qb < NQB:
                    nk = 4 * (qb + 1)
                    S.op("pool", lambda e, nk=nk, qb=qb: e.memset(mT_[qb % 2][0][:, 0:nk, :], 0.0), writes=[mT_[qb % 2][1]])
                if qb >= 1:
                    head_loads(qb - 1, 0)
                for u in range(4):
                    if qb < NQB:
                        index_tile(qb, u)
                    if qb >= 1:
                        for hh in range(2):
                            h = 2 * u + hh
                            if h + 1 < 8:
                                head_loads(qb - 1, h + 1)
                            attn_head(qb - 1, h)
                    if qb < NQB:
                        mask_transposes(qb, u)

    def stage_merge(self, l, xsrc):
        S, I, SL = self.S, self.I, self.S_len
        pb = self.pb
        TB = min(1024, SL)
        NH = TB // 512
        with Stage(self) as stg:
            a16, r_a16 = stg.sb([128, 24, TB], BF16, "a16")
            s16, r_s16 = stg.sb([128, 8, TB], BF16, "s16")
            mg, r_mg = stg.sb([128, KC, TB], BF16, "mg")
            xc_ = [stg.sb([128, TB], F32, "xc") for _ in range(3)]
            gt_ = [stg.sb([128, 3, 512], F32, "gt") for _ in range(2)]
            w_ = [(stg.sb([128, 16, 256], BF16, "w1"), stg.sb([128, 8, 256], BF16, "w2"), stg.sb([128, 8, 256], BF16, "w3"))
                  for _ in range(2)]
            m1, r_m1 = stg.sb([128, 512], F32, "m1")
            m2, r_m2 = stg.sb([128, 512], F32, "m2")
            gtm = self.mod[:, l, 32:48]
            jobs = []
            for tb in range(SL // TB):
                for dg in range(8):
                    jobs.append((tb, "br", dg))
                for dg in range(8):
                    jobs.append((tb, "out", dg))

            def wload(ji):
                tb, kind, dg = jobs[ji]
                (w1, r_w1), (w2, r_w2), (w3, r_w3) = w_[ji % 2]
                if kind == "br":
                    self.load_wgroup(w1, r_w1, I["w_br_ssd"][l], dg * 256, 256, 16)
                    self.load_wgroup(w2, r_w2, I["w_br_sc"][l], dg * 256, 256, 8)
                    self.load_wgroup(w3, r_w3, I["w_br_att"][l], dg * 256, 256, 8)
                else:
                    self.load_wgroup(w1, r_w1, I["w_out"][l], dg * 256, 256, 16)

            gi = 0
            xi = 0
            wload(0)
            for ji, (tb, kind, dg) in enumerate(jobs):
                t0 = tb * TB
                if ji + 1 < len(jobs):
                    wload(ji + 1)
                (w1, r_w1), (w2, r_w2), (w3, r_w3) = w_[ji % 2]
                if kind == "br" and dg == 0:
                    for k0 in range(0, 24, 8):
                        S.dma("pool", a16[:, k0:k0 + 8, :],
                              self.TT2[k0 * 128:(k0 + 8) * 128, t0:t0 + TB].rearrange("(k p) t -> p k t", p=128),
                              writes=[r_a16], join=True)
                    S.dma("sp", s16[:], self.yscT[:, t0:t0 + TB].rearrange("(k p) t -> p k t", p=128), writes=[r_s16])
                for j in range(2):
                    dc = dg * 2 + j
                    cs = slice(j * 128, (j + 1) * 128)
                    if kind == "br":
                        for hf in range(NH):
                            ts_ = slice(hf * 512, (hf + 1) * 512)
                            gt, r_gt = gt_[gi % 2]
                            S.dma("sp", gt[:], self.gT[:, t0 + hf * 512:t0 + (hf + 1) * 512].rearrange(
                                "(b dc p) t -> dc p b t", b=3, p=128)[dc], writes=[r_gt])
                            p1, p1r = pb[0 + (gi % 2) * 3]
                            p2, p2r = pb[1 + (gi % 2) * 3]
                            p3, p3r = pb[2 + (gi % 2) * 3]
                            gi += 1
                            for kc in range(16):
                                S.op("pe", lambda e, w1=w1, kc=kc, cs=cs, p1=p1, ts_=ts_: e.matmul(
                                    p1[:, :], lhsT=w1[:, kc, cs], rhs=a16[:, kc, ts_], start=(kc == 0), stop=(kc == 15)),
                                    reads=[r_w1, r_a16], writes=[p1r], sig=(kc == 15))
                            for kc in range(8):
                                S.op("pe", lambda e, w2=w2, kc=kc, cs=cs, p2=p2, ts_=ts_: e.matmul(
                                    p2[:, :], lhsT=w2[:, kc, cs], rhs=s16[:, kc, ts_], start=(kc == 0), stop=(kc == 7)),
                                    reads=[r_w2, r_s16], writes=[p2r], sig=(kc == 7))
                            for kc in range(8):
                                S.op("pe", lambda e, w3=w3, kc=kc, cs=cs, p3=p3, ts_=ts_: e.matmul(
                                    p3[:, :], lhsT=w3[:, kc, cs], rhs=a16[:, 16 + kc, ts_], start=(kc == 0), stop=(kc == 7)),
                                    reads=[r_w3, r_a16], writes=[p3r], sig=(kc == 7))
                            S.op("dve", lambda e, gt=gt, p1=p1: e.tensor_tensor(out=m1[:], in0=p1[:, :], in1=gt[:, 0, :], op=ALU.mult),
                                 reads=[p1r, r_gt], writes=[r_m1])
                            S.op("dve", lambda e, gt=gt, p2=p2: e.tensor_tensor(out=m2[:], in0=p2[:, :], in1=gt[:, 1, :], op=ALU.mult),
                                 reads=[p2r, r_gt], writes=[r_m2])
                            S.op("dve", lambda e: e.tensor_tensor(out=m1[:], in0=m1[:], in1=m2[:], op=ALU.add),
                                 reads=[r_m1, r_m2], writes=[r_m1])
                            S.op("dve", lambda e, gt=gt, p3=p3: e.tensor_tensor(out=m2[:], in0=p3[:, :], in1=gt[:, 2, :], op=ALU.mult),
                                 reads=[p3r, r_gt], writes=[r_m2])
                            S.op("dve", lambda e, dc=dc, ts_=ts_: e.tensor_tensor(out=mg[:, dc, ts_], in0=m1[:], in1=m2[:], op=ALU.add),
                                 reads=[r_m1, r_m2], writes=[r_mg])
                    else:
                        xc, r_xc = xc_[xi % 3]
                        xi += 1
                        S.dma("sp", xc[:], xsrc[dc * 128:(dc + 1) * 128, t0:t0 + TB], writes=[r_xc])
                        for hf in range(NH):
                            ts_ = slice(hf * 512, (hf + 1) * 512)
                            p1, p1r = pb[6 + hf % 2]
                            for kc in range(16):
                                S.op("pe", lambda e, w1=w1, kc=kc, cs=cs, p1=p1, ts_=ts_: e.matmul(
                                    p1[:, :], lhsT=w1[:, kc, cs], rhs=mg[:, kc, ts_], start=(kc == 0), stop=(kc == 15)),
                                    reads=[r_w1, r_mg], writes=[p1r], sig=(kc == 15))
                            S.op("dve", lambda e, dc=dc, p1=p1, xc=xc, ts_=ts_: e.scalar_tensor_tensor(
                                out=xc[:, ts_], in0=p1[:, :], scalar=gtm[:, dc:dc + 1], in1=xc[:, ts_], op0=ALU.mult, op1=ALU.add),
                                reads=[p1r, r_xc, self.r_mod], writes=[r_xc])
                        S.dma("sp", self.xres[dc * 128:(dc + 1) * 128, t0:t0 + TB], xc[:], reads=[r_xc])

    def stage_ffn(self, l):
        S, I, SL = self.S, self.I, self.S_len
        pb = self.pb
        gtf = self.mod[:, l, 80:96]
        with Stage(self) as stg:
            xt, r_xt = stg.sb([128, KC, 512], F32, "xt")
            hT, r_hT = stg.sb([128, KC, 512], BF16, "hT")
            aT, r_aT = stg.sb([128, HC, 512], BF16, "aT")
            sg_ = [stg.sb([128, 512], F32, "sg") for _ in range(2)]
            wb_ = [stg.sb([128, 44, 256], BF16, "wffn") for _ in range(2)]
            wi = 0
            for tb in range(SL // 512):
                t0 = tb * 512
                xv = self.xres[:, t0:t0 + 512].rearrange("(dc p) t -> p dc t", p=128)
                for q4 in range(2):
                    S.dma("sp", xt[:, q4 * 8:(q4 + 1) * 8, :], xv[:, q4 * 8:(q4 + 1) * 8, :], reads=[self.r_xres], writes=[r_xt], join=True)
                self.norm_adaln(stg, xt, r_xt, 512, self.gm[:, l, 1, :], self.mod[:, l, 48:64], hT, r_hT)
                for hg in range(HC // 2):
                    wt, wr = wb_[wi % 2]
                    wi += 1
                    srcg = I["w_g"][l][:, hg * 256:(hg + 1) * 256].rearrange("(kc p) n -> p kc n", p=128)
                    srcu = I["w_u"][l][:, hg * 256:(hg + 1) * 256].rearrange("(kc p) n -> p kc n", p=128)
                    S.dma("pool", wt[:, 0:16, :], srcg, writes=[wr], join=True)
                    S.dma("pool", wt[:, 16:32, :], srcu, writes=[wr], join=True)
                    for j in range(2):
                        hc = hg * 2 + j
                        cs = slice(j * 128, (j + 1) * 128)
                        pg, pgr = pb[(hc % 2) * 2]
                        pu, pur = pb[(hc % 2) * 2 + 1]
                        for kc in range(16):
                            S.op("pe", lambda e, wt=wt, kc=kc, cs=cs, pg=pg: e.matmul(
                                pg[:, :], lhsT=wt[:, kc, cs], rhs=hT[:, kc, :], start=(kc == 0), stop=(kc == 15)),
                                reads=[wr, r_hT], writes=[pgr], sig=(kc == 15))
                        for kc in range(16):
                            S.op("pe", lambda e, wt=wt, kc=kc, cs=cs, pu=pu: e.matmul(
                                pu[:, :], lhsT=wt[:, 16 + kc, cs], rhs=hT[:, kc, :], start=(kc == 0), stop=(kc == 15)),
                                reads=[wr, r_hT], writes=[pur], sig=(kc == 15))
                        sg, r_sg = sg_[hc % 2]
                        S.op("act", lambda e, sg=sg, pg=pg: e.activation(out=sg[:], in_=pg[:, :], func=AF.Silu), reads=[pgr], writes=[r_sg])
                        S.op("dve", lambda e, sg=sg, pu=pu, hc=hc: e.tensor_tensor(out=aT[:, hc, :], in0=sg[:], in1=pu[:, :], op=ALU.mult),
                             reads=[r_sg, pur], writes=[r_aT])
                for dg in range(8):
                    wt, wr = wb_[wi % 2]
                    wi += 1
                    src = I["w_d"][l][:, dg * 256:(dg + 1) * 256].rearrange("(kc p) n -> p kc n", p=128)
                    S.dma("pool", wt[:, 0:22, :], src[:, 0:22, :], writes=[wr], join=True)
                    S.dma("pool", wt[:, 22:44, :], src[:, 22:44, :], writes=[wr], join=True)
                    for j in range(2):
                        dc = dg * 2 + j
                        cs = slice(j * 128, (j + 1) * 128)
                        p1, p1r = pb[4 + dc % 2]
                        for kc in range(HC):
                            S.op("pe", lambda e, wt=wt, kc=kc, cs=cs, p1=p1: e.matmul(
                                p1[:, :], lhsT=wt[:, kc, cs], rhs=aT[:, kc, :], start=(kc == 0), stop=(kc == HC - 1)),
                                reads=[wr, r_aT], writes=[p1r], sig=(kc == HC - 1))
                        S.op("dve", lambda e, dc=dc, p1=p1: e.scalar_tensor_tensor(
                            out=xt[:, dc, :], in0=p1[:, :], scalar=gtf[:, dc:dc + 1], in1=xt[:, dc, :], op0=ALU.mult, op1=ALU.add),
                            reads=[p1r, r_xt, self.r_mod], writes=[r_xt])
                for q4 in range(2):
                    S.dma("sp", self.xres[:, t0:t0 + 512].rearrange("(dc p) t -> p dc t", p=128)[:, q4 * 8:(q4 + 1) * 8, :],
                          xt[:, q4 * 8:(q4 + 1) * 8, :], reads=[r_xt], writes=[self.r_xres])

    def stage_final(self):
        S, I, SL = self.S, self.I, self.S_len
        with Stage(self) as stg:
            xt, r_xt = stg.sb([128, KC, 512], F32, "xt")
            ho, r_ho = stg.sb([128, KC, 512], F32, "ho")
            gf, r_gf = stg.sb([128, KC], F32, "gf")
            S.dma("sp", gf[:], I["g_final_pk"], writes=[self.r_gm])
            for tb in range(SL // 512):
                t0 = tb * 512
                xv = self.xres[:, t0:t0 + 512].rearrange("(dc p) t -> p dc t", p=128)
                for q4 in range(2):
                    S.dma("sp", xt[:, q4 * 8:(q4 + 1) * 8, :], xv[:, q4 * 8:(q4 + 1) * 8, :], reads=[self.r_xres], writes=[r_xt], join=True)
                self.norm_adaln(stg, xt, r_xt, 512, gf, None, ho, r_ho)
                for q4 in range(2):
                    S.dma("sp", self.outT[:, t0:t0 + 512].rearrange("(dc p) t -> p dc t", p=128)[:, q4 * 8:(q4 + 1) * 8, :],
                          ho[:, q4 * 8:(q4 + 1) * 8, :], reads=[r_ho])


def _pk(v, n):
    return np.ascontiguousarray(np.asarray(v, np.float32).reshape(n, 128).T)


def make_inputs(inp, b, depth):
    L = depth
    f = lambda a: np.ascontiguousarray(np.asarray(a, np.float32))
    m = {}
    m["xT"] = np.ascontiguousarray(np.asarray(inp["x"][b], np.float32).T)
    m["c_pk"] = _pk(inp["c"][b], KC)
    m["w_ada"] = f(inp["w_ada"][:L])
    m["b_ada_pk"] = np.stack([_pk(inp["b_ada"][l], 96) for l in range(L)])
    m["g_mix_pk"] = np.stack([_pk(inp["g_mix"][l], KC) for l in range(L)])
    m["g_ffn_pk"] = np.stack([_pk(inp["g_ffn"][l], KC) for l in range(L)])
    m["g_final_pk"] = _pk(inp["g_final"], KC)
    m["w_in"] = f(inp["w_in"][:L])
    m["b_gate_pk"] = np.stack([_pk(np.asarray(inp["b_gate"][l]).reshape(-1), 48) for l in range(L)])
    cw = np.asarray(inp["ssd_conv_w"], np.float32)[:L]
    m["conv_w_pk"] = np.ascontiguousarray(cw.reshape(L, 4, 32, 128).transpose(0, 3, 2, 1))
    m["conv_b_pk"] = np.stack([_pk(inp["ssd_conv_b"][l], 32) for l in range(L)])
    m["dtb"] = f(np.asarray(inp["ssd_dt_bias"])[:L].reshape(L, 32, 1))
    m["alog"] = f(np.asarray(inp["ssd_a_log"])[:L].reshape(L, 32, 1))
    m["dsk32"] = f(np.asarray(inp["ssd_d"], np.float32)[:L])
    m["ng_row"] = f(inp["ssd_norm_g"][:L])
    sw = np.asarray(inp["sc_conv_w"], np.float32)[:L]
    m["scw_pk"] = np.ascontiguousarray(sw.reshape(L, 3, 8, 128).transpose(0, 3, 2, 1))
    m["w_br_ssd"] = f(inp["w_br_ssd"][:L])
    m["w_br_sc"] = f(inp["w_br_sc"][:L])
    m["w_br_att"] = f(inp["w_br_att"][:L])
    m["w_out"] = f(inp["w_out"][:L])
    m["w_g"] = f(inp["w_ffn_gate"][:L])
    m["w_u"] = f(inp["w_ffn_up"][:L])
    m["w_d"] = f(inp["w_ffn_down"][:L])
    return m


_CACHE = {}


def kernel(**inputs):
    x = np.asarray(inputs["x"])
    Bsz, SL, _ = x.shape
    depth = np.asarray(inputs["w_in"]).shape[0]
    key = (SL, depth)
    if key not in _CACHE:
        _CACHE[key] = Builder(SL, depth).build()
    nc = _CACHE[key]
    in_maps = [make_inputs(inputs, b, depth) for b in range(Bsz)]
    res = run_bass_kernel_spmd(nc, in_maps, core_ids=list(range(Bsz)))
    out = np.stack([np.ascontiguousarray(res.results[b]["outT"].T) for b in range(Bsz)])
    return out.astype(np.float32)
```

```python
import numpy as np
from contextlib import ExitStack
import concourse.bass as bass
import concourse.mybir as mybir
from concourse.bass_utils import run_bass_kernel_spmd

F32 = mybir.dt.float32
BF16 = mybir.dt.bfloat16
ALU = mybir.AluOpType
AF = mybir.ActivationFunctionType

D = 2048
KC = 16
N_IN = 19568
FFN = 5632
HC = FFN // 128
EPS = 1e-6
NEG = -1.0e30
C_Z, C_XBC, C_DT, C_SC, C_Q, C_K, C_V, C_IQ, C_IK, C_IW, C_G = (
    0, 2048, 6144, 6176, 9248, 10272, 11296, 12320, 13344, 13408, 13424)
T_XS, T_B, T_SZ, T_V, T_DT, T_A, T_IW, T_END = 0, 2048, 3072, 5120, 6144, 6176, 6208, 6224


class Res:
    __slots__ = ("name", "w", "rs", "joined")

    def __init__(self, name=""):
        self.name = name
        self.w = []
        self.rs = {}
        self.joined = False


class Sched:
    ENG = ("pe", "act", "dve", "pool", "sp")

    def __init__(self, nc, stack, n_dma_sems=12):
        self.nc = nc
        self.eng = {"pe": nc.tensor, "act": nc.scalar, "dve": nc.vector,
                    "pool": nc.gpsimd, "sp": nc.sync}
        self.ops = {k: [] for k in self.ENG}
        self.cnt = {k: 0 for k in self.ENG}
        self.seen = {k: {} for k in self.ENG}
        self.sem = {k: stack.enter_context(nc.semaphore("s_" + k)) for k in self.ENG}
        self.live = set()
        self.dpool = {}
        self.drr = {}
        for q in ("sp", "pool"):
            self.dpool[q] = [[stack.enter_context(nc.semaphore(f"d_{q}{i}")), 0, (q, i)]
                             for i in range(n_dma_sems)]
            self.drr[q] = 0

    def _need(self, E, tok):
        key, sem, val = tok
        if key == E and E == "pe":
            return
        if self.seen[E].get(key, 0) >= val:
            return
        self.seen[E][key] = val
        eng = self.eng[E]
        self.ops[E].append(lambda: eng.wait_ge(sem, val))

    def _deps(self, E, reads, writes, join=False):
        for r in reads:
            for t in r.w:
                self._need(E, t)
        for w in writes:
            if not (join and w.joined):
                for t in w.w:
                    self._need(E, t)
            for t in w.rs.values():
                self._need(E, t)

    def _commit(self, tok, reads, writes, join=False):
        for r in reads:
            r.rs[tok[0]] = tok
            self.live.add(r)
        for w in writes:
            if join and w.joined:
                w.w.append(tok)
            else:
                w.w = [tok]
            w.joined = join
            w.rs = {}
            self.live.add(w)

    def op(self, E, fn, reads=(), writes=(), sig=True):
        reads = [r for r in reads if r is not None]
        writes = [w for w in writes if w is not None]
        self._deps(E, reads, writes)
        sem = self.sem[E]
        eng = self.eng[E]
        if sig:
            self.cnt[E] += 1
            tok = (E, sem, self.cnt[E])
            self.ops[E].append(lambda: fn(eng).then_inc(sem, 1))
        else:
            assert E == "pe"
            tok = (E, sem, self.cnt[E] + 1)
            self.ops[E].append(lambda: fn(eng))
        self._commit(tok, reads, writes)
        return tok

    def dma(self, q, out, in_, reads=(), writes=(), join=False):
        reads = [r for r in reads if r is not None]
        writes = [w for w in writes if w is not None]
        pool = self.dpool[q]
        ent = pool[self.drr[q]]
        self.drr[q] = (self.drr[q] + 1) % len(pool)
        if ent[1] > 0:
            self._need(q, (ent[2], ent[0], ent[1]))
        self._deps(q, reads, writes, join)
        ent[1] += 16
        sem = ent[0]
        tok = (ent[2], sem, ent[1])
        eng = self.eng[q]
        self.ops[q].append(lambda: eng.dma_start(out=out, in_=in_).then_inc(sem, 16))
        self._commit(tok, reads, writes, join)
        return tok

    def barrier(self):
        toks = []
        for P in self.ENG:
            if self.cnt[P] > 0:
                toks.append((P, self.sem[P], self.cnt[P]))
        for q in self.dpool:
            for ent in self.dpool[q]:
                if ent[1] > 0:
                    toks.append((ent[2], ent[0], ent[1]))
        for E in self.ENG:
            for t in toks:
                self._need(E, t)
        for r in self.live:
            r.w = []
            r.rs = {}
            r.joined = False
        self.live = set()

    def emit(self):
        ops = self.ops
        with self.nc.Block() as block:
            @block.sync
            def _(e):
                for f in ops["sp"]:
                    f()

            @block.scalar
            def _(e):
                for f in ops["act"]:
                    f()

            @block.vector
            def _(e):
                for f in ops["dve"]:
                    f()

            @block.gpsimd
            def _(e):
                for f in ops["pool"]:
                    f()

            @block.tensor
            def _(e):
                for f in ops["pe"]:
                    f()


class Stage:
    def __init__(self, k):
        self.k = k
        self.st = ExitStack()
        self.n = 0
        self.cache = {}

    def __enter__(self):
        return self

    def sb(self, shape, dt, name=None):
        self.n += 1
        self.k.uid += 1
        t = self.st.enter_context(self.k.nc.sbuf_tensor(f"{name or 't'}_{self.k.uid}", list(shape), dt))
        return t, Res(name or "t")

    def __exit__(self, *a):
        self.k.S.barrier()
        self.st.close()
        return False


class Builder:
    def __init__(self, S_len, depth, dbg=()):
        self.S_len = S_len
        self.depth = depth
        self.dbg = dbg
        self.uid = 0
        self.nc = bass.Bass("TRN2", target_bir_lowering=False)
        self.top = ExitStack()
        self.S = Sched(self.nc, self.top)

    def din(self, name, shape, dt=F32):
        return self.nc.dram_tensor(name, list(shape), dt, kind="ExternalInput").ap()

    def dscr(self, name, shape, dt=F32):
        kind = "ExternalOutput" if name in self.dbg else "Internal"
        return self.nc.dram_tensor(name, list(shape), dt, kind=kind).ap(), None

    def build(self):
        nc, S, L, SL = self.nc, self.S, self.depth, self.S_len
        I = {}
        I["xT"] = self.din("xT", [D, SL])
        I["c_pk"] = self.din("c_pk", [128, KC])
        I["w_ada"] = self.din("w_ada", [L, D, 6 * D])
        I["b_ada_pk"] = self.din("b_ada_pk", [L, 128, 96])
        I["g_mix_pk"] = self.din("g_mix_pk", [L, 128, KC])
        I["g_ffn_pk"] = self.din("g_ffn_pk", [L, 128, KC])
        I["g_final_pk"] = self.din("g_final_pk", [128, KC])
        I["w_in"] = self.din("w_in", [L, D, N_IN])
        I["b_gate_pk"] = self.din("b_gate_pk", [L, 128, 48])
        I["conv_w_pk"] = self.din("conv_w_pk", [L, 128, 32, 4])
        I["conv_b_pk"] = self.din("conv_b_pk", [L, 128, 32])
        I["dtb"] = self.din("dtb", [L, 32, 1])
        I["alog"] = self.din("alog", [L, 32, 1])
        I["dsk32"] = self.din("dsk32", [L, 32])
        I["ng_row"] = self.din("ng_row", [L, D])
        I["scw_pk"] = self.din("scw_pk", [L, 128, 8, 3])
        I["w_br_ssd"] = self.din("w_br_ssd", [L, D, D])
        I["w_br_sc"] = self.din("w_br_sc", [L, 1024, D])
        I["w_br_att"] = self.din("w_br_att", [L, 1024, D])
        I["w_out"] = self.din("w_out", [L, D, D])
        I["w_g"] = self.din("w_g", [L, D, FFN])
        I["w_u"] = self.din("w_u", [L, D, FFN])
        I["w_d"] = self.din("w_d", [L, FFN, D])
        self.I = I
        self.outT = self.nc.dram_tensor("outT", [D, SL], F32, kind="ExternalOutput").ap()
        self.xres, self.r_xres = self.dscr("xres", [D, SL])
        self.TT, self.r_TT = self.dscr("TT", [T_END, SL])
        self.TOK, self.r_TOK = self.dscr("TOK", [SL, T_END])
        self.xbcT, self.r_xbcT = self.dscr("xbcT", [4096, SL])
        self.scT, self.r_scT = self.dscr("scT", [3072, SL])
        self.qT, self.r_qT = self.dscr("qT", [1024, SL], BF16)
        self.kT, self.r_kT = self.dscr("kT", [1024, SL], BF16)
        self.iqT, self.r_iqT = self.dscr("iqT", [1024, SL], BF16)
        self.ikT, self.r_ikT = self.dscr("ikT", [64, SL], BF16)
        self.gT, self.r_gT = self.dscr("gT", [3 * D, SL])
        self.BCT, self.r_BCT = self.dscr("BCT", [2048, SL], BF16)
        self.yscT, self.r_yscT = self.dscr("yscT", [1024, SL], BF16)
        self.TOK2, self.r_TOK2 = self.dscr("TOK2", [SL, 3072])
        self.TT2, self.r_TT2 = self.dscr("TT2", [3072, SL])

        top = self.top
        self.pb = []
        for i in range(8):
            t = top.enter_context(nc.psum_tensor(f"pb{i}", [128, 512], F32))
            self.pb.append((t, Res(f"pb{i}")))
        def psb(name, shape, dt=F32):
            return top.enter_context(nc.sbuf_tensor(name, list(shape), dt)), Res(name)
        self.ident, self.r_ident = psb("ident", [128, 128])
        self.ones, self.r_ones = psb("ones", [128, 128])
        self.triu, self.r_triu = psb("triu", [64, 64])
        self.ntriu, self.r_ntriu = psb("ntriu", [64, 64])
        self.r3, self.r_r3 = psb("r3", [64, 32, 64])
        self.eps_t, self.r_eps = psb("eps_t", [128, 1])
        self.nbig, self.r_nbig = psb("nbig", [128, 1])
        self.p2tab, self.r_p2tab = psb("p2tab", [128, 32])
        self.mod, self.r_mod = psb("mod", [128, L, 96])
        self.gm, self.r_gm = psb("gm", [128, L, 2, KC])
        self.csb, self.r_csb = psb("csb", [128, KC], BF16)
        ident, ones, triu, ntriu, r3 = self.ident, self.ones, self.triu, self.ntriu, self.r3
        S.op("pool", lambda e: e.memset(ident[:], 0.0), writes=[self.r_ident])
        S.op("pool", lambda e: e.affine_select(out=ident[:], in_=ident[:], pattern=[[-1, 128]],
                                               compare_op=ALU.not_equal, fill=1.0, base=0, channel_multiplier=1),
             reads=[self.r_ident], writes=[self.r_ident])
        S.op("pool", lambda e: e.memset(ones[:], 1.0), writes=[self.r_ones])
        S.op("pool", lambda e: e.memset(triu[:], 1.0), writes=[self.r_triu])
        S.op("pool", lambda e: e.affine_select(out=triu[:], in_=triu[:], pattern=[[1, 64]],
                                               compare_op=ALU.is_ge, fill=0.0, base=0, channel_multiplier=-1),
             reads=[self.r_triu], writes=[self.r_triu])
        S.op("pool", lambda e: e.tensor_scalar(out=ntriu[:], in0=triu[:], scalar1=-1.0, scalar2=None, op0=ALU.mult),
             reads=[self.r_triu], writes=[self.r_ntriu])
        S.op("pool", lambda e: e.memset(r3[:], 0.0), writes=[self.r_r3])
        S.op("pool", lambda e: e.affine_select(out=r3[:], in_=r3[:], pattern=[[0, 32], [1, 64]],
                                               compare_op=ALU.is_ge, fill=-30000.0, base=0, channel_multiplier=-1),
             reads=[self.r_r3], writes=[self.r_r3])
        S.op("dve", lambda e: e.memset(self.eps_t[:], EPS), writes=[self.r_eps])
        S.op("dve", lambda e: e.memset(self.nbig[:], NEG / 2), writes=[self.r_nbig])
        for j in range(32):
            S.op("dve", lambda e, j=j: e.memset(self.p2tab[:, j:j + 1], 2.0 ** -j), writes=[self.r_p2tab])

        self.stage_mod()
        for l in range(L):
            self.stage_proj(l, I["xT"] if l == 0 else self.xres)
            if "stop_proj" in self.dbg:
                break
            self.stage_conv(l)
            self.transpose_pass(self.TT, self.r_TT, self.TOK, self.r_TOK, T_END, SL)
            if "stop_conv" in self.dbg:
                break
            self.stage_ssd(l)
            if "stop_ssd" in self.dbg:
                break
            self.stage_attn(l)
            self.transpose_pass(self.TOK2, self.r_TOK2, self.TT2, self.r_TT2, SL, 3072)
            if "stop_attn" in self.dbg:
                break
            self.stage_merge(l, I["xT"] if l == 0 else self.xres)
            if "stop_merge" in self.dbg:
                break
            self.stage_ffn(l)
        else:
            self.stage_final()
        S.barrier()
        S.emit()
        return nc

    def load_wgroup(self, stg_t, stg_r, W, col0, ncols, nk):
        src = W[:, col0:col0 + ncols].rearrange("(kc p) n -> p kc n", p=128)
        half = nk // 2 if nk >= 8 else nk
        for k0 in range(0, nk, half):
            self.S.dma("pool", stg_t[:, k0:k0 + half, 0:ncols], src[:, k0:k0 + half, :], writes=[stg_r], join=True)

    def norm_adaln(self, stg, xt, r_xt, N, gm_ap, sh_ap, hT, r_hT):
        S = self.S
        if ("norm", N) not in stg.cache:
            stg.cache[("norm", N)] = ([stg.sb([128, 512], F32, "sq") for _ in range(2)], stg.sb([128, N], F32, "rstd"),
                                      [stg.sb([128, N], F32, "ntmp") for _ in range(2)])
        sqs, (rstd, r_rstd), tmps = stg.cache[("norm", N)]
        ones, eps_t = self.ones, self.eps_t
        for hf in range(N // 512):
            pbt, pbr = self.pb[hf % 2]
            for dc in range(KC):
                s_t, s_r = sqs[dc % 2]
                S.op("act", lambda e, s_t=s_t, dc=dc, hf=hf: e.activation(
                    out=s_t[:], in_=xt[:, dc, hf * 512:(hf + 1) * 512], func=AF.Square),
                    reads=[r_xt], writes=[s_r])
                S.op("pe", lambda e, s_t=s_t, dc=dc, pbt=pbt: e.matmul(
                    pbt[:], lhsT=ones[:], rhs=s_t[:], start=(dc == 0), stop=(dc == KC - 1)),
                    reads=[s_r, self.r_ones], writes=[pbr])
            S.op("act", lambda e, pbt=pbt, hf=hf: e.activation(
                out=rstd[:, hf * 512:(hf + 1) * 512], in_=pbt[:], func=AF.Sqrt, scale=1.0 / D, bias=eps_t[:, 0:1]),
                reads=[pbr, self.r_eps], writes=[r_rstd])
        S.op("dve", lambda e: e.reciprocal(out=rstd[:], in_=rstd[:]), reads=[r_rstd], writes=[r_rstd])
        for dc in range(KC):
            tm, r_tm = tmps[dc % 2]
            S.op("dve", lambda e, dc=dc, tm=tm: e.tensor_tensor(out=tm[:], in0=xt[:, dc, :], in1=rstd[:], op=ALU.mult),
                 reads=[r_xt, r_rstd], writes=[r_tm])
            if sh_ap is not None:
                S.op("act", lambda e, dc=dc, tm=tm: e.activation(out=hT[:, dc, :], in_=tm[:], func=AF.Identity,
                                                                 scale=gm_ap[:, dc:dc + 1], bias=sh_ap[:, dc:dc + 1]),
                     reads=[r_tm, self.r_mod, self.r_gm], writes=[r_hT])
            else:
                S.op("act", lambda e, dc=dc, tm=tm: e.activation(out=hT[:, dc, :], in_=tm[:], func=AF.Copy,
                                                                 scale=gm_ap[:, dc:dc + 1]),
                     reads=[r_tm, self.r_mod, self.r_gm], writes=[r_hT])

    def stage_mod(self):
        S, I, L = self.S, self.I, self.depth
        with Stage(self) as stg:
            cf, r_cf = stg.sb([128, KC], F32, "cf")
            S.dma("sp", cf[:], I["c_pk"], writes=[r_cf])
            S.op("act", lambda e: e.activation(out=self.csb[:], in_=cf[:], func=AF.Silu), reads=[r_cf], writes=[self.r_csb])
            wb = [stg.sb([128, KC, 512], BF16, "wada") for _ in range(2)]
            bpk, r_bpk = stg.sb([128, L, 96], F32, "bpk")
            gmx, r_gmx = stg.sb([128, L, 2, KC], F32, "gmx")
            for l in range(L):
                S.dma("sp", bpk[:, l, :], I["b_ada_pk"][l], writes=[r_bpk], join=True)
                S.dma("sp", gmx[:, l, 0, :], I["g_mix_pk"][l], writes=[r_gmx], join=True)
                S.dma("sp", gmx[:, l, 1, :], I["g_ffn_pk"][l], writes=[r_gmx], join=True)
            gi = 0
            for l in range(L):
                pbt, pbr = self.pb[l % 2]
                for g in range(24):
                    wt, wr = wb[gi % 2]
                    gi += 1
                    self.load_wgroup(wt, wr, I["w_ada"][l], g * 512, 512, KC)
                    for j in range(4):
                        col = g * 4 + j
                        for kc in range(KC):
                            S.op("pe", lambda e, wt=wt, j=j, kc=kc, col=col, pbt=pbt: e.matmul(
                                pbt[:, col:col + 1], lhsT=wt[:, kc, j * 128:(j + 1) * 128], rhs=self.csb[:, kc:kc + 1],
                                start=(kc == 0), stop=(kc == KC - 1)),
                                reads=[wr, self.r_csb], writes=[pbr], sig=(kc == KC - 1))
                S.op("dve", lambda e, l=l, pbt=pbt: e.tensor_tensor(out=self.mod[:, l, :], in0=pbt[:, 0:96], in1=bpk[:, l, :],
                                                                    op=ALU.add),
                     reads=[pbr, r_bpk], writes=[self.r_mod])
                for v, sci in ((0, 1), (1, 4)):
                    S.op("dve", lambda e, l=l, v=v, sci=sci: e.scalar_tensor_tensor(
                        out=self.gm[:, l, v, :], in0=self.mod[:, l, sci * 16:(sci + 1) * 16], scalar=1.0,
                        in1=gmx[:, l, v, :], op0=ALU.add, op1=ALU.mult),
                        reads=[self.r_mod, r_gmx], writes=[self.r_gm])

    def stage_proj(self, l, xsrc):
        S, I, SL = self.S, self.I, self.S_len
        TB = min(1024, SL)
        W = I["w_in"][l]
        segs = [
            (C_Z, 2048, "silu", self.TT, self.r_TT, T_SZ),
            (C_XBC, 4096, "copy", self.xbcT, self.r_xbcT, 0),
            (C_DT, 32, "dt", self.TT, self.r_TT, T_DT),
            (C_SC, 3072, "copy", self.scT, self.r_scT, 0),
            (C_Q, 1024, "copy16", self.qT, self.r_qT, 0),
            (C_K, 1024, "copy16", self.kT, self.r_kT, 0),
            (C_V, 1024, "copy", self.TT, self.r_TT, T_V),
            (C_IQ, 1024, "copy16", self.iqT, self.r_iqT, 0),
            (C_IK, 64, "copy16", self.ikT, self.r_ikT, 0),
            (C_IW, 16, "copy", self.TT, self.r_TT, T_IW),
            (C_G, 6144, "gate", self.gT, self.r_gT, 0),
        ]
        with Stage(self) as stg:
            xt, r_xt = stg.sb([128, KC, TB], F32, "xt")
            hT, r_hT = stg.sb([128, KC, TB], BF16, "hT")
            wb = [stg.sb([128, KC, 512], BF16, "win") for _ in range(2)]
            ob32 = [stg.sb([128, TB], F32, "ob32") for _ in range(3)]
            ob16 = [stg.sb([128, TB], BF16, "ob16") for _ in range(2)]
            bg, r_bg = stg.sb([128, 48], F32, "bg")
            dtb, r_dtb = stg.sb([32, 1], F32, "dtb")
            nA, r_nA = stg.sb([32, 1], F32, "nA")
            av, r_av = stg.sb([32, TB], F32, "av")
            S.dma("sp", bg[:], I["b_gate_pk"][l], writes=[r_bg])
            S.dma("sp", dtb[:], I["dtb"][l], writes=[r_dtb])
            S.dma("sp", nA[:], I["alog"][l], writes=[r_nA])
            S.op("act", lambda e: e.activation(out=nA[:], in_=nA[:], func=AF.Exp), reads=[r_nA], writes=[r_nA])
            S.op("dve", lambda e: e.tensor_scalar(out=nA[:], in0=nA[:], scalar1=-1.0, scalar2=None, op0=ALU.mult),
                 reads=[r_nA], writes=[r_nA])
            gi = 0
            oi = 0
            ev = 0
            for tb in range(SL // TB):
                t0 = tb * TB
                xv = xsrc[:, t0:t0 + TB].rearrange("(dc p) t -> p dc t", p=128)
                for q4 in range(4):
                    S.dma("sp", xt[:, q4 * 4:(q4 + 1) * 4, :], xv[:, q4 * 4:(q4 + 1) * 4, :],
                          reads=[self.r_xres], writes=[r_xt], join=True)
                self.norm_adaln(stg, xt, r_xt, TB, self.gm[:, l, 0, :], self.mod[:, l, 0:16], hT, r_hT)
                for (c0, ncols, kind, dst, dst_r, drow) in segs:
                    for g0 in range(0, ncols, 512):
                        gn = min(512, ncols - g0)
                        wt, wr = wb[gi % 2]
                        gi += 1
                        self.load_wgroup(wt, wr, W, c0 + g0, gn, KC)
                        for j0 in range(0, gn, 128):
                            m = min(128, gn - j0)
                            use16 = kind == "copy16"
                            if use16:
                                ot, orr = ob16[oi % 2]
                            else:
                                ot, orr = ob32[oi % 3]
                            oi += 1
                            for hf in range(TB // 512):
                                pbt, pbr = self.pb[2 + (ev % 4)]
                                for kc in range(KC):
                                    S.op("pe", lambda e, wt=wt, kc=kc, j0=j0, m=m, hf=hf, pbt=pbt: e.matmul(
                                        pbt[0:m, :], lhsT=wt[:, kc, j0:j0 + m], rhs=hT[:, kc, hf * 512:(hf + 1) * 512],
                                        start=(kc == 0), stop=(kc == KC - 1)),
                                        reads=[wr, r_hT], writes=[pbr], sig=(kc == KC - 1))
                                osl = ot[0:m, hf * 512:(hf + 1) * 512]
                                if kind == "silu":
                                    S.op("act", lambda e, osl=osl, pbt=pbt, m=m: e.activation(out=osl, in_=pbt[0:m, :], func=AF.Silu),
                                         reads=[pbr], writes=[orr])
                                elif kind == "gate":
                                    gc = (g0 + j0) // 128
                                    S.op("act", lambda e, osl=osl, pbt=pbt, gc=gc: e.activation(
                                        out=osl, in_=pbt[:, :], func=AF.Sigmoid, bias=bg[:, gc:gc + 1]),
                                        reads=[pbr, r_bg], writes=[orr])
                                elif kind == "dt":
                                    S.op("act", lambda e, osl=osl, pbt=pbt: e.activation(
                                        out=osl, in_=pbt[0:32, :], func=AF.Exp, bias=dtb[:, 0:1]),
                                        reads=[pbr, r_dtb], writes=[orr])
                                    S.op("act", lambda e, osl=osl: e.activation(out=osl, in_=osl, func=AF.Ln, bias=1.0),
                                         reads=[orr], writes=[orr])
                                    S.op("dve", lambda e, osl=osl, hf=hf: e.tensor_scalar(
                                        out=av[:, hf * 512:(hf + 1) * 512], in0=osl, scalar1=nA[:, 0:1], scalar2=None, op0=ALU.mult),
                                        reads=[orr, r_nA], writes=[r_av])
                                else:
                                    if ev % 2 == 0:
                                        S.op("act", lambda e, osl=osl, pbt=pbt, m=m: e.copy(osl, pbt[0:m, :]),
                                             reads=[pbr], writes=[orr])
                                    else:
                                        S.op("dve", lambda e, osl=osl, pbt=pbt, m=m: e.tensor_copy(osl, pbt[0:m, :]),
                                             reads=[pbr], writes=[orr])
                                ev += 1
                            r0 = drow + g0 + j0
                            S.dma("sp", dst[r0:r0 + m, t0:t0 + TB], ot[0:m, :], reads=[orr], writes=[dst_r])
                            if kind == "dt":
                                S.dma("sp", self.TT[T_A:T_A + 32, t0:t0 + TB], av[:, :], reads=[r_av], writes=[self.r_TT])

    def stage_conv(self, l):
        S, I, SL = self.S, self.I, self.S_len
        TB = min(1024, SL)
        with Stage(self) as stg:
            cw, r_cw = stg.sb([128, 32, 4], F32, "cw")
            cb, r_cb = stg.sb([128, 32], F32, "cb")
            sw, r_sw = stg.sb([128, 8, 3], F32, "sw")
            S.dma("sp", cw[:], I["conv_w_pk"][l], writes=[r_cw])
            S.dma("sp", cb[:], I["conv_b_pk"][l], writes=[r_cb])
            S.dma("sp", sw[:], I["scw_pk"][l], writes=[r_sw])
            xin = [stg.sb([128, TB + 3], F32, "xin") for _ in range(2)]
            acc = [stg.sb([128, TB], F32, "acc") for _ in range(2)]
            o32 = [stg.sb([128, TB], F32, "o32") for _ in range(2)]
            o16 = [stg.sb([128, TB], BF16, "o16") for _ in range(2)]
            it = 0
            for rc in range(32):
                for tb in range(SL // TB):
                    t0 = tb * TB
                    xi, xr = xin[it % 2]
                    ac, ar = acc[it % 2]
                    o3, o3r = o32[it % 2]
                    o6, o6r = o16[it % 2]
                    it += 1
                    rows = slice(rc * 128, (rc + 1) * 128)
                    if tb == 0:
                        S.op("dve", lambda e, xi=xi: e.memset(xi[:, 0:3], 0.0), writes=[xr])
                        S.dma("sp", xi[:, 3:3 + TB], self.xbcT[rows, 0:TB], reads=[self.r_xbcT], writes=[xr])
                    else:
                        S.dma("sp", xi[:, :], self.xbcT[rows, t0 - 3:t0 + TB], reads=[self.r_xbcT], writes=[xr])
                    S.op("act", lambda e, xi=xi, ac=ac, rc=rc: e.activation(
                        out=ac[:], in_=xi[:, 3:3 + TB], func=AF.Identity, scale=cw[:, rc, 3:4], bias=cb[:, rc:rc + 1]),
                        reads=[xr, r_cw, r_cb], writes=[ar])
                    for k in (2, 1, 0):
                        S.op("dve", lambda e, xi=xi, ac=ac, rc=rc, k=k: e.scalar_tensor_tensor(
                            out=ac[:], in0=xi[:, k:k + TB], scalar=cw[:, rc, k:k + 1], in1=ac[:], op0=ALU.mult, op1=ALU.add),
                            reads=[xr, r_cw, ar], writes=[ar])
                    if rc < 24:
                        S.op("act", lambda e, ac=ac, o3=o3: e.activation(out=o3[:], in_=ac[:], func=AF.Silu),
                             reads=[ar], writes=[o3r])
                        S.dma("pool", self.TT[rc * 128:(rc + 1) * 128, t0:t0 + TB], o3[:], reads=[o3r], writes=[self.r_TT])
                        if rc >= 16:
                            S.op("dve", lambda e, o3=o3, o6=o6: e.tensor_copy(o6[:], o3[:]), reads=[o3r], writes=[o6r])
                    else:
                        S.op("act", lambda e, ac=ac, o6=o6: e.activation(out=o6[:], in_=ac[:], func=AF.Silu),
                             reads=[ar], writes=[o6r])
                    if rc >= 16:
                        S.dma("pool", self.BCT[(rc - 16) * 128:(rc - 15) * 128, t0:t0 + TB], o6[:], reads=[o6r], writes=[self.r_BCT])
            cin = [stg.sb([128, TB + 2], F32, "cin") for _ in range(2)]
            hin = [stg.sb([128, TB + 2], F32, "hin") for _ in range(2)]
            bin_ = [stg.sb([128, TB], F32, "bin") for _ in range(2)]
            for rc in range(8):
                for tb in range(SL // TB):
                    t0 = tb * TB
                    ci, cr = cin[it % 2]
                    hi, hr = hin[it % 2]
                    bi, br = bin_[it % 2]
                    ac, ar = acc[it % 2]
                    o6, o6r = o16[it % 2]
                    it += 1
                    if tb == 0:
                        S.op("dve", lambda e, ci=ci: e.memset(ci[:, 0:2], 0.0), writes=[cr])
                        S.op("dve", lambda e, hi=hi: e.memset(hi[:, 0:2], 0.0), writes=[hr])
                        S.dma("sp", ci[:, 2:2 + TB], self.scT[1024 + rc * 128:1024 + (rc + 1) * 128, 0:TB],
                              reads=[self.r_scT], writes=[cr])
                        S.dma("sp", hi[:, 2:2 + TB], self.scT[2048 + rc * 128:2048 + (rc + 1) * 128, 0:TB],
                              reads=[self.r_scT], writes=[hr])
                    else:
                        S.dma("sp", ci[:, :], self.scT[1024 + rc * 128:1024 + (rc + 1) * 128, t0 - 2:t0 + TB],
                              reads=[self.r_scT], writes=[cr])
                        S.dma("sp", hi[:, :], self.scT[2048 + rc * 128:2048 + (rc + 1) * 128, t0 - 2:t0 + TB],
                              reads=[self.r_scT], writes=[hr])
                    S.dma("sp", bi[:, :], self.scT[rc * 128:(rc + 1) * 128, t0:t0 + TB], reads=[self.r_scT], writes=[br])
                    S.op("dve", lambda e, ci=ci, hi=hi: e.tensor_tensor(out=ci[:], in0=ci[:], in1=hi[:], op=ALU.mult),
                         reads=[cr, hr], writes=[cr])
                    S.op("act", lambda e, ci=ci, ac=ac, rc=rc: e.activation(
                        out=ac[:], in_=ci[:, 2:2 + TB], func=AF.Copy, scale=sw[:, rc, 2:3]),
                        reads=[cr, r_sw], writes=[ar])
                    for k in (1, 0):
                        S.op("dve", lambda e, ci=ci, ac=ac, rc=rc, k=k: e.scalar_tensor_tensor(
                            out=ac[:], in0=ci[:, k:k + TB], scalar=sw[:, rc, k:k + 1], in1=ac[:], op0=ALU.mult, op1=ALU.add),
                            reads=[cr, r_sw, ar], writes=[ar])
                    S.op("dve", lambda e, ac=ac, bi=bi, o6=o6: e.tensor_tensor(out=o6[:], in0=ac[:], in1=bi[:], op=ALU.mult),
                         reads=[ar, br], writes=[o6r])
                    S.dma("pool", self.yscT[rc * 128:(rc + 1) * 128, t0:t0 + TB], o6[:], reads=[o6r], writes=[self.r_yscT])

    def transpose_pass(self, src, r_src, dst, r_dst, R, C):
        S = self.S
        with Stage(self) as stg:
            it_ = [stg.sb([128, 8, 512], F32, "tin") for _ in range(2)]
            ot_ = [stg.sb([128, 4, 1024], F32, "tout") for _ in range(2)]
            it = 0
            ev = 0
            for r0 in range(0, R, 1024):
                rn = min(1024, R - r0)
                nch = (rn + 127) // 128
                for c0 in range(0, C, 512):
                    ti, tir = it_[it % 2]
                    to, tor = ot_[it % 2]
                    it += 1
                    nfull = rn // 128
                    if nfull:
                        S.dma("sp", ti[:, 0:nfull, :],
                              src[r0:r0 + nfull * 128, c0:c0 + 512].rearrange("(k p) c -> p k c", p=128),
                              reads=[r_src], writes=[tir], join=True)
                    if rn % 128:
                        mm = rn % 128
                        S.dma("sp", ti[0:mm, nfull, :], src[r0 + nfull * 128:r0 + rn, c0:c0 + 512],
                              reads=[r_src], writes=[tir], join=True)
                    for j in range(4):
                        for k4 in range(0, nch, 4):
                            pbt, pbr = self.pb[ev % 4]
                            kn = min(4, nch - k4)
                            wtot = 0
                            for k in range(k4, k4 + kn):
                                m = min(128, rn - k * 128)
                                S.op("pe", lambda e, ti=ti, k=k, j=j, m=m, pbt=pbt, k4=k4: e.transpose(
                                    pbt[:, (k - k4) * 128:(k - k4) * 128 + m], ti[0:m, k, j * 128:(j + 1) * 128], self.ident[0:m, 0:m]),
                                    reads=[tir, self.r_ident], writes=[pbr])
                                wtot = (k - k4) * 128 + m
                            dsl = to[:, j, k4 * 128:k4 * 128 + wtot]
                            if ev % 2 == 0:
                                S.op("act", lambda e, dsl=dsl, pbt=pbt, wtot=wtot: e.copy(dsl, pbt[:, 0:wtot]), reads=[pbr], writes=[tor])
                            else:
                                S.op("dve", lambda e, dsl=dsl, pbt=pbt, wtot=wtot: e.tensor_copy(dsl, pbt[:, 0:wtot]), reads=[pbr], writes=[tor])
                            ev += 1
                    S.dma("pool", dst[c0:c0 + 512, r0:r0 + rn].rearrange("(j p) r -> p j r", p=128), to[:, :, 0:rn],
                          reads=[tor], writes=[r_dst])

    def stage_ssd(self, l):
        S, I, SL = self.S, self.I, self.S_len
        pb = self.pb
        A_, B_, C_, Z_, CB_ = (pb[0], pb[1]), (pb[2], pb[3]), (pb[4], pb[5]), pb[6], pb[7]
        with Stage(self) as stg:
            dbc, r_dbc = stg.sb([64, 32], F32, "dbc")
            ngb, r_ngb = stg.sb([64, D], F32, "ngb")
            S.dma("sp", dbc[:], I["dsk32"][l:l + 1, :].to_broadcast([64, 32]), writes=[r_dbc])
            S.dma("sp", ngb[:], I["ng_row"][l:l + 1, :].to_broadcast([64, D]), writes=[r_ngb])
            h32, r_h32 = stg.sb([128, D], F32, "h32")
            h16, r_h16 = stg.sb([128, D], BF16, "h16")
            S.op("dve", lambda e: e.memset(h32[:], 0.0), writes=[r_h32])
            S.op("dve", lambda e: e.memset(h16[:], 0.0), writes=[r_h16])
            tokx_ = [stg.sb([64, 4096], F32, "tokx") for _ in range(3)]
            dta_ = [stg.sb([64, 64], F32, "dta") for _ in range(3)]
            bcb_ = [stg.sb([128, 16, 256], BF16, "bcb") for _ in range(2)]
            acs, r_acs = stg.sb([64, 32], F32, "acs")
            ecs_ = [stg.sb([64, 32], F32, "ecs") for _ in range(2)]
            dte, r_dte = stg.sb([64, 32], F32, "dte")
            cd_ = [stg.sb([128, 32], F32, "cd") for _ in range(2)]
            R1, r_R1 = stg.sb([64, 32, 64], F32, "R1")
            R2, r_R2 = stg.sb([64, 32, 64], F32, "R2")
            xdt_ = [stg.sb([64, D], BF16, "xdt") for _ in range(2)]
            xdtd_ = [stg.sb([64, D], BF16, "xdtd") for _ in range(2)]
            bt16_ = [stg.sb([64, 1024], BF16, "bt16") for _ in range(3)]
            cbs, r_cbs = stg.sb([64, 512], BF16, "cbs")
            LT, r_LT = stg.sb([64, D], BF16, "LT")
            MT_ = [stg.sb([64, D], BF16, "MT") for _ in range(2)]
            yv, r_yv = stg.sb([64, 1024], F32, "yv")
            t2, r_t2 = stg.sb([64, 1024], F32, "t2")
            gb, r_gb = stg.sb([64, D], F32, "gb")
            gn_ = [stg.sb([64, D], F32, "gn") for _ in range(1)]
            hs, r_hs = stg.sb([128, 1024], F32, "hs")
            ss, r_ss = stg.sb([64, 2], F32, "ss")
            triu, ntriu, ones, r3, ident = self.triu, self.ntriu, self.ones, self.r3, self.ident
            nchunk = SL // 64

            def loads(c):
                t0 = c * 64
                tokx, r_tokx = tokx_[c % 3]
                dta, r_dta = dta_[c % 3]
                bt16, r_bt16 = bt16_[c % 3]
                if c % 4 == 0:
                    bcb, r_bcb = bcb_[(c // 4) % 2]
                    S.dma("sp", bcb[:], self.BCT[:, t0:t0 + 256].rearrange("(g n) t -> n g t", n=128), writes=[r_bcb])
                S.dma("sp", tokx[:, 0:D], self.TOK[t0:t0 + 64, 0:D], writes=[r_tokx], join=True)
                S.dma("sp", tokx[:, D:2 * D], self.TOK[t0:t0 + 64, T_SZ:T_SZ + D], writes=[r_tokx], join=True)
                S.dma("sp", dta[:], self.TOK[t0:t0 + 64, T_DT:T_DT + 64], writes=[r_dta])
                S.dma("pool", bt16[:], self.TOK[t0:t0 + 64, T_B:T_B + 1024], writes=[r_bt16])

            def phase1(c):
                tokx, r_tokx = tokx_[c % 3]
                dta, r_dta = dta_[c % 3]
                bcb, r_bcb = bcb_[(c // 4) % 2]
                ecs, r_ecs = ecs_[c % 2]
                cd, r_cd = cd_[c % 2]
                xdt, r_xdt = xdt_[c % 2]
                xdtd, r_xdtd = xdtd_[c % 2]
                MT, r_MT = MT_[c % 2]
                tq = (c % 4) * 64
                dt_ap = dta[:, 0:32]
                a_ap = dta[:, 32:64]
                zt, zr = Z_
                S.op("pe", lambda e: e.matmul(zt[0:64, 0:32], lhsT=triu[:, :], rhs=a_ap, start=True, stop=True),
                     reads=[r_dta, self.r_triu], writes=[zr])
                S.op("pe", lambda e: e.matmul(zt[:, 32:64], lhsT=ones[0:64, :], rhs=a_ap, start=True, stop=True),
                     reads=[r_dta, self.r_ones], writes=[zr])
                S.op("act", lambda e: e.copy(acs[:], zt[0:64, 0:32]), reads=[zr], writes=[r_acs])
                S.op("act", lambda e: e.activation(out=ecs[:], in_=zt[0:64, 0:32], func=AF.Exp), reads=[zr], writes=[r_ecs])
                S.op("act", lambda e: e.activation(out=cd[:], in_=zt[:, 32:64], func=AF.Exp), reads=[zr], writes=[r_cd])
                S.op("dve", lambda e: e.tensor_tensor(out=dte[:], in0=zt[0:64, 32:64], in1=acs[:], op=ALU.subtract),
                     reads=[zr, r_acs], writes=[r_dte])
                S.op("act", lambda e: e.activation(out=dte[:], in_=dte[:], func=AF.Exp), reads=[r_dte], writes=[r_dte])
                S.op("dve", lambda e: e.tensor_tensor(
                    out=R1[:], in0=a_ap.unsqueeze(2).to_broadcast([64, 32, 64]),
                    in1=triu[:, :].unsqueeze(1).to_broadcast([64, 32, 64]), op=ALU.mult),
                    reads=[r_dta, self.r_triu], writes=[r_R1])
                S.op("pool", lambda e: e.tensor_copy(R2[:], a_ap.unsqueeze(2).to_broadcast([64, 32, 64])),
                     reads=[r_dta], writes=[r_R2])
                S.op("dve", lambda e: e.tensor_tensor(
                    out=xdt[:].rearrange("s (h p) -> s h p", h=32), in0=tokx[:, 0:D].rearrange("s (h p) -> s h p", h=32),
                    in1=dt_ap.unsqueeze(2).to_broadcast([64, 32, 64]), op=ALU.mult),
                    reads=[r_tokx, r_dta], writes=[r_xdt])
                S.op("dve", lambda e: e.tensor_tensor(
                    out=xdtd[:].rearrange("s (h p) -> s h p", h=32), in0=xdt[:].rearrange("s (h p) -> s h p", h=32),
                    in1=dte[:, :].unsqueeze(2).to_broadcast([64, 32, 64]), op=ALU.mult),
                    reads=[r_xdt, r_dte], writes=[r_xdtd])
                cbt, cbr = CB_
                for g in range(8):
                    S.op("pe", lambda e, g=g: e.matmul(
                        cbt[0:64, g * 64:(g + 1) * 64], lhsT=bcb[:, g, tq:tq + 64], rhs=bcb[:, 8 + g, tq:tq + 64],
                        start=True, stop=True), reads=[r_bcb], writes=[cbr], sig=(g == 7))
                S.op("act", lambda e: e.copy(cbs[:], cbt[0:64, :]), reads=[cbr], writes=[r_cbs])
                for q in range(4):
                    at, ar = A_[q % 2]
                    hsl = slice(q * 8, q * 8 + 8)
                    S.op("pe", lambda e, at=at, hsl=hsl: e.matmul(at[0:64, :], lhsT=ones[0:64, 0:64], rhs=R1[:, hsl, :],
                                                                  start=True, stop=False),
                         reads=[r_R1, self.r_ones], writes=[ar], sig=False)
                    S.op("pe", lambda e, at=at, hsl=hsl: e.matmul(at[0:64, :], lhsT=ntriu[:, :], rhs=R2[:, hsl, :],
                                                                  start=False, stop=False),
                         reads=[r_R2, self.r_ntriu], writes=[ar], sig=False)
                    S.op("pe", lambda e, at=at, hsl=hsl: e.matmul(at[0:64, :], lhsT=ident[0:64, 0:64], rhs=r3[:, hsl, :],
                                                                  start=False, stop=True),
                         reads=[self.r_r3, self.r_ident], writes=[ar])
                    S.op("act", lambda e, at=at, q=q: e.activation(out=LT[:, q * 512:(q + 1) * 512], in_=at[0:64, :], func=AF.Exp),
                         reads=[ar], writes=[r_LT])
                S.op("dve", lambda e: e.tensor_tensor(
                    out=MT[:].rearrange("s (g r t) -> s g r t", g=8, r=4),
                    in0=LT[:].rearrange("s (g r t) -> s g r t", g=8, r=4),
                    in1=cbs[:, :].rearrange("s (g t) -> s g t", g=8).unsqueeze(2).to_broadcast([64, 8, 4, 64]),
                    op=ALU.mult), reads=[r_LT, r_cbs], writes=[r_MT])

            def phase2(c):
                t0 = c * 64
                tokx, r_tokx = tokx_[c % 3]
                bcb, r_bcb = bcb_[(c // 4) % 2]
                ecs, r_ecs = ecs_[c % 2]
                cd, r_cd = cd_[c % 2]
                xdt, r_xdt = xdt_[c % 2]
                xdtd, r_xdtd = xdtd_[c % 2]
                bt16, r_bt16 = bt16_[c % 3]
                MT, r_MT = MT_[c % 2]
                gnt, r_gnt = gn_[0]
                tq = (c % 4) * 64
                for hh in range(2):
                    hs0 = hh * 16
                    for h in range(16):
                        bt_, br_ = B_[h // 8]
                        S.op("pe", lambda e, h=h, bt_=bt_, hs0=hs0: e.matmul(
                            bt_[0:64, (h % 8) * 64:(h % 8 + 1) * 64], lhsT=MT[:, (hs0 + h) * 64:(hs0 + h + 1) * 64],
                            rhs=xdt[:, (hs0 + h) * 64:(hs0 + h + 1) * 64], start=True, stop=True),
                            reads=[r_MT, r_xdt], writes=[br_], sig=(h % 8 == 7))
                    for g in range(4):
                        ct_, cr_ = C_[g // 2]
                        gg = hh * 4 + g
                        S.op("pe", lambda e, g=g, gg=gg, ct_=ct_: e.matmul(
                            ct_[0:64, (g % 2) * 256:(g % 2 + 1) * 256], lhsT=bcb[:, 8 + gg, tq:tq + 64],
                            rhs=h16[:, gg * 256:(gg + 1) * 256], start=True, stop=True),
                            reads=[r_bcb, r_h16], writes=[cr_], sig=(g % 2 == 1))
                    for b in range(2):
                        ct_, cr_ = C_[b]
                        bt_, br_ = B_[b]
                        S.op("dve", lambda e, ct_=ct_, b=b, hs0=hs0: e.tensor_tensor(
                            out=yv[:, b * 512:(b + 1) * 512].rearrange("s (h p) -> s h p", h=8),
                            in0=ct_[0:64, :].rearrange("s (h p) -> s h p", h=8),
                            in1=ecs[:, hs0 + b * 8:hs0 + b * 8 + 8].unsqueeze(2).to_broadcast([64, 8, 64]), op=ALU.mult),
                            reads=[cr_, r_ecs], writes=[r_yv])
                        S.op("dve", lambda e, bt_=bt_, b=b: e.tensor_tensor(
                            out=yv[:, b * 512:(b + 1) * 512], in0=yv[:, b * 512:(b + 1) * 512], in1=bt_[0:64, :], op=ALU.add),
                            reads=[br_, r_yv], writes=[r_yv])
                    for g in range(4):
                        ct_, cr_ = C_[g // 2]
                        gg = hh * 4 + g
                        S.op("pe", lambda e, g=g, gg=gg, ct_=ct_: e.matmul(
                            ct_[:, (g % 2) * 256:(g % 2 + 1) * 256], lhsT=bt16[:, gg * 128:(gg + 1) * 128],
                            rhs=xdtd[:, gg * 256:(gg + 1) * 256], start=True, stop=True),
                            reads=[r_bt16, r_xdtd], writes=[cr_], sig=(g % 2 == 1))
                    S.op("pool", lambda e, hh=hh: e.tensor_tensor(
                        out=t2[:].rearrange("s (h p) -> s h p", h=16),
                        in0=tokx[:, hh * 1024:(hh + 1) * 1024].rearrange("s (h p) -> s h p", h=16),
                        in1=dbc[:, hh * 16:(hh + 1) * 16].unsqueeze(2).to_broadcast([64, 16, 64]), op=ALU.mult),
                        reads=[r_tokx, r_dbc], writes=[r_t2])
                    S.op("pool", lambda e: e.tensor_tensor(out=t2[:], in0=t2[:], in1=yv[:], op=ALU.add),
                         reads=[r_t2, r_yv], writes=[r_t2])
                    S.op("dve", lambda e, hh=hh: e.tensor_tensor(
                        out=gb[:, hh * 1024:(hh + 1) * 1024], in0=t2[:], in1=tokx[:, D + hh * 1024:D + (hh + 1) * 1024], op=ALU.mult),
                        reads=[r_t2, r_tokx], writes=[r_gb])
                    S.op("dve", lambda e, hh=hh, hs0=hs0: e.tensor_tensor(
                        out=hs[:].rearrange("n (h p) -> n h p", h=16),
                        in0=h32[:, hh * 1024:(hh + 1) * 1024].rearrange("n (h p) -> n h p", h=16),
                        in1=cd[:, hs0:hs0 + 16].unsqueeze(2).to_broadcast([128, 16, 64]), op=ALU.mult),
                        reads=[r_h32, r_cd], writes=[r_hs])
                    for b in range(2):
                        ct_, cr_ = C_[b]
                        S.op("dve", lambda e, ct_=ct_, b=b, hh=hh: e.tensor_tensor(
                            out=h32[:, hh * 1024 + b * 512:hh * 1024 + (b + 1) * 512], in0=hs[:, b * 512:(b + 1) * 512],
                            in1=ct_[:, :], op=ALU.add), reads=[r_hs, cr_], writes=[r_h32])
                    S.op("act", lambda e, hh=hh: e.copy(h16[:, hh * 1024:(hh + 1) * 1024], h32[:, hh * 1024:(hh + 1) * 1024]),
                         reads=[r_h32], writes=[r_h16])
                S.op("act", lambda e: e.activation(out=gnt[:], in_=gb[:], func=AF.Square, accum_out=ss[:, 0:1]),
                     reads=[r_gb], writes=[r_gnt, r_ss])
                S.op("act", lambda e: e.activation(out=ss[:, 1:2], in_=ss[:, 0:1], func=AF.Sqrt, scale=1.0 / D, bias=self.eps_t[0:64, 0:1]),
                     reads=[r_ss, self.r_eps], writes=[r_ss])
                S.op("dve", lambda e: e.reciprocal(out=ss[:, 1:2], in_=ss[:, 1:2]), reads=[r_ss], writes=[r_ss])
                S.op("dve", lambda e: e.scalar_tensor_tensor(out=gnt[:], in0=gb[:], scalar=ss[:, 1:2], in1=ngb[:],
                                                             op0=ALU.mult, op1=ALU.mult),
                     reads=[r_gb, r_ss, r_ngb], writes=[r_gnt])
                S.dma("sp", self.TOK2[t0:t0 + 64, 0:D], gnt[:], reads=[r_gnt])

            loads(0)
            if nchunk > 1:
                loads(1)
            phase1(0)
            for c in range(nchunk):
                if c + 2 < nchunk:
                    loads(c + 2)
                if c + 1 < nchunk:
                    phase1(c + 1)
                phase2(c)

    def stage_attn(self, l):
        S, I, SL = self.S, self.I, self.S_len
        pb = self.pb
        NT = SL // 128
        NQB = SL // 512
        NBIS = 22
        scale = 128.0 ** -0.5
        with Stage(self) as stg:
            ik2, r_ik2 = stg.sb([128, SL], BF16, "ik2")
            S.dma("sp", ik2[0:64, :], self.ikT[:, :], writes=[r_ik2], join=True)
            S.dma("sp", ik2[64:128, :], self.ikT[:, :], writes=[r_ik2], join=True)
            acc, r_acc = stg.sb([128, SL], F32, "acc")
            wk, r_wk = stg.sb([128, SL], F32, "wk")
            bs, r_bs = stg.sb([128, 8], F32, "bs")
            wtab, r_wtab = stg.sb([128, 32], F32, "wtab")
            mT_ = [stg.sb([128, NT, 512], BF16, "maskT") for _ in range(2)]
            iq_ = [stg.sb([128, 8, 128], BF16, "iqt") for _ in range(2)]
            wt_ = [stg.sb([128, 16], F32, "wtok") for _ in range(2)]
            rb_ = [stg.sb([128, 512], F32, "rbuf") for _ in range(3)]
            kh_ = [stg.sb([128, SL], BF16, "kh") for _ in range(2)]
            qh_ = [stg.sb([128, 512], BF16, "qh") for _ in range(2)]
            va_ = [stg.sb([128, NT, 129], BF16, "va") for _ in range(2)]
            pt_ = [stg.sb([128, 512], BF16, "pt") for _ in range(3)]
            ya_ = [stg.sb([128, 4, 1024], BF16, "ya") for _ in range(1)]
            rd, r_rd = stg.sb([128, 4], F32, "rd")
            for v in range(2):
                S.op("pool", lambda e, v=v: e.memset(va_[v][0][:, :, 128:129], 1.0), writes=[va_[v][1]])
            st = {"qi": 0, "ri": 0, "pi": 0, "hi": 0}

            def index_tile(qb, u):
                maskT, r_maskT = mT_[qb % 2]
                qt = 4 * qb + u
                t0 = qt * 128
                Kq = 128 * (qt + 1)
                iqt, r_iqt = iq_[st["qi"] % 2]
                wtk, r_wtk = wt_[st["qi"] % 2]
                st["qi"] += 1
                S.dma("sp", iqt[:], self.iqT[:, t0:t0 + 128].rearrange("(c p) t -> p c t", p=128), writes=[r_iqt])
                S.dma("sp", wtk[:], self.TOK[t0:t0 + 128, T_IW:T_IW + 16], writes=[r_wtk])
                for s0 in range(0, Kq, 512):
                    ncol = min(512, Kq - s0)
                    for h in range(16):
                        pbt, pbr = pb[2 + h % 2]
                        hp = (h % 2) * 64
                        S.op("pe", lambda e, iqt=iqt, h=h, hp=hp, s0=s0, ncol=ncol, pbt=pbt: e.matmul(
                            pbt[:, 0:ncol], lhsT=iqt[hp:hp + 64, h // 2, :], rhs=ik2[hp:hp + 64, s0:s0 + ncol],
                            start=True, stop=True), reads=[r_iqt, r_ik2], writes=[pbr])
                        rbt, rbr = rb_[st["ri"] % 3]
                        st["ri"] += 1
                        S.op("act", lambda e, rbt=rbt, pbt=pbt, ncol=ncol: e.activation(
                            out=rbt[:, 0:ncol], in_=pbt[:, 0:ncol], func=AF.Relu), reads=[pbr], writes=[rbr])
                        if h == 0:
                            S.op("dve", lambda e, rbt=rbt, wtk=wtk, s0=s0, ncol=ncol: e.tensor_scalar(
                                out=acc[:, s0:s0 + ncol], in0=rbt[:, 0:ncol], scalar1=wtk[:, 0:1], scalar2=None, op0=ALU.mult),
                                reads=[rbr, r_wtk], writes=[r_acc])
                        else:
                            S.op("dve", lambda e, rbt=rbt, wtk=wtk, s0=s0, ncol=ncol, h=h: e.scalar_tensor_tensor(
                                out=acc[:, s0:s0 + ncol], in0=rbt[:, 0:ncol], scalar=wtk[:, h:h + 1], in1=acc[:, s0:s0 + ncol],
                                op0=ALU.mult, op1=ALU.add), reads=[rbr, r_wtk, r_acc], writes=[r_acc])
                if Kq > 256:
                    S.op("dve", lambda e, Kq=Kq: e.tensor_reduce(out=bs[:, 0:1], in_=acc[:, 0:Kq], axis=mybir.AxisListType.X, op=ALU.min),
                         reads=[r_acc], writes=[r_bs])
                    S.op("dve", lambda e, Kq=Kq: e.tensor_reduce(out=bs[:, 5:6], in_=acc[:, 0:Kq], axis=mybir.AxisListType.X, op=ALU.max),
                         reads=[r_acc], writes=[r_bs])
                    S.op("dve", lambda e: e.tensor_tensor(out=bs[:, 1:2], in0=bs[:, 5:6], in1=bs[:, 0:1], op=ALU.subtract),
                         reads=[r_bs], writes=[r_bs])
                    S.op("dve", lambda e: e.tensor_scalar(out=bs[:, 1:2], in0=bs[:, 1:2], scalar1=1.0001, scalar2=1e-6,
                                                          op0=ALU.mult, op1=ALU.add), reads=[r_bs], writes=[r_bs])
                S.op("dve", lambda e, Kq=Kq: e.memset(acc[0:64, Kq - 64:Kq], NEG), reads=[r_acc], writes=[r_acc])
                if Kq > 256:
                    S.op("dve", lambda e: e.tensor_scalar(out=wtab[:, 0:NBIS + 2], in0=self.p2tab[:, 0:NBIS + 2], scalar1=bs[:, 1:2], scalar2=None,
                                                          op0=ALU.mult), reads=[r_bs, self.r_p2tab], writes=[r_wtab])
                    S.op("dve", lambda e: e.tensor_tensor(out=bs[:, 2:3], in0=bs[:, 0:1], in1=wtab[:, 1:2], op=ALU.add),
                         reads=[r_bs, r_wtab], writes=[r_bs])
                    for k in range(NBIS):
                        S.op("dve", lambda e, Kq=Kq: e.tensor_scalar(out=wk[:, 0:Kq], in0=acc[:, 0:Kq], scalar1=bs[:, 2:3], scalar2=0.0,
                                                                     op0=ALU.is_ge, op1=ALU.add, accum_out=bs[:, 3:4]),
                             reads=[r_acc, r_bs], writes=[r_wk, r_bs])
                        S.op("dve", lambda e: e.tensor_scalar(out=bs[:, 4:5], in0=bs[:, 3:4], scalar1=256.0, scalar2=0.5,
                                                              op0=ALU.is_ge, op1=ALU.subtract), reads=[r_bs], writes=[r_bs])
                        S.op("dve", lambda e, k=k: e.scalar_tensor_tensor(out=bs[:, 2:3], in0=bs[:, 4:5], scalar=wtab[:, k + 1:k + 2],
                                                                          in1=bs[:, 2:3], op0=ALU.mult, op1=ALU.add),
                             reads=[r_bs, r_wtab], writes=[r_bs])
                    S.op("dve", lambda e: e.tensor_tensor(out=bs[:, 0:1], in0=bs[:, 2:3], in1=wtab[:, NBIS + 1:NBIS + 2], op=ALU.subtract),
                         reads=[r_bs, r_wtab], writes=[r_bs])
                    thr_ap, thr_r = bs, r_bs
                else:
                    thr_ap, thr_r = self.nbig, self.r_nbig
                S.op("dve", lambda e, Kq=Kq, thr_ap=thr_ap: e.tensor_scalar(
                    out=wk[:, 0:Kq], in0=acc[:, 0:Kq], scalar1=thr_ap[:, 0:1], scalar2=None, op0=ALU.is_ge),
                    reads=[r_acc, thr_r], writes=[r_wk])

            def mask_transposes(qb, u):
                maskT, r_maskT = mT_[qb % 2]
                qt = 4 * qb + u
                for j4 in range(0, qt + 1, 4):
                    jn = min(4, qt + 1 - j4)
                    pbt, pbr = pb[2]
                    for j in range(j4, j4 + jn):
                        S.op("pe", lambda e, j=j, j4=j4, pbt=pbt: e.transpose(
                            pbt[:, (j - j4) * 128:(j - j4 + 1) * 128], wk[:, j * 128:(j + 1) * 128], self.ident[:, :]),
                            reads=[r_wk, self.r_ident], writes=[pbr])
                    S.op("act", lambda e, j4=j4, jn=jn, u=u, pbt=pbt, maskT=maskT: e.copy(
                        maskT[:, j4:j4 + jn, u * 128:(u + 1) * 128], pbt[:, 0:jn * 128].rearrange("p (j t) -> p j t", j=jn)),
                        reads=[pbr], writes=[r_maskT])

            def head_loads(qb, h):
                nk = 4 * (qb + 1)
                K = nk * 128
                kh, r_kh = kh_[h % 2]
                qh, r_qh = qh_[h % 2]
                va, r_va = va_[h % 2]
                S.dma("sp", kh[:, 0:K], self.kT[h * 128:(h + 1) * 128, 0:K], writes=[r_kh])
                S.dma("sp", qh[:, :], self.qT[h * 128:(h + 1) * 128, qb * 512:(qb + 1) * 512], writes=[r_qh])
                for j8 in range(0, nk, 8):
                    jn8 = min(8, nk - j8)
                    S.dma("pool", va[:, j8:j8 + jn8, 0:128],
                          self.TOK[j8 * 128:(j8 + jn8) * 128, T_V + h * 128:T_V + (h + 1) * 128].rearrange("(j s) d -> s j d", s=128),
                          writes=[r_va], join=True)

            def attn_head(qb, h):
                maskT, r_maskT = mT_[qb % 2]
                ya, r_ya = ya_[0]
                nk = 4 * (qb + 1)
                kh, r_kh = kh_[h % 2]
                qh, r_qh = qh_[h % 2]
                va, r_va = va_[h % 2]
                for j in range(nk):
                    pbt, pbr = pb[j % 2]
                    S.op("pe", lambda e, kh=kh, qh=qh, j=j, pbt=pbt: e.matmul(
                        pbt[:, :], lhsT=kh[:, j * 128:(j + 1) * 128], rhs=qh[:, :], start=True, stop=True),
                        reads=[r_kh, r_qh], writes=[pbr])
                    pt, r_pt = pt_[st["pi"] % 3]
                    st["pi"] += 1
                    S.op("act", lambda e, pt=pt, pbt=pbt: e.activation(out=pt[:], in_=pbt[:, :], func=AF.Exp, scale=scale),
                         reads=[pbr], writes=[r_pt])
                    S.op("pool", lambda e, pt=pt, j=j, maskT=maskT: e.tensor_tensor(out=pt[:], in0=pt[:], in1=maskT[:, j, :], op=ALU.mult),
                         reads=[r_pt, r_maskT], writes=[r_pt])
                    for u in range(4):
                        jl = 4 * qb + u
                        if j > jl:
                            continue
                        ot, orr = pb[4 + u]
                        S.op("pe", lambda e, pt=pt, va=va, j=j, u=u, ot=ot, jl=jl: e.matmul(
                            ot[:, 0:129], lhsT=pt[:, u * 128:(u + 1) * 128], rhs=va[:, j, :], start=(j == 0), stop=(j == jl)),
                            reads=[r_pt, r_va], writes=[orr])
                for u in range(4):
                    ot, orr = pb[4 + u]
                    S.op("act", lambda e, ot=ot, u=u: e.copy(rd[:, u:u + 1], ot[:, 128:129]), reads=[orr], writes=[r_rd])
                    S.op("dve", lambda e, u=u: e.reciprocal(out=rd[:, u:u + 1], in_=rd[:, u:u + 1]), reads=[r_rd], writes=[r_rd])
                    S.op("act", lambda e, ot=ot, u=u, h=h: e.activation(
                        out=ya[:, u, h * 128:(h + 1) * 128], in_=ot[:, 0:128], func=AF.Copy, scale=rd[:, u:u + 1]),
                        reads=[orr, r_rd], writes=[r_ya])
                if h == 7:
                    S.dma("pool", self.TOK2[qb * 512:(qb + 1) * 512, D:D + 1024].rearrange("(u p) c -> p u c", p=128), ya[:],
                          reads=[r_ya])

            for qb in range(NQB + 1):
                if qb < NQB:
                    nk = 4 * (qb + 1)
                    S.op("pool", lambda e, nk=nk, qb=qb: e.memset(mT_[qb % 2][0][:, 0:nk, :], 0.0), writes=[mT_[qb % 2][1]])
                if qb >= 1:
                    head_loads(qb - 1, 0)
                for u in range(4):
                    if qb < NQB:
                        index_tile(qb, u)
                    if qb >= 1:
                        for hh in range(2):
                            h = 2 * u + hh
                            if h + 1 < 8:
                                head_loads(qb - 1, h + 1)
                            attn_head(qb - 1, h)
                    if qb < NQB:
                        mask_transposes(qb, u)

    def stage_merge(self, l, xsrc):
        S, I, SL = self.S, self.I, self.S_len
        pb = self.pb
        TB = min(1024, SL)
        NH = TB // 512
        with Stage(self) as stg:
            a16, r_a16 = stg.sb([128, 24, TB], BF16, "a16")
            s16, r_s16 = stg.sb([128, 8, TB], BF16, "s16")
            mg, r_mg = stg.sb([128, KC, TB], BF16, "mg")
            xc_ = [stg.sb([128, TB], F32, "xc") for _ in range(3)]
            gt_ = [stg.sb([128, 3, 512], F32, "gt") for _ in range(2)]
            w_ = [(stg.sb([128, 16, 256], BF16, "w1"), stg.sb([128, 8, 256], BF16, "w2"), stg.sb([128, 8, 256], BF16, "w3"))
                  for _ in range(2)]
            m1, r_m1 = stg.sb([128, 512], F32, "m1")
            m2, r_m2 = stg.sb([128, 512], F32, "m2")
            gtm = self.mod[:, l, 32:48]
            jobs = []
            for tb in range(SL // TB):
                for dg in range(8):
                    jobs.append((tb, "br", dg))
                for dg in range(8):
                    jobs.append((tb, "out", dg))

            def wload(ji):
                tb, kind, dg = jobs[ji]
                (w1, r_w1), (w2, r_w2), (w3, r_w3) = w_[ji % 2]
                if kind == "br":
                    self.load_wgroup(w1, r_w1, I["w_br_ssd"][l], dg * 256, 256, 16)
                    self.load_wgroup(w2, r_w2, I["w_br_sc"][l], dg * 256, 256, 8)
                    self.load_wgroup(w3, r_w3, I["w_br_att"][l], dg * 256, 256, 8)
                else:
                    self.load_wgroup(w1, r_w1, I["w_out"][l], dg * 256, 256, 16)

            gi = 0
            xi = 0
            wload(0)
            for ji, (tb, kind, dg) in enumerate(jobs):
                t0 = tb * TB
                if ji + 1 < len(jobs):
                    wload(ji + 1)
                (w1, r_w1), (w2, r_w2), (w3, r_w3) = w_[ji % 2]
                if kind == "br" and dg == 0:
                    for k0 in range(0, 24, 8):
                        S.dma("pool", a16[:, k0:k0 + 8, :],
                              self.TT2[k0 * 128:(k0 + 8) * 128, t0:t0 + TB].rearrange("(k p) t -> p k t", p=128),
                              writes=[r_a16], join=True)
                    S.dma("sp", s16[:], self.yscT[:, t0:t0 + TB].rearrange("(k p) t -> p k t", p=128), writes=[r_s16])
                for j in range(2):
                    dc = dg * 2 + j
                    cs = slice(j * 128, (j + 1) * 128)
                    if kind == "br":
                        for hf in range(NH):
                            ts_ = slice(hf * 512, (hf + 1) * 512)
                            gt, r_gt = gt_[gi % 2]
                            S.dma("sp", gt[:], self.gT[:, t0 + hf * 512:t0 + (hf + 1) * 512].rearrange(
                                "(b dc p) t -> dc p b t", b=3, p=128)[dc], writes=[r_gt])
                            p1, p1r = pb[0 + (gi % 2) * 3]
                            p2, p2r = pb[1 + (gi % 2) * 3]
                            p3, p3r = pb[2 + (gi % 2) * 3]
                            gi += 1
                            for kc in range(16):
                                S.op("pe", lambda e, w1=w1, kc=kc, cs=cs, p1=p1, ts_=ts_: e.matmul(
                                    p1[:, :], lhsT=w1[:, kc, cs], rhs=a16[:, kc, ts_], start=(kc == 0), stop=(kc == 15)),
                                    reads=[r_w1, r_a16], writes=[p1r], sig=(kc == 15))
                            for kc in range(8):
                                S.op("pe", lambda e, w2=w2, kc=kc, cs=cs, p2=p2, ts_=ts_: e.matmul(
                                    p2[:, :], lhsT=w2[:, kc, cs], rhs=s16[:, kc, ts_], start=(kc == 0), stop=(kc == 7)),
                                    reads=[r_w2, r_s16], writes=[p2r], sig=(kc == 7))
                            for kc in range(8):
                                S.op("pe", lambda e, w3=w3, kc=kc, cs=cs, p3=p3, ts_=ts_: e.matmul(
                                    p3[:, :], lhsT=w3[:, kc, cs], rhs=a16[:, 16 + kc, ts_], start=(kc == 0), stop=(kc == 7)),
                                    reads=[r_w3, r_a16], writes=[p3r], sig=(kc == 7))
                            S.op("dve", lambda e, gt=gt, p1=p1: e.tensor_tensor(out=m1[:], in0=p1[:, :], in1=gt[:, 0, :], op=ALU.mult),
                                 reads=[p1r, r_gt], writes=[r_m1])
                            S.op("dve", lambda e, gt=gt, p2=p2: e.tensor_tensor(out=m2[:], in0=p2[:, :], in1=gt[:, 1, :], op=ALU.mult),
                                 reads=[p2r, r_gt], writes=[r_m2])
                            S.op("dve", lambda e: e.tensor_tensor(out=m1[:], in0=m1[:], in1=m2[:], op=ALU.add),
                                 reads=[r_m1, r_m2], writes=[r_m1])
                            S.op("dve", lambda e, gt=gt, p3=p3: e.tensor_tensor(out=m2[:], in0=p3[:, :], in1=gt[:, 2, :], op=ALU.mult),
                                 reads=[p3r, r_gt], writes=[r_m2])
                            S.op("dve", lambda e, dc=dc, ts_=ts_: e.tensor_tensor(out=mg[:, dc, ts_], in0=m1[:], in1=m2[:], op=ALU.add),
                                 reads=[r_m1, r_m2], writes=[r_mg])
                    else:
                        xc, r_xc = xc_[xi % 3]
                        xi += 1
                        S.dma("sp", xc[:], xsrc[dc * 128:(dc + 1) * 128, t0:t0 + TB], writes=[r_xc])
                        for hf in range(NH):
                            ts_ = slice(hf * 512, (hf + 1) * 512)
                            p1, p1r = pb[6 + hf % 2]
                            for kc in range(16):
                                S.op("pe", lambda e, w1=w1, kc=kc, cs=cs, p1=p1, ts_=ts_: e.matmul(
                                    p1[:, :], lhsT=w1[:, kc, cs], rhs=mg[:, kc, ts_], start=(kc == 0), stop=(kc == 15)),
                                    reads=[r_w1, r_mg], writes=[p1r], sig=(kc == 15))
                            S.op("dve", lambda e, dc=dc, p1=p1, xc=xc, ts_=ts_: e.scalar_tensor_tensor(
                                out=xc[:, ts_], in0=p1[:, :], scalar=gtm[:, dc:dc + 1], in1=xc[:, ts_], op0=ALU.mult, op1=ALU.add),
                                reads=[p1r, r_xc, self.r_mod], writes=[r_xc])
                        S.dma("sp", self.xres[dc * 128:(dc + 1) * 128, t0:t0 + TB], xc[:], reads=[r_xc])

    def stage_ffn(self, l):
        S, I, SL = self.S, self.I, self.S_len
        pb = self.pb
        gtf = self.mod[:, l, 80:96]
        with Stage(self) as stg:
            xt, r_xt = stg.sb([128, KC, 512], F32, "xt")
            hT, r_hT = stg.sb([128, KC, 512], BF16, "hT")
            aT, r_aT = stg.sb([128, HC, 512], BF16, "aT")
            sg_ = [stg.sb([128, 512], F32, "sg") for _ in range(2)]
            wb_ = [stg.sb([128, 44, 256], BF16, "wffn") for _ in range(2)]
            wi = 0
            for tb in range(SL // 512):
                t0 = tb * 512
                xv = self.xres[:, t0:t0 + 512].rearrange("(dc p) t -> p dc t", p=128)
                for q4 in range(2):
                    S.dma("sp", xt[:, q4 * 8:(q4 + 1) * 8, :], xv[:, q4 * 8:(q4 + 1) * 8, :], reads=[self.r_xres], writes=[r_xt], join=True)
                self.norm_adaln(stg, xt, r_xt, 512, self.gm[:, l, 1, :], self.mod[:, l, 48:64], hT, r_hT)
                for hg in range(HC // 2):
                    wt, wr = wb_[wi % 2]
                    wi += 1
                    srcg = I["w_g"][l][:, hg * 256:(hg + 1) * 256].rearrange("(kc p) n -> p kc n", p=128)
                    srcu = I["w_u"][l][:, hg * 256:(hg + 1) * 256].rearrange("(kc p) n -> p kc n", p=128)
                    S.dma("pool", wt[:, 0:16, :], srcg, writes=[wr], join=True)
                    S.dma("pool", wt[:, 16:32, :], srcu, writes=[wr], join=True)
                    for j in range(2):
                        hc = hg * 2 + j
                        cs = slice(j * 128, (j + 1) * 128)
                        pg, pgr = pb[(hc % 2) * 2]
                        pu, pur = pb[(hc % 2) * 2 + 1]
                        for kc in range(16):
                            S.op("pe", lambda e, wt=wt, kc=kc, cs=cs, pg=pg: e.matmul(
                                pg[:, :], lhsT=wt[:, kc, cs], rhs=hT[:, kc, :], start=(kc == 0), stop=(kc == 15)),
                                reads=[wr, r_hT], writes=[pgr], sig=(kc == 15))
                        for kc in range(16):
                            S.op("pe", lambda e, wt=wt, kc=kc, cs=cs, pu=pu: e.matmul(
                                pu[:, :], lhsT=wt[:, 16 + kc, cs], rhs=hT[:, kc, :], start=(kc == 0), stop=(kc == 15)),
                                reads=[wr, r_hT], writes=[pur], sig=(kc == 15))
                        sg, r_sg = sg_[hc % 2]
                        S.op("act", lambda e, sg=sg, pg=pg: e.activation(out=sg[:], in_=pg[:, :], func=AF.Silu), reads=[pgr], writes=[r_sg])
                        S.op("dve", lambda e, sg=sg, pu=pu, hc=hc: e.tensor_tensor(out=aT[:, hc, :], in0=sg[:], in1=pu[:, :], op=ALU.mult),
                             reads=[r_sg, pur], writes=[r_aT])
                for dg in range(8):
                    wt, wr = wb_[wi % 2]
                    wi += 1
                    src = I["w_d"][l][:, dg * 256:(dg + 1) * 256].rearrange("(kc p) n -> p kc n", p=128)
                    S.dma("pool", wt[:, 0:22, :], src[:, 0:22, :], writes=[wr], join=True)
                    S.dma("pool", wt[:, 22:44, :], src[:, 22:44, :], writes=[wr], join=True)
                    for j in range(2):
                        dc = dg * 2 + j
                        cs = slice(j * 128, (j + 1) * 128)
                        p1, p1r = pb[4 + dc % 2]
                        for kc in range(HC):
                            S.op("pe", lambda e, wt=wt, kc=kc, cs=cs, p1=p1: e.matmul(
                                p1[:, :], lhsT=wt[:, kc, cs], rhs=aT[:, kc, :], start=(kc == 0), stop=(kc == HC - 1)),
                                reads=[wr, r_aT], writes=[p1r], sig=(kc == HC - 1))
                        S.op("dve", lambda e, dc=dc, p1=p1: e.scalar_tensor_tensor(
                            out=xt[:, dc, :], in0=p1[:, :], scalar=gtf[:, dc:dc + 1], in1=xt[:, dc, :], op0=ALU.mult, op1=ALU.add),
                            reads=[p1r, r_xt, self.r_mod], writes=[r_xt])
                for q4 in range(2):
                    S.dma("sp", self.xres[:, t0:t0 + 512].rearrange("(dc p) t -> p dc t", p=128)[:, q4 * 8:(q4 + 1) * 8, :],
                          xt[:, q4 * 8:(q4 + 1) * 8, :], reads=[r_xt], writes=[self.r_xres])

    def stage_final(self):
        S, I, SL = self.S, self.I, self.S_len
        with Stage(self) as stg:
            xt, r_xt = stg.sb([128, KC, 512], F32, "xt")
            ho, r_ho = stg.sb([128, KC, 512], F32, "ho")
            gf, r_gf = stg.sb([128, KC], F32, "gf")
            S.dma("sp", gf[:], I["g_final_pk"], writes=[self.r_gm])
            for tb in range(SL // 512):
                t0 = tb * 512
                xv = self.xres[:, t0:t0 + 512].rearrange("(dc p) t -> p dc t", p=128)
                for q4 in range(2):
                    S.dma("sp", xt[:, q4 * 8:(q4 + 1) * 8, :], xv[:, q4 * 8:(q4 + 1) * 8, :], reads=[self.r_xres], writes=[r_xt], join=True)
                self.norm_adaln(stg, xt, r_xt, 512, gf, None, ho, r_ho)
                for q4 in range(2):
                    S.dma("sp", self.outT[:, t0:t0 + 512].rearrange("(dc p) t -> p dc t", p=128)[:, q4 * 8:(q4 + 1) * 8, :],
                          ho[:, q4 * 8:(q4 + 1) * 8, :], reads=[r_ho])


def _pk(v, n):
    return np.ascontiguousarray(np.asarray(v, np.float32).reshape(n, 128).T)


def make_inputs(inp, b, depth):
    L = depth
    f = lambda a: np.ascontiguousarray(np.asarray(a, np.float32))
    m = {}
    m["xT"] = np.ascontiguousarray(np.asarray(inp["x"][b], np.float32).T)
    m["c_pk"] = _pk(inp["c"][b], KC)
    m["w_ada"] = f(inp["w_ada"][:L])
    m["b_ada_pk"] = np.stack([_pk(inp["b_ada"][l], 96) for l in range(L)])
    m["g_mix_pk"] = np.stack([_pk(inp["g_mix"][l], KC) for l in range(L)])
    m["g_ffn_pk"] = np.stack([_pk(inp["g_ffn"][l], KC) for l in range(L)])
    m["g_final_pk"] = _pk(inp["g_final"], KC)
    m["w_in"] = f(inp["w_in"][:L])
    m["b_gate_pk"] = np.stack([_pk(np.asarray(inp["b_gate"][l]).reshape(-1), 48) for l in range(L)])
    cw = np.asarray(inp["ssd_conv_w"], np.float32)[:L]
    m["conv_w_pk"] = np.ascontiguousarray(cw.reshape(L, 4, 32, 128).transpose(0, 3, 2, 1))
    m["conv_b_pk"] = np.stack([_pk(inp["ssd_conv_b"][l], 32) for l in range(L)])
    m["dtb"] = f(np.asarray(inp["ssd_dt_bias"])[:L].reshape(L, 32, 1))
    m["alog"] = f(np.asarray(inp["ssd_a_log"])[:L].reshape(L, 32, 1))
    m["dsk32"] = f(np.asarray(inp["ssd_d"], np.float32)[:L])
    m["ng_row"] = f(inp["ssd_norm_g"][:L])
    sw = np.asarray(inp["sc_conv_w"], np.float32)[:L]
    m["scw_pk"] = np.ascontiguousarray(sw.reshape(L, 3, 8, 128).transpose(0, 3, 2, 1))
    m["w_br_ssd"] = f(inp["w_br_ssd"][:L])
    m["w_br_sc"] = f(inp["w_br_sc"][:L])
    m["w_br_att"] = f(inp["w_br_att"][:L])
    m["w_out"] = f(inp["w_out"][:L])
    m["w_g"] = f(inp["w_ffn_gate"][:L])
    m["w_u"] = f(inp["w_ffn_up"][:L])
    m["w_d"] = f(inp["w_ffn_down"][:L])
    return m


_CACHE = {}


def kernel(**inputs):
    x = np.asarray(inputs["x"])
    Bsz, SL, _ = x.shape
    depth = np.asarray(inputs["w_in"]).shape[0]
    key = (SL, depth)
    if key not in _CACHE:
        _CACHE[key] = Builder(SL, depth).build()
    nc = _CACHE[key]
    in_maps = [make_inputs(inputs, b, depth) for b in range(Bsz)]
    res = run_bass_kernel_spmd(nc, in_maps, core_ids=list(range(Bsz)))
    out = np.stack([np.ascontiguousarray(res.results[b]["outT"].T) for b in range(Bsz)])
    return out.astype(np.float32)
```

```python
import numpy as np
from contextlib import ExitStack
import concourse.bass as bass
import concourse.mybir as mybir
from concourse.bass_utils import run_bass_kernel_spmd

F32 = mybir.dt.float32
BF16 = mybir.dt.bfloat16
ALU = mybir.AluOpType
AF = mybir.ActivationFunctionType

D = 2048
KC = 16
N_IN = 19568
FFN = 5632
HC = FFN // 128
EPS = 1e-6
NEG = -1.0e30
C_Z, C_XBC, C_DT, C_SC, C_Q, C_K, C_V, C_IQ, C_IK, C_IW, C_G = (
    0, 2048, 6144, 6176, 9248, 10272, 11296, 12320, 13344, 13408, 13424)
T_XS, T_B, T_SZ, T_V, T_DT, T_A, T_IW, T_END = 0, 2048, 3072, 5120, 6144, 6176, 6208, 6224


class Res:
    __slots__ = ("name", "w", "rs", "joined")

    def __init__(self, name=""):
        self.name = name
        self.w = []
        self.rs = {}
        self.joined = False


class Sched:
    ENG = ("pe", "act", "dve", "pool", "sp")

    def __init__(self, nc, stack, n_dma_sems=12):
        self.nc = nc
        self.eng = {"pe": nc.tensor, "act": nc.scalar, "dve": nc.vector,
                    "pool": nc.gpsimd, "sp": nc.sync}
        self.ops = {k: [] for k in self.ENG}
        self.cnt = {k: 0 for k in self.ENG}
        self.seen = {k: {} for k in self.ENG}
        self.sem = {k: stack.enter_context(nc.semaphore("s_" + k)) for k in self.ENG}
        self.live = set()
        self.dpool = {}
        self.drr = {}
        for q in ("sp", "pool"):
            self.dpool[q] = [[stack.enter_context(nc.semaphore(f"d_{q}{i}")), 0, (q, i)]
                             for i in range(n_dma_sems)]
            self.drr[q] = 0

    def _need(self, E, tok):
        key, sem, val = tok
        if key == E and E == "pe":
            return
        if self.seen[E].get(key, 0) >= val:
            return
        self.seen[E][key] = val
        eng = self.eng[E]
        self.ops[E].append(lambda: eng.wait_ge(sem, val))

    def _deps(self, E, reads, writes, join=False):
        for r in reads:
            for t in r.w:
                self._need(E, t)
        for w in writes:
            if not (join and w.joined):
                for t in w.w:
                    self._need(E, t)
            for t in w.rs.values():
                self._need(E, t)

    def _commit(self, tok, reads, writes, join=False):
        for r in reads:
            r.rs[tok[0]] = tok
            self.live.add(r)
        for w in writes:
            if join and w.joined:
                w.w.append(tok)
            else:
                w.w = [tok]
            w.joined = join
            w.rs = {}
            self.live.add(w)

    def op(self, E, fn, reads=(), writes=(), sig=True):
        reads = [r for r in reads if r is not None]
        writes = [w for w in writes if w is not None]
        self._deps(E, reads, writes)
        sem = self.sem[E]
        eng = self.eng[E]
        if sig:
            self.cnt[E] += 1
            tok = (E, sem, self.cnt[E])
            self.ops[E].append(lambda: fn(eng).then_inc(sem, 1))
        else:
            assert E == "pe"
            tok = (E, sem, self.cnt[E] + 1)
            self.ops[E].append(lambda: fn(eng))
        self._commit(tok, reads, writes)
        return tok

    def dma(self, q, out, in_, reads=(), writes=(), join=False):
        reads = [r for r in reads if r is not None]
        writes = [w for w in writes if w is not None]
        pool = self.dpool[q]
        ent = pool[self.drr[q]]
        self.drr[q] = (self.drr[q] + 1) % len(pool)
        if ent[1] > 0:
            self._need(q, (ent[2], ent[0], ent[1]))
        self._deps(q, reads, writes, join)
        ent[1] += 16
        sem = ent[0]
        tok = (ent[2], sem, ent[1])
        eng = self.eng[q]
        self.ops[q].append(lambda: eng.dma_start(out=out, in_=in_).then_inc(sem, 16))
        self._commit(tok, reads, writes, join)
        return tok

    def barrier(self):
        toks = []
        for P in self.ENG:
            if self.cnt[P] > 0:
                toks.append((P, self.sem[P], self.cnt[P]))
        for q in self.dpool:
            for ent in self.dpool[q]:
                if ent[1] > 0:
                    toks.append((ent[2], ent[0], ent[1]))
        for E in self.ENG:
            for t in toks:
                self._need(E, t)
        for r in self.live:
            r.w = []
            r.rs = {}
            r.joined = False
        self.live = set()

    def emit(self):
        ops = self.ops
        with self.nc.Block() as block:
            @block.sync
            def _(e):
                for f in ops["sp"]:
                    f()

            @block.scalar
            def _(e):
                for f in ops["act"]:
                    f()

            @block.vector
            def _(e):
                for f in ops["dve"]:
                    f()

            @block.gpsimd
            def _(e):
                for f in ops["pool"]:
                    f()

            @block.tensor
            def _(e):
                for f in ops["pe"]:
                    f()


class Stage:
    def __init__(self, k):
        self.k = k
        self.st = ExitStack()
        self.n = 0
        self.cache = {}

    def __enter__(self):
        return self

    def sb(self, shape, dt, name=None):
        self.n += 1
        self.k.uid += 1
        t = self.st.enter_context(self.k.nc.sbuf_tensor(f"{name or 't'}_{self.k.uid}", list(shape), dt))
        return t, Res(name or "t")

    def __exit__(self, *a):
        self.k.S.barrier()
        self.st.close()
        return False


class Builder:
    def __init__(self, S_len, depth, dbg=()):
        self.S_len = S_len
        self.depth = depth
        self.dbg = dbg
        self.uid = 0
        self.nc = bass.Bass("TRN2", target_bir_lowering=False)
        self.top = ExitStack()
        self.S = Sched(self.nc, self.top)

    def din(self, name, shape, dt=F32):
        return self.nc.dram_tensor(name, list(shape), dt, kind="ExternalInput").ap()

    def dscr(self, name, shape, dt=F32):
        kind = "ExternalOutput" if name in self.dbg else "Internal"
        return self.nc.dram_tensor(name, list(shape), dt, kind=kind).ap(), None

    def build(self):
        nc, S, L, SL = self.nc, self.S, self.depth, self.S_len
        I = {}
        I["xT"] = self.din("xT", [D, SL])
        I["c_pk"] = self.din("c_pk", [128, KC])
        I["w_ada"] = self.din("w_ada", [L, D, 6 * D])
        I["b_ada_pk"] = self.din("b_ada_pk", [L, 128, 96])
        I["g_mix_pk"] = self.din("g_mix_pk", [L, 128, KC])
        I["g_ffn_pk"] = self.din("g_ffn_pk", [L, 128, KC])
        I["g_final_pk"] = self.din("g_final_pk", [128, KC])
        I["w_in"] = self.din("w_in", [L, D, N_IN])
        I["b_gate_pk"] = self.din("b_gate_pk", [L, 128, 48])
        I["conv_w_pk"] = self.din("conv_w_pk", [L, 128, 32, 4])
        I["conv_b_pk"] = self.din("conv_b_pk", [L, 128, 32])
        I["dtb"] = self.din("dtb", [L, 32, 1])
        I["alog"] = self.din("alog", [L, 32, 1])
        I["dsk32"] = self.din("dsk32", [L, 32])
        I["ng_row"] = self.din("ng_row", [L, D])
        I["scw_pk"] = self.din("scw_pk", [L, 128, 8, 3])
        I["w_br_ssd"] = self.din("w_br_ssd", [L, D, D])
        I["w_br_sc"] = self.din("w_br_sc", [L, 1024, D])
        I["w_br_att"] = self.din("w_br_att", [L, 1024, D])
        I["w_out"] = self.din("w_out", [L, D, D])
        I["w_g"] = self.din("w_g", [L, D, FFN])
        I["w_u"] = self.din("w_u", [L, D, FFN])
        I["w_d"] = self.din("w_d", [L, FFN, D])
        self.I = I
        self.outT = self.nc.dram_tensor("outT", [D, SL], F32, kind="ExternalOutput").ap()
        self.xres, self.r_xres = self.dscr("xres", [D, SL])
        self.TT, self.r_TT = self.dscr("TT", [T_END, SL])
        self.TOK, self.r_TOK = self.dscr("TOK", [SL, T_END])
        self.xbcT, self.r_xbcT = self.dscr("xbcT", [4096, SL])
        self.scT, self.r_scT = self.dscr("scT", [3072, SL])
        self.qT, self.r_qT = self.dscr("qT", [1024, SL], BF16)
        self.kT, self.r_kT = self.dscr("kT", [1024, SL], BF16)
        self.iqT, self.r_iqT = self.dscr("iqT", [1024, SL], BF16)
        self.ikT, self.r_ikT = self.dscr("ikT", [64, SL], BF16)
        self.gT, self.r_gT = self.dscr("gT", [3 * D, SL])
        self.BCT, self.r_BCT = self.dscr("BCT", [2048, SL], BF16)
        self.yscT, self.r_yscT = self.dscr("yscT", [1024, SL], BF16)
        self.TOK2, self.r_TOK2 = self.dscr("TOK2", [SL, 3072])
        self.TT2, self.r_TT2 = self.dscr("TT2", [3072, SL])

        top = self.top
        self.pb = []
        for i in range(8):
            t = top.enter_context(nc.psum_tensor(f"pb{i}", [128, 512], F32))
            self.pb.append((t, Res(f"pb{i}")))
        def psb(name, shape, dt=F32):
            return top.enter_context(nc.sbuf_tensor(name, list(shape), dt)), Res(name)
        self.ident, self.r_ident = psb("ident", [128, 128])
        self.ones, self.r_ones = psb("ones", [128, 128])
        self.triu, self.r_triu = psb("triu", [64, 64])
        self.ntriu, self.r_ntriu = psb("ntriu", [64, 64])
        self.r3, self.r_r3 = psb("r3", [64, 32, 64])
        self.ones16, self.r_ones16 = psb("ones16", [64, 64], BF16)
        self.ntriu16, self.r_ntriu16 = psb("ntriu16", [64, 64], BF16)
        self.ident16, self.r_ident16 = psb("ident16", [64, 64], BF16)
        self.r3_16, self.r_r3_16 = psb("r3_16", [64, 32, 64], BF16)
        self.eps_t, self.r_eps = psb("eps_t", [128, 1])
        self.nbig, self.r_nbig = psb("nbig", [128, 1])
        self.p2tab, self.r_p2tab = psb("p2tab", [128, 32])
        self.mod, self.r_mod = psb("mod", [128, L, 96])
        self.gm, self.r_gm = psb("gm", [128, L, 2, KC])
        self.csb, self.r_csb = psb("csb", [128, KC], BF16)
        ident, ones, triu, ntriu, r3 = self.ident, self.ones, self.triu, self.ntriu, self.r3
        S.op("pool", lambda e: e.memset(ident[:], 0.0), writes=[self.r_ident])
        S.op("pool", lambda e: e.affine_select(out=ident[:], in_=ident[:], pattern=[[-1, 128]],
                                               compare_op=ALU.not_equal, fill=1.0, base=0, channel_multiplier=1),
             reads=[self.r_ident], writes=[self.r_ident])
        S.op("pool", lambda e: e.memset(ones[:], 1.0), writes=[self.r_ones])
        S.op("pool", lambda e: e.memset(triu[:], 1.0), writes=[self.r_triu])
        S.op("pool", lambda e: e.affine_select(out=triu[:], in_=triu[:], pattern=[[1, 64]],
                                               compare_op=ALU.is_ge, fill=0.0, base=0, channel_multiplier=-1),
             reads=[self.r_triu], writes=[self.r_triu])
        S.op("pool", lambda e: e.tensor_scalar(out=ntriu[:], in0=triu[:], scalar1=-1.0, scalar2=None, op0=ALU.mult),
             reads=[self.r_triu], writes=[self.r_ntriu])
        S.op("pool", lambda e: e.memset(r3[:], 0.0), writes=[self.r_r3])
        S.op("pool", lambda e: e.affine_select(out=r3[:], in_=r3[:], pattern=[[0, 32], [1, 64]],
                                               compare_op=ALU.is_ge, fill=-30000.0, base=0, channel_multiplier=-1),
             reads=[self.r_r3], writes=[self.r_r3])
        S.op("pool", lambda e: e.tensor_copy(self.ones16[:], ones[0:64, 0:64]), reads=[self.r_ones], writes=[self.r_ones16])
        S.op("pool", lambda e: e.tensor_copy(self.ntriu16[:], ntriu[:, :]), reads=[self.r_ntriu], writes=[self.r_ntriu16])
        S.op("pool", lambda e: e.tensor_copy(self.ident16[:], ident[0:64, 0:64]), reads=[self.r_ident], writes=[self.r_ident16])
        S.op("pool", lambda e: e.tensor_copy(self.r3_16[:], r3[:]), reads=[self.r_r3], writes=[self.r_r3_16])
        S.op("dve", lambda e: e.memset(self.eps_t[:], EPS), writes=[self.r_eps])
        S.op("dve", lambda e: e.memset(self.nbig[:], NEG / 2), writes=[self.r_nbig])
        for j in range(32):
            S.op("dve", lambda e, j=j: e.memset(self.p2tab[:, j:j + 1], 2.0 ** -j), writes=[self.r_p2tab])

        self.stage_mod()
        for l in range(L):
            self.stage_proj(l, I["xT"] if l == 0 else self.xres)
            if "stop_proj" in self.dbg:
                break
            self.stage_conv(l)
            self.transpose_pass(self.TT, self.r_TT, self.TOK, self.r_TOK, T_END, SL)
            if "stop_conv" in self.dbg:
                break
            self.stage_ssd(l)
            if "stop_ssd" in self.dbg:
                break
            self.stage_attn(l)
            self.transpose_pass(self.TOK2, self.r_TOK2, self.TT2, self.r_TT2, SL, 3072)
            if "stop_attn" in self.dbg:
                break
            self.stage_merge(l, I["xT"] if l == 0 else self.xres)
            if "stop_merge" in self.dbg:
                break
            self.stage_ffn(l)
        else:
            self.stage_final()
        S.barrier()
        S.emit()
        return nc

    def load_wgroup(self, stg_t, stg_r, W, col0, ncols, nk):
        src = W[:, col0:col0 + ncols].rearrange("(kc p) n -> p kc n", p=128)
        half = nk // 2 if nk >= 8 else nk
        for k0 in range(0, nk, half):
            self.S.dma("pool", stg_t[:, k0:k0 + half, 0:ncols], src[:, k0:k0 + half, :], writes=[stg_r], join=True)

    def norm_adaln(self, stg, xt, r_xt, N, gm_ap, sh_ap, hT, r_hT):
        S = self.S
        if ("norm", N) not in stg.cache:
            stg.cache[("norm", N)] = ([stg.sb([128, 512], F32, "sq") for _ in range(2)], stg.sb([128, N], F32, "rstd"),
                                      [stg.sb([128, N], F32, "ntmp") for _ in range(2)])
        sqs, (rstd, r_rstd), tmps = stg.cache[("norm", N)]
        ones, eps_t = self.ones, self.eps_t
        for hf in range(N // 512):
            pbt, pbr = self.pb[hf % 2]
            for dc in range(KC):
                s_t, s_r = sqs[dc % 2]
                S.op("act", lambda e, s_t=s_t, dc=dc, hf=hf: e.activation(
                    out=s_t[:], in_=xt[:, dc, hf * 512:(hf + 1) * 512], func=AF.Square),
                    reads=[r_xt], writes=[s_r])
                S.op("pe", lambda e, s_t=s_t, dc=dc, pbt=pbt: e.matmul(
                    pbt[:], lhsT=ones[:], rhs=s_t[:], start=(dc == 0), stop=(dc == KC - 1)),
                    reads=[s_r, self.r_ones], writes=[pbr])
            S.op("act", lambda e, pbt=pbt, hf=hf: e.activation(
                out=rstd[:, hf * 512:(hf + 1) * 512], in_=pbt[:], func=AF.Sqrt, scale=1.0 / D, bias=eps_t[:, 0:1]),
                reads=[pbr, self.r_eps], writes=[r_rstd])
        S.op("dve", lambda e: e.reciprocal(out=rstd[:], in_=rstd[:]), reads=[r_rstd], writes=[r_rstd])
        for dc in range(KC):
            tm, r_tm = tmps[dc % 2]
            S.op("dve", lambda e, dc=dc, tm=tm: e.tensor_tensor(out=tm[:], in0=xt[:, dc, :], in1=rstd[:], op=ALU.mult),
                 reads=[r_xt, r_rstd], writes=[r_tm])
            if sh_ap is not None:
                S.op("act", lambda e, dc=dc, tm=tm: e.activation(out=hT[:, dc, :], in_=tm[:], func=AF.Identity,
                                                                 scale=gm_ap[:, dc:dc + 1], bias=sh_ap[:, dc:dc + 1]),
                     reads=[r_tm, self.r_mod, self.r_gm], writes=[r_hT])
            else:
                S.op("act", lambda e, dc=dc, tm=tm: e.activation(out=hT[:, dc, :], in_=tm[:], func=AF.Copy,
                                                                 scale=gm_ap[:, dc:dc + 1]),
                     reads=[r_tm, self.r_mod, self.r_gm], writes=[r_hT])

    def stage_mod(self):
        S, I, L = self.S, self.I, self.depth
        with Stage(self) as stg:
            cf, r_cf = stg.sb([128, KC], F32, "cf")
            S.dma("sp", cf[:], I["c_pk"], writes=[r_cf])
            S.op("act", lambda e: e.activation(out=self.csb[:], in_=cf[:], func=AF.Silu), reads=[r_cf], writes=[self.r_csb])
            wb = [stg.sb([128, KC, 512], BF16, "wada") for _ in range(2)]
            bpk, r_bpk = stg.sb([128, L, 96], F32, "bpk")
            gmx, r_gmx = stg.sb([128, L, 2, KC], F32, "gmx")
            for l in range(L):
                S.dma("sp", bpk[:, l, :], I["b_ada_pk"][l], writes=[r_bpk], join=True)
                S.dma("sp", gmx[:, l, 0, :], I["g_mix_pk"][l], writes=[r_gmx], join=True)
                S.dma("sp", gmx[:, l, 1, :], I["g_ffn_pk"][l], writes=[r_gmx], join=True)
            gi = 0
            for l in range(L):
                pbt, pbr = self.pb[l % 2]
                for g in range(24):
                    wt, wr = wb[gi % 2]
                    gi += 1
                    self.load_wgroup(wt, wr, I["w_ada"][l], g * 512, 512, KC)
                    for j in range(4):
                        col = g * 4 + j
                        for kc in range(KC):
                            S.op("pe", lambda e, wt=wt, j=j, kc=kc, col=col, pbt=pbt: e.matmul(
                                pbt[:, col:col + 1], lhsT=wt[:, kc, j * 128:(j + 1) * 128], rhs=self.csb[:, kc:kc + 1],
                                start=(kc == 0), stop=(kc == KC - 1)),
                                reads=[wr, self.r_csb], writes=[pbr], sig=(kc == KC - 1))
                S.op("dve", lambda e, l=l, pbt=pbt: e.tensor_tensor(out=self.mod[:, l, :], in0=pbt[:, 0:96], in1=bpk[:, l, :],
                                                                    op=ALU.add),
                     reads=[pbr, r_bpk], writes=[self.r_mod])
                for v, sci in ((0, 1), (1, 4)):
                    S.op("dve", lambda e, l=l, v=v, sci=sci: e.scalar_tensor_tensor(
                        out=self.gm[:, l, v, :], in0=self.mod[:, l, sci * 16:(sci + 1) * 16], scalar=1.0,
                        in1=gmx[:, l, v, :], op0=ALU.add, op1=ALU.mult),
                        reads=[self.r_mod, r_gmx], writes=[self.r_gm])

    def stage_proj(self, l, xsrc):
        S, I, SL = self.S, self.I, self.S_len
        TB = min(1024, SL)
        W = I["w_in"][l]
        segs = [
            (C_Z, 2048, "silu", self.TT, self.r_TT, T_SZ),
            (C_XBC, 4096, "copy", self.xbcT, self.r_xbcT, 0),
            (C_DT, 32, "dt", self.TT, self.r_TT, T_DT),
            (C_SC, 3072, "copy", self.scT, self.r_scT, 0),
            (C_Q, 1024, "copy16", self.qT, self.r_qT, 0),
            (C_K, 1024, "copy16", self.kT, self.r_kT, 0),
            (C_V, 1024, "copy", self.TT, self.r_TT, T_V),
            (C_IQ, 1024, "copy16", self.iqT, self.r_iqT, 0),
            (C_IK, 64, "copy16", self.ikT, self.r_ikT, 0),
            (C_IW, 16, "copy", self.TT, self.r_TT, T_IW),
            (C_G, 6144, "gate", self.gT, self.r_gT, 0),
        ]
        with Stage(self) as stg:
            xt, r_xt = stg.sb([128, KC, TB], F32, "xt")
            hT, r_hT = stg.sb([128, KC, TB], BF16, "hT")
            wb = [stg.sb([128, KC, 512], BF16, "win") for _ in range(2)]
            ob32 = [stg.sb([128, TB], F32, "ob32") for _ in range(3)]
            ob16 = [stg.sb([128, TB], BF16, "ob16") for _ in range(2)]
            bg, r_bg = stg.sb([128, 48], F32, "bg")
            dtb, r_dtb = stg.sb([32, 1], F32, "dtb")
            nA, r_nA = stg.sb([32, 1], F32, "nA")
            av, r_av = stg.sb([32, TB], F32, "av")
            S.dma("sp", bg[:], I["b_gate_pk"][l], writes=[r_bg])
            S.dma("sp", dtb[:], I["dtb"][l], writes=[r_dtb])
            S.dma("sp", nA[:], I["alog"][l], writes=[r_nA])
            S.op("act", lambda e: e.activation(out=nA[:], in_=nA[:], func=AF.Exp), reads=[r_nA], writes=[r_nA])
            S.op("dve", lambda e: e.tensor_scalar(out=nA[:], in0=nA[:], scalar1=-1.0, scalar2=None, op0=ALU.mult),
                 reads=[r_nA], writes=[r_nA])
            gi = 0
            oi = 0
            ev = 0
            for tb in range(SL // TB):
                t0 = tb * TB
                xv = xsrc[:, t0:t0 + TB].rearrange("(dc p) t -> p dc t", p=128)
                for q4 in range(4):
                    S.dma("sp", xt[:, q4 * 4:(q4 + 1) * 4, :], xv[:, q4 * 4:(q4 + 1) * 4, :],
                          reads=[self.r_xres], writes=[r_xt], join=True)
                self.norm_adaln(stg, xt, r_xt, TB, self.gm[:, l, 0, :], self.mod[:, l, 0:16], hT, r_hT)
                for (c0, ncols, kind, dst, dst_r, drow) in segs:
                    for g0 in range(0, ncols, 512):
                        gn = min(512, ncols - g0)
                        wt, wr = wb[gi % 2]
                        gi += 1
                        self.load_wgroup(wt, wr, W, c0 + g0, gn, KC)
                        for j0 in range(0, gn, 128):
                            m = min(128, gn - j0)
                            use16 = kind == "copy16"
                            if use16:
                                ot, orr = ob16[oi % 2]
                            else:
                                ot, orr = ob32[oi % 3]
                            oi += 1
                            for hf in range(TB // 512):
                                pbt, pbr = self.pb[2 + (ev % 4)]
                                for kc in range(KC):
                                    S.op("pe", lambda e, wt=wt, kc=kc, j0=j0, m=m, hf=hf, pbt=pbt: e.matmul(
                                        pbt[0:m, :], lhsT=wt[:, kc, j0:j0 + m], rhs=hT[:, kc, hf * 512:(hf + 1) * 512],
                                        start=(kc == 0), stop=(kc == KC - 1)),
                                        reads=[wr, r_hT], writes=[pbr], sig=(kc == KC - 1))
                                osl = ot[0:m, hf * 512:(hf + 1) * 512]
                                if kind == "silu":
                                    S.op("act", lambda e, osl=osl, pbt=pbt, m=m: e.activation(out=osl, in_=pbt[0:m, :], func=AF.Silu),
                                         reads=[pbr], writes=[orr])
                                elif kind == "gate":
                                    gc = (g0 + j0) // 128
                                    S.op("act", lambda e, osl=osl, pbt=pbt, gc=gc: e.activation(
                                        out=osl, in_=pbt[:, :], func=AF.Sigmoid, bias=bg[:, gc:gc + 1]),
                                        reads=[pbr, r_bg], writes=[orr])
                                elif kind == "dt":
                                    S.op("act", lambda e, osl=osl, pbt=pbt: e.activation(
                                        out=osl, in_=pbt[0:32, :], func=AF.Exp, bias=dtb[:, 0:1]),
                                        reads=[pbr, r_dtb], writes=[orr])
                                    S.op("act", lambda e, osl=osl: e.activation(out=osl, in_=osl, func=AF.Ln, bias=1.0),
                                         reads=[orr], writes=[orr])
                                    S.op("dve", lambda e, osl=osl, hf=hf: e.tensor_scalar(
                                        out=av[:, hf * 512:(hf + 1) * 512], in0=osl, scalar1=nA[:, 0:1], scalar2=None, op0=ALU.mult),
                                        reads=[orr, r_nA], writes=[r_av])
                                else:
                                    if ev % 2 == 0:
                                        S.op("act", lambda e, osl=osl, pbt=pbt, m=m: e.copy(osl, pbt[0:m, :]),
                                             reads=[pbr], writes=[orr])
                                    else:
                                        S.op("dve", lambda e, osl=osl, pbt=pbt, m=m: e.tensor_copy(osl, pbt[0:m, :]),
                                             reads=[pbr], writes=[orr])
                                ev += 1
                            r0 = drow + g0 + j0
                            S.dma("sp", dst[r0:r0 + m, t0:t0 + TB], ot[0:m, :], reads=[orr], writes=[dst_r])
                            if kind == "dt":
                                S.dma("sp", self.TT[T_A:T_A + 32, t0:t0 + TB], av[:, :], reads=[r_av], writes=[self.r_TT])

    def stage_conv(self, l):
        S, I, SL = self.S, self.I, self.S_len
        TB = min(1024, SL)
        with Stage(self) as stg:
            cw, r_cw = stg.sb([128, 32, 4], F32, "cw")
            cb, r_cb = stg.sb([128, 32], F32, "cb")
            sw, r_sw = stg.sb([128, 8, 3], F32, "sw")
            S.dma("sp", cw[:], I["conv_w_pk"][l], writes=[r_cw])
            S.dma("sp", cb[:], I["conv_b_pk"][l], writes=[r_cb])
            S.dma("sp", sw[:], I["scw_pk"][l], writes=[r_sw])
            xin = [stg.sb([128, TB + 3], F32, "xin") for _ in range(2)]
            acc = [stg.sb([128, TB], F32, "acc") for _ in range(2)]
            o32 = [stg.sb([128, TB], F32, "o32") for _ in range(2)]
            o16 = [stg.sb([128, TB], BF16, "o16") for _ in range(2)]
            it = 0
            for rc in range(32):
                for tb in range(SL // TB):
                    t0 = tb * TB
                    xi, xr = xin[it % 2]
                    ac, ar = acc[it % 2]
                    o3, o3r = o32[it % 2]
                    o6, o6r = o16[it % 2]
                    it += 1
                    rows = slice(rc * 128, (rc + 1) * 128)
                    if tb == 0:
                        S.op("dve", lambda e, xi=xi: e.memset(xi[:, 0:3], 0.0), writes=[xr])
                        S.dma("sp", xi[:, 3:3 + TB], self.xbcT[rows, 0:TB], reads=[self.r_xbcT], writes=[xr])
                    else:
                        S.dma("sp", xi[:, :], self.xbcT[rows, t0 - 3:t0 + TB], reads=[self.r_xbcT], writes=[xr])
                    S.op("act", lambda e, xi=xi, ac=ac, rc=rc: e.activation(
                        out=ac[:], in_=xi[:, 3:3 + TB], func=AF.Identity, scale=cw[:, rc, 3:4], bias=cb[:, rc:rc + 1]),
                        reads=[xr, r_cw, r_cb], writes=[ar])
                    for k in (2, 1, 0):
                        S.op("dve", lambda e, xi=xi, ac=ac, rc=rc, k=k: e.scalar_tensor_tensor(
                            out=ac[:], in0=xi[:, k:k + TB], scalar=cw[:, rc, k:k + 1], in1=ac[:], op0=ALU.mult, op1=ALU.add),
                            reads=[xr, r_cw, ar], writes=[ar])
                    if rc < 24:
                        S.op("act", lambda e, ac=ac, o3=o3: e.activation(out=o3[:], in_=ac[:], func=AF.Silu),
                             reads=[ar], writes=[o3r])
                        S.dma("pool", self.TT[rc * 128:(rc + 1) * 128, t0:t0 + TB], o3[:], reads=[o3r], writes=[self.r_TT])
                        if rc >= 16:
                            S.op("dve", lambda e, o3=o3, o6=o6: e.tensor_copy(o6[:], o3[:]), reads=[o3r], writes=[o6r])
                    else:
                        S.op("act", lambda e, ac=ac, o6=o6: e.activation(out=o6[:], in_=ac[:], func=AF.Silu),
                             reads=[ar], writes=[o6r])
                    if rc >= 16:
                        S.dma("pool", self.BCT[(rc - 16) * 128:(rc - 15) * 128, t0:t0 + TB], o6[:], reads=[o6r], writes=[self.r_BCT])
            cin = [stg.sb([128, TB + 2], F32, "cin") for _ in range(2)]
            hin = [stg.sb([128, TB + 2], F32, "hin") for _ in range(2)]
            bin_ = [stg.sb([128, TB], F32, "bin") for _ in range(2)]
            for rc in range(8):
                for tb in range(SL // TB):
                    t0 = tb * TB
                    ci, cr = cin[it % 2]
                    hi, hr = hin[it % 2]
                    bi, br = bin_[it % 2]
                    ac, ar = acc[it % 2]
                    o6, o6r = o16[it % 2]
                    it += 1
                    if tb == 0:
                        S.op("dve", lambda e, ci=ci: e.memset(ci[:, 0:2], 0.0), writes=[cr])
                        S.op("dve", lambda e, hi=hi: e.memset(hi[:, 0:2], 0.0), writes=[hr])
                        S.dma("sp", ci[:, 2:2 + TB], self.scT[1024 + rc * 128:1024 + (rc + 1) * 128, 0:TB],
                              reads=[self.r_scT], writes=[cr])
                        S.dma("sp", hi[:, 2:2 + TB], self.scT[2048 + rc * 128:2048 + (rc + 1) * 128, 0:TB],
                              reads=[self.r_scT], writes=[hr])
                    else:
                        S.dma("sp", ci[:, :], self.scT[1024 + rc * 128:1024 + (rc + 1) * 128, t0 - 2:t0 + TB],
                              reads=[self.r_scT], writes=[cr])
                        S.dma("sp", hi[:, :], self.scT[2048 + rc * 128:2048 + (rc + 1) * 128, t0 - 2:t0 + TB],
                              reads=[self.r_scT], writes=[hr])
                    S.dma("sp", bi[:, :], self.scT[rc * 128:(rc + 1) * 128, t0:t0 + TB], reads=[self.r_scT], writes=[br])
                    S.op("dve", lambda e, ci=ci, hi=hi: e.tensor_tensor(out=ci[:], in0=ci[:], in1=hi[:], op=ALU.mult),
                         reads=[cr, hr], writes=[cr])
                    S.op("act", lambda e, ci=ci, ac=ac, rc=rc: e.activation(
                        out=ac[:], in_=ci[:, 2:2 + TB], func=AF.Copy, scale=sw[:, rc, 2:3]),
                        reads=[cr, r_sw], writes=[ar])
                    for k in (1, 0):
                        S.op("dve", lambda e, ci=ci, ac=ac, rc=rc, k=k: e.scalar_tensor_tensor(
                            out=ac[:], in0=ci[:, k:k + TB], scalar=sw[:, rc, k:k + 1], in1=ac[:], op0=ALU.mult, op1=ALU.add),
                            reads=[cr, r_sw, ar], writes=[ar])
                    S.op("dve", lambda e, ac=ac, bi=bi, o6=o6: e.tensor_tensor(out=o6[:], in0=ac[:], in1=bi[:], op=ALU.mult),
                         reads=[ar, br], writes=[o6r])
                    S.dma("pool", self.yscT[rc * 128:(rc + 1) * 128, t0:t0 + TB], o6[:], reads=[o6r], writes=[self.r_yscT])

    def transpose_pass(self, src, r_src, dst, r_dst, R, C):
        S = self.S
        with Stage(self) as stg:
            it_ = [stg.sb([128, 8, 512], F32, "tin") for _ in range(2)]
            ot_ = [stg.sb([128, 4, 1024], F32, "tout") for _ in range(2)]
            it = 0
            ev = 0
            for r0 in range(0, R, 1024):
                rn = min(1024, R - r0)
                nch = (rn + 127) // 128
                for c0 in range(0, C, 512):
                    ti, tir = it_[it % 2]
                    to, tor = ot_[it % 2]
                    it += 1
                    nfull = rn // 128
                    if nfull:
                        S.dma("sp", ti[:, 0:nfull, :],
                              src[r0:r0 + nfull * 128, c0:c0 + 512].rearrange("(k p) c -> p k c", p=128),
                              reads=[r_src], writes=[tir], join=True)
                    if rn % 128:
                        mm = rn % 128
                        S.dma("sp", ti[0:mm, nfull, :], src[r0 + nfull * 128:r0 + rn, c0:c0 + 512],
                              reads=[r_src], writes=[tir], join=True)
                    for j in range(4):
                        for k4 in range(0, nch, 4):
                            pbt, pbr = self.pb[ev % 4]
                            kn = min(4, nch - k4)
                            wtot = 0
                            for k in range(k4, k4 + kn):
                                m = min(128, rn - k * 128)
                                S.op("pe", lambda e, ti=ti, k=k, j=j, m=m, pbt=pbt, k4=k4: e.transpose(
                                    pbt[:, (k - k4) * 128:(k - k4) * 128 + m], ti[0:m, k, j * 128:(j + 1) * 128], self.ident[0:m, 0:m]),
                                    reads=[tir, self.r_ident], writes=[pbr])
                                wtot = (k - k4) * 128 + m
                            dsl = to[:, j, k4 * 128:k4 * 128 + wtot]
                            if ev % 2 == 0:
                                S.op("act", lambda e, dsl=dsl, pbt=pbt, wtot=wtot: e.copy(dsl, pbt[:, 0:wtot]), reads=[pbr], writes=[tor])
                            else:
                                S.op("dve", lambda e, dsl=dsl, pbt=pbt, wtot=wtot: e.tensor_copy(dsl, pbt[:, 0:wtot]), reads=[pbr], writes=[tor])
                            ev += 1
                    S.dma("pool", dst[c0:c0 + 512, r0:r0 + rn].rearrange("(j p) r -> p j r", p=128), to[:, :, 0:rn],
                          reads=[tor], writes=[r_dst])

    def stage_ssd(self, l):
        S, I, SL = self.S, self.I, self.S_len
        pb = self.pb
        A_, B_, C_, Z_, CB_ = (pb[0], pb[1]), (pb[2], pb[3]), (pb[4], pb[5]), pb[6], pb[7]
        with Stage(self) as stg:
            dbc, r_dbc = stg.sb([64, 32], F32, "dbc")
            ngb, r_ngb = stg.sb([64, D], F32, "ngb")
            S.dma("sp", dbc[:], I["dsk32"][l:l + 1, :].to_broadcast([64, 32]), writes=[r_dbc])
            S.dma("sp", ngb[:], I["ng_row"][l:l + 1, :].to_broadcast([64, D]), writes=[r_ngb])
            h32, r_h32 = stg.sb([128, D], F32, "h32")
            h16, r_h16 = stg.sb([128, D], BF16, "h16")
            S.op("dve", lambda e: e.memset(h32[:], 0.0), writes=[r_h32])
            S.op("dve", lambda e: e.memset(h16[:], 0.0), writes=[r_h16])
            tokx_ = [stg.sb([64, 4096], F32, "tokx") for _ in range(3)]
            dta_ = [stg.sb([64, 64], F32, "dta") for _ in range(3)]
            bcb_ = [stg.sb([128, 16, 256], BF16, "bcb") for _ in range(2)]
            acs, r_acs = stg.sb([64, 32], F32, "acs")
            ecs_ = [stg.sb([64, 32], F32, "ecs") for _ in range(2)]
            dte, r_dte = stg.sb([64, 32], F32, "dte")
            cd_ = [stg.sb([128, 32], F32, "cd") for _ in range(2)]
            R1h, r_R1h = stg.sb([64, 32, 64], BF16, "R1h")
            R1l, r_R1l = stg.sb([64, 32, 64], BF16, "R1l")
            R2h, r_R2h = stg.sb([64, 32, 64], BF16, "R2h")
            R2l, r_R2l = stg.sb([64, 32, 64], BF16, "R2l")
            ah16, r_ah16 = stg.sb([64, 32], BF16, "ah16")
            ah32, r_ah32 = stg.sb([64, 32], F32, "ah32")
            al32, r_al32 = stg.sb([64, 32], F32, "al32")
            xdt_ = [stg.sb([64, D], BF16, "xdt") for _ in range(2)]
            xdtd_ = [stg.sb([64, D], BF16, "xdtd") for _ in range(2)]
            bt16_ = [stg.sb([64, 1024], BF16, "bt16") for _ in range(3)]
            cbs, r_cbs = stg.sb([64, 512], BF16, "cbs")
            LT, r_LT = stg.sb([64, D], BF16, "LT")
            MT_ = [stg.sb([64, D], BF16, "MT") for _ in range(2)]
            yv, r_yv = stg.sb([64, 1024], F32, "yv")
            t2, r_t2 = stg.sb([64, 1024], F32, "t2")
            gb, r_gb = stg.sb([64, D], F32, "gb")
            gn_ = [stg.sb([64, D], F32, "gn") for _ in range(1)]
            hs, r_hs = stg.sb([128, 1024], F32, "hs")
            ss, r_ss = stg.sb([64, 2], F32, "ss")
            triu, ntriu, ones, r3, ident = self.triu, self.ntriu, self.ones, self.r3, self.ident
            nchunk = SL // 64

            def loads(c):
                t0 = c * 64
                tokx, r_tokx = tokx_[c % 3]
                dta, r_dta = dta_[c % 3]
                bt16, r_bt16 = bt16_[c % 3]
                if c % 4 == 0:
                    bcb, r_bcb = bcb_[(c // 4) % 2]
                    S.dma("sp", bcb[:], self.BCT[:, t0:t0 + 256].rearrange("(g n) t -> n g t", n=128), writes=[r_bcb])
                S.dma("sp", tokx[:, 0:D], self.TOK[t0:t0 + 64, 0:D], writes=[r_tokx], join=True)
                S.dma("sp", tokx[:, D:2 * D], self.TOK[t0:t0 + 64, T_SZ:T_SZ + D], writes=[r_tokx], join=True)
                S.dma("sp", dta[:], self.TOK[t0:t0 + 64, T_DT:T_DT + 64], writes=[r_dta])
                S.dma("pool", bt16[:], self.TOK[t0:t0 + 64, T_B:T_B + 1024], writes=[r_bt16])

            def phase1(c):
                tokx, r_tokx = tokx_[c % 3]
                dta, r_dta = dta_[c % 3]
                bcb, r_bcb = bcb_[(c // 4) % 2]
                ecs, r_ecs = ecs_[c % 2]
                cd, r_cd = cd_[c % 2]
                xdt, r_xdt = xdt_[c % 2]
                xdtd, r_xdtd = xdtd_[c % 2]
                MT, r_MT = MT_[c % 2]
                tq = (c % 4) * 64
                dt_ap = dta[:, 0:32]
                a_ap = dta[:, 32:64]
                zt, zr = Z_
                S.op("pe", lambda e: e.matmul(zt[0:64, 0:32], lhsT=triu[:, :], rhs=a_ap, start=True, stop=True),
                     reads=[r_dta, self.r_triu], writes=[zr])
                S.op("pe", lambda e: e.matmul(zt[:, 32:64], lhsT=ones[0:64, :], rhs=a_ap, start=True, stop=True),
                     reads=[r_dta, self.r_ones], writes=[zr])
                S.op("act", lambda e: e.copy(acs[:], zt[0:64, 0:32]), reads=[zr], writes=[r_acs])
                S.op("act", lambda e: e.activation(out=ecs[:], in_=zt[0:64, 0:32], func=AF.Exp), reads=[zr], writes=[r_ecs])
                S.op("act", lambda e: e.activation(out=cd[:], in_=zt[:, 32:64], func=AF.Exp), reads=[zr], writes=[r_cd])
                S.op("dve", lambda e: e.tensor_tensor(out=dte[:], in0=zt[0:64, 32:64], in1=acs[:], op=ALU.subtract),
                     reads=[zr, r_acs], writes=[r_dte])
                S.op("act", lambda e: e.activation(out=dte[:], in_=dte[:], func=AF.Exp), reads=[r_dte], writes=[r_dte])
                S.op("act", lambda e: e.copy(ah16[:], a_ap), reads=[r_dta], writes=[r_ah16])
                S.op("act", lambda e: e.copy(ah32[:], ah16[:]), reads=[r_ah16], writes=[r_ah32])
                S.op("dve", lambda e: e.tensor_tensor(out=al32[:], in0=a_ap, in1=ah32[:], op=ALU.subtract),
                     reads=[r_dta, r_ah32], writes=[r_al32])
                S.op("dve", lambda e: e.tensor_tensor(
                    out=R1h[:], in0=ah32[:, :].unsqueeze(2).to_broadcast([64, 32, 64]),
                    in1=triu[:, :].unsqueeze(1).to_broadcast([64, 32, 64]), op=ALU.mult),
                    reads=[r_ah32, self.r_triu], writes=[r_R1h])
                S.op("dve", lambda e: e.tensor_tensor(
                    out=R1l[:], in0=al32[:, :].unsqueeze(2).to_broadcast([64, 32, 64]),
                    in1=triu[:, :].unsqueeze(1).to_broadcast([64, 32, 64]), op=ALU.mult),
                    reads=[r_al32, self.r_triu], writes=[r_R1l])
                S.op("pool", lambda e: e.tensor_copy(R2h[:], ah32[:, :].unsqueeze(2).to_broadcast([64, 32, 64])),
                     reads=[r_ah32], writes=[r_R2h])
                S.op("pool", lambda e: e.tensor_copy(R2l[:], al32[:, :].unsqueeze(2).to_broadcast([64, 32, 64])),
                     reads=[r_al32], writes=[r_R2l])
                S.op("dve", lambda e: e.tensor_tensor(
                    out=xdt[:].rearrange("s (h p) -> s h p", h=32), in0=tokx[:, 0:D].rearrange("s (h p) -> s h p", h=32),
                    in1=dt_ap.unsqueeze(2).to_broadcast([64, 32, 64]), op=ALU.mult),
                    reads=[r_tokx, r_dta], writes=[r_xdt])
                S.op("dve", lambda e: e.tensor_tensor(
                    out=xdtd[:].rearrange("s (h p) -> s h p", h=32), in0=xdt[:].rearrange("s (h p) -> s h p", h=32),
                    in1=dte[:, :].unsqueeze(2).to_broadcast([64, 32, 64]), op=ALU.mult),
                    reads=[r_xdt, r_dte], writes=[r_xdtd])
                cbt, cbr = CB_
                for g in range(8):
                    S.op("pe", lambda e, g=g: e.matmul(
                        cbt[0:64, g * 64:(g + 1) * 64], lhsT=bcb[:, g, tq:tq + 64], rhs=bcb[:, 8 + g, tq:tq + 64],
                        start=True, stop=True), reads=[r_bcb], writes=[cbr], sig=(g == 7))
                S.op("act", lambda e: e.copy(cbs[:], cbt[0:64, :]), reads=[cbr], writes=[r_cbs])
                for q in range(4):
                    at, ar = A_[q % 2]
                    hsl = slice(q * 8, q * 8 + 8)
                    S.op("pe", lambda e, at=at, hsl=hsl: e.matmul(at[0:64, :], lhsT=self.ones16[:, :], rhs=R1h[:, hsl, :],
                                                                  start=True, stop=False),
                         reads=[r_R1h, self.r_ones16], writes=[ar], sig=False)
                    S.op("pe", lambda e, at=at, hsl=hsl: e.matmul(at[0:64, :], lhsT=self.ones16[:, :], rhs=R1l[:, hsl, :],
                                                                  start=False, stop=False),
                         reads=[r_R1l, self.r_ones16], writes=[ar], sig=False)
                    S.op("pe", lambda e, at=at, hsl=hsl: e.matmul(at[0:64, :], lhsT=self.ntriu16[:, :], rhs=R2h[:, hsl, :],
                                                                  start=False, stop=False),
                         reads=[r_R2h, self.r_ntriu16], writes=[ar], sig=False)
                    S.op("pe", lambda e, at=at, hsl=hsl: e.matmul(at[0:64, :], lhsT=self.ntriu16[:, :], rhs=R2l[:, hsl, :],
                                                                  start=False, stop=False),
                         reads=[r_R2l, self.r_ntriu16], writes=[ar], sig=False)
                    S.op("pe", lambda e, at=at, hsl=hsl: e.matmul(at[0:64, :], lhsT=self.ident16[:, :], rhs=self.r3_16[:, hsl, :],
                                                                  start=False, stop=True),
                         reads=[self.r_r3_16, self.r_ident16], writes=[ar])
                    S.op("act", lambda e, at=at, q=q: e.activation(out=LT[:, q * 512:(q + 1) * 512], in_=at[0:64, :], func=AF.Exp),
                         reads=[ar], writes=[r_LT])
                S.op("dve", lambda e: e.tensor_tensor(
                    out=MT[:].rearrange("s (g r t) -> s g r t", g=8, r=4),
                    in0=LT[:].rearrange("s (g r t) -> s g r t", g=8, r=4),
                    in1=cbs[:, :].rearrange("s (g t) -> s g t", g=8).unsqueeze(2).to_broadcast([64, 8, 4, 64]),
                    op=ALU.mult), reads=[r_LT, r_cbs], writes=[r_MT])

            def phase2(c):
                t0 = c * 64
                tokx, r_tokx = tokx_[c % 3]
                bcb, r_bcb = bcb_[(c // 4) % 2]
                ecs, r_ecs = ecs_[c % 2]
                cd, r_cd = cd_[c % 2]
                xdt, r_xdt = xdt_[c % 2]
                xdtd, r_xdtd = xdtd_[c % 2]
                bt16, r_bt16 = bt16_[c % 3]
                MT, r_MT = MT_[c % 2]
                gnt, r_gnt = gn_[0]
                tq = (c % 4) * 64
                for hh in range(2):
                    hs0 = hh * 16
                    for h in range(16):
                        bt_, br_ = B_[h // 8]
                        S.op("pe", lambda e, h=h, bt_=bt_, hs0=hs0: e.matmul(
                            bt_[0:64, (h % 8) * 64:(h % 8 + 1) * 64], lhsT=MT[:, (hs0 + h) * 64:(hs0 + h + 1) * 64],
                            rhs=xdt[:, (hs0 + h) * 64:(hs0 + h + 1) * 64], start=True, stop=True),
                            reads=[r_MT, r_xdt], writes=[br_], sig=(h % 8 == 7))
                    for g in range(4):
                        ct_, cr_ = C_[g // 2]
                        gg = hh * 4 + g
                        S.op("pe", lambda e, g=g, gg=gg, ct_=ct_: e.matmul(
                            ct_[0:64, (g % 2) * 256:(g % 2 + 1) * 256], lhsT=bcb[:, 8 + gg, tq:tq + 64],
                            rhs=h16[:, gg * 256:(gg + 1) * 256], start=True, stop=True),
                            reads=[r_bcb, r_h16], writes=[cr_], sig=(g % 2 == 1))
                    for b in range(2):
                        ct_, cr_ = C_[b]
                        bt_, br_ = B_[b]
                        S.op("dve", lambda e, ct_=ct_, b=b, hs0=hs0: e.tensor_tensor(
                            out=yv[:, b * 512:(b + 1) * 512].rearrange("s (h p) -> s h p", h=8),
                            in0=ct_[0:64, :].rearrange("s (h p) -> s h p", h=8),
                            in1=ecs[:, hs0 + b * 8:hs0 + b * 8 + 8].unsqueeze(2).to_broadcast([64, 8, 64]), op=ALU.mult),
                            reads=[cr_, r_ecs], writes=[r_yv])
                        S.op("dve", lambda e, bt_=bt_, b=b: e.tensor_tensor(
                            out=yv[:, b * 512:(b + 1) * 512], in0=yv[:, b * 512:(b + 1) * 512], in1=bt_[0:64, :], op=ALU.add),
                            reads=[br_, r_yv], writes=[r_yv])
                    for g in range(4):
                        ct_, cr_ = C_[g // 2]
                        gg = hh * 4 + g
                        S.op("pe", lambda e, g=g, gg=gg, ct_=ct_: e.matmul(
                            ct_[:, (g % 2) * 256:(g % 2 + 1) * 256], lhsT=bt16[:, gg * 128:(gg + 1) * 128],
                            rhs=xdtd[:, gg * 256:(gg + 1) * 256], start=True, stop=True),
                            reads=[r_bt16, r_xdtd], writes=[cr_], sig=(g % 2 == 1))
                    S.op("pool", lambda e, hh=hh: e.tensor_tensor(
                        out=t2[:].rearrange("s (h p) -> s h p", h=16),
                        in0=tokx[:, hh * 1024:(hh + 1) * 1024].rearrange("s (h p) -> s h p", h=16),
                        in1=dbc[:, hh * 16:(hh + 1) * 16].unsqueeze(2).to_broadcast([64, 16, 64]), op=ALU.mult),
                        reads=[r_tokx, r_dbc], writes=[r_t2])
                    S.op("pool", lambda e: e.tensor_tensor(out=t2[:], in0=t2[:], in1=yv[:], op=ALU.add),
                         reads=[r_t2, r_yv], writes=[r_t2])
                    S.op("dve", lambda e, hh=hh: e.tensor_tensor(
                        out=gb[:, hh * 1024:(hh + 1) * 1024], in0=t2[:], in1=tokx[:, D + hh * 1024:D + (hh + 1) * 1024], op=ALU.mult),
                        reads=[r_t2, r_tokx], writes=[r_gb])
                    S.op("dve", lambda e, hh=hh, hs0=hs0: e.tensor_tensor(
                        out=hs[:].rearrange("n (h p) -> n h p", h=16),
                        in0=h32[:, hh * 1024:(hh + 1) * 1024].rearrange("n (h p) -> n h p", h=16),
                        in1=cd[:, hs0:hs0 + 16].unsqueeze(2).to_broadcast([128, 16, 64]), op=ALU.mult),
                        reads=[r_h32, r_cd], writes=[r_hs])
                    for b in range(2):
                        ct_, cr_ = C_[b]
                        S.op("dve", lambda e, ct_=ct_, b=b, hh=hh: e.tensor_tensor(
                            out=h32[:, hh * 1024 + b * 512:hh * 1024 + (b + 1) * 512], in0=hs[:, b * 512:(b + 1) * 512],
                            in1=ct_[:, :], op=ALU.add), reads=[r_hs, cr_], writes=[r_h32])
                    S.op("act", lambda e, hh=hh: e.copy(h16[:, hh * 1024:(hh + 1) * 1024], h32[:, hh * 1024:(hh + 1) * 1024]),
                         reads=[r_h32], writes=[r_h16])
                S.op("act", lambda e: e.activation(out=gnt[:], in_=gb[:], func=AF.Square, accum_out=ss[:, 0:1]),
                     reads=[r_gb], writes=[r_gnt, r_ss])
                S.op("act", lambda e: e.activation(out=ss[:, 1:2], in_=ss[:, 0:1], func=AF.Sqrt, scale=1.0 / D, bias=self.eps_t[0:64, 0:1]),
                     reads=[r_ss, self.r_eps], writes=[r_ss])
                S.op("dve", lambda e: e.reciprocal(out=ss[:, 1:2], in_=ss[:, 1:2]), reads=[r_ss], writes=[r_ss])
                S.op("dve", lambda e: e.scalar_tensor_tensor(out=gnt[:], in0=gb[:], scalar=ss[:, 1:2], in1=ngb[:],
                                                             op0=ALU.mult, op1=ALU.mult),
                     reads=[r_gb, r_ss, r_ngb], writes=[r_gnt])
                S.dma("sp", self.TOK2[t0:t0 + 64, 0:D], gnt[:], reads=[r_gnt])

            loads(0)
            if nchunk > 1:
                loads(1)
            phase1(0)
            for c in range(nchunk):
                if c + 2 < nchunk:
                    loads(c + 2)
                if c + 1 < nchunk:
                    phase1(c + 1)
                phase2(c)

    def stage_attn(self, l):
        S, I, SL = self.S, self.I, self.S_len
        pb = self.pb
        NT = SL // 128
        NQB = SL // 512
        NBIS = 22
        scale = 128.0 ** -0.5
        with Stage(self) as stg:
            ik2, r_ik2 = stg.sb([128, SL], BF16, "ik2")
            S.dma("sp", ik2[0:64, :], self.ikT[:, :], writes=[r_ik2], join=True)
            S.dma("sp", ik2[64:128, :], self.ikT[:, :], writes=[r_ik2], join=True)
            acc, r_acc = stg.sb([128, SL], F32, "acc")
            wk, r_wk = stg.sb([128, SL], F32, "wk")
            bs, r_bs = stg.sb([128, 8], F32, "bs")
            wtab, r_wtab = stg.sb([128, 32], F32, "wtab")
            mT_ = [stg.sb([128, NT, 512], BF16, "maskT") for _ in range(2)]
            iq_ = [stg.sb([128, 8, 128], BF16, "iqt") for _ in range(2)]
            wt_ = [stg.sb([128, 16], F32, "wtok") for _ in range(2)]
            rb_ = [stg.sb([128, 512], F32, "rbuf") for _ in range(3)]
            kh_ = [stg.sb([128, SL], BF16, "kh") for _ in range(2)]
            qh_ = [stg.sb([128, 512], BF16, "qh") for _ in range(2)]
            va_ = [stg.sb([128, NT, 129], BF16, "va") for _ in range(2)]
            pt_ = [stg.sb([128, 512], BF16, "pt") for _ in range(3)]
            ya_ = [stg.sb([128, 4, 1024], BF16, "ya") for _ in range(1)]
            rd, r_rd = stg.sb([128, 4], F32, "rd")
            for v in range(2):
                S.op("pool", lambda e, v=v: e.memset(va_[v][0][:, :, 128:129], 1.0), writes=[va_[v][1]])
            st = {"qi": 0, "ri": 0, "pi": 0, "hi": 0}

            def index_tile(qb, u):
                maskT, r_maskT = mT_[qb % 2]
                qt = 4 * qb + u
                t0 = qt * 128
                Kq = 128 * (qt + 1)
                iqt, r_iqt = iq_[st["qi"] % 2]
                wtk, r_wtk = wt_[st["qi"] % 2]
                st["qi"] += 1
                S.dma("sp", iqt[:], self.iqT[:, t0:t0 + 128].rearrange("(c p) t -> p c t", p=128), writes=[r_iqt])
                S.dma("sp", wtk[:], self.TOK[t0:t0 + 128, T_IW:T_IW + 16], writes=[r_wtk])
                for s0 in range(0, Kq, 512):
                    ncol = min(512, Kq - s0)
                    for h in range(16):
                        pbt, pbr = pb[2 + h % 2]
                        hp = (h % 2) * 64
                        S.op("pe", lambda e, iqt=iqt, h=h, hp=hp, s0=s0, ncol=ncol, pbt=pbt: e.matmul(
                            pbt[:, 0:ncol], lhsT=iqt[hp:hp + 64, h // 2, :], rhs=ik2[hp:hp + 64, s0:s0 + ncol],
                            start=True, stop=True), reads=[r_iqt, r_ik2], writes=[pbr])
                        rbt, rbr = rb_[st["ri"] % 3]
                        st["ri"] += 1
                        S.op("act", lambda e, rbt=rbt, pbt=pbt, ncol=ncol: e.activation(
                            out=rbt[:, 0:ncol], in_=pbt[:, 0:ncol], func=AF.Relu), reads=[pbr], writes=[rbr])
                        if h == 0:
                            S.op("dve", lambda e, rbt=rbt, wtk=wtk, s0=s0, ncol=ncol: e.tensor_scalar(
                                out=acc[:, s0:s0 + ncol], in0=rbt[:, 0:ncol], scalar1=wtk[:, 0:1], scalar2=None, op0=ALU.mult),
                                reads=[rbr, r_wtk], writes=[r_acc])
                        else:
                            S.op("dve", lambda e, rbt=rbt, wtk=wtk, s0=s0, ncol=ncol, h=h: e.scalar_tensor_tensor(
                                out=acc[:, s0:s0 + ncol], in0=rbt[:, 0:ncol], scalar=wtk[:, h:h + 1], in1=acc[:, s0:s0 + ncol],
                                op0=ALU.mult, op1=ALU.add), reads=[rbr, r_wtk, r_acc], writes=[r_acc])
                if Kq > 256:
                    S.op("dve", lambda e, Kq=Kq: e.tensor_reduce(out=bs[:, 0:1], in_=acc[:, 0:Kq], axis=mybir.AxisListType.X, op=ALU.min),
                         reads=[r_acc], writes=[r_bs])
                    S.op("dve", lambda e, Kq=Kq: e.tensor_reduce(out=bs[:, 5:6], in_=acc[:, 0:Kq], axis=mybir.AxisListType.X, op=ALU.max),
                         reads=[r_acc], writes=[r_bs])
                    S.op("dve", lambda e: e.tensor_tensor(out=bs[:, 1:2], in0=bs[:, 5:6], in1=bs[:, 0:1], op=ALU.subtract),
                         reads=[r_bs], writes=[r_bs])
                    S.op("dve", lambda e: e.tensor_scalar(out=bs[:, 1:2], in0=bs[:, 1:2], scalar1=1.0001, scalar2=1e-6,
                                                          op0=ALU.mult, op1=ALU.add), reads=[r_bs], writes=[r_bs])
                S.op("dve", lambda e, Kq=Kq: e.memset(acc[0:64, Kq - 64:Kq], NEG), reads=[r_acc], writes=[r_acc])
                if Kq > 256:
                    S.op("dve", lambda e: e.tensor_scalar(out=wtab[:, 0:NBIS + 2], in0=self.p2tab[:, 0:NBIS + 2], scalar1=bs[:, 1:2], scalar2=None,
                                                          op0=ALU.mult), reads=[r_bs, self.r_p2tab], writes=[r_wtab])
                    S.op("dve", lambda e: e.tensor_tensor(out=bs[:, 2:3], in0=bs[:, 0:1], in1=wtab[:, 1:2], op=ALU.add),
                         reads=[r_bs, r_wtab], writes=[r_bs])
                    for k in range(NBIS):
                        S.op("dve", lambda e, Kq=Kq: e.tensor_scalar(out=wk[:, 0:Kq], in0=acc[:, 0:Kq], scalar1=bs[:, 2:3], scalar2=0.0,
                                                                     op0=ALU.is_ge, op1=ALU.add, accum_out=bs[:, 3:4]),
                             reads=[r_acc, r_bs], writes=[r_wk, r_bs])
                        S.op("dve", lambda e: e.tensor_scalar(out=bs[:, 4:5], in0=bs[:, 3:4], scalar1=256.0, scalar2=0.5,
                                                              op0=ALU.is_ge, op1=ALU.subtract), reads=[r_bs], writes=[r_bs])
                        S.op("dve", lambda e, k=k: e.scalar_tensor_tensor(out=bs[:, 2:3], in0=bs[:, 4:5], scalar=wtab[:, k + 1:k + 2],
                                                                          in1=bs[:, 2:3], op0=ALU.mult, op1=ALU.add),
                             reads=[r_bs, r_wtab], writes=[r_bs])
                    S.op("dve", lambda e: e.tensor_tensor(out=bs[:, 0:1], in0=bs[:, 2:3], in1=wtab[:, NBIS + 1:NBIS + 2], op=ALU.subtract),
                         reads=[r_bs, r_wtab], writes=[r_bs])
                    thr_ap, thr_r = bs, r_bs
                else:
                    thr_ap, thr_r = self.nbig, self.r_nbig
                S.op("dve", lambda e, Kq=Kq, thr_ap=thr_ap: e.tensor_scalar(
                    out=wk[:, 0:Kq], in0=acc[:, 0:Kq], scalar1=thr_ap[:, 0:1], scalar2=None, op0=ALU.is_ge),
                    reads=[r_acc, thr_r], writes=[r_wk])

            def mask_transposes(qb, u):
                maskT, r_maskT = mT_[qb % 2]
                qt = 4 * qb + u
                for j4 in range(0, qt + 1, 4):
                    jn = min(4, qt + 1 - j4)
                    pbt, pbr = pb[2]
                    for j in range(j4, j4 + jn):
                        S.op("pe", lambda e, j=j, j4=j4, pbt=pbt: e.transpose(
                            pbt[:, (j - j4) * 128:(j - j4 + 1) * 128], wk[:, j * 128:(j + 1) * 128], self.ident[:, :]),
                            reads=[r_wk, self.r_ident], writes=[pbr])
                    S.op("act", lambda e, j4=j4, jn=jn, u=u, pbt=pbt, maskT=maskT: e.copy(
                        maskT[:, j4:j4 + jn, u * 128:(u + 1) * 128], pbt[:, 0:jn * 128].rearrange("p (j t) -> p j t", j=jn)),
                        reads=[pbr], writes=[r_maskT])

            def head_loads(qb, h):
                nk = 4 * (qb + 1)
                K = nk * 128
                kh, r_kh = kh_[h % 2]
                qh, r_qh = qh_[h % 2]
                va, r_va = va_[h % 2]
                S.dma("sp", kh[:, 0:K], self.kT[h * 128:(h + 1) * 128, 0:K], writes=[r_kh])
                S.dma("sp", qh[:, :], self.qT[h * 128:(h + 1) * 128, qb * 512:(qb + 1) * 512], writes=[r_qh])
                for j8 in range(0, nk, 8):
                    jn8 = min(8, nk - j8)
                    S.dma("pool", va[:, j8:j8 + jn8, 0:128],
                          self.TOK[j8 * 128:(j8 + jn8) * 128, T_V + h * 128:T_V + (h + 1) * 128].rearrange("(j s) d -> s j d", s=128),
                          writes=[r_va], join=True)

            def attn_head(qb, h):
                maskT, r_maskT = mT_[qb % 2]
                ya, r_ya = ya_[0]
                nk = 4 * (qb + 1)
                kh, r_kh = kh_[h % 2]
                qh, r_qh = qh_[h % 2]
                va, r_va = va_[h % 2]
                for j in range(nk):
                    pbt, pbr = pb[j % 2]
                    S.op("pe", lambda e, kh=kh, qh=qh, j=j, pbt=pbt: e.matmul(
                        pbt[:, :], lhsT=kh[:, j * 128:(j + 1) * 128], rhs=qh[:, :], start=True, stop=True),
                        reads=[r_kh, r_qh], writes=[pbr])
                    pt, r_pt = pt_[st["pi"] % 3]
                    st["pi"] += 1
                    S.op("act", lambda e, pt=pt, pbt=pbt: e.activation(out=pt[:], in_=pbt[:, :], func=AF.Exp, scale=scale),
                         reads=[pbr], writes=[r_pt])
                    S.op("pool", lambda e, pt=pt, j=j, maskT=maskT: e.tensor_tensor(out=pt[:], in0=pt[:], in1=maskT[:, j, :], op=ALU.mult),
                         reads=[r_pt, r_maskT], writes=[r_pt])
                    for u in range(4):
                        jl = 4 * qb + u
                        if j > jl:
                            continue
                        ot, orr = pb[4 + u]
                        S.op("pe", lambda e, pt=pt, va=va, j=j, u=u, ot=ot, jl=jl: e.matmul(
                            ot[:, 0:129], lhsT=pt[:, u * 128:(u + 1) * 128], rhs=va[:, j, :], start=(j == 0), stop=(j == jl)),
                            reads=[r_pt, r_va], writes=[orr])
                for u in range(4):
                    ot, orr = pb[4 + u]
                    S.op("act", lambda e, ot=ot, u=u: e.copy(rd[:, u:u + 1], ot[:, 128:129]), reads=[orr], writes=[r_rd])
                    S.op("dve", lambda e, u=u: e.reciprocal(out=rd[:, u:u + 1], in_=rd[:, u:u + 1]), reads=[r_rd], writes=[r_rd])
                    S.op("act", lambda e, ot=ot, u=u, h=h: e.activation(
                        out=ya[:, u, h * 128:(h + 1) * 128], in_=ot[:, 0:128], func=AF.Copy, scale=rd[:, u:u + 1]),
                        reads=[orr, r_rd], writes=[r_ya])
                if h == 7:
                    S.dma("pool", self.TOK2[qb * 512:(qb + 1) * 512, D:D + 1024].rearrange("(u p) c -> p u c", p=128), ya[:],
                          reads=[r_ya])

            for qb in range(NQB + 1):
                if qb < NQB:
                    nk = 4 * (qb + 1)
                    S.op("pool", lambda e, nk=nk, qb=qb: e.memset(mT_[qb % 2][0][:, 0:nk, :], 0.0), writes=[mT_[qb % 2][1]])
                if qb >= 1:
                    head_loads(qb - 1, 0)
                for u in range(4):
                    if qb < NQB:
                        index_tile(qb, u)
                    if qb >= 1:
                        for hh in range(2):
                            h = 2 * u + hh
                            if h + 1 < 8:
                                head_loads(qb - 1, h + 1)
                            attn_head(qb - 1, h)
                    if qb < NQB:
                        mask_transposes(qb, u)

    def stage_merge(self, l, xsrc):
        S, I, SL = self.S, self.I, self.S_len
        pb = self.pb
        TB = min(1024, SL)
        NH = TB // 512
        with Stage(self) as stg:
            a16, r_a16 = stg.sb([128, 24, TB], BF16, "a16")
            s16, r_s16 = stg.sb([128, 8, TB], BF16, "s16")
            mg, r_mg = stg.sb([128, KC, TB], BF16, "mg")
            xc_ = [stg.sb([128, TB], F32, "xc") for _ in range(3)]
            gt_ = [stg.sb([128, 3, 512], F32, "gt") for _ in range(2)]
            w_ = [(stg.sb([128, 16, 256], BF16, "w1"), stg.sb([128, 8, 256], BF16, "w2"), stg.sb([128, 8, 256], BF16, "w3"))
                  for _ in range(2)]
            m1, r_m1 = stg.sb([128, 512], F32, "m1")
            m2, r_m2 = stg.sb([128, 512], F32, "m2")
            gtm = self.mod[:, l, 32:48]
            jobs = []
            for tb in range(SL // TB):
                for dg in range(8):
                    jobs.append((tb, "br", dg))
                for dg in range(8):
                    jobs.append((tb, "out", dg))

            def wload(ji):
                tb, kind, dg = jobs[ji]
                (w1, r_w1), (w2, r_w2), (w3, r_w3) = w_[ji % 2]
                if kind == "br":
                    self.load_wgroup(w1, r_w1, I["w_br_ssd"][l], dg * 256, 256, 16)
                    self.load_wgroup(w2, r_w2, I["w_br_sc"][l], dg * 256, 256, 8)
                    self.load_wgroup(w3, r_w3, I["w_br_att"][l], dg * 256, 256, 8)
                else:
                    self.load_wgroup(w1, r_w1, I["w_out"][l], dg * 256, 256, 16)

            gi = 0
            xi = 0
            wload(0)
            for ji, (tb, kind, dg) in enumerate(jobs):
                t0 = tb * TB
                if ji + 1 < len(jobs):
                    wload(ji + 1)
                (w1, r_w1), (w2, r_w2), (w3, r_w3) = w_[ji % 2]
                if kind == "br" and dg == 0:
                    for k0 in range(0, 24, 8):
                        S.dma("pool", a16[:, k0:k0 + 8, :],
                              self.TT2[k0 * 128:(k0 + 8) * 128, t0:t0 + TB].rearrange("(k p) t -> p k t", p=128),
                              writes=[r_a16], join=True)
                    S.dma("sp", s16[:], self.yscT[:, t0:t0 + TB].rearrange("(k p) t -> p k t", p=128), writes=[r_s16])
                for j in range(2):
                    dc = dg * 2 + j
                    cs = slice(j * 128, (j + 1) * 128)
                    if kind == "br":
                        for hf in range(NH):
                            ts_ = slice(hf * 512, (hf + 1) * 512)
                            gt, r_gt = gt_[gi % 2]
                            S.dma("sp", gt[:], self.gT[:, t0 + hf * 512:t0 + (hf + 1) * 512].rearrange(
                                "(b dc p) t -> dc p b t", b=3, p=128)[dc], writes=[r_gt])
                            p1, p1r = pb[0 + (gi % 2) * 3]
                            p2, p2r = pb[1 + (gi % 2) * 3]
                            p3, p3r = pb[2 + (gi % 2) * 3]
                            gi += 1
                            for kc in range(16):
                                S.op("pe", lambda e, w1=w1, kc=kc, cs=cs, p1=p1, ts_=ts_: e.matmul(
                                    p1[:, :], lhsT=w1[:, kc, cs], rhs=a16[:, kc, ts_], start=(kc == 0), stop=(kc == 15)),
                                    reads=[r_w1, r_a16], writes=[p1r], sig=(kc == 15))
                            for kc in range(8):
                                S.op("pe", lambda e, w2=w2, kc=kc, cs=cs, p2=p2, ts_=ts_: e.matmul(
                                    p2[:, :], lhsT=w2[:, kc, cs], rhs=s16[:, kc, ts_], start=(kc == 0), stop=(kc == 7)),
                                    reads=[r_w2, r_s16], writes=[p2r], sig=(kc == 7))
                            for kc in range(8):
                                S.op("pe", lambda e, w3=w3, kc=kc, cs=cs, p3=p3, ts_=ts_: e.matmul(
                                    p3[:, :], lhsT=w3[:, kc, cs], rhs=a16[:, 16 + kc, ts_], start=(kc == 0), stop=(kc == 7)),
                                    reads=[r_w3, r_a16], writes=[p3r], sig=(kc == 7))
                            S.op("dve", lambda e, gt=gt, p1=p1: e.tensor_tensor(out=m1[:], in0=p1[:, :], in1=gt[:, 0, :], op=ALU.mult),
                                 reads=[p1r, r_gt], writes=[r_m1])
                            S.op("dve", lambda e, gt=gt, p2=p2: e.tensor_tensor(out=m2[:], in0=p2[:, :], in1=gt[:, 1, :], op=ALU.mult),
                                 reads=[p2r, r_gt], writes=[r_m2])
                            S.op("dve", lambda e: e.tensor_tensor(out=m1[:], in0=m1[:], in1=m2[:], op=ALU.add),
                                 reads=[r_m1, r_m2], writes=[r_m1])
                            S.op("dve", lambda e, gt=gt, p3=p3: e.tensor_tensor(out=m2[:], in0=p3[:, :], in1=gt[:, 2, :], op=ALU.mult),
                                 reads=[p3r, r_gt], writes=[r_m2])
                            S.op("dve", lambda e, dc=dc, ts_=ts_: e.tensor_tensor(out=mg[:, dc, ts_], in0=m1[:], in1=m2[:], op=ALU.add),
                                 reads=[r_m1, r_m2], writes=[r_mg])
                    else:
                        xc, r_xc = xc_[xi % 3]
                        xi += 1
                        S.dma("sp", xc[:], xsrc[dc * 128:(dc + 1) * 128, t0:t0 + TB], writes=[r_xc])
                        for hf in range(NH):
                            ts_ = slice(hf * 512, (hf + 1) * 512)
                            p1, p1r = pb[6 + hf % 2]
                            for kc in range(16):
                                S.op("pe", lambda e, w1=w1, kc=kc, cs=cs, p1=p1, ts_=ts_: e.matmul(
                                    p1[:, :], lhsT=w1[:, kc, cs], rhs=mg[:, kc, ts_], start=(kc == 0), stop=(kc == 15)),
                                    reads=[r_w1, r_mg], writes=[p1r], sig=(kc == 15))
                            S.op("dve", lambda e, dc=dc, p1=p1, xc=xc, ts_=ts_: e.scalar_tensor_tensor(
                                out=xc[:, ts_], in0=p1[:, :], scalar=gtm[:, dc:dc + 1], in1=xc[:, ts_], op0=ALU.mult, op1=ALU.add),
                                reads=[p1r, r_xc, self.r_mod], writes=[r_xc])
                        S.dma("sp", self.xres[dc * 128:(dc + 1) * 128, t0:t0 + TB], xc[:], reads=[r_xc])

    def stage_ffn(self, l):
        S, I, SL = self.S, self.I, self.S_len
        pb = self.pb
        gtf = self.mod[:, l, 80:96]
        with Stage(self) as stg:
            xt, r_xt = stg.sb([128, KC, 512], F32, "xt")
            hT, r_hT = stg.sb([128, KC, 512], BF16, "hT")
            aT, r_aT = stg.sb([128, HC, 512], BF16, "aT")
            sg_ = [stg.sb([128, 512], F32, "sg") for _ in range(2)]
            wb_ = [stg.sb([128, 44, 256], BF16, "wffn") for _ in range(2)]
            wi = 0
            for tb in range(SL // 512):
                t0 = tb * 512
                xv = self.xres[:, t0:t0 + 512].rearrange("(dc p) t -> p dc t", p=128)
                for q4 in range(2):
                    S.dma("sp", xt[:, q4 * 8:(q4 + 1) * 8, :], xv[:, q4 * 8:(q4 + 1) * 8, :], reads=[self.r_xres], writes=[r_xt], join=True)
                self.norm_adaln(stg, xt, r_xt, 512, self.gm[:, l, 1, :], self.mod[:, l, 48:64], hT, r_hT)
                for hg in range(HC // 2):
                    wt, wr = wb_[wi % 2]
                    wi += 1
                    srcg = I["w_g"][l][:, hg * 256:(hg + 1) * 256].rearrange("(kc p) n -> p kc n", p=128)
                    srcu = I["w_u"][l][:, hg * 256:(hg + 1) * 256].rearrange("(kc p) n -> p kc n", p=128)
                    S.dma("pool", wt[:, 0:16, :], srcg, writes=[wr], join=True)
                    S.dma("pool", wt[:, 16:32, :], srcu, writes=[wr], join=True)
                    for j in range(2):
                        hc = hg * 2 + j
                        cs = slice(j * 128, (j + 1) * 128)
                        pg, pgr = pb[(hc % 2) * 2]
                        pu, pur = pb[(hc % 2) * 2 + 1]
                        for kc in range(16):
                            S.op("pe", lambda e, wt=wt, kc=kc, cs=cs, pg=pg: e.matmul(
                                pg[:, :], lhsT=wt[:, kc, cs], rhs=hT[:, kc, :], start=(kc == 0), stop=(kc == 15)),
                                reads=[wr, r_hT], writes=[pgr], sig=(kc == 15))
                        for kc in range(16):
                            S.op("pe", lambda e, wt=wt, kc=kc, cs=cs, pu=pu: e.matmul(
                                pu[:, :], lhsT=wt[:, 16 + kc, cs], rhs=hT[:, kc, :], start=(kc == 0), stop=(kc == 15)),
                                reads=[wr, r_hT], writes=[pur], sig=(kc == 15))
                        sg, r_sg = sg_[hc % 2]
                        S.op("act", lambda e, sg=sg, pg=pg: e.activation(out=sg[:], in_=pg[:, :], func=AF.Silu), reads=[pgr], writes=[r_sg])
                        S.op("dve", lambda e, sg=sg, pu=pu, hc=hc: e.tensor_tensor(out=aT[:, hc, :], in0=sg[:], in1=pu[:, :], op=ALU.mult),
                             reads=[r_sg, pur], writes=[r_aT])
                for dg in range(8):
                    wt, wr = wb_[wi % 2]
                    wi += 1
                    src = I["w_d"][l][:, dg * 256:(dg + 1) * 256].rearrange("(kc p) n -> p kc n", p=128)
                    S.dma("pool", wt[:, 0:22, :], src[:, 0:22, :], writes=[wr], join=True)
                    S.dma("pool", wt[:, 22:44, :], src[:, 22:44, :], writes=[wr], join=True)
                    for j in range(2):
                        dc = dg * 2 + j
                        cs = slice(j * 128, (j + 1) * 128)
                        p1, p1r = pb[4 + dc % 2]
                        for kc in range(HC):
                            S.op("pe", lambda e, wt=wt, kc=kc, cs=cs, p1=p1: e.matmul(
                                p1[:, :], lhsT=wt[:, kc, cs], rhs=aT[:, kc, :], start=(kc == 0), stop=(kc == HC - 1)),
                                reads=[wr, r_aT], writes=[p1r], sig=(kc == HC - 1))
                        S.op("dve", lambda e, dc=dc, p1=p1: e.scalar_tensor_tensor(
                            out=xt[:, dc, :], in0=p1[:, :], scalar=gtf[:, dc:dc + 1], in1=xt[:, dc, :], op0=ALU.mult, op1=ALU.add),
                            reads=[p1r, r_xt, self.r_mod], writes=[r_xt])
                for q4 in range(2):
                    S.dma("sp", self.xres[:, t0:t0 + 512].rearrange("(dc p) t -> p dc t", p=128)[:, q4 * 8:(q4 + 1) * 8, :],
                          xt[:, q4 * 8:(q4 + 1) * 8, :], reads=[r_xt], writes=[self.r_xres])

    def stage_final(self):
        S, I, SL = self.S, self.I, self.S_len
        with Stage(self) as stg:
            xt, r_xt = stg.sb([128, KC, 512], F32, "xt")
            ho, r_ho = stg.sb([128, KC, 512], F32, "ho")
            gf, r_gf = stg.sb([128, KC], F32, "gf")
            S.dma("sp", gf[:], I["g_final_pk"], writes=[self.r_gm])
            for tb in range(SL // 512):
                t0 = tb * 512
                xv = self.xres[:, t0:t0 + 512].rearrange("(dc p) t -> p dc t", p=128)
                for q4 in range(2):
                    S.dma("sp", xt[:, q4 * 8:(q4 + 1) * 8, :], xv[:, q4 * 8:(q4 + 1) * 8, :], reads=[self.r_xres], writes=[r_xt], join=True)
                self.norm_adaln(stg, xt, r_xt, 512, gf, None, ho, r_ho)
                for q4 in range(2):
                    S.dma("sp", self.outT[:, t0:t0 + 512].rearrange("(dc p) t -> p dc t", p=128)[:, q4 * 8:(q4 + 1) * 8, :],
                          ho[:, q4 * 8:(q4 + 1) * 8, :], reads=[r_ho])


def _pk(v, n):
    return np.ascontiguousarray(np.asarray(v, np.float32).reshape(n, 128).T)


def make_inputs(inp, b, depth):
    L = depth
    f = lambda a: np.ascontiguousarray(np.asarray(a, np.float32))
    m = {}
    m["xT"] = np.ascontiguousarray(np.asarray(inp["x"][b], np.float32).T)
    m["c_pk"] = _pk(inp["c"][b], KC)
    m["w_ada"] = f(inp["w_ada"][:L])
    m["b_ada_pk"] = np.stack([_pk(inp["b_ada"][l], 96) for l in range(L)])
    m["g_mix_pk"] = np.stack([_pk(inp["g_mix"][l], KC) for l in range(L)])
    m["g_ffn_pk"] = np.stack([_pk(inp["g_ffn"][l], KC) for l in range(L)])
    m["g_final_pk"] = _pk(inp["g_final"], KC)
    m["w_in"] = f(inp["w_in"][:L])
    m["b_gate_pk"] = np.stack([_pk(np.asarray(inp["b_gate"][l]).reshape(-1), 48) for l in range(L)])
    cw = np.asarray(inp["ssd_conv_w"], np.float32)[:L]
    m["conv_w_pk"] = np.ascontiguousarray(cw.reshape(L, 4, 32, 128).transpose(0, 3, 2, 1))
    m["conv_b_pk"] = np.stack([_pk(inp["ssd_conv_b"][l], 32) for l in range(L)])
    m["dtb"] = f(np.asarray(inp["ssd_dt_bias"])[:L].reshape(L, 32, 1))
    m["alog"] = f(np.asarray(inp["ssd_a_log"])[:L].reshape(L, 32, 1))
    m["dsk32"] = f(np.asarray(inp["ssd_d"], np.float32)[:L])
    m["ng_row"] = f(inp["ssd_norm_g"][:L])
    sw = np.asarray(inp["sc_conv_w"], np.float32)[:L]
    m["scw_pk"] = np.ascontiguousarray(sw.reshape(L, 3, 8, 128).transpose(0, 3, 2, 1))
    m["w_br_ssd"] = f(inp["w_br_ssd"][:L])
    m["w_br_sc"] = f(inp["w_br_sc"][:L])
    m["w_br_att"] = f(inp["w_br_att"][:L])
    m["w_out"] = f(inp["w_out"][:L])
    m["w_g"] = f(inp["w_ffn_gate"][:L])
    m["w_u"] = f(inp["w_ffn_up"][:L])
    m["w_d"] = f(inp["w_ffn_down"][:L])
    return m


_CACHE = {}


def kernel(**inputs):
    x = np.asarray(inputs["x"])
    Bsz, SL, _ = x.shape
    depth = np.asarray(inputs["w_in"]).shape[0]
    key = (SL, depth)
    if key not in _CACHE:
        _CACHE[key] = Builder(SL, depth).build()
    nc = _CACHE[key]
    in_maps = [make_inputs(inputs, b, depth) for b in range(Bsz)]
    res = run_bass_kernel_spmd(nc, in_maps, core_ids=list(range(Bsz)))
    out = np.stack([np.ascontiguousarray(res.results[b]["outT"].T) for b in range(Bsz)])
    return out.astype(np.float32)
```

```python
import numpy as np
from contextlib import ExitStack
import concourse.bass as bass
import concourse.mybir as mybir
from concourse.bass_utils import run_bass_kernel_spmd

F32 = mybir.dt.float32
BF16 = mybir.dt.bfloat16
ALU = mybir.AluOpType
AF = mybir.ActivationFunctionType

D = 2048
KC = 16
N_IN = 19568
FFN = 5632
HC = FFN // 128
EPS = 1e-6
NEG = -1.0e30
C_Z, C_XBC, C_DT, C_SC, C_Q, C_K, C_V, C_IQ, C_IK, C_IW, C_G = (
    0, 2048, 6144, 6176, 9248, 10272, 11296, 12320, 13344, 13408, 13424)
T_XS, T_B, T_SZ, T_V, T_DT, T_A, T_IW, T_END = 0, 2048, 3072, 5120, 6144, 6176, 6208, 6224


class Res:
    __slots__ = ("name", "w", "rs", "joined")

    def __init__(self, name=""):
        self.name = name
        self.w = []
        self.rs = {}
        self.joined = False


class Sched:
    ENG = ("pe", "act", "dve", "pool", "sp")

    def __init__(self, nc, stack, n_dma_sems=12):
        self.nc = nc
        self.eng = {"pe": nc.tensor, "act": nc.scalar, "dve": nc.vector,
                    "pool": nc.gpsimd, "sp": nc.sync}
        self.ops = {k: [] for k in self.ENG}
        self.cnt = {k: 0 for k in self.ENG}
        self.seen = {k: {} for k in self.ENG}
        self.sem = {k: stack.enter_context(nc.semaphore("s_" + k)) for k in self.ENG}
        self.live = set()
        self.dpool = {}
        self.drr = {}
        for q in ("sp", "pool"):
            self.dpool[q] = [[stack.enter_context(nc.semaphore(f"d_{q}{i}")), 0, (q, i)]
                             for i in range(n_dma_sems)]
            self.drr[q] = 0

    def _need(self, E, tok):
        key, sem, val = tok
        if key == E and E == "pe":
            return
        if self.seen[E].get(key, 0) >= val:
            return
        self.seen[E][key] = val
        eng = self.eng[E]
        self.ops[E].append(lambda: eng.wait_ge(sem, val))

    def _deps(self, E, reads, writes, join=False):
        for r in reads:
            for t in r.w:
                self._need(E, t)
        for w in writes:
            if not (join and w.joined):
                for t in w.w:
                    self._need(E, t)
            for t in w.rs.values():
                self._need(E, t)

    def _commit(self, tok, reads, writes, join=False):
        for r in reads:
            r.rs[tok[0]] = tok
            self.live.add(r)
        for w in writes:
            if join and w.joined:
                w.w.append(tok)
            else:
                w.w = [tok]
            w.joined = join
            w.rs = {}
            self.live.add(w)

    def op(self, E, fn, reads=(), writes=(), sig=True):
        reads = [r for r in reads if r is not None]
        writes = [w for w in writes if w is not None]
        self._deps(E, reads, writes)
        sem = self.sem[E]
        eng = self.eng[E]
        if sig:
            self.cnt[E] += 1
            tok = (E, sem, self.cnt[E])
            self.ops[E].append(lambda: fn(eng).then_inc(sem, 1))
        else:
            assert E == "pe"
            tok = (E, sem, self.cnt[E] + 1)
            self.ops[E].append(lambda: fn(eng))
        self._commit(tok, reads, writes)
        return tok

    def dma(self, q, out, in_, reads=(), writes=(), join=False):
        reads = [r for r in reads if r is not None]
        writes = [w for w in writes if w is not None]
        pool = self.dpool[q]
        ent = pool[self.drr[q]]
        self.drr[q] = (self.drr[q] + 1) % len(pool)
        if ent[1] > 0:
            self._need(q, (ent[2], ent[0], ent[1]))
        self._deps(q, reads, writes, join)
        ent[1] += 16
        sem = ent[0]
        tok = (ent[2], sem, ent[1])
        eng = self.eng[q]
        self.ops[q].append(lambda: eng.dma_start(out=out, in_=in_).then_inc(sem, 16))
        self._commit(tok, reads, writes, join)
        return tok

    def barrier(self):
        toks = []
        for P in self.ENG:
            if self.cnt[P] > 0:
                toks.append((P, self.sem[P], self.cnt[P]))
        for q in self.dpool:
            for ent in self.dpool[q]:
                if ent[1] > 0:
                    toks.append((ent[2], ent[0], ent[1]))
        for E in self.ENG:
            for t in toks:
                self._need(E, t)
        for r in self.live:
            r.w = []
            r.rs = {}
            r.joined = False
        self.live = set()

    def emit(self):
        ops = self.ops
        with self.nc.Block() as block:
            @block.sync
            def _(e):
                for f in ops["sp"]:
                    f()

            @block.scalar
            def _(e):
                for f in ops["act"]:
                    f()

            @block.vector
            def _(e):
                for f in ops["dve"]:
                    f()

            @block.gpsimd
            def _(e):
                for f in ops["pool"]:
                    f()

            @block.tensor
            def _(e):
                for f in ops["pe"]:
                    f()


class Stage:
    def __init__(self, k):
        self.k = k
        self.st = ExitStack()
        self.n = 0
        self.cache = {}

    def __enter__(self):
        return self

    def sb(self, shape, dt, name=None):
        self.n += 1
        self.k.uid += 1
        t = self.st.enter_context(self.k.nc.sbuf_tensor(f"{name or 't'}_{self.k.uid}", list(shape), dt))
        return t, Res(name or "t")

    def __exit__(self, *a):
        self.k.S.barrier()
        self.st.close()
        return False


class Builder:
    def __init__(self, S_len, depth, dbg=()):
        self.S_len = S_len
        self.depth = depth
        self.dbg = dbg
        self.uid = 0
        self.nc = bass.Bass("TRN2", target_bir_lowering=False)
        self.top = ExitStack()
        self.S = Sched(self.nc, self.top)

    def din(self, name, shape, dt=F32):
        return self.nc.dram_tensor(name, list(shape), dt, kind="ExternalInput").ap()

    def dscr(self, name, shape, dt=F32):
        kind = "ExternalOutput" if name in self.dbg else "Internal"
        return self.nc.dram_tensor(name, list(shape), dt, kind=kind).ap(), None

    def build(self):
        nc, S, L, SL = self.nc, self.S, self.depth, self.S_len
        I = {}
        I["xT"] = self.din("xT", [D, SL])
        I["c_pk"] = self.din("c_pk", [128, KC])
        I["w_ada"] = self.din("w_ada", [L, D, 6 * D])
        I["b_ada_pk"] = self.din("b_ada_pk", [L, 128, 96])
        I["g_mix_pk"] = self.din("g_mix_pk", [L, 128, KC])
        I["g_ffn_pk"] = self.din("g_ffn_pk", [L, 128, KC])
        I["g_final_pk"] = self.din("g_final_pk", [128, KC])
        I["w_in"] = self.din("w_in", [L, D, N_IN])
        I["b_gate_pk"] = self.din("b_gate_pk", [L, 128, 48])
        I["conv_w_pk"] = self.din("conv_w_pk", [L, 128, 32, 4])
        I["conv_b_pk"] = self.din("conv_b_pk", [L, 128, 32])
        I["dtb"] = self.din("dtb", [L, 32, 1])
        I["alog"] = self.din("alog", [L, 32, 1])
        I["dsk32"] = self.din("dsk32", [L, 32])
        I["ng_row"] = self.din("ng_row", [L, D])
        I["scw_pk"] = self.din("scw_pk", [L, 128, 8, 3])
        I["w_br_ssd"] = self.din("w_br_ssd", [L, D, D])
        I["w_br_sc"] = self.din("w_br_sc", [L, 1024, D])
        I["w_br_att"] = self.din("w_br_att", [L, 1024, D])
        I["w_out"] = self.din("w_out", [L, D, D])
        I["w_g"] = self.din("w_g", [L, D, FFN])
        I["w_u"] = self.din("w_u", [L, D, FFN])
        I["w_d"] = self.din("w_d", [L, FFN, D])
        self.I = I
        self.outT = self.nc.dram_tensor("outT", [D, SL], F32, kind="ExternalOutput").ap()
        self.xres, self.r_xres = self.dscr("xres", [D, SL])
        self.TT, self.r_TT = self.dscr("TT", [T_END, SL])
        self.TOK, self.r_TOK = self.dscr("TOK", [SL, T_END])
        self.xbcT, self.r_xbcT = self.dscr("xbcT", [4096, SL])
        self.scT, self.r_scT = self.dscr("scT", [3072, SL])
        self.qT, self.r_qT = self.dscr("qT", [1024, SL], BF16)
        self.kT, self.r_kT = self.dscr("kT", [1024, SL], BF16)
        self.iqT, self.r_iqT = self.dscr("iqT", [1024, SL], BF16)
        self.ikT, self.r_ikT = self.dscr("ikT", [64, SL], BF16)
        self.gT, self.r_gT = self.dscr("gT", [3 * D, SL])
        self.BCT, self.r_BCT = self.dscr("BCT", [2048, SL], BF16)
        self.yscT, self.r_yscT = self.dscr("yscT", [1024, SL], BF16)
        self.TOK2, self.r_TOK2 = self.dscr("TOK2", [SL, 3072])
        self.TT2, self.r_TT2 = self.dscr("TT2", [3072, SL])

        top = self.top
        self.pb = []
        for i in range(8):
            t = top.enter_context(nc.psum_tensor(f"pb{i}", [128, 512], F32))
            self.pb.append((t, Res(f"pb{i}")))
        def psb(name, shape, dt=F32):
            return top.enter_context(nc.sbuf_tensor(name, list(shape), dt)), Res(name)
        self.ident, self.r_ident = psb("ident", [128, 128])
        self.ones, self.r_ones = psb("ones", [128, 128])
        self.triu, self.r_triu = psb("triu", [64, 64])
        self.ntriu, self.r_ntriu = psb("ntriu", [64, 64])
        self.r3, self.r_r3 = psb("r3", [64, 32, 64])
        self.ones16, self.r_ones16 = psb("ones16", [64, 64], BF16)
        self.ntriu16, self.r_ntriu16 = psb("ntriu16", [64, 64], BF16)
        self.ident16, self.r_ident16 = psb("ident16", [64, 64], BF16)
        self.r3_16, self.r_r3_16 = psb("r3_16", [64, 32, 64], BF16)
        self.eps_t, self.r_eps = psb("eps_t", [128, 1])
        self.nbig, self.r_nbig = psb("nbig", [128, 1])
        self.p2tab, self.r_p2tab = psb("p2tab", [128, 32])
        self.mod, self.r_mod = psb("mod", [128, L, 96])
        self.gm, self.r_gm = psb("gm", [128, L, 2, KC])
        self.csb, self.r_csb = psb("csb", [128, KC], BF16)
        ident, ones, triu, ntriu, r3 = self.ident, self.ones, self.triu, self.ntriu, self.r3
        S.op("pool", lambda e: e.memset(ident[:], 0.0), writes=[self.r_ident])
        S.op("pool", lambda e: e.affine_select(out=ident[:], in_=ident[:], pattern=[[-1, 128]],
                                               compare_op=ALU.not_equal, fill=1.0, base=0, channel_multiplier=1),
             reads=[self.r_ident], writes=[self.r_ident])
        S.op("pool", lambda e: e.memset(ones[:], 1.0), writes=[self.r_ones])
        S.op("pool", lambda e: e.memset(triu[:], 1.0), writes=[self.r_triu])
        S.op("pool", lambda e: e.affine_select(out=triu[:], in_=triu[:], pattern=[[1, 64]],
                                               compare_op=ALU.is_ge, fill=0.0, base=0, channel_multiplier=-1),
             reads=[self.r_triu], writes=[self.r_triu])
        S.op("pool", lambda e: e.tensor_scalar(out=ntriu[:], in0=triu[:], scalar1=-1.0, scalar2=None, op0=ALU.mult),
             reads=[self.r_triu], writes=[self.r_ntriu])
        S.op("pool", lambda e: e.memset(r3[:], 0.0), writes=[self.r_r3])
        S.op("pool", lambda e: e.affine_select(out=r3[:], in_=r3[:], pattern=[[0, 32], [1, 64]],
                                               compare_op=ALU.is_ge, fill=-30000.0, base=0, channel_multiplier=-1),
             reads=[self.r_r3], writes=[self.r_r3])
        S.op("pool", lambda e: e.tensor_copy(self.ones16[:], ones[0:64, 0:64]), reads=[self.r_ones], writes=[self.r_ones16])
        S.op("pool", lambda e: e.tensor_copy(self.ntriu16[:], ntriu[:, :]), reads=[self.r_ntriu], writes=[self.r_ntriu16])
        S.op("pool", lambda e: e.tensor_copy(self.ident16[:], ident[0:64, 0:64]), reads=[self.r_ident], writes=[self.r_ident16])
        S.op("pool", lambda e: e.tensor_copy(self.r3_16[:], r3[:]), reads=[self.r_r3], writes=[self.r_r3_16])
        S.op("dve", lambda e: e.memset(self.eps_t[:], EPS), writes=[self.r_eps])
        S.op("dve", lambda e: e.memset(self.nbig[:], NEG / 2), writes=[self.r_nbig])
        for j in range(32):
            S.op("dve", lambda e, j=j: e.memset(self.p2tab[:, j:j + 1], 2.0 ** -j), writes=[self.r_p2tab])

        self.stage_mod()
        for l in range(L):
            self.stage_proj(l, I["xT"] if l == 0 else self.xres)
            if "stop_proj" in self.dbg:
                break
            self.stage_conv(l)
            if "stop_conv" in self.dbg:
                break
            self.stage_ssd(l)
            if "stop_ssd" in self.dbg:
                break
            self.stage_attn(l)
            self.transpose_pass(self.TOK2, self.r_TOK2, self.TT2, self.r_TT2, SL, 3072)
            if "stop_attn" in self.dbg:
                break
            self.stage_merge(l, I["xT"] if l == 0 else self.xres)
            if "stop_merge" in self.dbg:
                break
            self.stage_ffn(l)
        else:
            self.stage_final()
        S.barrier()
        S.emit()
        return nc

    def load_wgroup(self, stg_t, stg_r, W, col0, ncols, nk):
        src = W[:, col0:col0 + ncols].rearrange("(kc p) n -> p kc n", p=128)
        half = nk // 2 if nk >= 8 else nk
        for k0 in range(0, nk, half):
            self.S.dma("pool", stg_t[:, k0:k0 + half, 0:ncols], src[:, k0:k0 + half, :], writes=[stg_r], join=True)

    def norm_adaln(self, stg, xt, r_xt, N, gm_ap, sh_ap, hT, r_hT):
        S = self.S
        if ("norm", N) not in stg.cache:
            stg.cache[("norm", N)] = ([stg.sb([128, 512], F32, "sq") for _ in range(2)], stg.sb([128, N], F32, "rstd"),
                                      [stg.sb([128, N], F32, "ntmp") for _ in range(2)])
        sqs, (rstd, r_rstd), tmps = stg.cache[("norm", N)]
        ones, eps_t = self.ones, self.eps_t
        for hf in range(N // 512):
            pbt, pbr = self.pb[hf % 2]
            for dc in range(KC):
                s_t, s_r = sqs[dc % 2]
                S.op("act", lambda e, s_t=s_t, dc=dc, hf=hf: e.activation(
                    out=s_t[:], in_=xt[:, dc, hf * 512:(hf + 1) * 512], func=AF.Square),
                    reads=[r_xt], writes=[s_r])
                S.op("pe", lambda e, s_t=s_t, dc=dc, pbt=pbt: e.matmul(
                    pbt[:], lhsT=ones[:], rhs=s_t[:], start=(dc == 0), stop=(dc == KC - 1)),
                    reads=[s_r, self.r_ones], writes=[pbr])
            S.op("act", lambda e, pbt=pbt, hf=hf: e.activation(
                out=rstd[:, hf * 512:(hf + 1) * 512], in_=pbt[:], func=AF.Sqrt, scale=1.0 / D, bias=eps_t[:, 0:1]),
                reads=[pbr, self.r_eps], writes=[r_rstd])
        S.op("dve", lambda e: e.reciprocal(out=rstd[:], in_=rstd[:]), reads=[r_rstd], writes=[r_rstd])
        for dc in range(KC):
            tm, r_tm = tmps[dc % 2]
            S.op("dve", lambda e, dc=dc, tm=tm: e.tensor_tensor(out=tm[:], in0=xt[:, dc, :], in1=rstd[:], op=ALU.mult),
                 reads=[r_xt, r_rstd], writes=[r_tm])
            if sh_ap is not None:
                S.op("act", lambda e, dc=dc, tm=tm: e.activation(out=hT[:, dc, :], in_=tm[:], func=AF.Identity,
                                                                 scale=gm_ap[:, dc:dc + 1], bias=sh_ap[:, dc:dc + 1]),
                     reads=[r_tm, self.r_mod, self.r_gm], writes=[r_hT])
            else:
                S.op("act", lambda e, dc=dc, tm=tm: e.activation(out=hT[:, dc, :], in_=tm[:], func=AF.Copy,
                                                                 scale=gm_ap[:, dc:dc + 1]),
                     reads=[r_tm, self.r_mod, self.r_gm], writes=[r_hT])

    def stage_mod(self):
        S, I, L = self.S, self.I, self.depth
        with Stage(self) as stg:
            cf, r_cf = stg.sb([128, KC], F32, "cf")
            S.dma("sp", cf[:], I["c_pk"], writes=[r_cf])
            S.op("act", lambda e: e.activation(out=self.csb[:], in_=cf[:], func=AF.Silu), reads=[r_cf], writes=[self.r_csb])
            wb = [stg.sb([128, KC, 512], BF16, "wada") for _ in range(2)]
            bpk, r_bpk = stg.sb([128, L, 96], F32, "bpk")
            gmx, r_gmx = stg.sb([128, L, 2, KC], F32, "gmx")
            for l in range(L):
                S.dma("sp", bpk[:, l, :], I["b_ada_pk"][l], writes=[r_bpk], join=True)
                S.dma("sp", gmx[:, l, 0, :], I["g_mix_pk"][l], writes=[r_gmx], join=True)
                S.dma("sp", gmx[:, l, 1, :], I["g_ffn_pk"][l], writes=[r_gmx], join=True)
            gi = 0
            for l in range(L):
                pbt, pbr = self.pb[l % 2]
                for g in range(24):
                    wt, wr = wb[gi % 2]
                    gi += 1
                    self.load_wgroup(wt, wr, I["w_ada"][l], g * 512, 512, KC)
                    for j in range(4):
                        col = g * 4 + j
                        for kc in range(KC):
                            S.op("pe", lambda e, wt=wt, j=j, kc=kc, col=col, pbt=pbt: e.matmul(
                                pbt[:, col:col + 1], lhsT=wt[:, kc, j * 128:(j + 1) * 128], rhs=self.csb[:, kc:kc + 1],
                                start=(kc == 0), stop=(kc == KC - 1)),
                                reads=[wr, self.r_csb], writes=[pbr], sig=(kc == KC - 1))
                S.op("dve", lambda e, l=l, pbt=pbt: e.tensor_tensor(out=self.mod[:, l, :], in0=pbt[:, 0:96], in1=bpk[:, l, :],
                                                                    op=ALU.add),
                     reads=[pbr, r_bpk], writes=[self.r_mod])
                for v, sci in ((0, 1), (1, 4)):
                    S.op("dve", lambda e, l=l, v=v, sci=sci: e.scalar_tensor_tensor(
                        out=self.gm[:, l, v, :], in0=self.mod[:, l, sci * 16:(sci + 1) * 16], scalar=1.0,
                        in1=gmx[:, l, v, :], op0=ALU.add, op1=ALU.mult),
                        reads=[self.r_mod, r_gmx], writes=[self.r_gm])

    def stage_proj(self, l, xsrc):
        S, I, SL = self.S, self.I, self.S_len
        TB = min(1024, SL)
        W = I["w_in"][l]
        segs = [
            (C_Z, 2048, "silu", self.TT, self.r_TT, T_SZ),
            (C_XBC, 4096, "copy", self.xbcT, self.r_xbcT, 0),
            (C_DT, 32, "dt", self.TT, self.r_TT, T_DT),
            (C_SC, 3072, "copy", self.scT, self.r_scT, 0),
            (C_Q, 1024, "copy16", self.qT, self.r_qT, 0),
            (C_K, 1024, "copy16", self.kT, self.r_kT, 0),
            (C_V, 1024, "copy", self.TT, self.r_TT, T_V),
            (C_IQ, 1024, "copy16", self.iqT, self.r_iqT, 0),
            (C_IK, 64, "copy16", self.ikT, self.r_ikT, 0),
            (C_IW, 16, "copy", self.TT, self.r_TT, T_IW),
            (C_G, 6144, "gate", self.gT, self.r_gT, 0),
        ]
        with Stage(self) as stg:
            xt, r_xt = stg.sb([128, KC, TB], F32, "xt")
            hT, r_hT = stg.sb([128, KC, TB], BF16, "hT")
            wb = [stg.sb([128, KC, 512], BF16, "win") for _ in range(2)]
            ob32 = [stg.sb([128, TB], F32, "ob32") for _ in range(3)]
            ob16 = [stg.sb([128, TB], BF16, "ob16") for _ in range(2)]
            bg, r_bg = stg.sb([128, 48], F32, "bg")
            dtb, r_dtb = stg.sb([32, 1], F32, "dtb")
            nA, r_nA = stg.sb([32, 1], F32, "nA")
            av, r_av = stg.sb([32, TB], F32, "av")
            S.dma("sp", bg[:], I["b_gate_pk"][l], writes=[r_bg])
            S.dma("sp", dtb[:], I["dtb"][l], writes=[r_dtb])
            S.dma("sp", nA[:], I["alog"][l], writes=[r_nA])
            S.op("act", lambda e: e.activation(out=nA[:], in_=nA[:], func=AF.Exp), reads=[r_nA], writes=[r_nA])
            S.op("dve", lambda e: e.tensor_scalar(out=nA[:], in0=nA[:], scalar1=-1.0, scalar2=None, op0=ALU.mult),
                 reads=[r_nA], writes=[r_nA])
            gi = 0
            oi = 0
            ev = 0
            for tb in range(SL // TB):
                t0 = tb * TB
                xv = xsrc[:, t0:t0 + TB].rearrange("(dc p) t -> p dc t", p=128)
                for q4 in range(4):
                    S.dma("sp", xt[:, q4 * 4:(q4 + 1) * 4, :], xv[:, q4 * 4:(q4 + 1) * 4, :],
                          reads=[self.r_xres], writes=[r_xt], join=True)
                self.norm_adaln(stg, xt, r_xt, TB, self.gm[:, l, 0, :], self.mod[:, l, 0:16], hT, r_hT)
                for (c0, ncols, kind, dst, dst_r, drow) in segs:
                    for g0 in range(0, ncols, 512):
                        gn = min(512, ncols - g0)
                        wt, wr = wb[gi % 2]
                        gi += 1
                        self.load_wgroup(wt, wr, W, c0 + g0, gn, KC)
                        for j0 in range(0, gn, 128):
                            m = min(128, gn - j0)
                            use16 = kind == "copy16"
                            if use16:
                                ot, orr = ob16[oi % 2]
                            else:
                                ot, orr = ob32[oi % 3]
                            oi += 1
                            for hf in range(TB // 512):
                                pbt, pbr = self.pb[2 + (ev % 4)]
                                for kc in range(KC):
                                    S.op("pe", lambda e, wt=wt, kc=kc, j0=j0, m=m, hf=hf, pbt=pbt: e.matmul(
                                        pbt[0:m, :], lhsT=wt[:, kc, j0:j0 + m], rhs=hT[:, kc, hf * 512:(hf + 1) * 512],
                                        start=(kc == 0), stop=(kc == KC - 1)),
                                        reads=[wr, r_hT], writes=[pbr], sig=(kc == KC - 1))
                                osl = ot[0:m, hf * 512:(hf + 1) * 512]
                                if kind == "silu":
                                    S.op("act", lambda e, osl=osl, pbt=pbt, m=m: e.activation(out=osl, in_=pbt[0:m, :], func=AF.Silu),
                                         reads=[pbr], writes=[orr])
                                elif kind == "gate":
                                    gc = (g0 + j0) // 128
                                    S.op("act", lambda e, osl=osl, pbt=pbt, gc=gc: e.activation(
                                        out=osl, in_=pbt[:, :], func=AF.Sigmoid, bias=bg[:, gc:gc + 1]),
                                        reads=[pbr, r_bg], writes=[orr])
                                elif kind == "dt":
                                    S.op("act", lambda e, osl=osl, pbt=pbt: e.activation(
                                        out=osl, in_=pbt[0:32, :], func=AF.Exp, bias=dtb[:, 0:1]),
                                        reads=[pbr, r_dtb], writes=[orr])
                                    S.op("act", lambda e, osl=osl: e.activation(out=osl, in_=osl, func=AF.Ln, bias=1.0),
                                         reads=[orr], writes=[orr])
                                    S.op("dve", lambda e, osl=osl, hf=hf: e.tensor_scalar(
                                        out=av[:, hf * 512:(hf + 1) * 512], in0=osl, scalar1=nA[:, 0:1], scalar2=None, op0=ALU.mult),
                                        reads=[orr, r_nA], writes=[r_av])
                                else:
                                    if ev % 2 == 0:
                                        S.op("act", lambda e, osl=osl, pbt=pbt, m=m: e.copy(osl, pbt[0:m, :]),
                                             reads=[pbr], writes=[orr])
                                    else:
                                        S.op("dve", lambda e, osl=osl, pbt=pbt, m=m: e.tensor_copy(osl, pbt[0:m, :]),
                                             reads=[pbr], writes=[orr])
                                ev += 1
                            r0 = drow + g0 + j0
                            S.dma("sp", dst[r0:r0 + m, t0:t0 + TB], ot[0:m, :], reads=[orr], writes=[dst_r])
                            if kind == "dt":
                                S.dma("sp", self.TT[T_A:T_A + 32, t0:t0 + TB], av[:, :], reads=[r_av], writes=[self.r_TT])

    def stage_conv(self, l):
        S, I, SL = self.S, self.I, self.S_len
        TB = min(1024, SL)
        r_TTc = Res("TTc")
        with Stage(self) as stg:
            tjobs_free, tjobs_dep = self.transpose_jobs(stg, self.TT, self.TOK, T_END, SL, r_TTc)
            nfree = len(tjobs_free)
            cw, r_cw = stg.sb([128, 32, 4], F32, "cw")
            cb, r_cb = stg.sb([128, 32], F32, "cb")
            sw, r_sw = stg.sb([128, 8, 3], F32, "sw")
            S.dma("sp", cw[:], I["conv_w_pk"][l], writes=[r_cw])
            S.dma("sp", cb[:], I["conv_b_pk"][l], writes=[r_cb])
            S.dma("sp", sw[:], I["scw_pk"][l], writes=[r_sw])
            xin = [stg.sb([128, TB + 3], F32, "xin") for _ in range(2)]
            acc = [stg.sb([128, TB], F32, "acc") for _ in range(2)]
            o32 = [stg.sb([128, TB], F32, "o32") for _ in range(2)]
            o16 = [stg.sb([128, TB], BF16, "o16") for _ in range(2)]
            it = 0
            for rc in range(32):
                for tb in range(SL // TB):
                    if tjobs_free and (it % 4 == 0):
                        tjobs_free.pop(0)()
                    t0 = tb * TB
                    xi, xr = xin[it % 2]
                    ac, ar = acc[it % 2]
                    o3, o3r = o32[it % 2]
                    o6, o6r = o16[it % 2]
                    it += 1
                    rows = slice(rc * 128, (rc + 1) * 128)
                    if tb == 0:
                        S.op("dve", lambda e, xi=xi: e.memset(xi[:, 0:3], 0.0), writes=[xr])
                        S.dma("sp", xi[:, 3:3 + TB], self.xbcT[rows, 0:TB], reads=[self.r_xbcT], writes=[xr])
                    else:
                        S.dma("sp", xi[:, :], self.xbcT[rows, t0 - 3:t0 + TB], reads=[self.r_xbcT], writes=[xr])
                    S.op("act", lambda e, xi=xi, ac=ac, rc=rc: e.activation(
                        out=ac[:], in_=xi[:, 3:3 + TB], func=AF.Identity, scale=cw[:, rc, 3:4], bias=cb[:, rc:rc + 1]),
                        reads=[xr, r_cw, r_cb], writes=[ar])
                    for k in (2, 1, 0):
                        S.op("dve", lambda e, xi=xi, ac=ac, rc=rc, k=k: e.scalar_tensor_tensor(
                            out=ac[:], in0=xi[:, k:k + TB], scalar=cw[:, rc, k:k + 1], in1=ac[:], op0=ALU.mult, op1=ALU.add),
                            reads=[xr, r_cw, ar], writes=[ar])
                    if rc < 24:
                        S.op("act", lambda e, ac=ac, o3=o3: e.activation(out=o3[:], in_=ac[:], func=AF.Silu),
                             reads=[ar], writes=[o3r])
                        S.dma("pool", self.TT[rc * 128:(rc + 1) * 128, t0:t0 + TB], o3[:], reads=[o3r], writes=[r_TTc], join=True)
                        if rc >= 16:
                            S.op("dve", lambda e, o3=o3, o6=o6: e.tensor_copy(o6[:], o3[:]), reads=[o3r], writes=[o6r])
                    else:
                        S.op("act", lambda e, ac=ac, o6=o6: e.activation(out=o6[:], in_=ac[:], func=AF.Silu),
                             reads=[ar], writes=[o6r])
                    if rc >= 16:
                        S.dma("pool", self.BCT[(rc - 16) * 128:(rc - 15) * 128, t0:t0 + TB], o6[:], reads=[o6r], writes=[self.r_BCT])
            cin = [stg.sb([128, TB + 2], F32, "cin") for _ in range(2)]
            hin = [stg.sb([128, TB + 2], F32, "hin") for _ in range(2)]
            bin_ = [stg.sb([128, TB], F32, "bin") for _ in range(2)]
            for rc in range(8):
                for tb in range(SL // TB):
                    t0 = tb * TB
                    ci, cr = cin[it % 2]
                    hi, hr = hin[it % 2]
                    bi, br = bin_[it % 2]
                    ac, ar = acc[it % 2]
                    o6, o6r = o16[it % 2]
                    it += 1
                    if tb == 0:
                        S.op("dve", lambda e, ci=ci: e.memset(ci[:, 0:2], 0.0), writes=[cr])
                        S.op("dve", lambda e, hi=hi: e.memset(hi[:, 0:2], 0.0), writes=[hr])
                        S.dma("sp", ci[:, 2:2 + TB], self.scT[1024 + rc * 128:1024 + (rc + 1) * 128, 0:TB],
                              reads=[self.r_scT], writes=[cr])
                        S.dma("sp", hi[:, 2:2 + TB], self.scT[2048 + rc * 128:2048 + (rc + 1) * 128, 0:TB],
                              reads=[self.r_scT], writes=[hr])
                    else:
                        S.dma("sp", ci[:, :], self.scT[1024 + rc * 128:1024 + (rc + 1) * 128, t0 - 2:t0 + TB],
                              reads=[self.r_scT], writes=[cr])
                        S.dma("sp", hi[:, :], self.scT[2048 + rc * 128:2048 + (rc + 1) * 128, t0 - 2:t0 + TB],
                              reads=[self.r_scT], writes=[hr])
                    S.dma("sp", bi[:, :], self.scT[rc * 128:(rc + 1) * 128, t0:t0 + TB], reads=[self.r_scT], writes=[br])
                    S.op("dve", lambda e, ci=ci, hi=hi: e.tensor_tensor(out=ci[:], in0=ci[:], in1=hi[:], op=ALU.mult),
                         reads=[cr, hr], writes=[cr])
                    S.op("act", lambda e, ci=ci, ac=ac, rc=rc: e.activation(
                        out=ac[:], in_=ci[:, 2:2 + TB], func=AF.Copy, scale=sw[:, rc, 2:3]),
                        reads=[cr, r_sw], writes=[ar])
                    for k in (1, 0):
                        S.op("dve", lambda e, ci=ci, ac=ac, rc=rc, k=k: e.scalar_tensor_tensor(
                            out=ac[:], in0=ci[:, k:k + TB], scalar=sw[:, rc, k:k + 1], in1=ac[:], op0=ALU.mult, op1=ALU.add),
                            reads=[cr, r_sw, ar], writes=[ar])
                    S.op("dve", lambda e, ac=ac, bi=bi, o6=o6: e.tensor_tensor(out=o6[:], in0=ac[:], in1=bi[:], op=ALU.mult),
                         reads=[ar, br], writes=[o6r])
                    S.dma("pool", self.yscT[rc * 128:(rc + 1) * 128, t0:t0 + TB], o6[:], reads=[o6r], writes=[self.r_yscT])
            for j in tjobs_free:
                j()
            for j in tjobs_dep:
                j()

    def transpose_jobs(self, stg, src, dst, R, C, r_dep):
        S = self.S
        it_ = [stg.sb([128, 8, 512], F32, "tin") for _ in range(2)]
        ot_ = [stg.sb([128, 4, 1024], F32, "tout") for _ in range(2)]
        stt = {"it": 0, "ev": 0}
        free, dep = [], []

        def job(r0, c0, rdep):
            rn = min(1024, R - r0)
            nch = (rn + 127) // 128
            ti, tir = it_[stt["it"] % 2]
            to, tor = ot_[stt["it"] % 2]
            stt["it"] += 1
            nfull = rn // 128
            rd = [rdep] if rdep is not None else []
            if nfull:
                S.dma("sp", ti[:, 0:nfull, :],
                      src[r0:r0 + nfull * 128, c0:c0 + 512].rearrange("(k p) c -> p k c", p=128),
                      reads=rd, writes=[tir], join=True)
            if rn % 128:
                mm = rn % 128
                S.dma("sp", ti[0:mm, nfull, :], src[r0 + nfull * 128:r0 + rn, c0:c0 + 512],
                      reads=rd, writes=[tir], join=True)
            for j in range(4):
                for k4 in range(0, nch, 4):
                    pbt, pbr = self.pb[stt["ev"] % 4]
                    kn = min(4, nch - k4)
                    wtot = 0
                    for k in range(k4, k4 + kn):
                        m = min(128, rn - k * 128)
                        S.op("pe", lambda e, ti=ti, k=k, j=j, m=m, pbt=pbt, k4=k4: e.transpose(
                            pbt[:, (k - k4) * 128:(k - k4) * 128 + m], ti[0:m, k, j * 128:(j + 1) * 128], self.ident[0:m, 0:m]),
                            reads=[tir, self.r_ident], writes=[pbr])
                        wtot = (k - k4) * 128 + m
                    dsl = to[:, j, k4 * 128:k4 * 128 + wtot]
                    if stt["ev"] % 2 == 0:
                        S.op("act", lambda e, dsl=dsl, pbt=pbt, wtot=wtot: e.copy(dsl, pbt[:, 0:wtot]), reads=[pbr], writes=[tor])
                    else:
                        S.op("dve", lambda e, dsl=dsl, pbt=pbt, wtot=wtot: e.tensor_copy(dsl, pbt[:, 0:wtot]), reads=[pbr], writes=[tor])
                    stt["ev"] += 1
            S.dma("pool", dst[c0:c0 + 512, r0:r0 + rn].rearrange("(j p) r -> p j r", p=128), to[:, :, 0:rn], reads=[tor])

        for r0 in range(0, R, 1024):
            for c0 in range(0, C, 512):
                if r0 < 3072:
                    dep.append(lambda r0=r0, c0=c0: job(r0, c0, r_dep))
                else:
                    free.append(lambda r0=r0, c0=c0: job(r0, c0, None))
        return free, dep

    def transpose_pass(self, src, r_src, dst, r_dst, R, C):
        S = self.S
        with Stage(self) as stg:
            it_ = [stg.sb([128, 8, 512], F32, "tin") for _ in range(2)]
            ot_ = [stg.sb([128, 4, 1024], F32, "tout") for _ in range(2)]
            it = 0
            ev = 0
            for r0 in range(0, R, 1024):
                rn = min(1024, R - r0)
                nch = (rn + 127) // 128
                for c0 in range(0, C, 512):
                    ti, tir = it_[it % 2]
                    to, tor = ot_[it % 2]
                    it += 1
                    nfull = rn // 128
                    if nfull:
                        S.dma("sp", ti[:, 0:nfull, :],
                              src[r0:r0 + nfull * 128, c0:c0 + 512].rearrange("(k p) c -> p k c", p=128),
                              reads=[r_src], writes=[tir], join=True)
                    if rn % 128:
                        mm = rn % 128
                        S.dma("sp", ti[0:mm, nfull, :], src[r0 + nfull * 128:r0 + rn, c0:c0 + 512],
                              reads=[r_src], writes=[tir], join=True)
                    for j in range(4):
                        for k4 in range(0, nch, 4):
                            pbt, pbr = self.pb[ev % 4]
                            kn = min(4, nch - k4)
                            wtot = 0
                            for k in range(k4, k4 + kn):
                                m = min(128, rn - k * 128)
                                S.op("pe", lambda e, ti=ti, k=k, j=j, m=m, pbt=pbt, k4=k4: e.transpose(
                                    pbt[:, (k - k4) * 128:(k - k4) * 128 + m], ti[0:m, k, j * 128:(j + 1) * 128], self.ident[0:m, 0:m]),
                                    reads=[tir, self.r_ident], writes=[pbr])
                                wtot = (k - k4) * 128 + m
                            dsl = to[:, j, k4 * 128:k4 * 128 + wtot]
                            if ev % 2 == 0:
                                S.op("act", lambda e, dsl=dsl, pbt=pbt, wtot=wtot: e.copy(dsl, pbt[:, 0:wtot]), reads=[pbr], writes=[tor])
                            else:
                                S.op("dve", lambda e, dsl=dsl, pbt=pbt, wtot=wtot: e.tensor_copy(dsl, pbt[:, 0:wtot]), reads=[pbr], writes=[tor])
                            ev += 1
                    S.dma("pool", dst[c0:c0 + 512, r0:r0 + rn].rearrange("(j p) r -> p j r", p=128), to[:, :, 0:rn],
                          reads=[tor], writes=[r_dst])

    def stage_ssd(self, l):
        S, I, SL = self.S, self.I, self.S_len
        pb = self.pb
        A_, B_, C_, Z_, CB_ = (pb[0], pb[1]), (pb[2], pb[3]), (pb[4], pb[5]), pb[6], pb[7]
        with Stage(self) as stg:
            dbc, r_dbc = stg.sb([64, 32], F32, "dbc")
            ngb, r_ngb = stg.sb([64, D], F32, "ngb")
            S.dma("sp", dbc[:], I["dsk32"][l:l + 1, :].to_broadcast([64, 32]), writes=[r_dbc])
            S.dma("sp", ngb[:], I["ng_row"][l:l + 1, :].to_broadcast([64, D]), writes=[r_ngb])
            h32, r_h32 = stg.sb([128, D], F32, "h32")
            h16, r_h16 = stg.sb([128, D], BF16, "h16")
            S.op("dve", lambda e: e.memset(h32[:], 0.0), writes=[r_h32])
            S.op("dve", lambda e: e.memset(h16[:], 0.0), writes=[r_h16])
            tokx_ = [stg.sb([64, 4096], F32, "tokx") for _ in range(3)]
            dta_ = [stg.sb([64, 64], F32, "dta") for _ in range(3)]
            bcb_ = [stg.sb([128, 16, 256], BF16, "bcb") for _ in range(2)]
            acs, r_acs = stg.sb([64, 32], F32, "acs")
            ecs_ = [stg.sb([64, 32], F32, "ecs") for _ in range(2)]
            dte, r_dte = stg.sb([64, 32], F32, "dte")
            cd_ = [stg.sb([128, 32], F32, "cd") for _ in range(2)]
            R1h, r_R1h = stg.sb([64, 32, 64], BF16, "R1h")
            R1l, r_R1l = stg.sb([64, 32, 64], BF16, "R1l")
            R2h, r_R2h = stg.sb([64, 32, 64], BF16, "R2h")
            R2l, r_R2l = stg.sb([64, 32, 64], BF16, "R2l")
            ah16, r_ah16 = stg.sb([64, 32], BF16, "ah16")
            ah32, r_ah32 = stg.sb([64, 32], F32, "ah32")
            al32, r_al32 = stg.sb([64, 32], F32, "al32")
            xdt_ = [stg.sb([64, D], BF16, "xdt") for _ in range(2)]
            xdtd_ = [stg.sb([64, D], BF16, "xdtd") for _ in range(2)]
            bt16_ = [stg.sb([64, 1024], BF16, "bt16") for _ in range(3)]
            cbs, r_cbs = stg.sb([64, 512], BF16, "cbs")
            LT, r_LT = stg.sb([64, D], BF16, "LT")
            MT_ = [stg.sb([64, D], BF16, "MT") for _ in range(2)]
            yv, r_yv = stg.sb([64, 1024], F32, "yv")
            t2, r_t2 = stg.sb([64, 1024], F32, "t2")
            gb, r_gb = stg.sb([64, D], F32, "gb")
            gn_ = [stg.sb([64, D], F32, "gn") for _ in range(1)]
            hs, r_hs = stg.sb([128, 1024], F32, "hs")
            ss, r_ss = stg.sb([64, 2], F32, "ss")
            triu, ntriu, ones, r3, ident = self.triu, self.ntriu, self.ones, self.r3, self.ident
            nchunk = SL // 64

            def loads(c):
                t0 = c * 64
                tokx, r_tokx = tokx_[c % 3]
                dta, r_dta = dta_[c % 3]
                bt16, r_bt16 = bt16_[c % 3]
                if c % 4 == 0:
                    bcb, r_bcb = bcb_[(c // 4) % 2]
                    S.dma("sp", bcb[:], self.BCT[:, t0:t0 + 256].rearrange("(g n) t -> n g t", n=128), writes=[r_bcb])
                S.dma("sp", tokx[:, 0:D], self.TOK[t0:t0 + 64, 0:D], writes=[r_tokx], join=True)
                S.dma("sp", tokx[:, D:2 * D], self.TOK[t0:t0 + 64, T_SZ:T_SZ + D], writes=[r_tokx], join=True)
                S.dma("sp", dta[:], self.TOK[t0:t0 + 64, T_DT:T_DT + 64], writes=[r_dta])
                S.dma("pool", bt16[:], self.TOK[t0:t0 + 64, T_B:T_B + 1024], writes=[r_bt16])

            def phase1_a(c):
                tokx, r_tokx = tokx_[c % 3]
                dta, r_dta = dta_[c % 3]
                bcb, r_bcb = bcb_[(c // 4) % 2]
                ecs, r_ecs = ecs_[c % 2]
                cd, r_cd = cd_[c % 2]
                xdt, r_xdt = xdt_[c % 2]
                xdtd, r_xdtd = xdtd_[c % 2]
                MT, r_MT = MT_[c % 2]
                tq = (c % 4) * 64
                dt_ap = dta[:, 0:32]
                a_ap = dta[:, 32:64]
                zt, zr = Z_
                S.op("pe", lambda e: e.matmul(zt[0:64, 0:32], lhsT=triu[:, :], rhs=a_ap, start=True, stop=True),
                     reads=[r_dta, self.r_triu], writes=[zr])
                S.op("pe", lambda e: e.matmul(zt[:, 32:64], lhsT=ones[0:64, :], rhs=a_ap, start=True, stop=True),
                     reads=[r_dta, self.r_ones], writes=[zr])
                S.op("act", lambda e: e.copy(acs[:], zt[0:64, 0:32]), reads=[zr], writes=[r_acs])
                S.op("act", lambda e: e.activation(out=ecs[:], in_=zt[0:64, 0:32], func=AF.Exp), reads=[zr], writes=[r_ecs])
                S.op("act", lambda e: e.activation(out=cd[:], in_=zt[:, 32:64], func=AF.Exp), reads=[zr], writes=[r_cd])
                S.op("dve", lambda e: e.tensor_tensor(out=dte[:], in0=zt[0:64, 32:64], in1=acs[:], op=ALU.subtract),
                     reads=[zr, r_acs], writes=[r_dte])
                S.op("act", lambda e: e.activation(out=dte[:], in_=dte[:], func=AF.Exp), reads=[r_dte], writes=[r_dte])
                S.op("act", lambda e: e.copy(ah16[:], a_ap), reads=[r_dta], writes=[r_ah16])
                S.op("act", lambda e: e.copy(ah32[:], ah16[:]), reads=[r_ah16], writes=[r_ah32])
                S.op("dve", lambda e: e.tensor_tensor(out=al32[:], in0=a_ap, in1=ah32[:], op=ALU.subtract),
                     reads=[r_dta, r_ah32], writes=[r_al32])
                S.op("dve", lambda e: e.tensor_tensor(
                    out=R1h[:], in0=ah32[:, :].unsqueeze(2).to_broadcast([64, 32, 64]),
                    in1=triu[:, :].unsqueeze(1).to_broadcast([64, 32, 64]), op=ALU.mult),
                    reads=[r_ah32, self.r_triu], writes=[r_R1h])
                S.op("dve", lambda e: e.tensor_tensor(
                    out=R1l[:], in0=al32[:, :].unsqueeze(2).to_broadcast([64, 32, 64]),
                    in1=triu[:, :].unsqueeze(1).to_broadcast([64, 32, 64]), op=ALU.mult),
                    reads=[r_al32, self.r_triu], writes=[r_R1l])
                S.op("pool", lambda e: e.tensor_copy(R2h[:], ah32[:, :].unsqueeze(2).to_broadcast([64, 32, 64])),
                     reads=[r_ah32], writes=[r_R2h])
                S.op("pool", lambda e: e.tensor_copy(R2l[:], al32[:, :].unsqueeze(2).to_broadcast([64, 32, 64])),
                     reads=[r_al32], writes=[r_R2l])
                S.op("dve", lambda e: e.tensor_tensor(
                    out=xdt[:].rearrange("s (h p) -> s h p", h=32), in0=tokx[:, 0:D].rearrange("s (h p) -> s h p", h=32),
                    in1=dt_ap.unsqueeze(2).to_broadcast([64, 32, 64]), op=ALU.mult),
                    reads=[r_tokx, r_dta], writes=[r_xdt])
                S.op("dve", lambda e: e.tensor_tensor(
                    out=xdtd[:].rearrange("s (h p) -> s h p", h=32), in0=xdt[:].rearrange("s (h p) -> s h p", h=32),
                    in1=dte[:, :].unsqueeze(2).to_broadcast([64, 32, 64]), op=ALU.mult),
                    reads=[r_xdt, r_dte], writes=[r_xdtd])
                cbt, cbr = CB_
                for g in range(8):
                    S.op("pe", lambda e, g=g: e.matmul(
                        cbt[0:64, g * 64:(g + 1) * 64], lhsT=bcb[:, g, tq:tq + 64], rhs=bcb[:, 8 + g, tq:tq + 64],
                        start=True, stop=True), reads=[r_bcb], writes=[cbr], sig=(g == 7))
                S.op("act", lambda e: e.copy(cbs[:], cbt[0:64, :]), reads=[cbr], writes=[r_cbs])
            def phase1_e(c, qs):
                for q in qs:
                    at, ar = A_[q % 2]
                    hsl = slice(q * 8, q * 8 + 8)
                    S.op("pe", lambda e, at=at, hsl=hsl: e.matmul(at[0:64, :], lhsT=self.ones16[:, :], rhs=R1h[:, hsl, :],
                                                                  start=True, stop=False),
                         reads=[r_R1h, self.r_ones16], writes=[ar], sig=False)
                    S.op("pe", lambda e, at=at, hsl=hsl: e.matmul(at[0:64, :], lhsT=self.ones16[:, :], rhs=R1l[:, hsl, :],
                                                                  start=False, stop=False),
                         reads=[r_R1l, self.r_ones16], writes=[ar], sig=False)
                    S.op("pe", lambda e, at=at, hsl=hsl: e.matmul(at[0:64, :], lhsT=self.ntriu16[:, :], rhs=R2h[:, hsl, :],
                                                                  start=False, stop=False),
                         reads=[r_R2h, self.r_ntriu16], writes=[ar], sig=False)
                    S.op("pe", lambda e, at=at, hsl=hsl: e.matmul(at[0:64, :], lhsT=self.ntriu16[:, :], rhs=R2l[:, hsl, :],
                                                                  start=False, stop=False),
                         reads=[r_R2l, self.r_ntriu16], writes=[ar], sig=False)
                    S.op("pe", lambda e, at=at, hsl=hsl: e.matmul(at[0:64, :], lhsT=self.ident16[:, :], rhs=self.r3_16[:, hsl, :],
                                                                  start=False, stop=True),
                         reads=[self.r_r3_16, self.r_ident16], writes=[ar])
                    S.op("act", lambda e, at=at, q=q: e.activation(out=LT[:, q * 512:(q + 1) * 512], in_=at[0:64, :], func=AF.Exp),
                         reads=[ar], writes=[r_LT])
            def phase1_mt(c):
                MT, r_MT = MT_[c % 2]
                S.op("dve", lambda e: e.tensor_tensor(
                    out=MT[:].rearrange("s (g r t) -> s g r t", g=8, r=4),
                    in0=LT[:].rearrange("s (g r t) -> s g r t", g=8, r=4),
                    in1=cbs[:, :].rearrange("s (g t) -> s g t", g=8).unsqueeze(2).to_broadcast([64, 8, 4, 64]),
                    op=ALU.mult), reads=[r_LT, r_cbs], writes=[r_MT])

            def phase2_half(c, hh):
                t0 = c * 64
                tokx, r_tokx = tokx_[c % 3]
                bcb, r_bcb = bcb_[(c // 4) % 2]
                ecs, r_ecs = ecs_[c % 2]
                cd, r_cd = cd_[c % 2]
                xdt, r_xdt = xdt_[c % 2]
                xdtd, r_xdtd = xdtd_[c % 2]
                bt16, r_bt16 = bt16_[c % 3]
                MT, r_MT = MT_[c % 2]
                gnt, r_gnt = gn_[0]
                tq = (c % 4) * 64
                if True:
                    hs0 = hh * 16
                    for h in range(16):
                        bt_, br_ = B_[h // 8]
                        S.op("pe", lambda e, h=h, bt_=bt_, hs0=hs0: e.matmul(
                            bt_[0:64, (h % 8) * 64:(h % 8 + 1) * 64], lhsT=MT[:, (hs0 + h) * 64:(hs0 + h + 1) * 64],
                            rhs=xdt[:, (hs0 + h) * 64:(hs0 + h + 1) * 64], start=True, stop=True),
                            reads=[r_MT, r_xdt], writes=[br_], sig=(h % 8 == 7))
                    for g in range(4):
                        ct_, cr_ = C_[g // 2]
                        gg = hh * 4 + g
                        S.op("pe", lambda e, g=g, gg=gg, ct_=ct_: e.matmul(
                            ct_[0:64, (g % 2) * 256:(g % 2 + 1) * 256], lhsT=bcb[:, 8 + gg, tq:tq + 64],
                            rhs=h16[:, gg * 256:(gg + 1) * 256], start=True, stop=True),
                            reads=[r_bcb, r_h16], writes=[cr_], sig=(g % 2 == 1))
                    for b in range(2):
                        ct_, cr_ = C_[b]
                        bt_, br_ = B_[b]
                        S.op("dve", lambda e, ct_=ct_, b=b, hs0=hs0: e.tensor_tensor(
                            out=yv[:, b * 512:(b + 1) * 512].rearrange("s (h p) -> s h p", h=8),
                            in0=ct_[0:64, :].rearrange("s (h p) -> s h p", h=8),
                            in1=ecs[:, hs0 + b * 8:hs0 + b * 8 + 8].unsqueeze(2).to_broadcast([64, 8, 64]), op=ALU.mult),
                            reads=[cr_, r_ecs], writes=[r_yv])
                        S.op("dve", lambda e, bt_=bt_, b=b: e.tensor_tensor(
                            out=yv[:, b * 512:(b + 1) * 512], in0=yv[:, b * 512:(b + 1) * 512], in1=bt_[0:64, :], op=ALU.add),
                            reads=[br_, r_yv], writes=[r_yv])
                    for g in range(4):
                        ct_, cr_ = C_[g // 2]
                        gg = hh * 4 + g
                        S.op("pe", lambda e, g=g, gg=gg, ct_=ct_: e.matmul(
                            ct_[:, (g % 2) * 256:(g % 2 + 1) * 256], lhsT=bt16[:, gg * 128:(gg + 1) * 128],
                            rhs=xdtd[:, gg * 256:(gg + 1) * 256], start=True, stop=True),
                            reads=[r_bt16, r_xdtd], writes=[cr_], sig=(g % 2 == 1))
                    S.op("pool", lambda e, hh=hh: e.tensor_tensor(
                        out=t2[:].rearrange("s (h p) -> s h p", h=16),
                        in0=tokx[:, hh * 1024:(hh + 1) * 1024].rearrange("s (h p) -> s h p", h=16),
                        in1=dbc[:, hh * 16:(hh + 1) * 16].unsqueeze(2).to_broadcast([64, 16, 64]), op=ALU.mult),
                        reads=[r_tokx, r_dbc], writes=[r_t2])
                    S.op("pool", lambda e: e.tensor_tensor(out=t2[:], in0=t2[:], in1=yv[:], op=ALU.add),
                         reads=[r_t2, r_yv], writes=[r_t2])
                    S.op("dve", lambda e, hh=hh: e.tensor_tensor(
                        out=gb[:, hh * 1024:(hh + 1) * 1024], in0=t2[:], in1=tokx[:, D + hh * 1024:D + (hh + 1) * 1024], op=ALU.mult),
                        reads=[r_t2, r_tokx], writes=[r_gb])
                    S.op("dve", lambda e, hh=hh, hs0=hs0: e.tensor_tensor(
                        out=hs[:].rearrange("n (h p) -> n h p", h=16),
                        in0=h32[:, hh * 1024:(hh + 1) * 1024].rearrange("n (h p) -> n h p", h=16),
                        in1=cd[:, hs0:hs0 + 16].unsqueeze(2).to_broadcast([128, 16, 64]), op=ALU.mult),
                        reads=[r_h32, r_cd], writes=[r_hs])
                    for b in range(2):
                        ct_, cr_ = C_[b]
                        S.op("dve", lambda e, ct_=ct_, b=b, hh=hh: e.tensor_tensor(
                            out=h32[:, hh * 1024 + b * 512:hh * 1024 + (b + 1) * 512], in0=hs[:, b * 512:(b + 1) * 512],
                            in1=ct_[:, :], op=ALU.add), reads=[r_hs, cr_], writes=[r_h32])
                    S.op("act", lambda e, hh=hh: e.copy(h16[:, hh * 1024:(hh + 1) * 1024], h32[:, hh * 1024:(hh + 1) * 1024]),
                         reads=[r_h32], writes=[r_h16])
            def phase2_tail(c):
                t0 = c * 64
                gnt, r_gnt = gn_[0]
                S.op("act", lambda e: e.activation(out=gnt[:], in_=gb[:], func=AF.Square, accum_out=ss[:, 0:1]),
                     reads=[r_gb], writes=[r_gnt, r_ss])
                S.op("act", lambda e: e.activation(out=ss[:, 1:2], in_=ss[:, 0:1], func=AF.Sqrt, scale=1.0 / D, bias=self.eps_t[0:64, 0:1]),
                     reads=[r_ss, self.r_eps], writes=[r_ss])
                S.op("dve", lambda e: e.reciprocal(out=ss[:, 1:2], in_=ss[:, 1:2]), reads=[r_ss], writes=[r_ss])
                S.op("dve", lambda e: e.scalar_tensor_tensor(out=gnt[:], in0=gb[:], scalar=ss[:, 1:2], in1=ngb[:],
                                                             op0=ALU.mult, op1=ALU.mult),
                     reads=[r_gb, r_ss, r_ngb], writes=[r_gnt])
                S.dma("sp", self.TOK2[t0:t0 + 64, 0:D], gnt[:], reads=[r_gnt])

            loads(0)
            if nchunk > 1:
                loads(1)
            phase1_a(0)
            phase1_e(0, (0, 1, 2, 3))
            phase1_mt(0)
            for c in range(nchunk):
                if c + 2 < nchunk:
                    loads(c + 2)
                nx = c + 1 < nchunk
                if nx:
                    phase1_a(c + 1)
                phase2_half(c, 0)
                if nx:
                    phase1_e(c + 1, (0, 1))
                phase2_half(c, 1)
                if nx:
                    phase1_e(c + 1, (2, 3))
                    phase1_mt(c + 1)
                phase2_tail(c)

    def stage_attn(self, l):
        S, I, SL = self.S, self.I, self.S_len
        pb = self.pb
        NT = SL // 128
        NQB = SL // 512
        NBIS = 22
        scale = 128.0 ** -0.5
        with Stage(self) as stg:
            ik2, r_ik2 = stg.sb([128, SL], BF16, "ik2")
            S.dma("sp", ik2[0:64, :], self.ikT[:, :], writes=[r_ik2], join=True)
            S.dma("sp", ik2[64:128, :], self.ikT[:, :], writes=[r_ik2], join=True)
            acc, r_acc = stg.sb([128, SL], F32, "acc")
            wk, r_wk = stg.sb([128, SL], F32, "wk")
            bs, r_bs = stg.sb([128, 8], F32, "bs")
            wtab, r_wtab = stg.sb([128, 32], F32, "wtab")
            mT_ = [stg.sb([128, NT, 512], BF16, "maskT") for _ in range(2)]
            iq_ = [stg.sb([128, 8, 128], BF16, "iqt") for _ in range(2)]
            wt_ = [stg.sb([128, 16], F32, "wtok") for _ in range(2)]
            rb_ = [stg.sb([128, 512], F32, "rbuf") for _ in range(3)]
            kh_ = [stg.sb([128, SL], BF16, "kh") for _ in range(2)]
            qh_ = [stg.sb([128, 512], BF16, "qh") for _ in range(2)]
            va_ = [stg.sb([128, NT, 129], BF16, "va") for _ in range(2)]
            pt_ = [stg.sb([128, 512], BF16, "pt") for _ in range(4)]
            ya_ = [stg.sb([128, 4, 1024], BF16, "ya") for _ in range(1)]
            rd, r_rd = stg.sb([128, 4], F32, "rd")
            for v in range(2):
                S.op("pool", lambda e, v=v: e.memset(va_[v][0][:, :, 128:129], 1.0), writes=[va_[v][1]])
            st = {"qi": 0, "ri": 0, "pi": 0, "hi": 0}

            def index_tile(qb, u):
                maskT, r_maskT = mT_[qb % 2]
                qt = 4 * qb + u
                t0 = qt * 128
                Kq = 128 * (qt + 1)
                iqt, r_iqt = iq_[st["qi"] % 2]
                wtk, r_wtk = wt_[st["qi"] % 2]
                st["qi"] += 1
                S.dma("sp", iqt[:], self.iqT[:, t0:t0 + 128].rearrange("(c p) t -> p c t", p=128), writes=[r_iqt])
                S.dma("sp", wtk[:], self.TOK[t0:t0 + 128, T_IW:T_IW + 16], writes=[r_wtk])
                for s0 in range(0, Kq, 512):
                    ncol = min(512, Kq - s0)
                    for h in range(16):
                        pbt, pbr = pb[2 + h % 2]
                        hp = (h % 2) * 64
                        S.op("pe", lambda e, iqt=iqt, h=h, hp=hp, s0=s0, ncol=ncol, pbt=pbt: e.matmul(
                            pbt[:, 0:ncol], lhsT=iqt[hp:hp + 64, h // 2, :], rhs=ik2[hp:hp + 64, s0:s0 + ncol],
                            start=True, stop=True), reads=[r_iqt, r_ik2], writes=[pbr])
                        rbt, rbr = rb_[st["ri"] % 3]
                        st["ri"] += 1
                        S.op("act", lambda e, rbt=rbt, pbt=pbt, ncol=ncol: e.activation(
                            out=rbt[:, 0:ncol], in_=pbt[:, 0:ncol], func=AF.Relu), reads=[pbr], writes=[rbr])
                        if h == 0:
                            S.op("dve", lambda e, rbt=rbt, wtk=wtk, s0=s0, ncol=ncol: e.tensor_scalar(
                                out=acc[:, s0:s0 + ncol], in0=rbt[:, 0:ncol], scalar1=wtk[:, 0:1], scalar2=None, op0=ALU.mult),
                                reads=[rbr, r_wtk], writes=[r_acc])
                        else:
                            S.op("dve", lambda e, rbt=rbt, wtk=wtk, s0=s0, ncol=ncol, h=h: e.scalar_tensor_tensor(
                                out=acc[:, s0:s0 + ncol], in0=rbt[:, 0:ncol], scalar=wtk[:, h:h + 1], in1=acc[:, s0:s0 + ncol],
                                op0=ALU.mult, op1=ALU.add), reads=[rbr, r_wtk, r_acc], writes=[r_acc])
                if Kq > 256:
                    S.op("dve", lambda e, Kq=Kq: e.tensor_reduce(out=bs[:, 0:1], in_=acc[:, 0:Kq], axis=mybir.AxisListType.X, op=ALU.min),
                         reads=[r_acc], writes=[r_bs])
                    S.op("dve", lambda e, Kq=Kq: e.tensor_reduce(out=bs[:, 5:6], in_=acc[:, 0:Kq], axis=mybir.AxisListType.X, op=ALU.max),
                         reads=[r_acc], writes=[r_bs])
                    S.op("dve", lambda e: e.tensor_tensor(out=bs[:, 1:2], in0=bs[:, 5:6], in1=bs[:, 0:1], op=ALU.subtract),
                         reads=[r_bs], writes=[r_bs])
                    S.op("dve", lambda e: e.tensor_scalar(out=bs[:, 1:2], in0=bs[:, 1:2], scalar1=1.0001, scalar2=1e-6,
                                                          op0=ALU.mult, op1=ALU.add), reads=[r_bs], writes=[r_bs])
                S.op("dve", lambda e, Kq=Kq: e.memset(acc[0:64, Kq - 64:Kq], NEG), reads=[r_acc], writes=[r_acc])
                if Kq > 256:
                    S.op("dve", lambda e: e.tensor_scalar(out=wtab[:, 0:NBIS + 2], in0=self.p2tab[:, 0:NBIS + 2], scalar1=bs[:, 1:2], scalar2=None,
                                                          op0=ALU.mult), reads=[r_bs, self.r_p2tab], writes=[r_wtab])
                    S.op("dve", lambda e: e.tensor_tensor(out=bs[:, 2:3], in0=bs[:, 0:1], in1=wtab[:, 1:2], op=ALU.add),
                         reads=[r_bs, r_wtab], writes=[r_bs])
                    for k in range(NBIS):
                        S.op("dve", lambda e, Kq=Kq: e.tensor_scalar(out=wk[:, 0:Kq], in0=acc[:, 0:Kq], scalar1=bs[:, 2:3], scalar2=0.0,
                                                                     op0=ALU.is_ge, op1=ALU.add, accum_out=bs[:, 3:4]),
                             reads=[r_acc, r_bs], writes=[r_wk, r_bs])
                        S.op("dve", lambda e: e.tensor_scalar(out=bs[:, 4:5], in0=bs[:, 3:4], scalar1=256.0, scalar2=0.5,
                                                              op0=ALU.is_ge, op1=ALU.subtract), reads=[r_bs], writes=[r_bs])
                        S.op("dve", lambda e, k=k: e.scalar_tensor_tensor(out=bs[:, 2:3], in0=bs[:, 4:5], scalar=wtab[:, k + 1:k + 2],
                                                                          in1=bs[:, 2:3], op0=ALU.mult, op1=ALU.add),
                             reads=[r_bs, r_wtab], writes=[r_bs])
                    S.op("dve", lambda e: e.tensor_tensor(out=bs[:, 0:1], in0=bs[:, 2:3], in1=wtab[:, NBIS + 1:NBIS + 2], op=ALU.subtract),
                         reads=[r_bs, r_wtab], writes=[r_bs])
                    thr_ap, thr_r = bs, r_bs
                else:
                    thr_ap, thr_r = self.nbig, self.r_nbig
                S.op("dve", lambda e, Kq=Kq, thr_ap=thr_ap: e.tensor_scalar(
                    out=wk[:, 0:Kq], in0=acc[:, 0:Kq], scalar1=thr_ap[:, 0:1], scalar2=None, op0=ALU.is_ge),
                    reads=[r_acc, thr_r], writes=[r_wk])

            def mask_transposes(qb, u):
                maskT, r_maskT = mT_[qb % 2]
                qt = 4 * qb + u
                for j4 in range(0, qt + 1, 4):
                    jn = min(4, qt + 1 - j4)
                    pbt, pbr = pb[2]
                    for j in range(j4, j4 + jn):
                        S.op("pe", lambda e, j=j, j4=j4, pbt=pbt: e.transpose(
                            pbt[:, (j - j4) * 128:(j - j4 + 1) * 128], wk[:, j * 128:(j + 1) * 128], self.ident[:, :]),
                            reads=[r_wk, self.r_ident], writes=[pbr])
                    S.op("act", lambda e, j4=j4, jn=jn, u=u, pbt=pbt, maskT=maskT: e.copy(
                        maskT[:, j4:j4 + jn, u * 128:(u + 1) * 128], pbt[:, 0:jn * 128].rearrange("p (j t) -> p j t", j=jn)),
                        reads=[pbr], writes=[r_maskT])

            def head_loads(qb, h):
                nk = 4 * (qb + 1)
                K = nk * 128
                kh, r_kh = kh_[h % 2]
                qh, r_qh = qh_[h % 2]
                va, r_va = va_[h % 2]
                S.dma("sp", kh[:, 0:K], self.kT[h * 128:(h + 1) * 128, 0:K], writes=[r_kh])
                S.dma("sp", qh[:, :], self.qT[h * 128:(h + 1) * 128, qb * 512:(qb + 1) * 512], writes=[r_qh])
                for j8 in range(0, nk, 8):
                    jn8 = min(8, nk - j8)
                    S.dma("pool", va[:, j8:j8 + jn8, 0:128],
                          self.TOK[j8 * 128:(j8 + jn8) * 128, T_V + h * 128:T_V + (h + 1) * 128].rearrange("(j s) d -> s j d", s=128),
                          writes=[r_va], join=True)

            def attn_head(qb, h):
                maskT, r_maskT = mT_[qb % 2]
                ya, r_ya = ya_[0]
                nk = 4 * (qb + 1)
                kh, r_kh = kh_[h % 2]
                qh, r_qh = qh_[h % 2]
                va, r_va = va_[h % 2]
                def st_tile(j):
                    pbt, pbr = pb[j % 2]
                    S.op("pe", lambda e, pbt=pbt: e.matmul(
                        pbt[:, :], lhsT=kh[:, j * 128:(j + 1) * 128], rhs=qh[:, :], start=True, stop=True),
                        reads=[r_kh, r_qh], writes=[pbr])
                    pt, r_pt = pt_[st["pi"] % 4]
                    st["pi"] += 1
                    S.op("act", lambda e, pt=pt, pbt=pbt: e.activation(out=pt[:], in_=pbt[:, :], func=AF.Exp, scale=scale),
                         reads=[pbr], writes=[r_pt])
                    S.op("pool", lambda e, pt=pt: e.tensor_tensor(out=pt[:], in0=pt[:], in1=maskT[:, j, :], op=ALU.mult),
                         reads=[r_pt, r_maskT], writes=[r_pt])
                    return pt, r_pt

                def pv_tile(j, pt, r_pt):
                    for u in range(4):
                        jl = 4 * qb + u
                        if j > jl:
                            continue
                        ot, orr = pb[4 + u]
                        S.op("pe", lambda e, u=u, ot=ot, jl=jl: e.matmul(
                            ot[:, 0:129], lhsT=pt[:, u * 128:(u + 1) * 128], rhs=va[:, j, :], start=(j == 0), stop=(j == jl)),
                            reads=[r_pt, r_va], writes=[orr])

                nxt = st_tile(0)
                for j in range(nk):
                    cur = nxt
                    if j + 1 < nk:
                        nxt = st_tile(j + 1)
                    pv_tile(j, *cur)
                for u in range(4):
                    ot, orr = pb[4 + u]
                    S.op("act", lambda e, ot=ot, u=u: e.copy(rd[:, u:u + 1], ot[:, 128:129]), reads=[orr], writes=[r_rd])
                    S.op("dve", lambda e, u=u: e.reciprocal(out=rd[:, u:u + 1], in_=rd[:, u:u + 1]), reads=[r_rd], writes=[r_rd])
                    S.op("act", lambda e, ot=ot, u=u, h=h: e.activation(
                        out=ya[:, u, h * 128:(h + 1) * 128], in_=ot[:, 0:128], func=AF.Copy, scale=rd[:, u:u + 1]),
                        reads=[orr, r_rd], writes=[r_ya])
                if h == 7:
                    S.dma("pool", self.TOK2[qb * 512:(qb + 1) * 512, D:D + 1024].rearrange("(u p) c -> p u c", p=128), ya[:],
                          reads=[r_ya])

            for qb in range(NQB + 1):
                if qb < NQB:
                    nk = 4 * (qb + 1)
                    S.op("pool", lambda e, nk=nk, qb=qb: e.memset(mT_[qb % 2][0][:, 0:nk, :], 0.0), writes=[mT_[qb % 2][1]])
                if qb >= 1:
                    head_loads(qb - 1, 0)
                for u in range(4):
                    if qb < NQB:
                        index_tile(qb, u)
                    if qb >= 1:
                        for hh in range(2):
                            h = 2 * u + hh
                            if h + 1 < 8:
                                head_loads(qb - 1, h + 1)
                            attn_head(qb - 1, h)
                    if qb < NQB:
                        mask_transposes(qb, u)

    def stage_merge(self, l, xsrc):
        S, I, SL = self.S, self.I, self.S_len
        pb = self.pb
        TB = min(1024, SL)
        NH = TB // 512
        with Stage(self) as stg:
            a16, r_a16 = stg.sb([128, 24, TB], BF16, "a16")
            s16, r_s16 = stg.sb([128, 8, TB], BF16, "s16")
            mg, r_mg = stg.sb([128, KC, TB], BF16, "mg")
            xc_ = [stg.sb([128, TB], F32, "xc") for _ in range(3)]
            gt_ = [stg.sb([128, 3, 512], F32, "gt") for _ in range(2)]
            w_ = [(stg.sb([128, 16, 256], BF16, "w1"), stg.sb([128, 8, 256], BF16, "w2"), stg.sb([128, 8, 256], BF16, "w3"))
                  for _ in range(2)]
            m1, r_m1 = stg.sb([128, 512], F32, "m1")
            m2, r_m2 = stg.sb([128, 512], F32, "m2")
            gtm = self.mod[:, l, 32:48]
            jobs = []
            for tb in range(SL // TB):
                for dg in range(8):
                    jobs.append((tb, "br", dg))
                for dg in range(8):
                    jobs.append((tb, "out", dg))

            def wload(ji):
                tb, kind, dg = jobs[ji]
                (w1, r_w1), (w2, r_w2), (w3, r_w3) = w_[ji % 2]
                if kind == "br":
                    self.load_wgroup(w1, r_w1, I["w_br_ssd"][l], dg * 256, 256, 16)
                    self.load_wgroup(w2, r_w2, I["w_br_sc"][l], dg * 256, 256, 8)
                    self.load_wgroup(w3, r_w3, I["w_br_att"][l], dg * 256, 256, 8)
                else:
                    self.load_wgroup(w1, r_w1, I["w_out"][l], dg * 256, 256, 16)

            gi = 0
            xi = 0
            wload(0)
            for ji, (tb, kind, dg) in enumerate(jobs):
                t0 = tb * TB
                if ji + 1 < len(jobs):
                    wload(ji + 1)
                (w1, r_w1), (w2, r_w2), (w3, r_w3) = w_[ji % 2]
                if kind == "br" and dg == 0:
                    for k0 in range(0, 24, 8):
                        S.dma("pool", a16[:, k0:k0 + 8, :],
                              self.TT2[k0 * 128:(k0 + 8) * 128, t0:t0 + TB].rearrange("(k p) t -> p k t", p=128),
                              writes=[r_a16], join=True)
                    S.dma("sp", s16[:], self.yscT[:, t0:t0 + TB].rearrange("(k p) t -> p k t", p=128), writes=[r_s16])
                for j in range(2):
                    dc = dg * 2 + j
                    cs = slice(j * 128, (j + 1) * 128)
                    if kind == "br":
                        for hf in range(NH):
                            ts_ = slice(hf * 512, (hf + 1) * 512)
                            gt, r_gt = gt_[gi % 2]
                            S.dma("sp", gt[:], self.gT[:, t0 + hf * 512:t0 + (hf + 1) * 512].rearrange(
                                "(b dc p) t -> dc p b t", b=3, p=128)[dc], writes=[r_gt])
                            p1, p1r = pb[0 + (gi % 2) * 3]
                            p2, p2r = pb[1 + (gi % 2) * 3]
                            p3, p3r = pb[2 + (gi % 2) * 3]
                            gi += 1
                            for kc in range(16):
                                S.op("pe", lambda e, w1=w1, kc=kc, cs=cs, p1=p1, ts_=ts_: e.matmul(
                                    p1[:, :], lhsT=w1[:, kc, cs], rhs=a16[:, kc, ts_], start=(kc == 0), stop=(kc == 15)),
                                    reads=[r_w1, r_a16], writes=[p1r], sig=(kc == 15))
                            for kc in range(8):
                                S.op("pe", lambda e, w2=w2, kc=kc, cs=cs, p2=p2, ts_=ts_: e.matmul(
                                    p2[:, :], lhsT=w2[:, kc, cs], rhs=s16[:, kc, ts_], start=(kc == 0), stop=(kc == 7)),
                                    reads=[r_w2, r_s16], writes=[p2r], sig=(kc == 7))
                            for kc in range(8):
                                S.op("pe", lambda e, w3=w3, kc=kc, cs=cs, p3=p3, ts_=ts_: e.matmul(
                                    p3[:, :], lhsT=w3[:, kc, cs], rhs=a16[:, 16 + kc, ts_], start=(kc == 0), stop=(kc == 7)),
                                    reads=[r_w3, r_a16], writes=[p3r], sig=(kc == 7))
                            S.op("dve", lambda e, gt=gt, p1=p1: e.tensor_tensor(out=m1[:], in0=p1[:, :], in1=gt[:, 0, :], op=ALU.mult),
                                 reads=[p1r, r_gt], writes=[r_m1])
                            S.op("dve", lambda e, gt=gt, p2=p2: e.tensor_tensor(out=m2[:], in0=p2[:, :], in1=gt[:, 1, :], op=ALU.mult),
                                 reads=[p2r, r_gt], writes=[r_m2])
                            S.op("dve", lambda e: e.tensor_tensor(out=m1[:], in0=m1[:], in1=m2[:], op=ALU.add),
                                 reads=[r_m1, r_m2], writes=[r_m1])
                            S.op("dve", lambda e, gt=gt, p3=p3: e.tensor_tensor(out=m2[:], in0=p3[:, :], in1=gt[:, 2, :], op=ALU.mult),
                                 reads=[p3r, r_gt], writes=[r_m2])
                            S.op("dve", lambda e, dc=dc, ts_=ts_: e.tensor_tensor(out=mg[:, dc, ts_], in0=m1[:], in1=m2[:], op=ALU.add),
                                 reads=[r_m1, r_m2], writes=[r_mg])
                    else:
                        xc, r_xc = xc_[xi % 3]
                        xi += 1
                        S.dma("sp", xc[:], xsrc[dc * 128:(dc + 1) * 128, t0:t0 + TB], writes=[r_xc])
                        for hf in range(NH):
                            ts_ = slice(hf * 512, (hf + 1) * 512)
                            p1, p1r = pb[6 + hf % 2]
                            for kc in range(16):
                                S.op("pe", lambda e, w1=w1, kc=kc, cs=cs, p1=p1, ts_=ts_: e.matmul(
                                    p1[:, :], lhsT=w1[:, kc, cs], rhs=mg[:, kc, ts_], start=(kc == 0), stop=(kc == 15)),
                                    reads=[r_w1, r_mg], writes=[p1r], sig=(kc == 15))
                            S.op("dve", lambda e, dc=dc, p1=p1, xc=xc, ts_=ts_: e.scalar_tensor_tensor(
                                out=xc[:, ts_], in0=p1[:, :], scalar=gtm[:, dc:dc + 1], in1=xc[:, ts_], op0=ALU.mult, op1=ALU.add),
                                reads=[p1r, r_xc, self.r_mod], writes=[r_xc])
                        S.dma("sp", self.xres[dc * 128:(dc + 1) * 128, t0:t0 + TB], xc[:], reads=[r_xc])

    def stage_ffn(self, l):
        S, I, SL = self.S, self.I, self.S_len
        pb = self.pb
        gtf = self.mod[:, l, 80:96]
        with Stage(self) as stg:
            xt, r_xt = stg.sb([128, KC, 512], F32, "xt")
            hT, r_hT = stg.sb([128, KC, 512], BF16, "hT")
            aT, r_aT = stg.sb([128, HC, 512], BF16, "aT")
            sg_ = [stg.sb([128, 512], F32, "sg") for _ in range(2)]
            wb_ = [stg.sb([128, 44, 256], BF16, "wffn") for _ in range(2)]
            wi = 0
            for tb in range(SL // 512):
                t0 = tb * 512
                xv = self.xres[:, t0:t0 + 512].rearrange("(dc p) t -> p dc t", p=128)
                for q4 in range(2):
                    S.dma("sp", xt[:, q4 * 8:(q4 + 1) * 8, :], xv[:, q4 * 8:(q4 + 1) * 8, :], reads=[self.r_xres], writes=[r_xt], join=True)
                self.norm_adaln(stg, xt, r_xt, 512, self.gm[:, l, 1, :], self.mod[:, l, 48:64], hT, r_hT)
                for hg in range(HC // 2):
                    wt, wr = wb_[wi % 2]
                    wi += 1
                    srcg = I["w_g"][l][:, hg * 256:(hg + 1) * 256].rearrange("(kc p) n -> p kc n", p=128)
                    srcu = I["w_u"][l][:, hg * 256:(hg + 1) * 256].rearrange("(kc p) n -> p kc n", p=128)
                    S.dma("pool", wt[:, 0:16, :], srcg, writes=[wr], join=True)
                    S.dma("pool", wt[:, 16:32, :], srcu, writes=[wr], join=True)
                    for j in range(2):
                        hc = hg * 2 + j
                        cs = slice(j * 128, (j + 1) * 128)
                        pg, pgr = pb[(hc % 2) * 2]
                        pu, pur = pb[(hc % 2) * 2 + 1]
                        for kc in range(16):
                            S.op("pe", lambda e, wt=wt, kc=kc, cs=cs, pg=pg: e.matmul(
                                pg[:, :], lhsT=wt[:, kc, cs], rhs=hT[:, kc, :], start=(kc == 0), stop=(kc == 15)),
                                reads=[wr, r_hT], writes=[pgr], sig=(kc == 15))
                        for kc in range(16):
                            S.op("pe", lambda e, wt=wt, kc=kc, cs=cs, pu=pu: e.matmul(
                                pu[:, :], lhsT=wt[:, 16 + kc, cs], rhs=hT[:, kc, :], start=(kc == 0), stop=(kc == 15)),
                                reads=[wr, r_hT], writes=[pur], sig=(kc == 15))
                        sg, r_sg = sg_[hc % 2]
                        S.op("act", lambda e, sg=sg, pg=pg: e.activation(out=sg[:], in_=pg[:, :], func=AF.Silu), reads=[pgr], writes=[r_sg])
                        S.op("dve", lambda e, sg=sg, pu=pu, hc=hc: e.tensor_tensor(out=aT[:, hc, :], in0=sg[:], in1=pu[:, :], op=ALU.mult),
                             reads=[r_sg, pur], writes=[r_aT])
                for dg in range(8):
                    wt, wr = wb_[wi % 2]
                    wi += 1
                    src = I["w_d"][l][:, dg * 256:(dg + 1) * 256].rearrange("(kc p) n -> p kc n", p=128)
                    S.dma("pool", wt[:, 0:22, :], src[:, 0:22, :], writes=[wr], join=True)
                    S.dma("pool", wt[:, 22:44, :], src[:, 22:44, :], writes=[wr], join=True)
                    for j in range(2):
                        dc = dg * 2 + j
                        cs = slice(j * 128, (j + 1) * 128)
                        p1, p1r = pb[4 + dc % 2]
                        for kc in range(HC):
                            S.op("pe", lambda e, wt=wt, kc=kc, cs=cs, p1=p1: e.matmul(
                                p1[:, :], lhsT=wt[:, kc, cs], rhs=aT[:, kc, :], start=(kc == 0), stop=(kc == HC - 1)),
                                reads=[wr, r_aT], writes=[p1r], sig=(kc == HC - 1))
                        S.op("dve", lambda e, dc=dc, p1=p1: e.scalar_tensor_tensor(
                            out=xt[:, dc, :], in0=p1[:, :], scalar=gtf[:, dc:dc + 1], in1=xt[:, dc, :], op0=ALU.mult, op1=ALU.add),
                            reads=[p1r, r_xt, self.r_mod], writes=[r_xt])
                for q4 in range(2):
                    S.dma("sp", self.xres[:, t0:t0 + 512].rearrange("(dc p) t -> p dc t", p=128)[:, q4 * 8:(q4 + 1) * 8, :],
                          xt[:, q4 * 8:(q4 + 1) * 8, :], reads=[r_xt], writes=[self.r_xres])

    def stage_final(self):
        S, I, SL = self.S, self.I, self.S_len
        with Stage(self) as stg:
            xt, r_xt = stg.sb([128, KC, 512], F32, "xt")
            ho, r_ho = stg.sb([128, KC, 512], F32, "ho")
            gf, r_gf = stg.sb([128, KC], F32, "gf")
            S.dma("sp", gf[:], I["g_final_pk"], writes=[self.r_gm])
            for tb in range(SL // 512):
                t0 = tb * 512
                xv = self.xres[:, t0:t0 + 512].rearrange("(dc p) t -> p dc t", p=128)
                for q4 in range(2):
                    S.dma("sp", xt[:, q4 * 8:(q4 + 1) * 8, :], xv[:, q4 * 8:(q4 + 1) * 8, :], reads=[self.r_xres], writes=[r_xt], join=True)
                self.norm_adaln(stg, xt, r_xt, 512, gf, None, ho, r_ho)
                for q4 in range(2):
                    S.dma("sp", self.outT[:, t0:t0 + 512].rearrange("(dc p) t -> p dc t", p=128)[:, q4 * 8:(q4 + 1) * 8, :],
                          ho[:, q4 * 8:(q4 + 1) * 8, :], reads=[r_ho])


def _pk(v, n):
    return np.ascontiguousarray(np.asarray(v, np.float32).reshape(n, 128).T)


def make_inputs(inp, b, depth):
    L = depth
    f = lambda a: np.ascontiguousarray(np.asarray(a, np.float32))
    m = {}
    m["xT"] = np.ascontiguousarray(np.asarray(inp["x"][b], np.float32).T)
    m["c_pk"] = _pk(inp["c"][b], KC)
    m["w_ada"] = f(inp["w_ada"][:L])
    m["b_ada_pk"] = np.stack([_pk(inp["b_ada"][l], 96) for l in range(L)])
    m["g_mix_pk"] = np.stack([_pk(inp["g_mix"][l], KC) for l in range(L)])
    m["g_ffn_pk"] = np.stack([_pk(inp["g_ffn"][l], KC) for l in range(L)])
    m["g_final_pk"] = _pk(inp["g_final"], KC)
    m["w_in"] = f(inp["w_in"][:L])
    m["b_gate_pk"] = np.stack([_pk(np.asarray(inp["b_gate"][l]).reshape(-1), 48) for l in range(L)])
    cw = np.asarray(inp["ssd_conv_w"], np.float32)[:L]
    m["conv_w_pk"] = np.ascontiguousarray(cw.reshape(L, 4, 32, 128).transpose(0, 3, 2, 1))
    m["conv_b_pk"] = np.stack([_pk(inp["ssd_conv_b"][l], 32) for l in range(L)])
    m["dtb"] = f(np.asarray(inp["ssd_dt_bias"])[:L].reshape(L, 32, 1))
    m["alog"] = f(np.asarray(inp["ssd_a_log"])[:L].reshape(L, 32, 1))
    m["dsk32"] = f(np.asarray(inp["ssd_d"], np.float32)[:L])
    m["ng_row"] = f(inp["ssd_norm_g"][:L])
    sw = np.asarray(inp["sc_conv_w"], np.float32)[:L]
    m["scw_pk"] = np.ascontiguousarray(sw.reshape(L, 3, 8, 128).transpose(0, 3, 2, 1))
    m["w_br_ssd"] = f(inp["w_br_ssd"][:L])
    m["w_br_sc"] = f(inp["w_br_sc"][:L])
    m["w_br_att"] = f(inp["w_br_att"][:L])
    m["w_out"] = f(inp["w_out"][:L])
    m["w_g"] = f(inp["w_ffn_gate"][:L])
    m["w_u"] = f(inp["w_ffn_up"][:L])
    m["w_d"] = f(inp["w_ffn_down"][:L])
    return m


_CACHE = {}


def kernel(**inputs):
    x = np.asarray(inputs["x"])
    Bsz, SL, _ = x.shape
    depth = np.asarray(inputs["w_in"]).shape[0]
    key = (SL, depth)
    if key not in _CACHE:
        _CACHE[key] = Builder(SL, depth).build()
    nc = _CACHE[key]
    in_maps = [make_inputs(inputs, b, depth) for b in range(Bsz)]
    res = run_bass_kernel_spmd(nc, in_maps, core_ids=list(range(Bsz)))
    out = np.stack([np.ascontiguousarray(res.results[b]["outT"].T) for b in range(Bsz)])
    return out.astype(np.float32)
```

```python
import numpy as np
from contextlib import ExitStack
import concourse.bass as bass
import concourse.mybir as mybir
from concourse.bass_utils import run_bass_kernel_spmd

F32 = mybir.dt.float32
BF16 = mybir.dt.bfloat16
ALU = mybir.AluOpType
AF = mybir.ActivationFunctionType

D = 2048
KC = 16
N_IN = 19568
FFN = 5632
HC = FFN // 128
EPS = 1e-6
NEG = -1.0e30
C_Z, C_XBC, C_DT, C_SC, C_Q, C_K, C_V, C_IQ, C_IK, C_IW, C_G = (
    0, 2048, 6144, 6176, 9248, 10272, 11296, 12320, 13344, 13408, 13424)
T_XS, T_B, T_SZ, T_V, T_DT, T_A, T_IW, T_END = 0, 2048, 3072, 5120, 6144, 6176, 6208, 6224


class Res:
    __slots__ = ("name", "w", "rs", "joined")

    def __init__(self, name=""):
        self.name = name
        self.w = []
        self.rs = {}
        self.joined = False


class Sched:
    ENG = ("pe", "act", "dve", "pool", "sp")

    def __init__(self, nc, stack, n_dma_sems=12):
        self.nc = nc
        self.eng = {"pe": nc.tensor, "act": nc.scalar, "dve": nc.vector,
                    "pool": nc.gpsimd, "sp": nc.sync}
        self.ops = {k: [] for k in self.ENG}
        self.cnt = {k: 0 for k in self.ENG}
        self.seen = {k: {} for k in self.ENG}
        self.sem = {k: stack.enter_context(nc.semaphore("s_" + k)) for k in self.ENG}
        self.live = set()
        self.dpool = {}
        self.drr = {}
        for q in ("sp", "pool"):
            self.dpool[q] = [[stack.enter_context(nc.semaphore(f"d_{q}{i}")), 0, (q, i)]
                             for i in range(n_dma_sems)]
            self.drr[q] = 0

    def _need(self, E, tok):
        key, sem, val = tok
        if key == E and E == "pe":
            return
        if self.seen[E].get(key, 0) >= val:
            return
        self.seen[E][key] = val
        eng = self.eng[E]
        self.ops[E].append(lambda: eng.wait_ge(sem, val))

    def _deps(self, E, reads, writes, join=False):
        for r in reads:
            for t in r.w:
                self._need(E, t)
        for w in writes:
            if not (join and w.joined):
                for t in w.w:
                    self._need(E, t)
            for t in w.rs.values():
                self._need(E, t)

    def _commit(self, tok, reads, writes, join=False):
        for r in reads:
            r.rs[tok[0]] = tok
            self.live.add(r)
        for w in writes:
            if join and w.joined:
                w.w.append(tok)
            else:
                w.w = [tok]
            w.joined = join
            w.rs = {}
            self.live.add(w)

    def op(self, E, fn, reads=(), writes=(), sig=True):
        reads = [r for r in reads if r is not None]
        writes = [w for w in writes if w is not None]
        self._deps(E, reads, writes)
        sem = self.sem[E]
        eng = self.eng[E]
        if sig:
            self.cnt[E] += 1
            tok = (E, sem, self.cnt[E])
            self.ops[E].append(lambda: fn(eng).then_inc(sem, 1))
        else:
            assert E == "pe"
            tok = (E, sem, self.cnt[E] + 1)
            self.ops[E].append(lambda: fn(eng))
        self._commit(tok, reads, writes)
        return tok

    def dma(self, q, out, in_, reads=(), writes=(), join=False):
        reads = [r for r in reads if r is not None]
        writes = [w for w in writes if w is not None]
        pool = self.dpool[q]
        ent = pool[self.drr[q]]
        self.drr[q] = (self.drr[q] + 1) % len(pool)
        if ent[1] > 0:
            self._need(q, (ent[2], ent[0], ent[1]))
        self._deps(q, reads, writes, join)
        ent[1] += 16
        sem = ent[0]
        tok = (ent[2], sem, ent[1])
        eng = self.eng[q]
        self.ops[q].append(lambda: eng.dma_start(out=out, in_=in_).then_inc(sem, 16))
        self._commit(tok, reads, writes, join)
        return tok

    def barrier(self):
        toks = []
        for P in self.ENG:
            if self.cnt[P] > 0:
                toks.append((P, self.sem[P], self.cnt[P]))
        for q in self.dpool:
            for ent in self.dpool[q]:
                if ent[1] > 0:
                    toks.append((ent[2], ent[0], ent[1]))
        for E in self.ENG:
            for t in toks:
                self._need(E, t)
        for r in self.live:
            r.w = []
            r.rs = {}
            r.joined = False
        self.live = set()

    def emit(self):
        ops = self.ops
        with self.nc.Block() as block:
            @block.sync
            def _(e):
                for f in ops["sp"]:
                    f()

            @block.scalar
            def _(e):
                for f in ops["act"]:
                    f()

            @block.vector
            def _(e):
                for f in ops["dve"]:
                    f()

            @block.gpsimd
            def _(e):
                for f in ops["pool"]:
                    f()

            @block.tensor
            def _(e):
                for f in ops["pe"]:
                    f()


class Stage:
    def __init__(self, k):
        self.k = k
        self.st = ExitStack()
        self.n = 0
        self.cache = {}

    def __enter__(self):
        return self

    def sb(self, shape, dt, name=None):
        self.n += 1
        self.k.uid += 1
        t = self.st.enter_context(self.k.nc.sbuf_tensor(f"{name or 't'}_{self.k.uid}", list(shape), dt))
        return t, Res(name or "t")

    def __exit__(self, *a):
        self.k.S.barrier()
        self.st.close()
        return False


class Builder:
    def __init__(self, S_len, depth, dbg=()):
        self.S_len = S_len
        self.depth = depth
        self.dbg = dbg
        self.uid = 0
        self.nc = bass.Bass("TRN2", target_bir_lowering=False)
        self.top = ExitStack()
        self.S = Sched(self.nc, self.top)

    def din(self, name, shape, dt=F32):
        return self.nc.dram_tensor(name, list(shape), dt, kind="ExternalInput").ap()

    def dscr(self, name, shape, dt=F32):
        kind = "ExternalOutput" if name in self.dbg else "Internal"
        return self.nc.dram_tensor(name, list(shape), dt, kind=kind).ap(), None

    def build(self):
        nc, S, L, SL = self.nc, self.S, self.depth, self.S_len
        I = {}
        I["xT"] = self.din("xT", [D, SL])
        I["c_pk"] = self.din("c_pk", [128, KC])
        I["w_ada"] = self.din("w_ada", [L, D, 6 * D])
        I["b_ada_pk"] = self.din("b_ada_pk", [L, 128, 96])
        I["g_mix_pk"] = self.din("g_mix_pk", [L, 128, KC])
        I["g_ffn_pk"] = self.din("g_ffn_pk", [L, 128, KC])
        I["g_final_pk"] = self.din("g_final_pk", [128, KC])
        I["w_in"] = self.din("w_in", [L, D, N_IN])
        I["b_gate_pk"] = self.din("b_gate_pk", [L, 128, 48])
        I["conv_w_pk"] = self.din("conv_w_pk", [L, 128, 32, 4])
        I["conv_b_pk"] = self.din("conv_b_pk", [L, 128, 32])
        I["dtb"] = self.din("dtb", [L, 32, 1])
        I["alog"] = self.din("alog", [L, 32, 1])
        I["dsk32"] = self.din("dsk32", [L, 32])
        I["ng_row"] = self.din("ng_row", [L, D])
        I["scw_pk"] = self.din("scw_pk", [L, 128, 8, 3])
        I["w_br_ssd"] = self.din("w_br_ssd", [L, D, D])
        I["w_br_sc"] = self.din("w_br_sc", [L, 1024, D])
        I["w_br_att"] = self.din("w_br_att", [L, 1024, D])
        I["w_out"] = self.din("w_out", [L, D, D])
        I["w_g"] = self.din("w_g", [L, D, FFN])
        I["w_u"] = self.din("w_u", [L, D, FFN])
        I["w_d"] = self.din("w_d", [L, FFN, D])
        self.I = I
        self.outT = self.nc.dram_tensor("outT", [D, SL], F32, kind="ExternalOutput").ap()
        self.xres, self.r_xres = self.dscr("xres", [D, SL])
        self.TT, self.r_TT = self.dscr("TT", [T_END, SL])
        self.TOK, self.r_TOK = self.dscr("TOK", [SL, T_END])
        self.xbcT, self.r_xbcT = self.dscr("xbcT", [4096, SL])
        self.scT, self.r_scT = self.dscr("scT", [3072, SL])
        self.qT, self.r_qT = self.dscr("qT", [1024, SL], BF16)
        self.kT, self.r_kT = self.dscr("kT", [1024, SL], BF16)
        self.iqT, self.r_iqT = self.dscr("iqT", [1024, SL], BF16)
        self.ikT, self.r_ikT = self.dscr("ikT", [64, SL], BF16)
        self.gT, self.r_gT = self.dscr("gT", [3 * D, SL])
        self.BCT, self.r_BCT = self.dscr("BCT", [2048, SL], BF16)
        self.yscT, self.r_yscT = self.dscr("yscT", [1024, SL], BF16)
        self.TOK2, self.r_TOK2 = self.dscr("TOK2", [SL, 3072])
        self.TT2, self.r_TT2 = self.dscr("TT2", [3072, SL])

        top = self.top
        self.pb = []
        for i in range(8):
            t = top.enter_context(nc.psum_tensor(f"pb{i}", [128, 512], F32))
            self.pb.append((t, Res(f"pb{i}")))
        def psb(name, shape, dt=F32):
            return top.enter_context(nc.sbuf_tensor(name, list(shape), dt)), Res(name)
        self.ident, self.r_ident = psb("ident", [128, 128])
        self.ones, self.r_ones = psb("ones", [128, 128])
        self.triu, self.r_triu = psb("triu", [64, 64])
        self.ntriu, self.r_ntriu = psb("ntriu", [64, 64])
        self.r3, self.r_r3 = psb("r3", [64, 32, 64])
        self.ones16, self.r_ones16 = psb("ones16", [64, 64], BF16)
        self.ntriu16, self.r_ntriu16 = psb("ntriu16", [64, 64], BF16)
        self.ident16, self.r_ident16 = psb("ident16", [64, 64], BF16)
        self.r3_16, self.r_r3_16 = psb("r3_16", [64, 32, 64], BF16)
        self.eps_t, self.r_eps = psb("eps_t", [128, 1])
        self.nbig, self.r_nbig = psb("nbig", [128, 1])
        self.p2tab, self.r_p2tab = psb("p2tab", [128, 32])
        self.mod, self.r_mod = psb("mod", [128, L, 96])
        self.gm, self.r_gm = psb("gm", [128, L, 2, KC])
        self.csb, self.r_csb = psb("csb", [128, KC], BF16)
        ident, ones, triu, ntriu, r3 = self.ident, self.ones, self.triu, self.ntriu, self.r3
        S.op("pool", lambda e: e.memset(ident[:], 0.0), writes=[self.r_ident])
        S.op("pool", lambda e: e.affine_select(out=ident[:], in_=ident[:], pattern=[[-1, 128]],
                                               compare_op=ALU.not_equal, fill=1.0, base=0, channel_multiplier=1),
             reads=[self.r_ident], writes=[self.r_ident])
        S.op("pool", lambda e: e.memset(ones[:], 1.0), writes=[self.r_ones])
        S.op("pool", lambda e: e.memset(triu[:], 1.0), writes=[self.r_triu])
        S.op("pool", lambda e: e.affine_select(out=triu[:], in_=triu[:], pattern=[[1, 64]],
                                               compare_op=ALU.is_ge, fill=0.0, base=0, channel_multiplier=-1),
             reads=[self.r_triu], writes=[self.r_triu])
        S.op("pool", lambda e: e.tensor_scalar(out=ntriu[:], in0=triu[:], scalar1=-1.0, scalar2=None, op0=ALU.mult),
             reads=[self.r_triu], writes=[self.r_ntriu])
        S.op("pool", lambda e: e.memset(r3[:], 0.0), writes=[self.r_r3])
        S.op("pool", lambda e: e.affine_select(out=r3[:], in_=r3[:], pattern=[[0, 32], [1, 64]],
                                               compare_op=ALU.is_ge, fill=-30000.0, base=0, channel_multiplier=-1),
             reads=[self.r_r3], writes=[self.r_r3])
        S.op("pool", lambda e: e.tensor_copy(self.ones16[:], ones[0:64, 0:64]), reads=[self.r_ones], writes=[self.r_ones16])
        S.op("pool", lambda e: e.tensor_copy(self.ntriu16[:], ntriu[:, :]), reads=[self.r_ntriu], writes=[self.r_ntriu16])
        S.op("pool", lambda e: e.tensor_copy(self.ident16[:], ident[0:64, 0:64]), reads=[self.r_ident], writes=[self.r_ident16])
        S.op("pool", lambda e: e.tensor_copy(self.r3_16[:], r3[:]), reads=[self.r_r3], writes=[self.r_r3_16])
        S.op("dve", lambda e: e.memset(self.eps_t[:], EPS), writes=[self.r_eps])
        S.op("dve", lambda e: e.memset(self.nbig[:], NEG / 2), writes=[self.r_nbig])
        for j in range(32):
            S.op("dve", lambda e, j=j: e.memset(self.p2tab[:, j:j + 1], 2.0 ** -j), writes=[self.r_p2tab])

        self.stage_mod()
        for l in range(L):
            self.stage_proj(l, I["xT"] if l == 0 else self.xres)
            if "stop_proj" in self.dbg:
                break
            self.stage_conv(l)
            if "stop_conv" in self.dbg:
                break
            self.stage_ssd(l)
            if "stop_ssd" in self.dbg:
                break
            self.stage_attn(l)
            self.transpose_pass(self.TOK2, self.r_TOK2, self.TT2, self.r_TT2, SL, 3072)
            if "stop_attn" in self.dbg:
                break
            self.stage_merge(l, I["xT"] if l == 0 else self.xres)
            if "stop_merge" in self.dbg:
                break
            self.stage_ffn(l)
        else:
            self.stage_final()
        S.barrier()
        S.emit()
        return nc

    def load_wgroup(self, stg_t, stg_r, W, col0, ncols, nk):
        src = W[:, col0:col0 + ncols].rearrange("(kc p) n -> p kc n", p=128)
        half = nk // 2 if nk >= 8 else nk
        for k0 in range(0, nk, half):
            self.S.dma("pool", stg_t[:, k0:k0 + half, 0:ncols], src[:, k0:k0 + half, :], writes=[stg_r], join=True)

    def norm_adaln(self, stg, xt, r_xt, N, gm_ap, sh_ap, hT, r_hT):
        S = self.S
        if ("norm", N) not in stg.cache:
            stg.cache[("norm", N)] = ([stg.sb([128, 512], F32, "sq") for _ in range(2)], stg.sb([128, N], F32, "rstd"),
                                      [stg.sb([128, N], F32, "ntmp") for _ in range(2)])
        sqs, (rstd, r_rstd), tmps = stg.cache[("norm", N)]
        ones, eps_t = self.ones, self.eps_t
        for hf in range(N // 512):
            pbt, pbr = self.pb[hf % 2]
            for dc in range(KC):
                s_t, s_r = sqs[dc % 2]
                S.op("act", lambda e, s_t=s_t, dc=dc, hf=hf: e.activation(
                    out=s_t[:], in_=xt[:, dc, hf * 512:(hf + 1) * 512], func=AF.Square),
                    reads=[r_xt], writes=[s_r])
                S.op("pe", lambda e, s_t=s_t, dc=dc, pbt=pbt: e.matmul(
                    pbt[:], lhsT=ones[:], rhs=s_t[:], start=(dc == 0), stop=(dc == KC - 1)),
                    reads=[s_r, self.r_ones], writes=[pbr])
            S.op("act", lambda e, pbt=pbt, hf=hf: e.activation(
                out=rstd[:, hf * 512:(hf + 1) * 512], in_=pbt[:], func=AF.Sqrt, scale=1.0 / D, bias=eps_t[:, 0:1]),
                reads=[pbr, self.r_eps], writes=[r_rstd])
        S.op("dve", lambda e: e.reciprocal(out=rstd[:], in_=rstd[:]), reads=[r_rstd], writes=[r_rstd])
        for dc in range(KC):
            tm, r_tm = tmps[dc % 2]
            S.op("dve", lambda e, dc=dc, tm=tm: e.tensor_tensor(out=tm[:], in0=xt[:, dc, :], in1=rstd[:], op=ALU.mult),
                 reads=[r_xt, r_rstd], writes=[r_tm])
            if sh_ap is not None:
                S.op("act", lambda e, dc=dc, tm=tm: e.activation(out=hT[:, dc, :], in_=tm[:], func=AF.Identity,
                                                                 scale=gm_ap[:, dc:dc + 1], bias=sh_ap[:, dc:dc + 1]),
                     reads=[r_tm, self.r_mod, self.r_gm], writes=[r_hT])
            else:
                S.op("act", lambda e, dc=dc, tm=tm: e.activation(out=hT[:, dc, :], in_=tm[:], func=AF.Copy,
                                                                 scale=gm_ap[:, dc:dc + 1]),
                     reads=[r_tm, self.r_mod, self.r_gm], writes=[r_hT])

    def stage_mod(self):
        S, I, L = self.S, self.I, self.depth
        with Stage(self) as stg:
            cf, r_cf = stg.sb([128, KC], F32, "cf")
            S.dma("sp", cf[:], I["c_pk"], writes=[r_cf])
            S.op("act", lambda e: e.activation(out=self.csb[:], in_=cf[:], func=AF.Silu), reads=[r_cf], writes=[self.r_csb])
            wb = [stg.sb([128, KC, 512], BF16, "wada") for _ in range(2)]
            bpk, r_bpk = stg.sb([128, L, 96], F32, "bpk")
            gmx, r_gmx = stg.sb([128, L, 2, KC], F32, "gmx")
            for l in range(L):
                S.dma("sp", bpk[:, l, :], I["b_ada_pk"][l], writes=[r_bpk], join=True)
                S.dma("sp", gmx[:, l, 0, :], I["g_mix_pk"][l], writes=[r_gmx], join=True)
                S.dma("sp", gmx[:, l, 1, :], I["g_ffn_pk"][l], writes=[r_gmx], join=True)
            gi = 0
            for l in range(L):
                pbt, pbr = self.pb[l % 2]
                for g in range(24):
                    wt, wr = wb[gi % 2]
                    gi += 1
                    self.load_wgroup(wt, wr, I["w_ada"][l], g * 512, 512, KC)
                    for j in range(4):
                        col = g * 4 + j
                        for kc in range(KC):
                            S.op("pe", lambda e, wt=wt, j=j, kc=kc, col=col, pbt=pbt: e.matmul(
                                pbt[:, col:col + 1], lhsT=wt[:, kc, j * 128:(j + 1) * 128], rhs=self.csb[:, kc:kc + 1],
                                start=(kc == 0), stop=(kc == KC - 1)),
                                reads=[wr, self.r_csb], writes=[pbr], sig=(kc == KC - 1))
                S.op("dve", lambda e, l=l, pbt=pbt: e.tensor_tensor(out=self.mod[:, l, :], in0=pbt[:, 0:96], in1=bpk[:, l, :],
                                                                    op=ALU.add),
                     reads=[pbr, r_bpk], writes=[self.r_mod])
                for v, sci in ((0, 1), (1, 4)):
                    S.op("dve", lambda e, l=l, v=v, sci=sci: e.scalar_tensor_tensor(
                        out=self.gm[:, l, v, :], in0=self.mod[:, l, sci * 16:(sci + 1) * 16], scalar=1.0,
                        in1=gmx[:, l, v, :], op0=ALU.add, op1=ALU.mult),
                        reads=[self.r_mod, r_gmx], writes=[self.r_gm])

    def stage_proj(self, l, xsrc):
        S, I, SL = self.S, self.I, self.S_len
        TB = min(1024, SL)
        W = I["w_in"][l]
        segs = [
            (C_Z, 2048, "silu", self.TT, self.r_TT, T_SZ),
            (C_XBC, 4096, "copy", self.xbcT, self.r_xbcT, 0),
            (C_DT, 32, "dt", self.TT, self.r_TT, T_DT),
            (C_SC, 3072, "copy", self.scT, self.r_scT, 0),
            (C_Q, 1024, "copy16", self.qT, self.r_qT, 0),
            (C_K, 1024, "copy16", self.kT, self.r_kT, 0),
            (C_V, 1024, "copy", self.TT, self.r_TT, T_V),
            (C_IQ, 1024, "copy16", self.iqT, self.r_iqT, 0),
            (C_IK, 64, "copy16", self.ikT, self.r_ikT, 0),
            (C_IW, 16, "copy", self.TT, self.r_TT, T_IW),
            (C_G, 6144, "gate", self.gT, self.r_gT, 0),
        ]
        with Stage(self) as stg:
            xt, r_xt = stg.sb([128, KC, TB], F32, "xt")
            hT, r_hT = stg.sb([128, KC, TB], BF16, "hT")
            wb = [stg.sb([128, KC, 512], BF16, "win") for _ in range(2)]
            ob32 = [stg.sb([128, TB], F32, "ob32") for _ in range(3)]
            ob16 = [stg.sb([128, TB], BF16, "ob16") for _ in range(2)]
            bg, r_bg = stg.sb([128, 48], F32, "bg")
            dtb, r_dtb = stg.sb([32, 1], F32, "dtb")
            nA, r_nA = stg.sb([32, 1], F32, "nA")
            av, r_av = stg.sb([32, TB], F32, "av")
            S.dma("sp", bg[:], I["b_gate_pk"][l], writes=[r_bg])
            S.dma("sp", dtb[:], I["dtb"][l], writes=[r_dtb])
            S.dma("sp", nA[:], I["alog"][l], writes=[r_nA])
            S.op("act", lambda e: e.activation(out=nA[:], in_=nA[:], func=AF.Exp), reads=[r_nA], writes=[r_nA])
            S.op("dve", lambda e: e.tensor_scalar(out=nA[:], in0=nA[:], scalar1=-1.0, scalar2=None, op0=ALU.mult),
                 reads=[r_nA], writes=[r_nA])
            gi = 0
            oi = 0
            ev = 0
            for tb in range(SL // TB):
                t0 = tb * TB
                xv = xsrc[:, t0:t0 + TB].rearrange("(dc p) t -> p dc t", p=128)
                for q4 in range(4):
                    S.dma("sp", xt[:, q4 * 4:(q4 + 1) * 4, :], xv[:, q4 * 4:(q4 + 1) * 4, :],
                          reads=[self.r_xres], writes=[r_xt], join=True)
                self.norm_adaln(stg, xt, r_xt, TB, self.gm[:, l, 0, :], self.mod[:, l, 0:16], hT, r_hT)
                for (c0, ncols, kind, dst, dst_r, drow) in segs:
                    for g0 in range(0, ncols, 512):
                        gn = min(512, ncols - g0)
                        wt, wr = wb[gi % 2]
                        gi += 1
                        self.load_wgroup(wt, wr, W, c0 + g0, gn, KC)
                        for j0 in range(0, gn, 128):
                            m = min(128, gn - j0)
                            use16 = kind == "copy16"
                            if use16:
                                ot, orr = ob16[oi % 2]
                            else:
                                ot, orr = ob32[oi % 3]
                            oi += 1
                            for hf in range(TB // 512):
                                pbt, pbr = self.pb[2 + (ev % 6)]
                                for kc in range(KC):
                                    S.op("pe", lambda e, wt=wt, kc=kc, j0=j0, m=m, hf=hf, pbt=pbt: e.matmul(
                                        pbt[0:m, :], lhsT=wt[:, kc, j0:j0 + m], rhs=hT[:, kc, hf * 512:(hf + 1) * 512],
                                        start=(kc == 0), stop=(kc == KC - 1)),
                                        reads=[wr, r_hT], writes=[pbr], sig=(kc == KC - 1))
                                osl = ot[0:m, hf * 512:(hf + 1) * 512]
                                if kind == "silu":
                                    S.op("act", lambda e, osl=osl, pbt=pbt, m=m: e.activation(out=osl, in_=pbt[0:m, :], func=AF.Silu),
                                         reads=[pbr], writes=[orr])
                                elif kind == "gate":
                                    gc = (g0 + j0) // 128
                                    S.op("act", lambda e, osl=osl, pbt=pbt, gc=gc: e.activation(
                                        out=osl, in_=pbt[:, :], func=AF.Sigmoid, bias=bg[:, gc:gc + 1]),
                                        reads=[pbr, r_bg], writes=[orr])
                                elif kind == "dt":
                                    S.op("act", lambda e, osl=osl, pbt=pbt: e.activation(
                                        out=osl, in_=pbt[0:32, :], func=AF.Exp, bias=dtb[:, 0:1]),
                                        reads=[pbr, r_dtb], writes=[orr])
                                    S.op("act", lambda e, osl=osl: e.activation(out=osl, in_=osl, func=AF.Ln, bias=1.0),
                                         reads=[orr], writes=[orr])
                                    S.op("dve", lambda e, osl=osl, hf=hf: e.tensor_scalar(
                                        out=av[:, hf * 512:(hf + 1) * 512], in0=osl, scalar1=nA[:, 0:1], scalar2=None, op0=ALU.mult),
                                        reads=[orr, r_nA], writes=[r_av])
                                else:
                                    if ev % 2 == 0:
                                        S.op("act", lambda e, osl=osl, pbt=pbt, m=m: e.copy(osl, pbt[0:m, :]),
                                             reads=[pbr], writes=[orr])
                                    else:
                                        S.op("dve", lambda e, osl=osl, pbt=pbt, m=m: e.tensor_copy(osl, pbt[0:m, :]),
                                             reads=[pbr], writes=[orr])
                                ev += 1
                            r0 = drow + g0 + j0
                            S.dma("sp", dst[r0:r0 + m, t0:t0 + TB], ot[0:m, :], reads=[orr], writes=[dst_r])
                            if kind == "dt":
                                S.dma("sp", self.TT[T_A:T_A + 32, t0:t0 + TB], av[:, :], reads=[r_av], writes=[self.r_TT])

    def stage_conv(self, l):
        S, I, SL = self.S, self.I, self.S_len
        TB = min(1024, SL)
        r_TTc = Res("TTc")
        with Stage(self) as stg:
            tjobs_free, tjobs_dep = self.transpose_jobs(stg, self.TT, self.TOK, T_END, SL, r_TTc)
            nfree = len(tjobs_free)
            cw, r_cw = stg.sb([128, 32, 4], F32, "cw")
            cb, r_cb = stg.sb([128, 32], F32, "cb")
            sw, r_sw = stg.sb([128, 8, 3], F32, "sw")
            S.dma("sp", cw[:], I["conv_w_pk"][l], writes=[r_cw])
            S.dma("sp", cb[:], I["conv_b_pk"][l], writes=[r_cb])
            S.dma("sp", sw[:], I["scw_pk"][l], writes=[r_sw])
            xin = [stg.sb([128, TB + 3], F32, "xin") for _ in range(2)]
            acc = [stg.sb([128, TB], F32, "acc") for _ in range(2)]
            o32 = [stg.sb([128, TB], F32, "o32") for _ in range(2)]
            o16 = [stg.sb([128, TB], BF16, "o16") for _ in range(2)]
            it = 0
            for rc in range(32):
                for tb in range(SL // TB):
                    if tjobs_free and (it % 4 == 0):
                        tjobs_free.pop(0)()
                    t0 = tb * TB
                    xi, xr = xin[it % 2]
                    ac, ar = acc[it % 2]
                    o3, o3r = o32[it % 2]
                    o6, o6r = o16[it % 2]
                    it += 1
                    rows = slice(rc * 128, (rc + 1) * 128)
                    if tb == 0:
                        S.op("dve", lambda e, xi=xi: e.memset(xi[:, 0:3], 0.0), writes=[xr])
                        S.dma("sp", xi[:, 3:3 + TB], self.xbcT[rows, 0:TB], reads=[self.r_xbcT], writes=[xr])
                    else:
                        S.dma("sp", xi[:, :], self.xbcT[rows, t0 - 3:t0 + TB], reads=[self.r_xbcT], writes=[xr])
                    S.op("act", lambda e, xi=xi, ac=ac, rc=rc: e.activation(
                        out=ac[:], in_=xi[:, 3:3 + TB], func=AF.Identity, scale=cw[:, rc, 3:4], bias=cb[:, rc:rc + 1]),
                        reads=[xr, r_cw, r_cb], writes=[ar])
                    for k in (2, 1, 0):
                        S.op("dve", lambda e, xi=xi, ac=ac, rc=rc, k=k: e.scalar_tensor_tensor(
                            out=ac[:], in0=xi[:, k:k + TB], scalar=cw[:, rc, k:k + 1], in1=ac[:], op0=ALU.mult, op1=ALU.add),
                            reads=[xr, r_cw, ar], writes=[ar])
                    if rc < 24:
                        S.op("act", lambda e, ac=ac, o3=o3: e.activation(out=o3[:], in_=ac[:], func=AF.Silu),
                             reads=[ar], writes=[o3r])
                        S.dma("pool", self.TT[rc * 128:(rc + 1) * 128, t0:t0 + TB], o3[:], reads=[o3r], writes=[r_TTc], join=True)
                        if rc >= 16:
                            S.op("dve", lambda e, o3=o3, o6=o6: e.tensor_copy(o6[:], o3[:]), reads=[o3r], writes=[o6r])
                    else:
                        S.op("act", lambda e, ac=ac, o6=o6: e.activation(out=o6[:], in_=ac[:], func=AF.Silu),
                             reads=[ar], writes=[o6r])
                    if rc >= 16:
                        S.dma("pool", self.BCT[(rc - 16) * 128:(rc - 15) * 128, t0:t0 + TB], o6[:], reads=[o6r], writes=[self.r_BCT])
            cin = [stg.sb([128, TB + 2], F32, "cin") for _ in range(2)]
            hin = [stg.sb([128, TB + 2], F32, "hin") for _ in range(2)]
            bin_ = [stg.sb([128, TB], F32, "bin") for _ in range(2)]
            for rc in range(8):
                for tb in range(SL // TB):
                    t0 = tb * TB
                    ci, cr = cin[it % 2]
                    hi, hr = hin[it % 2]
                    bi, br = bin_[it % 2]
                    ac, ar = acc[it % 2]
                    o6, o6r = o16[it % 2]
                    it += 1
                    if tb == 0:
                        S.op("dve", lambda e, ci=ci: e.memset(ci[:, 0:2], 0.0), writes=[cr])
                        S.op("dve", lambda e, hi=hi: e.memset(hi[:, 0:2], 0.0), writes=[hr])
                        S.dma("sp", ci[:, 2:2 + TB], self.scT[1024 + rc * 128:1024 + (rc + 1) * 128, 0:TB],
                              reads=[self.r_scT], writes=[cr])
                        S.dma("sp", hi[:, 2:2 + TB], self.scT[2048 + rc * 128:2048 + (rc + 1) * 128, 0:TB],
                              reads=[self.r_scT], writes=[hr])
                    else:
                        S.dma("sp", ci[:, :], self.scT[1024 + rc * 128:1024 + (rc + 1) * 128, t0 - 2:t0 + TB],
                              reads=[self.r_scT], writes=[cr])
                        S.dma("sp", hi[:, :], self.scT[2048 + rc * 128:2048 + (rc + 1) * 128, t0 - 2:t0 + TB],
                              reads=[self.r_scT], writes=[hr])
                    S.dma("sp", bi[:, :], self.scT[rc * 128:(rc + 1) * 128, t0:t0 + TB], reads=[self.r_scT], writes=[br])
                    S.op("dve", lambda e, ci=ci, hi=hi: e.tensor_tensor(out=ci[:], in0=ci[:], in1=hi[:], op=ALU.mult),
                         reads=[cr, hr], writes=[cr])
                    S.op("act", lambda e, ci=ci, ac=ac, rc=rc: e.activation(
                        out=ac[:], in_=ci[:, 2:2 + TB], func=AF.Copy, scale=sw[:, rc, 2:3]),
                        reads=[cr, r_sw], writes=[ar])
                    for k in (1, 0):
                        S.op("dve", lambda e, ci=ci, ac=ac, rc=rc, k=k: e.scalar_tensor_tensor(
                            out=ac[:], in0=ci[:, k:k + TB], scalar=sw[:, rc, k:k + 1], in1=ac[:], op0=ALU.mult, op1=ALU.add),
                            reads=[cr, r_sw, ar], writes=[ar])
                    S.op("dve", lambda e, ac=ac, bi=bi, o6=o6: e.tensor_tensor(out=o6[:], in0=ac[:], in1=bi[:], op=ALU.mult),
                         reads=[ar, br], writes=[o6r])
                    S.dma("pool", self.yscT[rc * 128:(rc + 1) * 128, t0:t0 + TB], o6[:], reads=[o6r], writes=[self.r_yscT])
            for j in tjobs_free:
                j()
            for j in tjobs_dep:
                j()

    def transpose_jobs(self, stg, src, dst, R, C, r_dep):
        S = self.S
        it_ = [stg.sb([128, 8, 512], F32, "tin") for _ in range(2)]
        ot_ = [stg.sb([128, 4, 1024], F32, "tout") for _ in range(2)]
        stt = {"it": 0, "ev": 0}
        free, dep = [], []

        def job(r0, c0, rdep):
            rn = min(1024, R - r0)
            nch = (rn + 127) // 128
            ti, tir = it_[stt["it"] % 2]
            to, tor = ot_[stt["it"] % 2]
            stt["it"] += 1
            nfull = rn // 128
            rd = [rdep] if rdep is not None else []
            if nfull:
                S.dma("sp", ti[:, 0:nfull, :],
                      src[r0:r0 + nfull * 128, c0:c0 + 512].rearrange("(k p) c -> p k c", p=128),
                      reads=rd, writes=[tir], join=True)
            if rn % 128:
                mm = rn % 128
                S.dma("sp", ti[0:mm, nfull, :], src[r0 + nfull * 128:r0 + rn, c0:c0 + 512],
                      reads=rd, writes=[tir], join=True)
            for j in range(4):
                for k4 in range(0, nch, 4):
                    pbt, pbr = self.pb[stt["ev"] % 4]
                    kn = min(4, nch - k4)
                    wtot = 0
                    for k in range(k4, k4 + kn):
                        m = min(128, rn - k * 128)
                        S.op("pe", lambda e, ti=ti, k=k, j=j, m=m, pbt=pbt, k4=k4: e.transpose(
                            pbt[:, (k - k4) * 128:(k - k4) * 128 + m], ti[0:m, k, j * 128:(j + 1) * 128], self.ident[0:m, 0:m]),
                            reads=[tir, self.r_ident], writes=[pbr])
                        wtot = (k - k4) * 128 + m
                    dsl = to[:, j, k4 * 128:k4 * 128 + wtot]
                    if stt["ev"] % 2 == 0:
                        S.op("act", lambda e, dsl=dsl, pbt=pbt, wtot=wtot: e.copy(dsl, pbt[:, 0:wtot]), reads=[pbr], writes=[tor])
                    else:
                        S.op("dve", lambda e, dsl=dsl, pbt=pbt, wtot=wtot: e.tensor_copy(dsl, pbt[:, 0:wtot]), reads=[pbr], writes=[tor])
                    stt["ev"] += 1
            S.dma("pool", dst[c0:c0 + 512, r0:r0 + rn].rearrange("(j p) r -> p j r", p=128), to[:, :, 0:rn], reads=[tor])

        for r0 in range(0, R, 1024):
            for c0 in range(0, C, 512):
                if r0 < 3072:
                    dep.append(lambda r0=r0, c0=c0: job(r0, c0, r_dep))
                else:
                    free.append(lambda r0=r0, c0=c0: job(r0, c0, None))
        return free, dep

    def transpose_pass(self, src, r_src, dst, r_dst, R, C):
        S = self.S
        with Stage(self) as stg:
            it_ = [stg.sb([128, 8, 512], F32, "tin") for _ in range(2)]
            ot_ = [stg.sb([128, 4, 1024], F32, "tout") for _ in range(2)]
            it = 0
            ev = 0
            for r0 in range(0, R, 1024):
                rn = min(1024, R - r0)
                nch = (rn + 127) // 128
                for c0 in range(0, C, 512):
                    ti, tir = it_[it % 2]
                    to, tor = ot_[it % 2]
                    it += 1
                    nfull = rn // 128
                    if nfull:
                        S.dma("sp", ti[:, 0:nfull, :],
                              src[r0:r0 + nfull * 128, c0:c0 + 512].rearrange("(k p) c -> p k c", p=128),
                              reads=[r_src], writes=[tir], join=True)
                    if rn % 128:
                        mm = rn % 128
                        S.dma("sp", ti[0:mm, nfull, :], src[r0 + nfull * 128:r0 + rn, c0:c0 + 512],
                              reads=[r_src], writes=[tir], join=True)
                    for j in range(4):
                        for k4 in range(0, nch, 4):
                            pbt, pbr = self.pb[ev % 4]
                            kn = min(4, nch - k4)
                            wtot = 0
                            for k in range(k4, k4 + kn):
                                m = min(128, rn - k * 128)
                                S.op("pe", lambda e, ti=ti, k=k, j=j, m=m, pbt=pbt, k4=k4: e.transpose(
                                    pbt[:, (k - k4) * 128:(k - k4) * 128 + m], ti[0:m, k, j * 128:(j + 1) * 128], self.ident[0:m, 0:m]),
                                    reads=[tir, self.r_ident], writes=[pbr])
                                wtot = (k - k4) * 128 + m
                            dsl = to[:, j, k4 * 128:k4 * 128 + wtot]
                            if ev % 2 == 0:
                                S.op("act", lambda e, dsl=dsl, pbt=pbt, wtot=wtot: e.copy(dsl, pbt[:, 0:wtot]), reads=[pbr], writes=[tor])
                            else:
                                S.op("dve", lambda e, dsl=dsl, pbt=pbt, wtot=wtot: e.tensor_copy(dsl, pbt[:, 0:wtot]), reads=[pbr], writes=[tor])
                            ev += 1
                    S.dma("pool", dst[c0:c0 + 512, r0:r0 + rn].rearrange("(j p) r -> p j r", p=128), to[:, :, 0:rn],
                          reads=[tor], writes=[r_dst])

    def stage_ssd(self, l):
        S, I, SL = self.S, self.I, self.S_len
        pb = self.pb
        A_, B_, C_, Z_, CB_ = (pb[0], pb[1]), (pb[2], pb[3]), (pb[4], pb[5]), pb[6], pb[7]
        with Stage(self) as stg:
            dbc, r_dbc = stg.sb([64, 32], F32, "dbc")
            ngb, r_ngb = stg.sb([64, D], F32, "ngb")
            S.dma("sp", dbc[:], I["dsk32"][l:l + 1, :].to_broadcast([64, 32]), writes=[r_dbc])
            S.dma("sp", ngb[:], I["ng_row"][l:l + 1, :].to_broadcast([64, D]), writes=[r_ngb])
            h32, r_h32 = stg.sb([128, D], F32, "h32")
            h16, r_h16 = stg.sb([128, D], BF16, "h16")
            S.op("dve", lambda e: e.memset(h32[:], 0.0), writes=[r_h32])
            S.op("dve", lambda e: e.memset(h16[:], 0.0), writes=[r_h16])
            tokx_ = [stg.sb([64, 4096], F32, "tokx") for _ in range(3)]
            dta_ = [stg.sb([64, 64], F32, "dta") for _ in range(3)]
            bcb_ = [stg.sb([128, 16, 256], BF16, "bcb") for _ in range(2)]
            acs, r_acs = stg.sb([64, 32], F32, "acs")
            ecs_ = [stg.sb([64, 32], F32, "ecs") for _ in range(2)]
            dte, r_dte = stg.sb([64, 32], F32, "dte")
            cd_ = [stg.sb([128, 32], F32, "cd") for _ in range(2)]
            R1h, r_R1h = stg.sb([64, 32, 64], BF16, "R1h")
            R1l, r_R1l = stg.sb([64, 32, 64], BF16, "R1l")
            R2h, r_R2h = stg.sb([64, 32, 64], BF16, "R2h")
            R2l, r_R2l = stg.sb([64, 32, 64], BF16, "R2l")
            ah16, r_ah16 = stg.sb([64, 32], BF16, "ah16")
            ah32, r_ah32 = stg.sb([64, 32], F32, "ah32")
            al32, r_al32 = stg.sb([64, 32], F32, "al32")
            xdt_ = [stg.sb([64, D], BF16, "xdt") for _ in range(2)]
            xdtd_ = [stg.sb([64, D], BF16, "xdtd") for _ in range(2)]
            bt16_ = [stg.sb([64, 1024], BF16, "bt16") for _ in range(3)]
            cbs, r_cbs = stg.sb([64, 512], BF16, "cbs")
            LT, r_LT = stg.sb([64, D], BF16, "LT")
            MT_ = [stg.sb([64, D], BF16, "MT") for _ in range(2)]
            yv, r_yv = stg.sb([64, 1024], F32, "yv")
            t2, r_t2 = stg.sb([64, 1024], F32, "t2")
            gb, r_gb = stg.sb([64, D], F32, "gb")
            gn_ = [stg.sb([64, D], F32, "gn") for _ in range(1)]
            hs, r_hs = stg.sb([128, 1024], F32, "hs")
            ss, r_ss = stg.sb([64, 2], F32, "ss")
            triu, ntriu, ones, r3, ident = self.triu, self.ntriu, self.ones, self.r3, self.ident
            nchunk = SL // 64

            def loads(c):
                t0 = c * 64
                tokx, r_tokx = tokx_[c % 3]
                dta, r_dta = dta_[c % 3]
                bt16, r_bt16 = bt16_[c % 3]
                if c % 4 == 0:
                    bcb, r_bcb = bcb_[(c // 4) % 2]
                    S.dma("sp", bcb[:], self.BCT[:, t0:t0 + 256].rearrange("(g n) t -> n g t", n=128), writes=[r_bcb])
                S.dma("sp", tokx[:, 0:D], self.TOK[t0:t0 + 64, 0:D], writes=[r_tokx], join=True)
                S.dma("sp", tokx[:, D:2 * D], self.TOK[t0:t0 + 64, T_SZ:T_SZ + D], writes=[r_tokx], join=True)
                S.dma("sp", dta[:], self.TOK[t0:t0 + 64, T_DT:T_DT + 64], writes=[r_dta])
                S.dma("pool", bt16[:], self.TOK[t0:t0 + 64, T_B:T_B + 1024], writes=[r_bt16])

            def phase1_a(c):
                tokx, r_tokx = tokx_[c % 3]
                dta, r_dta = dta_[c % 3]
                bcb, r_bcb = bcb_[(c // 4) % 2]
                ecs, r_ecs = ecs_[c % 2]
                cd, r_cd = cd_[c % 2]
                xdt, r_xdt = xdt_[c % 2]
                xdtd, r_xdtd = xdtd_[c % 2]
                MT, r_MT = MT_[c % 2]
                tq = (c % 4) * 64
                dt_ap = dta[:, 0:32]
                a_ap = dta[:, 32:64]
                zt, zr = Z_
                S.op("pe", lambda e: e.matmul(zt[0:64, 0:32], lhsT=triu[:, :], rhs=a_ap, start=True, stop=True),
                     reads=[r_dta, self.r_triu], writes=[zr])
                S.op("pe", lambda e: e.matmul(zt[:, 32:64], lhsT=ones[0:64, :], rhs=a_ap, start=True, stop=True),
                     reads=[r_dta, self.r_ones], writes=[zr])
                S.op("act", lambda e: e.copy(acs[:], zt[0:64, 0:32]), reads=[zr], writes=[r_acs])
                S.op("act", lambda e: e.activation(out=ecs[:], in_=zt[0:64, 0:32], func=AF.Exp), reads=[zr], writes=[r_ecs])
                S.op("act", lambda e: e.activation(out=cd[:], in_=zt[:, 32:64], func=AF.Exp), reads=[zr], writes=[r_cd])
                S.op("dve", lambda e: e.tensor_tensor(out=dte[:], in0=zt[0:64, 32:64], in1=acs[:], op=ALU.subtract),
                     reads=[zr, r_acs], writes=[r_dte])
                S.op("act", lambda e: e.activation(out=dte[:], in_=dte[:], func=AF.Exp), reads=[r_dte], writes=[r_dte])
                S.op("act", lambda e: e.copy(ah16[:], a_ap), reads=[r_dta], writes=[r_ah16])
                S.op("act", lambda e: e.copy(ah32[:], ah16[:]), reads=[r_ah16], writes=[r_ah32])
                S.op("dve", lambda e: e.tensor_tensor(out=al32[:], in0=a_ap, in1=ah32[:], op=ALU.subtract),
                     reads=[r_dta, r_ah32], writes=[r_al32])
                S.op("dve", lambda e: e.tensor_tensor(
                    out=R1h[:], in0=ah32[:, :].unsqueeze(2).to_broadcast([64, 32, 64]),
                    in1=triu[:, :].unsqueeze(1).to_broadcast([64, 32, 64]), op=ALU.mult),
                    reads=[r_ah32, self.r_triu], writes=[r_R1h])
                S.op("dve", lambda e: e.tensor_tensor(
                    out=R1l[:], in0=al32[:, :].unsqueeze(2).to_broadcast([64, 32, 64]),
                    in1=triu[:, :].unsqueeze(1).to_broadcast([64, 32, 64]), op=ALU.mult),
                    reads=[r_al32, self.r_triu], writes=[r_R1l])
                S.op("pool", lambda e: e.tensor_copy(R2h[:], ah32[:, :].unsqueeze(2).to_broadcast([64, 32, 64])),
                     reads=[r_ah32], writes=[r_R2h])
                S.op("pool", lambda e: e.tensor_copy(R2l[:], al32[:, :].unsqueeze(2).to_broadcast([64, 32, 64])),
                     reads=[r_al32], writes=[r_R2l])
                S.op("dve", lambda e: e.tensor_tensor(
                    out=xdt[:].rearrange("s (h p) -> s h p", h=32), in0=tokx[:, 0:D].rearrange("s (h p) -> s h p", h=32),
                    in1=dt_ap.unsqueeze(2).to_broadcast([64, 32, 64]), op=ALU.mult),
                    reads=[r_tokx, r_dta], writes=[r_xdt])
                S.op("dve", lambda e: e.tensor_tensor(
                    out=xdtd[:].rearrange("s (h p) -> s h p", h=32), in0=xdt[:].rearrange("s (h p) -> s h p", h=32),
                    in1=dte[:, :].unsqueeze(2).to_broadcast([64, 32, 64]), op=ALU.mult),
                    reads=[r_xdt, r_dte], writes=[r_xdtd])
                cbt, cbr = CB_
                for g in range(8):
                    S.op("pe", lambda e, g=g: e.matmul(
                        cbt[0:64, g * 64:(g + 1) * 64], lhsT=bcb[:, g, tq:tq + 64], rhs=bcb[:, 8 + g, tq:tq + 64],
                        start=True, stop=True), reads=[r_bcb], writes=[cbr], sig=(g == 7))
                S.op("act", lambda e: e.copy(cbs[:], cbt[0:64, :]), reads=[cbr], writes=[r_cbs])
            def phase1_e(c, qs):
                for q in qs:
                    at, ar = A_[q % 2]
                    hsl = slice(q * 8, q * 8 + 8)
                    S.op("pe", lambda e, at=at, hsl=hsl: e.matmul(at[0:64, :], lhsT=self.ones16[:, :], rhs=R1h[:, hsl, :],
                                                                  start=True, stop=False),
                         reads=[r_R1h, self.r_ones16], writes=[ar], sig=False)
                    S.op("pe", lambda e, at=at, hsl=hsl: e.matmul(at[0:64, :], lhsT=self.ones16[:, :], rhs=R1l[:, hsl, :],
                                                                  start=False, stop=False),
                         reads=[r_R1l, self.r_ones16], writes=[ar], sig=False)
                    S.op("pe", lambda e, at=at, hsl=hsl: e.matmul(at[0:64, :], lhsT=self.ntriu16[:, :], rhs=R2h[:, hsl, :],
                                                                  start=False, stop=False),
                         reads=[r_R2h, self.r_ntriu16], writes=[ar], sig=False)
                    S.op("pe", lambda e, at=at, hsl=hsl: e.matmul(at[0:64, :], lhsT=self.ntriu16[:, :], rhs=R2l[:, hsl, :],
                                                                  start=False, stop=False),
                         reads=[r_R2l, self.r_ntriu16], writes=[ar], sig=False)
                    S.op("pe", lambda e, at=at, hsl=hsl: e.matmul(at[0:64, :], lhsT=self.ident16[:, :], rhs=self.r3_16[:, hsl, :],
                                                                  start=False, stop=True),
                         reads=[self.r_r3_16, self.r_ident16], writes=[ar])
                    S.op("act", lambda e, at=at, q=q: e.activation(out=LT[:, q * 512:(q + 1) * 512], in_=at[0:64, :], func=AF.Exp),
                         reads=[ar], writes=[r_LT])
            def phase1_mt(c):
                MT, r_MT = MT_[c % 2]
                S.op("dve", lambda e: e.tensor_tensor(
                    out=MT[:].rearrange("s (g r t) -> s g r t", g=8, r=4),
                    in0=LT[:].rearrange("s (g r t) -> s g r t", g=8, r=4),
                    in1=cbs[:, :].rearrange("s (g t) -> s g t", g=8).unsqueeze(2).to_broadcast([64, 8, 4, 64]),
                    op=ALU.mult), reads=[r_LT, r_cbs], writes=[r_MT])

            def phase2_half(c, hh):
                t0 = c * 64
                tokx, r_tokx = tokx_[c % 3]
                bcb, r_bcb = bcb_[(c // 4) % 2]
                ecs, r_ecs = ecs_[c % 2]
                cd, r_cd = cd_[c % 2]
                xdt, r_xdt = xdt_[c % 2]
                xdtd, r_xdtd = xdtd_[c % 2]
                bt16, r_bt16 = bt16_[c % 3]
                MT, r_MT = MT_[c % 2]
                gnt, r_gnt = gn_[0]
                tq = (c % 4) * 64
                if True:
                    hs0 = hh * 16
                    for h in range(16):
                        bt_, br_ = B_[h // 8]
                        S.op("pe", lambda e, h=h, bt_=bt_, hs0=hs0: e.matmul(
                            bt_[0:64, (h % 8) * 64:(h % 8 + 1) * 64], lhsT=MT[:, (hs0 + h) * 64:(hs0 + h + 1) * 64],
                            rhs=xdt[:, (hs0 + h) * 64:(hs0 + h + 1) * 64], start=True, stop=True),
                            reads=[r_MT, r_xdt], writes=[br_], sig=(h % 8 == 7))
                    for g in range(4):
                        ct_, cr_ = C_[g // 2]
                        gg = hh * 4 + g
                        S.op("pe", lambda e, g=g, gg=gg, ct_=ct_: e.matmul(
                            ct_[0:64, (g % 2) * 256:(g % 2 + 1) * 256], lhsT=bcb[:, 8 + gg, tq:tq + 64],
                            rhs=h16[:, gg * 256:(gg + 1) * 256], start=True, stop=True),
                            reads=[r_bcb, r_h16], writes=[cr_], sig=(g % 2 == 1))
                    for b in range(2):
                        ct_, cr_ = C_[b]
                        bt_, br_ = B_[b]
                        S.op("dve", lambda e, ct_=ct_, b=b, hs0=hs0: e.tensor_tensor(
                            out=yv[:, b * 512:(b + 1) * 512].rearrange("s (h p) -> s h p", h=8),
                            in0=ct_[0:64, :].rearrange("s (h p) -> s h p", h=8),
                            in1=ecs[:, hs0 + b * 8:hs0 + b * 8 + 8].unsqueeze(2).to_broadcast([64, 8, 64]), op=ALU.mult),
                            reads=[cr_, r_ecs], writes=[r_yv])
                        S.op("dve", lambda e, bt_=bt_, b=b: e.tensor_tensor(
                            out=yv[:, b * 512:(b + 1) * 512], in0=yv[:, b * 512:(b + 1) * 512], in1=bt_[0:64, :], op=ALU.add),
                            reads=[br_, r_yv], writes=[r_yv])
                    for g in range(4):
                        ct_, cr_ = C_[g // 2]
                        gg = hh * 4 + g
                        S.op("pe", lambda e, g=g, gg=gg, ct_=ct_: e.matmul(
                            ct_[:, (g % 2) * 256:(g % 2 + 1) * 256], lhsT=bt16[:, gg * 128:(gg + 1) * 128],
                            rhs=xdtd[:, gg * 256:(gg + 1) * 256], start=True, stop=True),
                            reads=[r_bt16, r_xdtd], writes=[cr_], sig=(g % 2 == 1))
                    S.op("pool", lambda e, hh=hh: e.tensor_tensor(
                        out=t2[:].rearrange("s (h p) -> s h p", h=16),
                        in0=tokx[:, hh * 1024:(hh + 1) * 1024].rearrange("s (h p) -> s h p", h=16),
                        in1=dbc[:, hh * 16:(hh + 1) * 16].unsqueeze(2).to_broadcast([64, 16, 64]), op=ALU.mult),
                        reads=[r_tokx, r_dbc], writes=[r_t2])
                    S.op("pool", lambda e: e.tensor_tensor(out=t2[:], in0=t2[:], in1=yv[:], op=ALU.add),
                         reads=[r_t2, r_yv], writes=[r_t2])
                    S.op("dve", lambda e, hh=hh: e.tensor_tensor(
                        out=gb[:, hh * 1024:(hh + 1) * 1024], in0=t2[:], in1=tokx[:, D + hh * 1024:D + (hh + 1) * 1024], op=ALU.mult),
                        reads=[r_t2, r_tokx], writes=[r_gb])
                    S.op("dve", lambda e, hh=hh, hs0=hs0: e.tensor_tensor(
                        out=hs[:].rearrange("n (h p) -> n h p", h=16),
                        in0=h32[:, hh * 1024:(hh + 1) * 1024].rearrange("n (h p) -> n h p", h=16),
                        in1=cd[:, hs0:hs0 + 16].unsqueeze(2).to_broadcast([128, 16, 64]), op=ALU.mult),
                        reads=[r_h32, r_cd], writes=[r_hs])
                    for b in range(2):
                        ct_, cr_ = C_[b]
                        S.op("dve", lambda e, ct_=ct_, b=b, hh=hh: e.tensor_tensor(
                            out=h32[:, hh * 1024 + b * 512:hh * 1024 + (b + 1) * 512], in0=hs[:, b * 512:(b + 1) * 512],
                            in1=ct_[:, :], op=ALU.add), reads=[r_hs, cr_], writes=[r_h32])
                    S.op("act", lambda e, hh=hh: e.copy(h16[:, hh * 1024:(hh + 1) * 1024], h32[:, hh * 1024:(hh + 1) * 1024]),
                         reads=[r_h32], writes=[r_h16])
            def phase2_tail(c):
                t0 = c * 64
                gnt, r_gnt = gn_[0]
                S.op("act", lambda e: e.activation(out=gnt[:], in_=gb[:], func=AF.Square, accum_out=ss[:, 0:1]),
                     reads=[r_gb], writes=[r_gnt, r_ss])
                S.op("act", lambda e: e.activation(out=ss[:, 1:2], in_=ss[:, 0:1], func=AF.Sqrt, scale=1.0 / D, bias=self.eps_t[0:64, 0:1]),
                     reads=[r_ss, self.r_eps], writes=[r_ss])
                S.op("dve", lambda e: e.reciprocal(out=ss[:, 1:2], in_=ss[:, 1:2]), reads=[r_ss], writes=[r_ss])
                S.op("dve", lambda e: e.scalar_tensor_tensor(out=gnt[:], in0=gb[:], scalar=ss[:, 1:2], in1=ngb[:],
                                                             op0=ALU.mult, op1=ALU.mult),
                     reads=[r_gb, r_ss, r_ngb], writes=[r_gnt])
                S.dma("sp", self.TOK2[t0:t0 + 64, 0:D], gnt[:], reads=[r_gnt])

            loads(0)
            if nchunk > 1:
                loads(1)
            phase1_a(0)
            phase1_e(0, (0, 1, 2, 3))
            phase1_mt(0)
            for c in range(nchunk):
                if c + 2 < nchunk:
                    loads(c + 2)
                nx = c + 1 < nchunk
                if nx:
                    phase1_a(c + 1)
                phase2_half(c, 0)
                if nx:
                    phase1_e(c + 1, (0, 1))
                phase2_half(c, 1)
                if nx:
                    phase1_e(c + 1, (2, 3))
                    phase1_mt(c + 1)
                phase2_tail(c)

    def stage_attn(self, l):
        S, I, SL = self.S, self.I, self.S_len
        pb = self.pb
        NT = SL // 128
        NQB = SL // 512
        NBIS = 22
        scale = 128.0 ** -0.5
        with Stage(self) as stg:
            ik2, r_ik2 = stg.sb([128, SL], BF16, "ik2")
            S.dma("sp", ik2[0:64, :], self.ikT[:, :], writes=[r_ik2], join=True)
            S.dma("sp", ik2[64:128, :], self.ikT[:, :], writes=[r_ik2], join=True)
            acc, r_acc = stg.sb([128, SL], F32, "acc")
            wk, r_wk = stg.sb([128, SL], F32, "wk")
            bs, r_bs = stg.sb([128, 8], F32, "bs")
            wtab, r_wtab = stg.sb([128, 32], F32, "wtab")
            mT_ = [stg.sb([128, NT, 512], BF16, "maskT") for _ in range(2)]
            iq_ = [stg.sb([128, 8, 128], BF16, "iqt") for _ in range(2)]
            wt_ = [stg.sb([128, 16], F32, "wtok") for _ in range(2)]
            rb_ = [stg.sb([128, 512], F32, "rbuf") for _ in range(3)]
            kh_ = [stg.sb([128, SL], BF16, "kh") for _ in range(2)]
            qh_ = [stg.sb([128, 512], BF16, "qh") for _ in range(2)]
            va_ = [stg.sb([128, NT, 129], BF16, "va") for _ in range(2)]
            pt_ = [stg.sb([128, 512], BF16, "pt") for _ in range(4)]
            ya_ = [stg.sb([128, 4, 1024], BF16, "ya") for _ in range(1)]
            rd, r_rd = stg.sb([128, 4], F32, "rd")
            for v in range(2):
                S.op("pool", lambda e, v=v: e.memset(va_[v][0][:, :, 128:129], 1.0), writes=[va_[v][1]])
            st = {"qi": 0, "ri": 0, "pi": 0, "hi": 0}

            def index_tile(qb, u):
                maskT, r_maskT = mT_[qb % 2]
                qt = 4 * qb + u
                t0 = qt * 128
                Kq = 128 * (qt + 1)
                iqt, r_iqt = iq_[st["qi"] % 2]
                wtk, r_wtk = wt_[st["qi"] % 2]
                st["qi"] += 1
                S.dma("sp", iqt[:], self.iqT[:, t0:t0 + 128].rearrange("(c p) t -> p c t", p=128), writes=[r_iqt])
                S.dma("sp", wtk[:], self.TOK[t0:t0 + 128, T_IW:T_IW + 16], writes=[r_wtk])
                for s0 in range(0, Kq, 512):
                    ncol = min(512, Kq - s0)
                    for h in range(16):
                        pbt, pbr = pb[2 + h % 2]
                        hp = (h % 2) * 64
                        S.op("pe", lambda e, iqt=iqt, h=h, hp=hp, s0=s0, ncol=ncol, pbt=pbt: e.matmul(
                            pbt[:, 0:ncol], lhsT=iqt[hp:hp + 64, h // 2, :], rhs=ik2[hp:hp + 64, s0:s0 + ncol],
                            start=True, stop=True), reads=[r_iqt, r_ik2], writes=[pbr])
                        rbt, rbr = rb_[st["ri"] % 3]
                        st["ri"] += 1
                        S.op("act", lambda e, rbt=rbt, pbt=pbt, ncol=ncol: e.activation(
                            out=rbt[:, 0:ncol], in_=pbt[:, 0:ncol], func=AF.Relu), reads=[pbr], writes=[rbr])
                        if h == 0:
                            S.op("dve", lambda e, rbt=rbt, wtk=wtk, s0=s0, ncol=ncol: e.tensor_scalar(
                                out=acc[:, s0:s0 + ncol], in0=rbt[:, 0:ncol], scalar1=wtk[:, 0:1], scalar2=None, op0=ALU.mult),
                                reads=[rbr, r_wtk], writes=[r_acc])
                        else:
                            S.op("dve", lambda e, rbt=rbt, wtk=wtk, s0=s0, ncol=ncol, h=h: e.scalar_tensor_tensor(
                                out=acc[:, s0:s0 + ncol], in0=rbt[:, 0:ncol], scalar=wtk[:, h:h + 1], in1=acc[:, s0:s0 + ncol],
                                op0=ALU.mult, op1=ALU.add), reads=[rbr, r_wtk, r_acc], writes=[r_acc])
                if Kq > 256:
                    S.op("dve", lambda e, Kq=Kq: e.tensor_reduce(out=bs[:, 0:1], in_=acc[:, 0:Kq], axis=mybir.AxisListType.X, op=ALU.min),
                         reads=[r_acc], writes=[r_bs])
                    S.op("dve", lambda e, Kq=Kq: e.tensor_reduce(out=bs[:, 5:6], in_=acc[:, 0:Kq], axis=mybir.AxisListType.X, op=ALU.max),
                         reads=[r_acc], writes=[r_bs])
                    S.op("dve", lambda e: e.tensor_tensor(out=bs[:, 1:2], in0=bs[:, 5:6], in1=bs[:, 0:1], op=ALU.subtract),
                         reads=[r_bs], writes=[r_bs])
                    S.op("dve", lambda e: e.tensor_scalar(out=bs[:, 1:2], in0=bs[:, 1:2], scalar1=1.0001, scalar2=1e-6,
                                                          op0=ALU.mult, op1=ALU.add), reads=[r_bs], writes=[r_bs])
                S.op("dve", lambda e, Kq=Kq: e.memset(acc[0:64, Kq - 64:Kq], NEG), reads=[r_acc], writes=[r_acc])
                if Kq > 256:
                    S.op("dve", lambda e: e.tensor_scalar(out=wtab[:, 0:NBIS + 2], in0=self.p2tab[:, 0:NBIS + 2], scalar1=bs[:, 1:2], scalar2=None,
                                                          op0=ALU.mult), reads=[r_bs, self.r_p2tab], writes=[r_wtab])
                    S.op("dve", lambda e: e.tensor_tensor(out=bs[:, 2:3], in0=bs[:, 0:1], in1=wtab[:, 1:2], op=ALU.add),
                         reads=[r_bs, r_wtab], writes=[r_bs])
                    for k in range(NBIS):
                        S.op("dve", lambda e, Kq=Kq: e.tensor_scalar(out=wk[:, 0:Kq], in0=acc[:, 0:Kq], scalar1=bs[:, 2:3], scalar2=0.0,
                                                                     op0=ALU.is_ge, op1=ALU.add, accum_out=bs[:, 3:4]),
                             reads=[r_acc, r_bs], writes=[r_wk, r_bs])
                        S.op("dve", lambda e: e.tensor_scalar(out=bs[:, 4:5], in0=bs[:, 3:4], scalar1=256.0, scalar2=0.5,
                                                              op0=ALU.is_ge, op1=ALU.subtract), reads=[r_bs], writes=[r_bs])
                        S.op("dve", lambda e, k=k: e.scalar_tensor_tensor(out=bs[:, 2:3], in0=bs[:, 4:5], scalar=wtab[:, k + 1:k + 2],
                                                                          in1=bs[:, 2:3], op0=ALU.mult, op1=ALU.add),
                             reads=[r_bs, r_wtab], writes=[r_bs])
                    S.op("dve", lambda e: e.tensor_tensor(out=bs[:, 0:1], in0=bs[:, 2:3], in1=wtab[:, NBIS + 1:NBIS + 2], op=ALU.subtract),
                         reads=[r_bs, r_wtab], writes=[r_bs])
                    thr_ap, thr_r = bs, r_bs
                else:
                    thr_ap, thr_r = self.nbig, self.r_nbig
                S.op("dve", lambda e, Kq=Kq, thr_ap=thr_ap: e.tensor_scalar(
                    out=wk[:, 0:Kq], in0=acc[:, 0:Kq], scalar1=thr_ap[:, 0:1], scalar2=None, op0=ALU.is_ge),
                    reads=[r_acc, thr_r], writes=[r_wk])

            def mask_transposes(qb, u):
                maskT, r_maskT = mT_[qb % 2]
                qt = 4 * qb + u
                for j4 in range(0, qt + 1, 4):
                    jn = min(4, qt + 1 - j4)
                    pbt, pbr = pb[2]
                    for j in range(j4, j4 + jn):
                        S.op("pe", lambda e, j=j, j4=j4, pbt=pbt: e.transpose(
                            pbt[:, (j - j4) * 128:(j - j4 + 1) * 128], wk[:, j * 128:(j + 1) * 128], self.ident[:, :]),
                            reads=[r_wk, self.r_ident], writes=[pbr])
                    S.op("act", lambda e, j4=j4, jn=jn, u=u, pbt=pbt, maskT=maskT: e.copy(
                        maskT[:, j4:j4 + jn, u * 128:(u + 1) * 128], pbt[:, 0:jn * 128].rearrange("p (j t) -> p j t", j=jn)),
                        reads=[pbr], writes=[r_maskT])

            def head_loads(qb, h):
                nk = 4 * (qb + 1)
                K = nk * 128
                kh, r_kh = kh_[h % 2]
                qh, r_qh = qh_[h % 2]
                va, r_va = va_[h % 2]
                S.dma("sp", kh[:, 0:K], self.kT[h * 128:(h + 1) * 128, 0:K], writes=[r_kh])
                S.dma("sp", qh[:, :], self.qT[h * 128:(h + 1) * 128, qb * 512:(qb + 1) * 512], writes=[r_qh])
                for j8 in range(0, nk, 8):
                    jn8 = min(8, nk - j8)
                    S.dma("pool", va[:, j8:j8 + jn8, 0:128],
                          self.TOK[j8 * 128:(j8 + jn8) * 128, T_V + h * 128:T_V + (h + 1) * 128].rearrange("(j s) d -> s j d", s=128),
                          writes=[r_va], join=True)

            def attn_head(qb, h):
                maskT, r_maskT = mT_[qb % 2]
                ya, r_ya = ya_[0]
                nk = 4 * (qb + 1)
                kh, r_kh = kh_[h % 2]
                qh, r_qh = qh_[h % 2]
                va, r_va = va_[h % 2]
                def st_tile(j):
                    pbt, pbr = pb[j % 2]
                    S.op("pe", lambda e, pbt=pbt: e.matmul(
                        pbt[:, :], lhsT=kh[:, j * 128:(j + 1) * 128], rhs=qh[:, :], start=True, stop=True),
                        reads=[r_kh, r_qh], writes=[pbr])
                    pt, r_pt = pt_[st["pi"] % 4]
                    st["pi"] += 1
                    S.op("act", lambda e, pt=pt, pbt=pbt: e.activation(out=pt[:], in_=pbt[:, :], func=AF.Exp, scale=scale),
                         reads=[pbr], writes=[r_pt])
                    S.op("pool", lambda e, pt=pt: e.tensor_tensor(out=pt[:], in0=pt[:], in1=maskT[:, j, :], op=ALU.mult),
                         reads=[r_pt, r_maskT], writes=[r_pt])
                    return pt, r_pt

                def pv_tile(j, pt, r_pt):
                    for u in range(4):
                        jl = 4 * qb + u
                        if j > jl:
                            continue
                        ot, orr = pb[4 + u]
                        S.op("pe", lambda e, u=u, ot=ot, jl=jl: e.matmul(
                            ot[:, 0:129], lhsT=pt[:, u * 128:(u + 1) * 128], rhs=va[:, j, :], start=(j == 0), stop=(j == jl)),
                            reads=[r_pt, r_va], writes=[orr])

                nxt = st_tile(0)
                for j in range(nk):
                    cur = nxt
                    if j + 1 < nk:
                        nxt = st_tile(j + 1)
                    pv_tile(j, *cur)
                for u in range(4):
                    ot, orr = pb[4 + u]
                    S.op("act", lambda e, ot=ot, u=u: e.copy(rd[:, u:u + 1], ot[:, 128:129]), reads=[orr], writes=[r_rd])
                    S.op("dve", lambda e, u=u: e.reciprocal(out=rd[:, u:u + 1], in_=rd[:, u:u + 1]), reads=[r_rd], writes=[r_rd])
                    S.op("act", lambda e, ot=ot, u=u, h=h: e.activation(
                        out=ya[:, u, h * 128:(h + 1) * 128], in_=ot[:, 0:128], func=AF.Copy, scale=rd[:, u:u + 1]),
                        reads=[orr, r_rd], writes=[r_ya])
                if h == 7:
                    S.dma("pool", self.TOK2[qb * 512:(qb + 1) * 512, D:D + 1024].rearrange("(u p) c -> p u c", p=128), ya[:],
                          reads=[r_ya])

            for qb in range(NQB + 1):
                if qb < NQB:
                    nk = 4 * (qb + 1)
                    S.op("pool", lambda e, nk=nk, qb=qb: e.memset(mT_[qb % 2][0][:, 0:nk, :], 0.0), writes=[mT_[qb % 2][1]])
                if qb >= 1:
                    head_loads(qb - 1, 0)
                for u in range(4):
                    if qb < NQB:
                        index_tile(qb, u)
                    if qb >= 1:
                        for hh in range(2):
                            h = 2 * u + hh
                            if h + 1 < 8:
                                head_loads(qb - 1, h + 1)
                            attn_head(qb - 1, h)
                    if qb < NQB:
                        mask_transposes(qb, u)

    def stage_merge(self, l, xsrc):
        S, I, SL = self.S, self.I, self.S_len
        pb = self.pb
        TB = min(1024, SL)
        NH = TB // 512
        with Stage(self) as stg:
            a16, r_a16 = stg.sb([128, 24, TB], BF16, "a16")
            s16, r_s16 = stg.sb([128, 8, TB], BF16, "s16")
            mg, r_mg = stg.sb([128, KC, TB], BF16, "mg")
            xc_ = [stg.sb([128, TB], F32, "xc") for _ in range(3)]
            gt_ = [stg.sb([128, 3, 512], F32, "gt") for _ in range(2)]
            w_ = [(stg.sb([128, 16, 256], BF16, "w1"), stg.sb([128, 8, 256], BF16, "w2"), stg.sb([128, 8, 256], BF16, "w3"))
                  for _ in range(2)]
            m1, r_m1 = stg.sb([128, 512], F32, "m1")
            m2, r_m2 = stg.sb([128, 512], F32, "m2")
            gtm = self.mod[:, l, 32:48]
            jobs = []
            for tb in range(SL // TB):
                for dg in range(8):
                    jobs.append((tb, "br", dg))
                for dg in range(8):
                    jobs.append((tb, "out", dg))

            def wload(ji):
                tb, kind, dg = jobs[ji]
                (w1, r_w1), (w2, r_w2), (w3, r_w3) = w_[ji % 2]
                if kind == "br":
                    self.load_wgroup(w1, r_w1, I["w_br_ssd"][l], dg * 256, 256, 16)
                    self.load_wgroup(w2, r_w2, I["w_br_sc"][l], dg * 256, 256, 8)
                    self.load_wgroup(w3, r_w3, I["w_br_att"][l], dg * 256, 256, 8)
                else:
                    self.load_wgroup(w1, r_w1, I["w_out"][l], dg * 256, 256, 16)

            gi = 0
            xi = 0
            wload(0)
            for ji, (tb, kind, dg) in enumerate(jobs):
                t0 = tb * TB
                if ji + 1 < len(jobs):
                    wload(ji + 1)
                (w1, r_w1), (w2, r_w2), (w3, r_w3) = w_[ji % 2]
                if kind == "br" and dg == 0:
                    for k0 in range(0, 24, 8):
                        S.dma("pool", a16[:, k0:k0 + 8, :],
                              self.TT2[k0 * 128:(k0 + 8) * 128, t0:t0 + TB].rearrange("(k p) t -> p k t", p=128),
                              writes=[r_a16], join=True)
                    S.dma("sp", s16[:], self.yscT[:, t0:t0 + TB].rearrange("(k p) t -> p k t", p=128), writes=[r_s16])
                for j in range(2):
                    dc = dg * 2 + j
                    cs = slice(j * 128, (j + 1) * 128)
                    if kind == "br":
                        for hf in range(NH):
                            ts_ = slice(hf * 512, (hf + 1) * 512)
                            gt, r_gt = gt_[gi % 2]
                            S.dma("sp", gt[:], self.gT[:, t0 + hf * 512:t0 + (hf + 1) * 512].rearrange(
                                "(b dc p) t -> dc p b t", b=3, p=128)[dc], writes=[r_gt])
                            p1, p1r = pb[0 + (gi % 2) * 3]
                            p2, p2r = pb[1 + (gi % 2) * 3]
                            p3, p3r = pb[2 + (gi % 2) * 3]
                            gi += 1
                            for kc in range(16):
                                S.op("pe", lambda e, w1=w1, kc=kc, cs=cs, p1=p1, ts_=ts_: e.matmul(
                                    p1[:, :], lhsT=w1[:, kc, cs], rhs=a16[:, kc, ts_], start=(kc == 0), stop=(kc == 15)),
                                    reads=[r_w1, r_a16], writes=[p1r], sig=(kc == 15))
                            for kc in range(8):
                                S.op("pe", lambda e, w2=w2, kc=kc, cs=cs, p2=p2, ts_=ts_: e.matmul(
                                    p2[:, :], lhsT=w2[:, kc, cs], rhs=s16[:, kc, ts_], start=(kc == 0), stop=(kc == 7)),
                                    reads=[r_w2, r_s16], writes=[p2r], sig=(kc == 7))
                            for kc in range(8):
                                S.op("pe", lambda e, w3=w3, kc=kc, cs=cs, p3=p3, ts_=ts_: e.matmul(
                                    p3[:, :], lhsT=w3[:, kc, cs], rhs=a16[:, 16 + kc, ts_], start=(kc == 0), stop=(kc == 7)),
                                    reads=[r_w3, r_a16], writes=[p3r], sig=(kc == 7))
                            S.op("dve", lambda e, gt=gt, p1=p1: e.tensor_tensor(out=m1[:], in0=p1[:, :], in1=gt[:, 0, :], op=ALU.mult),
                                 reads=[p1r, r_gt], writes=[r_m1])
                            S.op("dve", lambda e, gt=gt, p2=p2: e.tensor_tensor(out=m2[:], in0=p2[:, :], in1=gt[:, 1, :], op=ALU.mult),
                                 reads=[p2r, r_gt], writes=[r_m2])
                            S.op("dve", lambda e: e.tensor_tensor(out=m1[:], in0=m1[:], in1=m2[:], op=ALU.add),
                                 reads=[r_m1, r_m2], writes=[r_m1])
                            S.op("dve", lambda e, gt=gt, p3=p3: e.tensor_tensor(out=m2[:], in0=p3[:, :], in1=gt[:, 2, :], op=ALU.mult),
                                 reads=[p3r, r_gt], writes=[r_m2])
                            S.op("dve", lambda e, dc=dc, ts_=ts_: e.tensor_tensor(out=mg[:, dc, ts_], in0=m1[:], in1=m2[:], op=ALU.add),
                                 reads=[r_m1, r_m2], writes=[r_mg])
                    else:
                        xc, r_xc = xc_[xi % 3]
                        xi += 1
                        S.dma("sp", xc[:], xsrc[dc * 128:(dc + 1) * 128, t0:t0 + TB], writes=[r_xc])
                        for hf in range(NH):
                            ts_ = slice(hf * 512, (hf + 1) * 512)
                            p1, p1r = pb[6 + hf % 2]
                            for kc in range(16):
                                S.op("pe", lambda e, w1=w1, kc=kc, cs=cs, p1=p1, ts_=ts_: e.matmul(
                                    p1[:, :], lhsT=w1[:, kc, cs], rhs=mg[:, kc, ts_], start=(kc == 0), stop=(kc == 15)),
                                    reads=[r_w1, r_mg], writes=[p1r], sig=(kc == 15))
                            S.op("dve", lambda e, dc=dc, p1=p1, xc=xc, ts_=ts_: e.scalar_tensor_tensor(
                                out=xc[:, ts_], in0=p1[:, :], scalar=gtm[:, dc:dc + 1], in1=xc[:, ts_], op0=ALU.mult, op1=ALU.add),
                                reads=[p1r, r_xc, self.r_mod], writes=[r_xc])
                        S.dma("sp", self.xres[dc * 128:(dc + 1) * 128, t0:t0 + TB], xc[:], reads=[r_xc])

    def stage_ffn(self, l):
        S, I, SL = self.S, self.I, self.S_len
        pb = self.pb
        gtf = self.mod[:, l, 80:96]
        with Stage(self) as stg:
            xt, r_xt = stg.sb([128, KC, 512], F32, "xt")
            hT, r_hT = stg.sb([128, KC, 512], BF16, "hT")
            aT, r_aT = stg.sb([128, HC, 512], BF16, "aT")
            sg_ = [stg.sb([128, 512], F32, "sg") for _ in range(2)]
            wb_ = [stg.sb([128, 44, 256], BF16, "wffn") for _ in range(2)]
            wi = 0
            for tb in range(SL // 512):
                t0 = tb * 512
                xv = self.xres[:, t0:t0 + 512].rearrange("(dc p) t -> p dc t", p=128)
                for q4 in range(2):
                    S.dma("sp", xt[:, q4 * 8:(q4 + 1) * 8, :], xv[:, q4 * 8:(q4 + 1) * 8, :], reads=[self.r_xres], writes=[r_xt], join=True)
                self.norm_adaln(stg, xt, r_xt, 512, self.gm[:, l, 1, :], self.mod[:, l, 48:64], hT, r_hT)
                for hg in range(HC // 2):
                    wt, wr = wb_[wi % 2]
                    wi += 1
                    srcg = I["w_g"][l][:, hg * 256:(hg + 1) * 256].rearrange("(kc p) n -> p kc n", p=128)
                    srcu = I["w_u"][l][:, hg * 256:(hg + 1) * 256].rearrange("(kc p) n -> p kc n", p=128)
                    S.dma("pool", wt[:, 0:16, :], srcg, writes=[wr], join=True)
                    S.dma("pool", wt[:, 16:32, :], srcu, writes=[wr], join=True)
                    for j in range(2):
                        hc = hg * 2 + j
                        cs = slice(j * 128, (j + 1) * 128)
                        pg, pgr = pb[(hc % 2) * 2]
                        pu, pur = pb[(hc % 2) * 2 + 1]
                        for kc in range(16):
                            S.op("pe", lambda e, wt=wt, kc=kc, cs=cs, pg=pg: e.matmul(
                                pg[:, :], lhsT=wt[:, kc, cs], rhs=hT[:, kc, :], start=(kc == 0), stop=(kc == 15)),
                                reads=[wr, r_hT], writes=[pgr], sig=(kc == 15))
                        for kc in range(16):
                            S.op("pe", lambda e, wt=wt, kc=kc, cs=cs, pu=pu: e.matmul(
                                pu[:, :], lhsT=wt[:, 16 + kc, cs], rhs=hT[:, kc, :], start=(kc == 0), stop=(kc == 15)),
                                reads=[wr, r_hT], writes=[pur], sig=(kc == 15))
                        sg, r_sg = sg_[hc % 2]
                        S.op("act", lambda e, sg=sg, pg=pg: e.activation(out=sg[:], in_=pg[:, :], func=AF.Silu), reads=[pgr], writes=[r_sg])
                        S.op("dve", lambda e, sg=sg, pu=pu, hc=hc: e.tensor_tensor(out=aT[:, hc, :], in0=sg[:], in1=pu[:, :], op=ALU.mult),
                             reads=[r_sg, pur], writes=[r_aT])
                for dg in range(8):
                    wt, wr = wb_[wi % 2]
                    wi += 1
                    src = I["w_d"][l][:, dg * 256:(dg + 1) * 256].rearrange("(kc p) n -> p kc n", p=128)
                    S.dma("pool", wt[:, 0:22, :], src[:, 0:22, :], writes=[wr], join=True)
                    S.dma("pool", wt[:, 22:44, :], src[:, 22:44, :], writes=[wr], join=True)
                    for j in range(2):
                        dc = dg * 2 + j
                        cs = slice(j * 128, (j + 1) * 128)
                        p1, p1r = pb[4 + dc % 4]
                        for kc in range(HC):
                            S.op("pe", lambda e, wt=wt, kc=kc, cs=cs, p1=p1: e.matmul(
                                p1[:, :], lhsT=wt[:, kc, cs], rhs=aT[:, kc, :], start=(kc == 0), stop=(kc == HC - 1)),
                                reads=[wr, r_aT], writes=[p1r], sig=(kc == HC - 1))
                        S.op("dve", lambda e, dc=dc, p1=p1: e.scalar_tensor_tensor(
                            out=xt[:, dc, :], in0=p1[:, :], scalar=gtf[:, dc:dc + 1], in1=xt[:, dc, :], op0=ALU.mult, op1=ALU.add),
                            reads=[p1r, r_xt, self.r_mod], writes=[r_xt])
                for q4 in range(2):
                    S.dma("sp", self.xres[:, t0:t0 + 512].rearrange("(dc p) t -> p dc t", p=128)[:, q4 * 8:(q4 + 1) * 8, :],
                          xt[:, q4 * 8:(q4 + 1) * 8, :], reads=[r_xt], writes=[self.r_xres])

    def stage_final(self):
        S, I, SL = self.S, self.I, self.S_len
        with Stage(self) as stg:
            xt, r_xt = stg.sb([128, KC, 512], F32, "xt")
            ho, r_ho = stg.sb([128, KC, 512], F32, "ho")
            gf, r_gf = stg.sb([128, KC], F32, "gf")
            S.dma("sp", gf[:], I["g_final_pk"], writes=[self.r_gm])
            for tb in range(SL // 512):
                t0 = tb * 512
                xv = self.xres[:, t0:t0 + 512].rearrange("(dc p) t -> p dc t", p=128)
                for q4 in range(2):
                    S.dma("sp", xt[:, q4 * 8:(q4 + 1) * 8, :], xv[:, q4 * 8:(q4 + 1) * 8, :], reads=[self.r_xres], writes=[r_xt], join=True)
                self.norm_adaln(stg, xt, r_xt, 512, gf, None, ho, r_ho)
                for q4 in range(2):
                    S.dma("sp", self.outT[:, t0:t0 + 512].rearrange("(dc p) t -> p dc t", p=128)[:, q4 * 8:(q4 + 1) * 8, :],
                          ho[:, q4 * 8:(q4 + 1) * 8, :], reads=[r_ho])


def _pk(v, n):
    return np.ascontiguousarray(np.asarray(v, np.float32).reshape(n, 128).T)


def make_inputs(inp, b, depth):
    L = depth
    f = lambda a: np.ascontiguousarray(np.asarray(a, np.float32))
    m = {}
    m["xT"] = np.ascontiguousarray(np.asarray(inp["x"][b], np.float32).T)
    m["c_pk"] = _pk(inp["c"][b], KC)
    m["w_ada"] = f(inp["w_ada"][:L])
    m["b_ada_pk"] = np.stack([_pk(inp["b_ada"][l], 96) for l in range(L)])
    m["g_mix_pk"] = np.stack([_pk(inp["g_mix"][l], KC) for l in range(L)])
    m["g_ffn_pk"] = np.stack([_pk(inp["g_ffn"][l], KC) for l in range(L)])
    m["g_final_pk"] = _pk(inp["g_final"], KC)
    m["w_in"] = f(inp["w_in"][:L])
    m["b_gate_pk"] = np.stack([_pk(np.asarray(inp["b_gate"][l]).reshape(-1), 48) for l in range(L)])
    cw = np.asarray(inp["ssd_conv_w"], np.float32)[:L]
    m["conv_w_pk"] = np.ascontiguousarray(cw.reshape(L, 4, 32, 128).transpose(0, 3, 2, 1))
    m["conv_b_pk"] = np.stack([_pk(inp["ssd_conv_b"][l], 32) for l in range(L)])
    m["dtb"] = f(np.asarray(inp["ssd_dt_bias"])[:L].reshape(L, 32, 1))
    m["alog"] = f(np.asarray(inp["ssd_a_log"])[:L].reshape(L, 32, 1))
    m["dsk32"] = f(np.asarray(inp["ssd_d"], np.float32)[:L])
    m["ng_row"] = f(inp["ssd_norm_g"][:L])
    sw = np.asarray(inp["sc_conv_w"], np.float32)[:L]
    m["scw_pk"] = np.ascontiguousarray(sw.reshape(L, 3, 8, 128).transpose(0, 3, 2, 1))
    m["w_br_ssd"] = f(inp["w_br_ssd"][:L])
    m["w_br_sc"] = f(inp["w_br_sc"][:L])
    m["w_br_att"] = f(inp["w_br_att"][:L])
    m["w_out"] = f(inp["w_out"][:L])
    m["w_g"] = f(inp["w_ffn_gate"][:L])
    m["w_u"] = f(inp["w_ffn_up"][:L])
    m["w_d"] = f(inp["w_ffn_down"][:L])
    return m


_CACHE = {}


def kernel(**inputs):
    x = np.asarray(inputs["x"])
    Bsz, SL, _ = x.shape
    depth = np.asarray(inputs["w_in"]).shape[0]
    key = (SL, depth)
    if key not in _CACHE:
        _CACHE[key] = Builder(SL, depth).build()
    nc = _CACHE[key]
    in_maps = [make_inputs(inputs, b, depth) for b in range(Bsz)]
    res = run_bass_kernel_spmd(nc, in_maps, core_ids=list(range(Bsz)))
    out = np.stack([np.ascontiguousarray(res.results[b]["outT"].T) for b in range(Bsz)])
    return out.astype(np.float32)
```
